# Optimizing a Trainium2 kernel written in Bass

```python
import math
import jax, jax.numpy as jnp
from jax import lax
import numpy as np

D_MODEL = 4096
BATCH = 4
SEQ = 4096
DEPTH = 1

CHUNK = 64
Q_BLOCK = 128
HEAD_DIM = 128
MIX_WIDTH = D_MODEL
H_FOX = (MIX_WIDTH // 2) // HEAD_DIM
H_DIFF = (MIX_WIDTH // 2) // (2 * HEAD_DIM)
N_BUCKETS = 32
MAX_DISTANCE = 128
N_MEM = 256
H_MEM = 4
MEM_HEAD_DIM = D_MODEL // H_MEM
D_FF = 4 * D_MODEL
NORM_EPS = 1e-6
SUBLN_EPS = 1e-5
NEG_INF = -1e30

W_FOX_Q = H_FOX * HEAD_DIM
W_FOX_K = H_FOX * HEAD_DIM
W_FOX_V = H_FOX * HEAD_DIM
W_FOX_F = H_FOX
W_DIFF_Q = H_DIFF * 2 * HEAD_DIM
W_DIFF_K = H_DIFF * 2 * HEAD_DIM
W_DIFF_V = H_DIFF * 2 * HEAD_DIM
IN_WIDTH = W_FOX_Q + W_FOX_K + W_FOX_V + W_FOX_F + W_DIFF_Q + W_DIFF_K + W_DIFF_V

kernel_name = "hybrid_fox_diffattn_stream_encoder"


def rms_norm(x, g, eps=NORM_EPS):
    xf = x.astype(jnp.float32)
    y = xf * lax.rsqrt(jnp.mean(xf * xf, axis=-1, keepdims=True) + eps)
    return (y * g.astype(jnp.float32)).astype(x.dtype)


def split_offsets():
    widths = [W_FOX_Q, W_FOX_K, W_FOX_V, W_FOX_F, W_DIFF_Q, W_DIFF_K, W_DIFF_V]
    offs, acc = [], 0
    for w in widths[:-1]:
        acc += w
        offs.append(acc)
    return offs


def t5_bucket(rel):
    half = N_BUCKETS // 2
    max_exact = half // 2
    ret = jnp.where(rel > 0, half, 0)
    n = jnp.abs(rel)
    nf = jnp.maximum(n, 1).astype(jnp.float32)
    large = max_exact + (jnp.log(nf / max_exact) / math.log(MAX_DISTANCE / max_exact)
                         * (half - max_exact)).astype(jnp.int32)
    large = jnp.minimum(large, half - 1)
    return ret + jnp.where(n < max_exact, n, large)


def forgetting_attention(q, k, v, log_f):
    B, S, H, Dh = q.shape
    nb = S // Q_BLOCK
    cum = jnp.cumsum(log_f, axis=1).transpose(0, 2, 1)
    cum_b = cum.reshape(B, H, nb, Q_BLOCK).transpose(2, 0, 1, 3)
    q_b = q.reshape(B, nb, Q_BLOCK, H, Dh).transpose(1, 0, 2, 3, 4)
    kf = k.astype(jnp.float32)
    kpos = jnp.arange(S)
    scale = Dh ** -0.5

    def block(args):
        i, qi, ci = args
        qpos = i * Q_BLOCK + jnp.arange(Q_BLOCK)
        logits = jnp.einsum('bqhd,bkhd->bhqk', qi.astype(jnp.float32), kf) * scale
        logits = logits + (ci[..., :, None] - cum[:, :, None, :])
        mask = kpos[None, :] <= qpos[:, None]
        logits = jnp.where(mask, logits, NEG_INF)
        p = jax.nn.softmax(logits, axis=-1).astype(v.dtype)
        return jnp.einsum('bhqk,bkhd->bqhd', p, v)

    out = lax.map(block, (jnp.arange(nb), q_b, cum_b))
    return out.transpose(1, 0, 2, 3, 4).reshape(B, S, H, Dh)


def differential_attention(q, k, v, lam, rel_bias, g_subln, lambda_init):
    B, S, H, _, Dh = q.shape
    nb = S // Q_BLOCK
    q_b = q.reshape(B, nb, Q_BLOCK, H, 2, Dh).transpose(1, 0, 2, 3, 4, 5)
    kf = k.astype(jnp.float32)
    kpos = jnp.arange(S)
    scale = Dh ** -0.5

    def block(args):
        i, qi = args
        qpos = i * Q_BLOCK + jnp.arange(Q_BLOCK)
        bias = rel_bias[t5_bucket(kpos[None, :] - qpos[:, None])]
        bias = bias.transpose(2, 0, 1).astype(jnp.float32)
        logits = jnp.einsum('bqhcd,bkhcd->bchqk', qi.astype(jnp.float32), kf) * scale + bias
        mask = (kpos // CHUNK)[None, :] <= (qpos // CHUNK)[:, None]
        logits = jnp.where(mask, logits, NEG_INF)
        p = jax.nn.softmax(logits, axis=-1)
        a = p[:, 0] - lam * p[:, 1]
        return jnp.einsum('bhqk,bkhe->bqhe', a.astype(v.dtype), v)

    out = lax.map(block, (jnp.arange(nb), q_b))
    out = out.transpose(1, 0, 2, 3, 4).reshape(B, S, H, 2 * Dh)
    out = rms_norm(out, g_subln, eps=SUBLN_EPS)
    return out * (1.0 - lambda_init)


def memory_cross_attention(c, m, wq, wk, wv, wo):
    B, S, _ = c.shape
    M = m.shape[1]
    q = (c @ wq).reshape(B, S, H_MEM, MEM_HEAD_DIM)
    k = (m @ wk).reshape(B, M, H_MEM, MEM_HEAD_DIM)
    v = (m @ wv).reshape(B, M, H_MEM, MEM_HEAD_DIM)
    logits = jnp.einsum('bqhd,bmhd->bhqm', q.astype(jnp.float32), k.astype(jnp.float32)) * MEM_HEAD_DIM ** -0.5
    p = jax.nn.softmax(logits, axis=-1).astype(v.dtype)
    o = jnp.einsum('bhqm,bmhd->bqhd', p, v).reshape(B, S, H_MEM * MEM_HEAD_DIM)
    return o @ wo


def setup_inputs(seed: int = 0) -> dict:
    key = jax.random.key(seed)
    ks = jax.random.split(key, 24)
    D, L = D_MODEL, DEPTH

    def nrm(k, shape, scale):
        return jax.random.normal(k, shape, jnp.float32) * scale

    def gain(k, shape):
        return 1.0 + 0.02 * jax.random.normal(k, shape, jnp.float32)

    return {
        "x": nrm(ks[0], (BATCH, SEQ, D), 1.0),
        "mem": nrm(ks[1], (BATCH, N_MEM, D), 1.0),
        "g_mix": gain(ks[2], (L, D)),
        "w_in": nrm(ks[3], (L, D, IN_WIDTH), D ** -0.5),
        "b_forget": 2.0 + 0.1 * jax.random.normal(ks[4], (L, H_FOX), jnp.float32),
        "lambda_q1": nrm(ks[5], (L, HEAD_DIM), 0.1),
        "lambda_k1": nrm(ks[6], (L, HEAD_DIM), 0.1),
        "lambda_q2": nrm(ks[7], (L, HEAD_DIM), 0.1),
        "lambda_k2": nrm(ks[8], (L, HEAD_DIM), 0.1),
        "g_subln": gain(ks[9], (L, 2 * HEAD_DIM)),
        "rel_bias": nrm(ks[10], (N_BUCKETS, H_DIFF), 0.5),
        "w_out": nrm(ks[11], (L, MIX_WIDTH, D), MIX_WIDTH ** -0.5),
        "g_cross": gain(ks[12], (L, D)),
        "g_mem": gain(ks[13], (L, D)),
        "wq_mem": nrm(ks[14], (L, D, H_MEM * MEM_HEAD_DIM), D ** -0.5),
        "wk_mem": nrm(ks[15], (L, D, H_MEM * MEM_HEAD_DIM), D ** -0.5),
        "wv_mem": nrm(ks[16], (L, D, H_MEM * MEM_HEAD_DIM), D ** -0.5),
        "wo_mem": nrm(ks[17], (L, H_MEM * MEM_HEAD_DIM, D), (H_MEM * MEM_HEAD_DIM) ** -0.5),
        "g_mlp": gain(ks[18], (L, D)),
        "w_up": nrm(ks[19], (L, D, D_FF), D ** -0.5),
        "w_down": nrm(ks[20], (L, D_FF, D), D_FF ** -0.5),
        "g_final": gain(ks[21], (D,)),
    }


def reference(x, mem, g_mix, w_in, b_forget, lambda_q1, lambda_k1, lambda_q2, lambda_k2,
              g_subln, rel_bias, w_out, g_cross, g_mem, wq_mem, wk_mem, wv_mem, wo_mem,
              g_mlp, w_up, w_down, g_final):
    B, S, _ = x.shape
    offs = split_offsets()
    h = x
    for l in range(DEPTH):
        a = rms_norm(h, g_mix[l])
        proj = a @ w_in[l]
        q_f, k_f, v_f, f_logit, q_d, k_d, v_d = jnp.split(proj, offs, axis=-1)

        log_f = jax.nn.log_sigmoid((f_logit + b_forget[l]).astype(jnp.float32))
        fox = forgetting_attention(q_f.reshape(B, S, H_FOX, HEAD_DIM),
                                   k_f.reshape(B, S, H_FOX, HEAD_DIM),
                                   v_f.reshape(B, S, H_FOX, HEAD_DIM), log_f)

        lambda_init = 0.8 - 0.6 * math.exp(-0.3 * l)
        lam = (jnp.exp(jnp.sum(lambda_q1[l].astype(jnp.float32) * lambda_k1[l].astype(jnp.float32)))
               - jnp.exp(jnp.sum(lambda_q2[l].astype(jnp.float32) * lambda_k2[l].astype(jnp.float32)))
               + lambda_init)
        diff = differential_attention(q_d.reshape(B, S, H_DIFF, 2, HEAD_DIM),
                                      k_d.reshape(B, S, H_DIFF, 2, HEAD_DIM),
                                      v_d.reshape(B, S, H_DIFF, 2 * HEAD_DIM),
                                      lam, rel_bias, g_subln[l], lambda_init)

        mixed = jnp.concatenate([fox.reshape(B, S, W_FOX_V), diff.reshape(B, S, W_DIFF_V)], axis=-1)
        h = h + mixed @ w_out[l]

        h = h + memory_cross_attention(rms_norm(h, g_cross[l]), rms_norm(mem, g_mem[l]),
                                       wq_mem[l], wk_mem[l], wv_mem[l], wo_mem[l])

        u = rms_norm(h, g_mlp[l]) @ w_up[l]
        h = h + jnp.square(jax.nn.relu(u)) @ w_down[l]
    return rms_norm(h, g_final)
```

```python
import math
from contextlib import ExitStack
import numpy as np
import concourse.bass as bass
import concourse.mybir as mybir
from concourse.bass_utils import run_bass_kernel_spmd

F32 = mybir.dt.float32
BF16 = mybir.dt.bfloat16
AF = mybir.ActivationFunctionType
ALU = mybir.AluOpType
AXX = mybir.AxisListType.X

D = 4096
S = 4096
NOWN = 2048
DFF = 16384
NMEM = 256
KC = 32
SCALE = 128.0 ** -0.5
SCALE_M = 1024.0 ** -0.5
NEG = -30000.0
LAMBDA_INIT = 0.8 - 0.6 * math.exp(0.0)
C_QF, C_KF, C_VF, C_F, C_QD, C_KD, C_VD = 0, 2048, 4096, 6144, 6160, 8208, 10256
SB_BASE = 20480
SB_LIMIT = 229376


class Ev:
    __slots__ = ("sem", "val")

    def __init__(self, sem, val):
        self.sem = sem
        self.val = val


class Sem:
    def __init__(self, h, name):
        self.h = h
        self.name = name
        self.n = 0


class Prog:
    def __init__(self, nc):
        self.nc = nc
        self.es = ExitStack()
        self.q = {e: [] for e in ("pe", "act", "dve", "pool", "sp")}
        self.waited = {}
        self.nsem = 0
        self.prog = {e: self.sem("prog_" + e) for e in ("pe", "act", "dve", "pool")}

    def sem(self, name):
        if not hasattr(self, "allsems"):
            self.allsems = []
            self.pool = []
            self.inuse = []
        if name.startswith("prog_"):
            self.nsem += 1
            name = f"{name}_{self.nsem}"
            sm = Sem(self.es.enter_context(self.nc.semaphore(name)), name)
            self.allsems.append(sm)
            return sm
        if self.pool:
            sm = self.pool.pop()
        else:
            self.nsem += 1
            name = f"{name}_{self.nsem}"
            sm = Sem(self.es.enter_context(self.nc.semaphore(name)), name)
            self.allsems.append(sm)
        self.inuse.append(sm)
        return sm

    def wait(self, eng, ev):
        if ev is None:
            return
        if isinstance(ev, (list, tuple)):
            for e in ev:
                self.wait(eng, e)
            return
        k = (eng, ev.sem.name)
        if self.waited.get(k, 0) >= ev.val:
            return
        self.waited[k] = ev.val
        self.q[eng].append(("w", ev.sem, ev.val))

    def op(self, eng, fn, waits=(), signal=True):
        self.wait(eng, waits)
        if signal:
            s = self.prog[eng]
            s.n += 1
            self.q[eng].append(("o", fn, s, 1))
            return Ev(s, s.n)
        self.q[eng].append(("o", fn, None, 0))
        return None

    def dma(self, eng, sem, fn, waits=()):
        self.wait(eng, waits)
        sem.n += 16
        self.q[eng].append(("o", fn, sem, 16))
        return Ev(sem, sem.n)

    def barrier(self):
        if not hasattr(self, "allsems"):
            self.allsems = []
        evs = [Ev(sm, sm.n) for sm in self.allsems if sm.n > 0]
        for eng in self.q:
            self.wait(eng, evs)
        self.pool.extend(self.inuse)
        self.inuse = []

    def emit(self):
        with self.nc.Block() as block:
            def mk(eng):
                def f(e):
                    for it in self.q[eng]:
                        if it[0] == "w":
                            e.wait_ge(it[1].h, it[2])
                        else:
                            ins = it[1](e)
                            if it[2] is not None:
                                ins.then_inc(it[2].h, it[3])
                return f
            block.tensor(mk("pe"))
            block.scalar(mk("act"))
            block.vector(mk("dve"))
            block.gpsimd(mk("pool"))
            block.sync(mk("sp"))


class SBAlloc:
    def __init__(self, nc):
        self.nc = nc
        self.off = SB_BASE
        self.cnt = 0

    def alloc(self, name, shape, dtype):
        self.cnt += 1
        nbytes = int(np.prod(shape[1:])) * (2 if dtype == BF16 else 4)
        nbytes = (nbytes + 63) // 64 * 64
        off = self.off
        assert off + nbytes <= SB_LIMIT, f"SBUF overflow at {name}: {off}+{nbytes}"
        self.off += nbytes
        return self.nc.alloc_sbuf_tensor_at(f"{name}_{self.cnt}", list(shape), dtype, offset=off)

    def mark(self):
        return self.off

    def release(self, m):
        self.off = m


class Ctx:
    pass


def gemm(cx, *, X, kc, wfn, ntok, colblocks, x_stream, Tt=1024, KP=16):
    P, sb = cx.P, cx.sb
    m0 = sb.mark()
    NW = 3
    Wt = [sb.alloc("gw", [128, KP, 512], BF16) for _ in range(NW)]
    wsem = [P.sem("gw") for _ in range(NW)]
    wfree = [cx.phase_ev] * NW
    nk = kc // KP
    if x_stream:
        NX = 3
        Xt = [sb.alloc("gx", [128, KP, Tt], BF16) for _ in range(NX)]
    else:
        NX = 1
        Xt = [sb.alloc("gx", [128, kc, Tt], BF16)]
    xsem = [P.sem("gx") for _ in range(NX)]
    xsem2 = P.sem("gx2")
    xfree = [cx.phase_ev] * NX
    for cb in colblocks:
        cb["epi"].setup(cx, Tt)
    ntt = ntok // Tt
    NT = min(512, Tt)
    cx.NT = NT
    pieces = [(tt, ci, kp) for tt in range(ntt) for ci in range(len(colblocks)) for kp in range(nk)]
    wload = {}
    xload = {}

    def load_w(i):
        tt, ci, kp = pieces[i]
        cb = colblocks[ci]
        slot = i % NW
        src = wfn(kp * KP, KP, cb["c0"], cb["cn"]).rearrange("(k p) c -> p k c", p=128)
        dst = Wt[slot][:, :, 0:cb["cn"]]
        wload[i] = P.dma("pool", wsem[slot], lambda e, d=dst, s=src: e.dma_start(out=d, in_=s),
                         waits=[wfree[slot]])

    def load_x(i):
        tt, ci, kp = pieces[i]
        if x_stream:
            slot = i % NX
            src = X[kp * KP:(kp + 1) * KP, :, tt * Tt:(tt + 1) * Tt].rearrange("k p t -> p k t")
            xload[i] = P.dma("pool", xsem[slot], lambda e, d=Xt[slot][:], s=src: e.dma_start(out=d, in_=s),
                             waits=[xfree[slot]])
        else:
            if ci == 0 and kp == 0:
                src = X[:, :, tt * Tt:(tt + 1) * Tt].rearrange("k p t -> p k t")
                half = kc // 2
                xload[(tt, 0)] = P.dma("pool", xsem[0], lambda e, d=Xt[0][:, 0:half, :], s=src[:, 0:half, :]: e.dma_start(out=d, in_=s),
                                       waits=[xfree[0]])
                xload[(tt, 1)] = P.dma("pool", xsem2, lambda e, d=Xt[0][:, half:kc, :], s=src[:, half:kc, :]: e.dma_start(out=d, in_=s))

    npieces = len(pieces)
    PRE = NW - 1
    if x_stream:
        for i in range(min(NX, npieces)):
            load_x(i)
    for i in range(min(PRE, npieces)):
        load_w(i)
    if not x_stream:
        load_x(0)
    ev = None
    for i, (tt, ci, kp) in enumerate(pieces):
        cb = colblocks[ci]
        epi = cb["epi"]
        cn = cb["cn"]
        if kp == 0:
            epi.begin(cx, tt, ci, cb)
        slot = i % NW
        P.wait("pe", wload[i])
        if x_stream:
            xs = i % NX
            P.wait("pe", xload[i])
            Xc = Xt[xs]
        else:
            P.wait("pe", xload[(tt, 0)])
            if (kp + 1) * KP > kc // 2:
                P.wait("pe", xload[(tt, 1)])
            Xc = Xt[0]
        if cb["mode"] == "fm":
            groups = [(cs, ts) for cs in range((cn + 127) // 128) for ts in range(Tt // NT)]
        else:
            groups = [(tb,) for tb in range(Tt // 128)]
        for gi, g in enumerate(groups):
            bank = gi
            if kp == 0:
                P.wait("pe", cx.bank_free[bank])
            for k in range(KP):
                first = (kp == 0 and k == 0)
                last = (kp == nk - 1 and k == KP - 1)
                endp = (gi == len(groups) - 1 and k == KP - 1)
                kk = k if x_stream else kp * KP + k
                if cb["mode"] == "fm":
                    cs, ts = g
                    m = min(128, cn - cs * 128)
                    out = cx.ps[bank][0:m, 0:NT]
                    lhsT = Wt[slot][:, k, cs * 128:cs * 128 + m]
                    rhs = Xc[:, kk, ts * NT:(ts + 1) * NT]
                else:
                    tb = g[0]
                    out = cx.ps[bank][:, 0:cn]
                    lhsT = Xc[:, kk, tb * 128:(tb + 1) * 128]
                    rhs = Wt[slot][:, k, 0:cn]
                ev = P.op("pe", lambda e, o=out, l=lhsT, r=rhs, st=first, sp=last:
                          e.matmul(o, lhsT=l, rhs=r, start=st, stop=sp), signal=(last or endp))
            if kp == nk - 1:
                cx.bank_free[bank] = epi.group(cx, cx.ps[bank], tt, ci, cb, g, ev)
        wfree[slot] = ev
        if x_stream:
            xfree[xs] = ev
            if i + NX < npieces:
                load_x(i + NX)
        else:
            if ci == len(colblocks) - 1 and kp == nk - 1:
                xfree[0] = ev
                if tt + 1 < ntt:
                    load_x(i + 1)
        if i + PRE < npieces:
            load_w(i + PRE)
        if kp == nk - 1:
            epi.end(cx, tt, ci, cb)
    cx.phase_ev = ev
    evs = [ev]
    for cb in colblocks:
        evs += cb["epi"].finish(cx)
    sb.release(m0)
    return evs


class EpiBase:
    def setup(self, cx, Tt):
        pass

    def begin(self, cx, tt, ci, cb):
        pass

    def end(self, cx, tt, ci, cb):
        pass

    def finish(self, cx):
        return []


class EpiCopyFM(EpiBase):
    def __init__(self, destfn, eng="act"):
        self.destfn = destfn
        self.eng = eng
        self.ready = False

    def setup(self, cx, Tt):
        if self.ready:
            return
        self.ready = True
        self.Tt = Tt
        self.stg = [cx.sb.alloc("stg", [128, 4, Tt], BF16) for _ in range(2)]
        self.ssem = [cx.P.sem("st") for _ in range(2)]
        self.sfree = [None, None]
        self.cnt = 0
        self.last = None

    def begin(self, cx, tt, ci, cb):
        self.buf = self.cnt % 2
        self.cnt += 1

    def group(self, cx, bank, tt, ci, cb, g, pe_ev):
        cs, ts = g
        m = min(128, cb["cn"] - cs * 128)
        NT = cx.NT
        o = self.stg[self.buf][0:m, cs, ts * NT:(ts + 1) * NT]
        i = bank[0:m, 0:NT]
        if self.eng == "act":
            fn = lambda e, o=o, i=i: e.activation(out=o, in_=i, func=AF.Copy)
        else:
            fn = lambda e, o=o, i=i: e.tensor_copy(out=o, in_=i)
        self.last = cx.P.op(self.eng, fn, waits=[pe_ev, self.sfree[self.buf]])
        return self.last

    def end(self, cx, tt, ci, cb):
        b = self.buf
        n = (cb["cn"] + 127) // 128
        dst = self.destfn(tt, ci, cb)
        src = self.stg[b][:, 0:n, :]
        self.sfree[b] = cx.P.dma("sp", self.ssem[b], lambda e, d=dst, s=src: e.dma_start(out=d, in_=s),
                                 waits=[self.last])

    def finish(self, cx):
        return [e for e in self.sfree if e is not None]


class EpiCopyTM(EpiBase):
    def __init__(self, destfn):
        self.destfn = destfn
        self.ready = False

    def setup(self, cx, Tt):
        if self.ready:
            return
        self.ready = True
        self.ntb = Tt // 128
        self.stg = [cx.sb.alloc("stgv", [128, 4, self.ntb, 128], BF16) for _ in range(2)]
        self.ssem = [cx.P.sem("stv") for _ in range(2)]
        self.sfree = [None, None]
        self.cnt = 0

    def begin(self, cx, tt, ci, cb):
        self.buf = self.cnt % 2
        self.cnt += 1

    def group(self, cx, bank, tt, ci, cb, g, pe_ev):
        tb = g[0]
        nh = cb["cn"] // 128
        o = self.stg[self.buf][:, 0:nh, tb, :]
        i = bank[:, 0:cb["cn"]].rearrange("p (h d) -> p h d", h=nh)
        self.last = cx.P.op("dve", lambda e, o=o, i=i: e.tensor_copy(out=o, in_=i),
                            waits=[pe_ev, self.sfree[self.buf]])
        return self.last

    def end(self, cx, tt, ci, cb):
        b = self.buf
        dst = self.destfn(tt, ci, cb)
        src = self.stg[b][:].rearrange("p h t d -> p h (t d)")
        self.sfree[b] = cx.P.dma("sp", self.ssem[b], lambda e, d=dst, s=src: e.dma_start(out=d, in_=s),
                                 waits=[self.last])

    def finish(self, cx):
        return [e for e in self.sfree if e is not None]


class EpiResid(EpiBase):
    def __init__(self, residfn, destfn):
        self.residfn = residfn
        self.destfn = destfn
        self.ready = False

    def setup(self, cx, Tt):
        if self.ready:
            return
        self.ready = True
        self.res = [cx.sb.alloc("res", [128, 4, Tt], F32) for _ in range(2)]
        self.rsem = [cx.P.sem("rs") for _ in range(2)]
        self.ssem = [cx.P.sem("str") for _ in range(2)]
        self.rfree = [None, None]
        self.rload = [None, None]
        self.cnt = 0

    def begin(self, cx, tt, ci, cb):
        b = self.cnt % 2
        self.buf = b
        self.cnt += 1
        src = self.residfn(tt, ci, cb)
        self.rload[b] = cx.P.dma("sp", self.rsem[b], lambda e, d=self.res[b][:], s=src: e.dma_start(out=d, in_=s),
                                 waits=[self.rfree[b]])

    def group(self, cx, bank, tt, ci, cb, g, pe_ev):
        cs, ts = g
        b = self.buf
        r = self.res[b][:, cs, ts * 512:(ts + 1) * 512]
        self.last = cx.P.op("dve", lambda e, i=bank, r=r: e.tensor_tensor(out=r, in0=i, in1=r, op=ALU.add),
                            waits=[pe_ev, self.rload[b]])
        return self.last

    def end(self, cx, tt, ci, cb):
        b = self.buf
        dst = self.destfn(tt, ci, cb)
        self.rfree[b] = cx.P.dma("sp", self.ssem[b], lambda e, d=dst, s=self.res[b][:]: e.dma_start(out=d, in_=s),
                                 waits=[self.last])

    def finish(self, cx):
        return [e for e in self.rfree if e is not None]


class EpiRelu2(EpiBase):
    def __init__(self, destfn):
        self.destfn = destfn
        self.ready = False

    def setup(self, cx, Tt):
        if self.ready:
            return
        self.ready = True
        self.tmp = [cx.sb.alloc("rtmp", [128, 512], F32) for _ in range(2)]
        self.tfree = [None, None]
        self.stg = [cx.sb.alloc("stgu", [128, 4, Tt], BF16) for _ in range(2)]
        self.ssem = [cx.P.sem("stu") for _ in range(2)]
        self.sfree = [None, None]
        self.cnt = 0
        self.gc = 0

    def begin(self, cx, tt, ci, cb):
        self.buf = self.cnt % 2
        self.cnt += 1

    def group(self, cx, bank, tt, ci, cb, g, pe_ev):
        cs, ts = g
        b = self.buf
        t = self.gc % 2
        self.gc += 1
        tm = self.tmp[t][:]
        a_ev = cx.P.op("act", lambda e, o=tm, i=bank: e.activation(out=o, in_=i, func=AF.Relu),
                       waits=[pe_ev, self.tfree[t]])
        o = self.stg[b][:, cs, ts * 512:(ts + 1) * 512]
        self.last = cx.P.op("dve", lambda e, o=o, i=tm: e.tensor_tensor(out=o, in0=i, in1=i, op=ALU.mult),
                            waits=[a_ev, self.sfree[b]])
        self.tfree[t] = self.last
        return a_ev

    def end(self, cx, tt, ci, cb):
        b = self.buf
        dst = self.destfn(tt, ci, cb)
        self.sfree[b] = cx.P.dma("sp", self.ssem[b], lambda e, d=dst, s=self.stg[b][:]: e.dma_start(out=d, in_=s),
                                 waits=[self.last])

    def finish(self, cx):
        return [e for e in self.sfree if e is not None]


class EpiSig(EpiBase):
    def __init__(self, sigT, bias):
        self.sigT = sigT
        self.bias = bias
        self.last = None

    def group(self, cx, bank, tt, ci, cb, g, pe_ev):
        cs, ts = g
        t0 = tt * 1024 + ts * 512
        o = self.sigT[0:16, t0:t0 + 512]
        self.last = cx.P.op("act", lambda e, o=o, i=bank[0:16, :], b=self.bias: e.activation(
            out=o, in_=i, func=AF.Sigmoid, bias=b, scale=1.0), waits=[pe_ev])
        return self.last

    def finish(self, cx):
        return [self.last]


def norm_pass(cx, src, dst, ntok, gi, out_dtype, TT=256, waits=()):
    P, sb = cx.P, cx.sb
    m0 = sb.mark()
    NXB = 4 if TT == 128 else (3 if out_dtype == BF16 else 2)
    xin = [sb.alloc("nx", [128, KC, TT], F32) for _ in range(NXB)]
    xsem = [P.sem("nx") for _ in range(NXB)]
    xfree = [None] * NXB
    sq = [sb.alloc("nsq", [128, KC, TT], BF16) for _ in range(2)]
    sqfree = [None, None]
    lnv = sb.alloc("nln", [128, TT], F32)
    rstd = [sb.alloc("nrstd", [128, TT], F32) for _ in range(2)]
    rfree = [None, None]
    ot = [sb.alloc("no", [128, KC, TT], out_dtype) for _ in range(2)]
    osem = [P.sem("no") for _ in range(2)]
    ofree = [None, None]
    nt = ntok // TT
    ld = {}
    sqev = {}
    KD = 32

    def load(t):
        b = t % NXB
        s_ = src[:, :, t * TT:(t + 1) * TT].rearrange("k p t -> p k t")
        ld[t] = P.dma("sp", xsem[b], lambda e, d=xin[b][:], s=s_: e.dma_start(out=d, in_=s),
                      waits=[xfree[b]] + list(waits))

    def square(t):
        b = t % NXB
        q = t % 2
        sqev[t] = P.op("act", lambda e, o=sq[q][:], i=xin[b][:]: e.activation(out=o, in_=i, func=AF.Square),
                       waits=[ld[t], sqfree[q]])

    for t in range(min(NXB - 1, nt)):
        load(t)
    square(0)
    for t in range(nt):
        if t + NXB - 1 < nt:
            load(t + NXB - 1)
        if t + 1 < nt:
            square(t + 1)
        b = t % NXB
        q = t % 2
        bank = t % 2
        P.wait("pe", [sqev[t], cx.bank_free[bank]])
        for k in range(KC):
            pe = P.op("pe", lambda e, o=cx.ps[bank][:, 0:TT], r=sq[q][:, k, :], st=(k == 0), sp=(k == KC - 1):
                      e.matmul(o, lhsT=cx.ones_b[:], rhs=r, start=st, stop=sp), signal=(k == KC - 1))
        sqfree[q] = pe
        a2 = P.op("act", lambda e, i=cx.ps[bank][:, 0:TT]: e.activation(
            out=lnv[:], in_=i, func=AF.Ln, bias=cx.eps6[:, 0:1], scale=1.0 / D), waits=[pe])
        cx.bank_free[bank] = a2
        a3 = P.op("act", lambda e, o=rstd[q][:]: e.activation(out=o, in_=lnv[:], func=AF.Exp, scale=-0.5),
                  waits=[a2, rfree[q]])
        last_d = last_p = None
        for k in range(KC):
            eng = "dve" if k < KD else "pool"
            ev = P.op(eng, lambda e, o=ot[t % 2][:, k, :], i=xin[b][:, k, :], g=cx.gvec[:, gi, k:k + 1],
                      r=rstd[q][:]: e.scalar_tensor_tensor(out=o, in0=i, scalar=g, in1=r, op0=ALU.mult, op1=ALU.mult),
                      waits=[a3, ofree[t % 2]])
            if eng == "dve":
                last_d = ev
            else:
                last_p = ev
        xfree[b] = [last_d, last_p]
        rfree[q] = [last_d, last_p]
        dd = dst[:, :, t * TT:(t + 1) * TT].rearrange("k p t -> p k t")
        ofree[t % 2] = P.dma("sp", osem[t % 2], lambda e, d=dd, s=ot[t % 2][:]: e.dma_start(out=d, in_=s),
                             waits=[last_d, last_p])
    sb.release(m0)
    return [e for e in ofree if e is not None]


def attn_prep(cx, dq_d, oh_d):
    P, sb = cx.P, cx.sb
    sm = cx.smalls
    sigT = cx.sigT
    cx.ctm = sb.alloc("ctm", [128, 32, 16], F32)
    cx.biask = sb.alloc("biask", [128, 16, 4, 32], F32)
    cx.lam = sb.alloc("lam", [128, 4], F32)
    cx.t5 = sb.alloc("t5", [128, 4, 8, 128], F32)
    cx.rb15s = sb.alloc("rb15s", [128, 8], F32)
    cx.gsub8 = sb.alloc("gsub8", [128, 2], F32)
    m0 = sb.mark()
    ones16 = sb.alloc("ones16", [16, S], F32)
    cT = sb.alloc("cT", [16, S], F32)
    dqf = sb.alloc("dqf", [16, S], F32)
    tmp2 = sb.alloc("tmp2", [16, NOWN], F32)
    dqo = sb.alloc("dqo", [16, NOWN], F32)
    hib = sb.alloc("hib", [16, NOWN], BF16)
    lob = sb.alloc("lob", [16, NOWN], BF16)
    crbc = sb.alloc("crbc", [128, 4, 16], F32)
    lamt = sb.alloc("lamt", [128, 2, 128], F32)
    rbext = sb.alloc("rbext", [64, 8], F32)
    ohc = [sb.alloc("ohc", [33, 32, 128], F32) for _ in range(2)]
    ones_f = sb.alloc("ones_f", [128, 128], F32)

    a0 = P.op("act", lambda e: e.activation(out=sigT[:], in_=sigT[:], func=AF.Ln))
    d0 = P.op("dve", lambda e: e.memset(ones16[:], 1.0))
    prev = [a0, d0]
    for sg in range(4):
        ini = 0.0 if sg == 0 else cT[:, sg * 1024 - 1:sg * 1024]
        pv = P.op("dve", lambda e, sg=sg, ini=ini: e.tensor_tensor_scan(
            out=cT[:, sg * 1024:(sg + 1) * 1024], data0=ones16[:, sg * 1024:(sg + 1) * 1024],
            data1=sigT[:, sg * 1024:(sg + 1) * 1024], initial=ini, op0=ALU.mult, op1=ALU.add), waits=prev)
        prev = [pv]
    dscan = prev[0]
    P.wait("pe", [dscan, cx.bank_free[0]])
    for kb in range(32):
        pe = P.op("pe", lambda e, kb=kb: e.matmul(cx.ps[0][:, kb * 16:(kb + 1) * 16],
                                                 lhsT=cT[0:16, kb * 128:(kb + 1) * 128],
                                                 rhs=cx.ident_f[0:16, 0:16], start=True, stop=True),
                  signal=(kb == 31))
    dctm = P.op("dve", lambda e: e.tensor_copy(out=cx.ctm[:].rearrange("p k h -> p (k h)"), in_=cx.ps[0]), waits=[pe])
    cx.bank_free[0] = dctm
    dz = P.op("dve", lambda e: e.memset(crbc[:, 0, :], 0.0))
    P.wait("pe", [dctm, cx.bank_free[1]])
    for j in range(1, 4):
        pe = P.op("pe", lambda e, j=j: e.matmul(cx.ps[1][:, j * 16:(j + 1) * 16], lhsT=cx.e127,
                                               rhs=cx.ctm[:, 8 * j - 1, :], start=True, stop=True),
                  signal=(j == 3))
    dcr = P.op("dve", lambda e: e.tensor_copy(out=crbc[:, 1:4, :].rearrange("p j h -> p (j h)"),
                                              in_=cx.ps[1][:, 16:64]), waits=[pe, dz])
    cx.bank_free[1] = dcr
    for h in range(16):
        for j in range(4):
            P.op("dve", lambda e, h=h, j=j: e.tensor_scalar(
                out=cx.biask[:, h, j, :], in0=cx.ctm[:, :, h], scalar1=crbc[:, j, h:h + 1], scalar2=-1.0,
                op0=ALU.subtract, op1=ALU.mult), waits=[dcr])
    evs = []
    for j in range(4):
        sc1 = 0.0 if j == 0 else cT[:, 1024 * j - 1:1024 * j]
        evs.append(P.op("dve", lambda e, j=j, sc1=sc1: e.tensor_scalar(
            out=dqf[:, 1024 * j:1024 * (j + 1)], in0=cT[:, 1024 * j:1024 * (j + 1)], scalar1=sc1,
            scalar2=1.0 / SCALE, op0=ALU.subtract, op1=ALU.mult), waits=[dscan]))
    v = dqf[:].rearrange("p (a r t) -> p a r t", r=2, t=128)
    t2v = tmp2[:].rearrange("p (a t) -> p a t", t=128)
    dqv = dqo[:].rearrange("p (a t) -> p a t", t=128)
    e1 = P.op("dve", lambda e: e.tensor_scalar(out=t2v, in0=v[:, :, 0, :], scalar1=cx.par[0:16, 0:1], scalar2=None,
                                               op0=ALU.mult), waits=evs)
    e2 = P.op("dve", lambda e: e.scalar_tensor_tensor(out=dqv, in0=v[:, :, 1, :], scalar=cx.par[0:16, 1:2], in1=t2v,
                                                      op0=ALU.mult, op1=ALU.add), waits=[e1])
    e3 = P.op("dve", lambda e: e.tensor_copy(out=hib[:], in_=dqo[:]), waits=[e2])
    e4 = P.op("dve", lambda e: e.tensor_copy(out=tmp2[:], in_=hib[:]), waits=[e3])
    e5 = P.op("dve", lambda e: e.tensor_tensor(out=tmp2[:], in0=dqo[:], in1=tmp2[:], op=ALU.subtract), waits=[e4])
    e6 = P.op("dve", lambda e: e.tensor_copy(out=lob[:], in_=tmp2[:]), waits=[e5])
    dsem = P.sem("dq")
    P.dma("sp", dsem, lambda e: e.dma_start(out=dq_d[:, 0, :], in_=hib[:]), waits=[e3])
    P.dma("sp", dsem, lambda e: e.dma_start(out=dq_d[:, 1, :], in_=lob[:]), waits=[e6])
    lv = sm[:, NS_LAM:NS_LAM + 512].rearrange("p (a d) -> p a d", a=4)
    l1 = P.op("dve", lambda e: e.tensor_tensor(out=lamt[:, 0, :], in0=lv[:, 0, :], in1=lv[:, 1, :], op=ALU.mult))
    l2 = P.op("dve", lambda e: e.tensor_tensor(out=lamt[:, 1, :], in0=lv[:, 2, :], in1=lv[:, 3, :], op=ALU.mult))
    l3 = P.op("dve", lambda e: e.tensor_reduce(out=cx.lam[:, 0:2], in_=lamt[:], axis=AXX, op=ALU.add), waits=[l1, l2])
    l4 = P.op("act", lambda e: e.activation(out=cx.lam[:, 2:4], in_=cx.lam[:, 0:2], func=AF.Exp), waits=[l3])
    l5 = P.op("dve", lambda e: e.tensor_tensor(out=cx.lam[:, 0:1], in0=cx.lam[:, 2:3], in1=cx.lam[:, 3:4],
                                               op=ALU.subtract), waits=[l4])
    l6 = P.op("dve", lambda e: e.tensor_scalar(out=cx.lam[:, 0:1], in0=cx.lam[:, 0:1], scalar1=LAMBDA_INIT, scalar2=None,
                                               op0=ALU.add), waits=[l5])
    P.op("dve", lambda e: e.tensor_scalar(out=cx.lam[:, 1:2], in0=cx.lam[:, 0:1], scalar1=-1.0, scalar2=None,
                                          op0=ALU.mult), waits=[l6])
    P.op("dve", lambda e: e.tensor_scalar(out=cx.gsub8[:], in0=sm[:, NS_GSUB:NS_GSUB + 2],
                                          scalar1=1.0 - LAMBDA_INIT, scalar2=None, op0=ALU.mult))
    r0 = P.op("dve", lambda e: e.memset(rbext[32:33, :], NEG))
    r1 = P.op("dve", lambda e: e.tensor_scalar(out=rbext[0:32, :], in0=sm[0:32, NS_RB:NS_RB + 8], scalar1=1.0 / SCALE,
                                               scalar2=None, op0=ALU.mult))
    r2 = P.op("dve", lambda e: e.tensor_scalar(out=cx.rb15s[:], in0=sm[:, NS_RB15:NS_RB15 + 8], scalar1=1.0 / SCALE,
                                               scalar2=None, op0=ALU.mult))
    r3 = P.op("dve", lambda e: e.memset(ones_f[:], 1.0))
    P.op("dve", lambda e: e.memset(cx.t5[:, 3, 0, :], NEG))
    for h in range(8):
        P.op("dve", lambda e, h=h: e.tensor_scalar(out=cx.t5[:, 2, h, :], in0=ones_f[:], scalar1=cx.rb15s[:, h:h + 1],
                                                   scalar2=None, op0=ALU.mult), waits=[r2, r3])
    osem = [P.sem("oh") for _ in range(2)]
    ofree = [None, None]
    ohv = oh_d.rearrange("b (t q k) -> b t q k", t=2, q=128)
    n = 0
    for ty in range(2):
        P.wait("pe", [cx.bank_free[2], cx.bank_free[3], r0, r1])
        for qc in range(4):
            b = n % 2
            n += 1
            ld = P.dma("sp", osem[b], lambda e, b=b, ty=ty, qc=qc: e.dma_start(
                out=ohc[b][:], in_=ohv[:, ty, qc * 32:(qc + 1) * 32, :]), waits=[ofree[b]])
            P.wait("pe", ld)
            for ql in range(32):
                q = qc * 32 + ql
                bank = 2 + (q * 8) // 512
                col = (q * 8) % 512
                pe = P.op("pe", lambda e, b=b, ql=ql, bank=bank, col=col: e.matmul(
                    cx.ps[bank][:, col:col + 8], lhsT=ohc[b][0:33, ql, :], rhs=rbext[0:33, 0:8], start=True, stop=True),
                    signal=(ql == 31))
            ofree[b] = pe
        dd = None
        for hb in range(2):
            dd = P.op("dve", lambda e, ty=ty, hb=hb: e.tensor_copy(
                out=cx.t5[:, ty, :, hb * 64:(hb + 1) * 64],
                in_=cx.ps[2 + hb].rearrange("p (q h) -> p h q", h=8)), waits=[pe])
            cx.bank_free[2 + hb] = dd
    sb.release(m0)


def attn_loads(cx, kinds, nheads, done_evs):
    pass


def fox_attention(cx, KT, QT, Vs, dq_d, mixT, maskF_d):
    P, sb = cx.P, cx.sb
    m0 = sb.mark()
    kt = [sb.alloc("kt", [128, S], BF16) for _ in range(2)]
    vt = [sb.alloc("vt", [128, 32, 128], BF16) for _ in range(2)]
    qt = [sb.alloc("qt", [128, NOWN], BF16) for _ in range(2)]
    dqt = [sb.alloc("dqt", [2, NOWN], BF16) for _ in range(2)]
    hsem = [P.sem("fh") for _ in range(2)]
    maskF = sb.alloc("maskF", [128, 8, 512], BF16)
    NPT = 3
    pt = [sb.alloc("pt", [128, 512], BF16) for _ in range(NPT)]
    ptfree = [None] * NPT
    rl = sb.alloc("rl", [128, 512], F32)
    ostg = [sb.alloc("ostg", [128, 512], BF16) for _ in range(2)]
    osem = [P.sem("fo") for _ in range(2)]
    ofree = [None, None]
    msem = P.sem("mk")
    mld = P.dma("pool", msem, lambda e: e.dma_start(out=maskF[:].rearrange("p r q -> p (r q)"), in_=maskF_d))
    hload = {}
    hdone = {}

    def load_head(h):
        sl = h % 2
        w = [hdone.get(h - 2)]
        P.dma("sp", hsem[sl], lambda e: e.dma_start(out=kt[sl][:], in_=KT[h]), waits=w)
        P.dma("sp", hsem[sl], lambda e: e.dma_start(out=vt[sl][:], in_=Vs[h]))
        P.dma("sp", hsem[sl], lambda e: e.dma_start(out=qt[sl][:], in_=QT[h]))
        hload[h] = P.dma("sp", hsem[sl], lambda e: e.dma_start(out=dqt[sl][:], in_=dq_d[h]))

    items = [(h, j, kb) for h in range(16) for j in range(4) for kb in range(8 * j + 8)]
    st_fin = [None]
    LA = 2
    exp_ev = {}
    load_head(0)
    load_head(1)

    def emit_S(t):
        h, j, kb = items[t]
        sl = h % 2
        sbank = t % 4
        P.wait("pe", [hload[h], cx.bank_free[sbank], mld])
        diag = kb >= 8 * j
        P.op("pe", lambda e: e.matmul(cx.ps[sbank], lhsT=kt[sl][:, kb * 128:(kb + 1) * 128],
                                      rhs=qt[sl][:, j * 512:(j + 1) * 512], start=True, stop=False), signal=False)
        pe = P.op("pe", lambda e: e.matmul(cx.ps[sbank], lhsT=cx.ones_b[0:2, :], rhs=dqt[sl][0:2, j * 512:(j + 1) * 512],
                                           start=False, stop=(not diag)), signal=(not diag))
        if diag:
            pe = P.op("pe", lambda e: e.matmul(cx.ps[sbank], lhsT=cx.ident_b[:], rhs=maskF[:, kb - 8 * j, :],
                                               start=False, stop=True))
        p = t % NPT
        ev = P.op("act", lambda e: e.activation(out=pt[p][:], in_=cx.ps[sbank], func=AF.Exp,
                                                bias=cx.biask[:, h, j, kb:kb + 1], scale=SCALE),
                  waits=[pe, ptfree[p]])
        exp_ev[t] = ev
        cx.bank_free[sbank] = ev

    def emit_PV(t):
        h, j, kb = items[t]
        sl = h % 2
        hj = h * 4 + j
        ob = 4 + hj % 2
        lb = 6 + hj % 2
        last = (kb == 8 * j + 7)
        if kb == 0:
            P.wait("pe", [cx.bank_free[ob], cx.bank_free[lb]])
        P.wait("pe", exp_ev[t])
        p = t % NPT
        P.op("pe", lambda e: e.matmul(cx.ps[ob], lhsT=vt[sl][:, kb, :], rhs=pt[p][:], start=(kb == 0), stop=last),
             signal=False)
        pe = P.op("pe", lambda e: e.matmul(cx.ps[lb], lhsT=cx.ones_b[:], rhs=pt[p][:], start=(kb == 0), stop=last))
        ptfree[p] = pe
        if last:
            o = hj % 2
            a1 = P.op("act", lambda e: e.activation(out=rl[:], in_=cx.ps[lb], func=AF.Ln), waits=[pe, st_fin[0]])
            d1 = P.op("act", lambda e: e.activation(out=rl[:], in_=rl[:], func=AF.Exp, scale=-1.0), waits=[a1])
            d2 = P.op("dve", lambda e: e.tensor_tensor(out=ostg[o][:], in0=cx.ps[ob], in1=rl[:], op=ALU.mult),
                      waits=[d1, ofree[o]])
            st_fin[0] = d2
            cx.bank_free[ob] = d2
            cx.bank_free[lb] = d2
            ofree[o] = P.dma("sp", osem[o], lambda e: e.dma_start(out=mixT[h, :, j * 512:(j + 1) * 512], in_=ostg[o][:]),
                             waits=[d2])
            if j == 3:
                hdone[h] = pe
                if h + 2 < 16:
                    load_head(h + 2)

    for t in range(len(items) + LA):
        if t < len(items):
            emit_S(t)
        if t >= LA:
            emit_PV(t - LA)
    sb.release(m0)


def diff_attention(cx, KT, QT, Vs, mixT):
    P, sb = cx.P, cx.sb
    sm = cx.smalls
    m0 = sb.mark()
    kt = [sb.alloc("dkt", [128, 2, S], BF16) for _ in range(2)]
    qt = [sb.alloc("dqt", [128, 2, NOWN], BF16) for _ in range(2)]
    vt = [sb.alloc("dvt", [128, 2, 32, 128], BF16) for _ in range(2)]
    bd = [sb.alloc("bd", [128, 9, 512], BF16) for _ in range(2)]
    tmpb = [sb.alloc("tmpb", [128, 128], F32) for _ in range(2)]
    hsem = [P.sem("dh") for _ in range(2)]
    NPT = 3
    pt = [sb.alloc("dpt", [128, 512], BF16) for _ in range(NPT)]
    ptfree = [None] * NPT
    r1 = sb.alloc("r1", [128, 512], F32)
    r2 = sb.alloc("r2", [128, 512], F32)
    t2 = sb.alloc("t2", [128, 512], F32)
    dfe = [sb.alloc("dfe", [128, 512], F32) for _ in range(2)]
    sqb = [sb.alloc("sqb", [128, 512], BF16) for _ in range(2)]
    lnv = sb.alloc("lnv", [128, 512], F32)
    rstd = sb.alloc("rstdd", [128, 512], F32)
    ostg = [sb.alloc("dostg", [128, 2, 512], BF16) for _ in range(2)]
    osem = [P.sem("do") for _ in range(2)]
    ofree = [None, None]
    hload = {}
    hdone = {}
    bdready = {}
    st = {"sc": 0, "fin": None, "tb": 0}

    def load_head(h):
        sl = h % 2
        w = [hdone.get(h - 2)]
        P.dma("sp", hsem[sl], lambda e: e.dma_start(out=kt[sl][:], in_=KT[16 + 2 * h:18 + 2 * h].rearrange("c p t -> p c t")),
              waits=w)
        P.dma("sp", hsem[sl], lambda e: e.dma_start(out=vt[sl][:], in_=Vs[16 + 2 * h:18 + 2 * h].rearrange("c p k d -> p c k d")))
        hload[h] = P.dma("sp", hsem[sl], lambda e: e.dma_start(
            out=qt[sl][:], in_=QT[16 + 2 * h:18 + 2 * h].rearrange("c p t -> p c t")))
        def base(t):
            if t < -1:
                return cx.t5[:, 2, h, :]
            if t == -1:
                return cx.t5[:, 1, h, :]
            if t == 0:
                return cx.t5[:, 0, h, :]
            return cx.t5[:, 3, 0, :]
        ev = None
        for r in range(-1, 8):
            for i in range(4):
                tb = st["tb"] % 2
                st["tb"] += 1
                b0 = base(r - 2 * i)
                b1 = base(r - 2 * i - 1)
                ea = P.op("dve", lambda e, tb=tb, b0=b0: e.tensor_scalar(out=tmpb[tb][:], in0=b0, scalar1=cx.par[:, 0:1],
                                                                        scalar2=None, op0=ALU.mult), waits=w)
                ev = P.op("dve", lambda e, tb=tb, b1=b1, r=r, i=i: e.scalar_tensor_tensor(
                    out=bd[sl][:, r + 1, i * 128:(i + 1) * 128], in0=b1, scalar=cx.par[:, 1:2], in1=tmpb[tb][:],
                    op0=ALU.mult, op1=ALU.add), waits=[ea])
        bdready[h] = ev

    items = [(h, j, c, kb) for h in range(8) for j in range(4) for c in range(2) for kb in range(8 * j + 8)]
    LA = 1
    DEFER = 10
    exp_ev = {}
    dfa = [sb.alloc("dfa", [128, 512], F32) for _ in range(2)]
    load_head(0)
    load_head(1)
    st.update(r1free=None, r2free=None, sqfree=None, pend=None, since=0)

    def emit_S(t):
        h, j, c, kb = items[t]
        sl = h % 2
        sbank = st["sc"] % 2
        st["sc"] += 1
        diag = kb >= 8 * j - 1
        P.wait("pe", [hload[h], cx.bank_free[sbank]])
        pe = P.op("pe", lambda e: e.matmul(cx.ps[sbank], lhsT=kt[sl][:, c, kb * 128:(kb + 1) * 128],
                                           rhs=qt[sl][:, c, j * 512:(j + 1) * 512], start=True, stop=(not diag)),
                  signal=(not diag))
        if diag:
            P.wait("pe", bdready[h])
            pe = P.op("pe", lambda e: e.matmul(cx.ps[sbank], lhsT=cx.ident_b[:], rhs=bd[sl][:, kb - 8 * j + 1, :],
                                               start=False, stop=True))
        p = t % NPT
        bias = sm[:, NS_RB15 + h:NS_RB15 + h + 1] if not diag else cx.eps6[:, 2:3]
        ev = P.op("act", lambda e: e.activation(out=pt[p][:], in_=cx.ps[sbank], func=AF.Exp, bias=bias, scale=SCALE),
                  waits=[pe, ptfree[p]])
        exp_ev[t] = ev
        cx.bank_free[sbank] = ev

    def emit_ss():
        h, j, sq_evs = st["pend"]
        st["pend"] = None
        o = (h * 4 + j) % 2
        sbank = st["sc"] % 2
        st["sc"] += 1
        P.wait("pe", [cx.bank_free[sbank]] + sq_evs)
        P.op("pe", lambda e: e.matmul(cx.ps[sbank], lhsT=cx.ones_b[:], rhs=sqb[0][:], start=True, stop=False), signal=False)
        pss = P.op("pe", lambda e: e.matmul(cx.ps[sbank], lhsT=cx.ones_b[:], rhs=sqb[1][:], start=False, stop=True))
        st["sqfree"] = pss
        a1 = P.op("act", lambda e: e.activation(out=lnv[:], in_=cx.ps[sbank], func=AF.Ln, bias=cx.eps6[:, 1:2],
                                                scale=1.0 / 256.0), waits=[pss, st["fin"]])
        cx.bank_free[sbank] = a1
        a2 = P.op("act", lambda e: e.activation(out=rstd[:], in_=lnv[:], func=AF.Exp, scale=-0.5), waits=[a1])
        fin = None
        for e_ in range(2):
            fin = P.op("dve", lambda e, e_=e_: e.scalar_tensor_tensor(
                out=ostg[o][:, e_, :], in0=dfe[e_][:], scalar=cx.gsub8[:, e_:e_ + 1], in1=rstd[:],
                op0=ALU.mult, op1=ALU.mult), waits=[a2, ofree[o]])
        st["fin"] = fin
        ofree[o] = P.dma("sp", osem[o], lambda e: e.dma_start(
            out=mixT[16 + 2 * h:18 + 2 * h, :, j * 512:(j + 1) * 512].rearrange("c p t -> p c t"), in_=ostg[o][:]),
            waits=[fin])
        if j == 3:
            hdone[h] = pss
            if h + 2 < 8:
                load_head(h + 2)

    def emit_PV(t):
        h, j, c, kb = items[t]
        sl = h % 2
        u = (h * 4 + j) * 2 + c
        sset = u % 2
        ob = 2 + 3 * sset
        lb = 4 + 3 * sset
        last = (kb == 8 * j + 7)
        if kb == 0:
            P.wait("pe", [cx.bank_free[ob], cx.bank_free[ob + 1], cx.bank_free[lb]])
        P.wait("pe", exp_ev[t])
        p = t % NPT
        for e_ in range(2):
            P.op("pe", lambda e, e_=e_: e.matmul(cx.ps[ob + e_], lhsT=vt[sl][:, e_, kb, :], rhs=pt[p][:],
                                                 start=(kb == 0), stop=last), signal=False)
        pe = P.op("pe", lambda e: e.matmul(cx.ps[lb], lhsT=cx.ones_b[:], rhs=pt[p][:], start=(kb == 0), stop=last))
        ptfree[p] = pe
        st["since"] += 1
        if st["pend"] is not None and st["since"] >= DEFER:
            emit_ss()
        if last and c == 0:
            a1 = P.op("act", lambda e: e.activation(out=r1[:], in_=cx.ps[lb], func=AF.Ln), waits=[pe, st["r1free"]])
            a2 = P.op("act", lambda e: e.activation(out=r1[:], in_=r1[:], func=AF.Exp, scale=-1.0), waits=[a1])
            dl = None
            for e_ in range(2):
                dl = P.op("dve", lambda e, e_=e_: e.tensor_tensor(out=dfa[e_][:], in0=cx.ps[ob + e_], in1=r1[:], op=ALU.mult),
                          waits=[a2])
            st["r1free"] = dl
            for b in (ob, ob + 1, lb):
                cx.bank_free[b] = dl
        if last and c == 1:
            if st["pend"] is not None:
                emit_ss()
            a1 = P.op("act", lambda e: e.activation(out=r2[:], in_=cx.ps[lb], func=AF.Ln), waits=[pe, st["r2free"]])
            a2 = P.op("act", lambda e: e.activation(out=r2[:], in_=r2[:], func=AF.Exp, scale=-1.0), waits=[a1])
            sq_evs = []
            dl = None
            for e_ in range(2):
                d5 = P.op("dve", lambda e, e_=e_: e.tensor_tensor(out=t2[:], in0=cx.ps[ob + e_], in1=r2[:], op=ALU.mult),
                          waits=[a2])
                d6 = P.op("dve", lambda e, e_=e_: e.scalar_tensor_tensor(
                    out=dfe[e_][:], in0=t2[:], scalar=cx.lam[:, 1:2], in1=dfa[e_][:], op0=ALU.mult, op1=ALU.add),
                    waits=[d5, st["fin"]])
                d7 = P.op("dve", lambda e, e_=e_: e.tensor_tensor(out=sqb[e_][:], in0=dfe[e_][:], in1=dfe[e_][:], op=ALU.mult),
                          waits=[d6, st["sqfree"]])
                sq_evs.append(d7)
                dl = d5
            st["r2free"] = dl
            for b in (ob, ob + 1, lb):
                cx.bank_free[b] = dl
            st["pend"] = (h, j, sq_evs)
            st["since"] = 0

    for t in range(len(items) + LA):
        if t < len(items):
            emit_S(t)
        if t >= LA:
            emit_PV(t - LA)
    if st["pend"] is not None:
        emit_ss()
    sb.release(m0)


def cross_attention(cx, qmT, kmT, VM, omT):
    P, sb = cx.P, cx.sb
    m0 = sb.mark()
    kmt = sb.alloc("kmt", [128, KC, NMEM], BF16)
    vmt = sb.alloc("vmt", [128, KC, 2, 128], BF16)
    qm = [sb.alloc("qm", [128, KC, 512], BF16) for _ in range(2)]
    qsem = [P.sem("cq") for _ in range(2)]
    qfree = [None, None]
    ksem = P.sem("ck")
    P.dma("sp", ksem, lambda e: e.dma_start(out=kmt[:], in_=kmT.rearrange("k p t -> p k t")))
    kld = P.dma("sp", ksem, lambda e: e.dma_start(out=vmt[:], in_=VM.rearrange("c p k d -> p c k d")))
    ptm = [sb.alloc("ptm", [128, 512], BF16) for _ in range(4)]
    ptfree = [None] * 4
    rl = sb.alloc("crl", [128, 512], F32)
    ostg = [sb.alloc("costg", [128, 8, 512], BF16) for _ in range(2)]
    osem = [P.sem("co") for _ in range(2)]
    ofree = [None, None]
    st = {"b": 0}

    def nbank():
        b = st["b"] % 8
        st["b"] += 1
        return b

    qld = {}

    def loadq(t):
        b = t % 2
        qld[t] = P.dma("sp", qsem[b], lambda e: e.dma_start(
            out=qm[b][:], in_=qmT[:, :, t * 512:(t + 1) * 512].rearrange("k p t -> p k t")), waits=[qfree[b]])

    loadq(0)
    n = 0
    rl_free = None
    for t in range(4):
        if t + 1 < 4:
            loadq(t + 1)
        qb = t % 2
        for hm in range(4):
            o = n % 2
            pts = []
            for mb in range(2):
                bk = nbank()
                P.wait("pe", [qld[t], kld, cx.bank_free[bk]])
                for ch in range(8):
                    pe = P.op("pe", lambda e, ch=ch, mb=mb, bk=bk, hm=hm, qb=qb: e.matmul(
                        cx.ps[bk], lhsT=kmt[:, 8 * hm + ch, mb * 128:(mb + 1) * 128], rhs=qm[qb][:, 8 * hm + ch, :],
                        start=(ch == 0), stop=(ch == 7)), signal=(ch == 7))
                p = (2 * n + mb) % 4
                ev = P.op("act", lambda e, p=p, bk=bk: e.activation(out=ptm[p][:], in_=cx.ps[bk], func=AF.Exp, scale=SCALE_M),
                          waits=[pe, ptfree[p]])
                cx.bank_free[bk] = ev
                pts.append((p, ev))
            if hm == 3:
                qfree[qb] = pe
            bl = nbank()
            P.wait("pe", [cx.bank_free[bl], pts[0][1], pts[1][1]])
            P.op("pe", lambda e, bl=bl, p=pts[0][0]: e.matmul(cx.ps[bl], lhsT=cx.ones_b[:], rhs=ptm[p][:], start=True, stop=False),
                 signal=False)
            pl = P.op("pe", lambda e, bl=bl, p=pts[1][0]: e.matmul(cx.ps[bl], lhsT=cx.ones_b[:], rhs=ptm[p][:], start=False, stop=True))
            d1 = P.op("dve", lambda e, bl=bl: e.reciprocal(out=rl[:], in_=cx.ps[bl]), waits=[pl, rl_free])
            cx.bank_free[bl] = d1
            dlast = None
            for e_ in range(8):
                bo = nbank()
                P.wait("pe", [cx.bank_free[bo]])
                P.op("pe", lambda e, bo=bo, e_=e_, p=pts[0][0], hm=hm: e.matmul(cx.ps[bo], lhsT=vmt[:, 8 * hm + e_, 0, :], rhs=ptm[p][:],
                                                                       start=True, stop=False), signal=False)
                po = P.op("pe", lambda e, bo=bo, e_=e_, p=pts[1][0], hm=hm: e.matmul(cx.ps[bo], lhsT=vmt[:, 8 * hm + e_, 1, :], rhs=ptm[p][:],
                                                                            start=False, stop=True))
                dlast = P.op("dve", lambda e, bo=bo, e_=e_, o=o: e.tensor_tensor(out=ostg[o][:, e_, :], in0=cx.ps[bo], in1=rl[:],
                                                                                 op=ALU.mult), waits=[po, d1, ofree[o]])
                cx.bank_free[bo] = dlast
            ptfree[pts[0][0]] = po
            ptfree[pts[1][0]] = po
            rl_free = dlast
            ofree[o] = P.dma("sp", osem[o], lambda e, o=o, hm=hm, t=t: e.dma_start(
                out=omT[8 * hm:8 * hm + 8, :, t * 512:(t + 1) * 512].rearrange("c p t -> p c t"), in_=ostg[o][:]),
                waits=[dlast])
            n += 1
    sb.release(m0)


NS_G, NS_GSUB, NS_PAR, NS_BF, NS_RB, NS_RB15, NS_LAM, NS_ID, NS_E127 = 0, 160, 162, 164, 165, 173, 181, 693, 821
NS = 949


def build(upto=99, debug=()):
    nc = bass.Bass("TRN2", target_bir_lowering=False)
    cx = Ctx()
    cx.nc = nc
    cx.P = P = Prog(nc)
    cx.sb = sb = SBAlloc(nc)
    cx.phase_ev = None
    cx.inputs = []
    cx.outputs = []

    def inp(name, shape, dt=F32):
        cx.inputs.append(name)
        return nc.dram_tensor(name, list(shape), dt, kind="ExternalInput").ap()

    def outp(name, shape, dt=F32):
        cx.outputs.append(name)
        return nc.dram_tensor(name, list(shape), dt, kind="ExternalOutput").ap()

    def scratch(name, shape, dt):
        return nc.dram_tensor(name, list(shape), dt).ap()

    psum = cx.es_ps = P.es.enter_context(nc.psum_tensor("ps", [128, 8, 512], F32))
    cx.ps = [psum[:, b, :] for b in range(8)]
    cx.bank_free = [None] * 8

    smalls_d = inp("smalls", [128, NS])
    smalls = sb.alloc("smalls", [128, NS], F32)
    cx.gvec = smalls[:, NS_G:NS_G + 160].rearrange("p (g k) -> p g k", g=5)
    cx.ident_f = smalls[:, NS_ID:NS_ID + 128]
    cx.e127 = smalls[:, NS_E127:NS_E127 + 128]
    cx.par = smalls[:, NS_PAR:NS_PAR + 2]
    cx.smalls = smalls
    cx.ones_b = sb.alloc("ones_b", [128, 128], BF16)
    cx.ident_b = sb.alloc("ident_b", [128, 128], BF16)
    cx.eps6 = sb.alloc("eps6", [128, 4], F32)
    csem = P.sem("const")
    ld = P.dma("sp", csem, lambda e: e.dma_start(out=smalls[:], in_=smalls_d))
    P.op("dve", lambda e: e.memset(cx.ones_b[:], 1.0))
    P.op("dve", lambda e: e.memset(cx.eps6[:, 0:1], 1e-6))
    P.op("dve", lambda e: e.memset(cx.eps6[:, 1:2], 1e-5))
    P.op("dve", lambda e: e.memset(cx.eps6[:, 2:4], 0.0))
    P.op("dve", lambda e: e.tensor_copy(out=cx.ident_b[:], in_=cx.ident_f), waits=[ld])
    P.barrier()

    dbg = {}

    def finish():
        P.barrier()
        dsem = P.sem("dbg")
        for name, (ap, shape, dt) in dbg.items():
            if name in debug:
                o = outp("dbg_" + name, shape, dt)
                P.dma("sp", dsem, lambda e, o=o, a=ap: e.dma_start(out=o, in_=a))
        P.barrier()
        P.emit()
        return nc, cx

    xa = inp("xa", [KC, 128, S])
    xo = inp("xo", [KC, 128, NOWN])
    aT_all = scratch("aT_all", [KC, 128, S], BF16)
    aT_own = scratch("aT_own", [KC, 128, NOWN], BF16)
    dbg["aT_all"] = (aT_all, [KC, 128, S], BF16)
    dbg["aT_own"] = (aT_own, [KC, 128, NOWN], BF16)
    norm_pass(cx, xa, aT_all, S, 0, BF16)
    norm_pass(cx, xo, aT_own, NOWN, 0, BF16)
    P.barrier()
    if upto <= 1:
        return finish()

    w_in = inp("w_in", [D, 12304])
    KT = scratch("KT", [32, 128, S], BF16)
    QT = scratch("QT", [32, 128, NOWN], BF16)
    Vs = scratch("Vs", [32, 128, 32, 128], BF16)
    dbg["KT"] = (KT, [32, 128, S], BF16)
    dbg["QT"] = (QT, [32, 128, NOWN], BF16)
    dbg["Vs"] = (Vs, [32, 128, 32, 128], BF16)
    attn_mark = sb.mark()
    sigT = sb.alloc("sigT", [16, S], F32)
    cx.sigT = sigT

    def w_in_fn(k0, kn, c0, cn):
        return w_in[k0 * 128:(k0 + kn) * 128, c0:c0 + cn]

    def kt_dest(base):
        def f(tt, ci, cb):
            h0 = base + (cb["c0"] - cb["cbase"]) // 128
            return KT[h0:h0 + 4, :, tt * 1024:(tt + 1) * 1024].rearrange("h p t -> p h t")
        return f

    def v_dest(base):
        def f(tt, ci, cb):
            h0 = base + (cb["c0"] - cb["cbase"]) // 128
            return Vs[h0:h0 + 4, :, tt * 8:(tt + 1) * 8, :].rearrange("h p k d -> p h (k d)")
        return f

    def q_dest(base):
        def f(tt, ci, cb):
            h0 = base + (cb["c0"] - cb["cbase"]) // 128
            return QT[h0:h0 + 4, :, tt * 1024:(tt + 1) * 1024].rearrange("h p t -> p h t")
        return f

    ekf = EpiCopyFM(lambda tt, ci, cb: kt_dest(cb["hb"])(tt, ci, cb))
    ekd = ekf
    evf = EpiCopyTM(lambda tt, ci, cb: v_dest(cb["hb"])(tt, ci, cb))
    evd = evf
    esg = EpiSig(sigT, smalls[0:16, NS_BF:NS_BF + 1])
    cbs = []
    for c in range(4):
        cbs.append(dict(c0=C_KF + 512 * c, cn=512, cbase=C_KF, hb=0, mode="fm", epi=ekf))
    for c in range(4):
        cbs.append(dict(c0=C_VF + 512 * c, cn=512, cbase=C_VF, hb=0, mode="tm", epi=evf))
    cbs.append(dict(c0=C_F, cn=16, cbase=C_F, hb=0, mode="fm", epi=esg))
    for c in range(4):
        cbs.append(dict(c0=C_KD + 512 * c, cn=512, cbase=C_KD, hb=16, mode="fm", epi=ekd))
    for c in range(4):
        cbs.append(dict(c0=C_VD + 512 * c, cn=512, cbase=C_VD, hb=16, mode="tm", epi=evd))
    if upto == 2:
        cbs = [cbs[0], cbs[4], cbs[8], cbs[9], cbs[13]]
    gemm(cx, X=aT_all, kc=KC, wfn=w_in_fn, ntok=S, colblocks=cbs, x_stream=False)
    P.barrier()
    if upto <= 2:
        return finish()
    eqf = EpiCopyFM(lambda tt, ci, cb: q_dest(cb["hb"])(tt, ci, cb))
    eqd = eqf
    cbs = []
    for c in range(4):
        cbs.append(dict(c0=C_QF + 512 * c, cn=512, cbase=C_QF, hb=0, mode="fm", epi=eqf))
    for c in range(4):
        cbs.append(dict(c0=C_QD + 512 * c, cn=512, cbase=C_QD, hb=16, mode="fm", epi=eqd))
    gemm(cx, X=aT_own, kc=KC, wfn=w_in_fn, ntok=NOWN, colblocks=cbs, x_stream=False)
    P.barrier()
    if upto <= 3:
        return finish()

    dq_d = scratch("dq_d", [16, 2, NOWN], BF16)
    oh_d = inp("oh", [33, 2 * 128 * 128])
    maskF_d = inp("maskF", [128, 8 * 512])
    mixT = scratch("mixT", [KC, 128, NOWN], BF16)
    dbg["mixT"] = (mixT, [KC, 128, NOWN], BF16)
    dbg["dq_d"] = (dq_d, [16, 2, NOWN], BF16)
    attn_prep(cx, dq_d, oh_d)
    P.barrier()
    if upto <= 4:
        return finish()
    w_dn0 = inp("w_dn0", [DFF // 2, D])
    w_dn1 = inp("w_dn1", [DFF // 2, D])
    wdn_bf = scratch("wdn_bf", [DFF, D], BF16)
    pcsem = P.sem("precast")
    for r in range(32):
        wsrc = (w_dn0 if r < 16 else w_dn1)[(r % 16) * 512:(r % 16 + 1) * 512, :]
        P.dma("pool", pcsem, lambda e, d=wdn_bf[r * 512:(r + 1) * 512, :], s_=wsrc: e.dma_start(out=d, in_=s_))
    fox_attention(cx, KT, QT, Vs, dq_d, mixT, maskF_d)
    P.barrier()
    if upto <= 5:
        return finish()
    diff_attention(cx, KT, QT, Vs, mixT)
    P.barrier()
    sb.release(attn_mark)
    if upto <= 6:
        return finish()

    def fm_dest(Y, Tt=1024):
        def f(tt, ci, cb):
            h0 = cb["c0"] // 128
            n = (cb["cn"] + 127) // 128
            return Y[h0:h0 + n, :, tt * Tt:(tt + 1) * Tt].rearrange("h p t -> p h t")
        return f

    def simple_w(w):
        def f(k0, kn, c0, cn):
            return w[k0 * 128:(k0 + kn) * 128, c0:c0 + cn]
        return f

    def cblocks(n, epi, mode="fm"):
        return [dict(c0=512 * c, cn=512, cbase=0, hb=0, mode=mode, epi=epi) for c in range(n)]

    w_out = inp("w_out", [D, D])
    h1T = scratch("h1T", [KC, 128, NOWN], F32)
    dbg["h1T"] = (h1T, [KC, 128, NOWN], F32)
    gemm(cx, X=mixT, kc=KC, wfn=simple_w(w_out), ntok=NOWN, colblocks=cblocks(8, EpiResid(fm_dest(xo), fm_dest(h1T))),
         x_stream=False)
    P.barrier()
    if upto <= 7:
        return finish()
    memT = inp("memT", [KC, 128, NMEM])
    cT_d = scratch("cT_d", [KC, 128, NOWN], BF16)
    mT_d = scratch("mT_d", [KC, 128, NMEM], BF16)
    norm_pass(cx, h1T, cT_d, NOWN, 1, BF16)
    norm_pass(cx, memT, mT_d, NMEM, 2, BF16)
    P.barrier()
    wk = inp("wk_mem", [D, D])
    wv = inp("wv_mem", [D, D])
    kmT = scratch("kmT", [KC, 128, NMEM], BF16)
    VM = scratch("VM", [KC, 128, 2, 128], BF16)
    gemm(cx, X=mT_d, kc=KC, wfn=simple_w(wk), ntok=NMEM, colblocks=cblocks(8, EpiCopyFM(fm_dest(kmT, 256))),
         x_stream=False, Tt=256)
    P.barrier()

    def vm_dest(tt, ci, cb):
        h0 = cb["c0"] // 128
        return VM[h0:h0 + 4, :, :, :].rearrange("h p k d -> p h (k d)")
    gemm(cx, X=mT_d, kc=KC, wfn=simple_w(wv), ntok=NMEM, colblocks=cblocks(8, EpiCopyTM(vm_dest), mode="tm"),
         x_stream=False, Tt=256)
    P.barrier()
    wq = inp("wq_mem", [D, D])
    qmT = scratch("qmT", [KC, 128, NOWN], BF16)
    gemm(cx, X=cT_d, kc=KC, wfn=simple_w(wq), ntok=NOWN, colblocks=cblocks(8, EpiCopyFM(fm_dest(qmT))), x_stream=False)
    P.barrier()
    omT = scratch("omT", [KC, 128, NOWN], BF16)
    dbg["omT"] = (omT, [KC, 128, NOWN], BF16)
    cross_attention(cx, qmT, kmT, VM, omT)
    P.barrier()
    if upto <= 11:
        return finish()
    wo = inp("wo_mem", [D, D])
    h2T = scratch("h2T", [KC, 128, NOWN], F32)
    dbg["h2T"] = (h2T, [KC, 128, NOWN], F32)
    gemm(cx, X=omT, kc=KC, wfn=simple_w(wo), ntok=NOWN, colblocks=cblocks(8, EpiResid(fm_dest(h1T), fm_dest(h2T))),
         x_stream=False)
    P.barrier()
    if upto <= 12:
        return finish()
    nT_d = scratch("nT_d", [KC, 128, NOWN], BF16)
    norm_pass(cx, h2T, nT_d, NOWN, 3, BF16)
    P.barrier()
    w_up0 = inp("w_up0", [D, DFF // 2])
    w_up1 = inp("w_up1", [D, DFF // 2])
    actT = scratch("actT", [DFF // 128, 128, NOWN], BF16)

    def wup_fn(k0, kn, c0, cn):
        w, c = (w_up0, c0) if c0 < DFF // 2 else (w_up1, c0 - DFF // 2)
        return w[k0 * 128:(k0 + kn) * 128, c:c + cn]
    gemm(cx, X=nT_d, kc=KC, wfn=wup_fn, ntok=NOWN, colblocks=cblocks(32, EpiRelu2(fm_dest(actT))), x_stream=False)
    P.barrier()
    h3T = scratch("h3T", [KC, 128, NOWN], F32)
    dbg["h3T"] = (h3T, [KC, 128, NOWN], F32)

    def wdn_fn(k0, kn, c0, cn):
        return wdn_bf[k0 * 128:(k0 + kn) * 128, c0:c0 + cn]
    gemm(cx, X=actT, kc=DFF // 128, wfn=wdn_fn, ntok=NOWN, colblocks=cblocks(8, EpiResid(fm_dest(h2T), fm_dest(h3T))),
         x_stream=True)
    P.barrier()
    outT = outp("outT", [KC, 128, NOWN])
    norm_pass(cx, h3T, outT, NOWN, 4, F32, TT=128)
    return finish()


def own_tokens(qh):
    blocks = [8 * j + 2 * i + qh for j in range(4) for i in range(4)]
    return np.concatenate([np.arange(bk * 128, (bk + 1) * 128) for bk in blocks])


def t5_bucket_np(rel):
    rel = np.asarray(rel, np.int32)
    ret = np.where(rel > 0, 16, 0).astype(np.int32)
    n = np.abs(rel)
    nf = np.maximum(n, 1).astype(np.float32)
    large = 8 + (np.log(nf / np.float32(8)) / np.float32(math.log(128 / 8)) * np.float32(8)).astype(np.int32)
    large = np.minimum(large, 15)
    return ret + np.where(n < 8, n, large)


def fox_mask_table(qh):
    m = np.zeros((128, 8, 4, 128), np.float32)
    kk = np.arange(128)[:, None]
    qq = np.arange(128)[None, :]
    diag = np.where(kk <= qq, 0.0, NEG).astype(np.float32)
    for r in range(8):
        for i in range(4):
            t = r - (2 * i + qh)
            if t == 0:
                m[:, r, i, :] = diag
            elif t > 0:
                m[:, r, i, :] = NEG
    return m.reshape(128, 8 * 512)


def t5_onehot():
    oh = np.zeros((33, 2, 128, 128), np.float32)
    q = np.arange(128)[:, None]
    k = np.arange(128)[None, :]
    bd = t5_bucket_np(k - q)
    allowed = (k // 64) <= (q // 64)
    bd = np.where(allowed, bd, 32)
    bn = t5_bucket_np(k - q - 128)
    for b in range(33):
        oh[b, 0] = (bd == b)
        oh[b, 1] = (bn == b)
    return oh.reshape(33, 2 * 128 * 128)


def prep_smalls(inp, qh):
    s = np.zeros((128, NS), np.float32)
    gs = [inp["g_mix"][0], inp["g_cross"][0], inp["g_mem"][0], inp["g_mlp"][0], inp["g_final"]]
    for gi, g in enumerate(gs):
        s[:, NS_G + gi * 32:NS_G + (gi + 1) * 32] = np.asarray(g, np.float32).reshape(32, 128).T
    s[:, NS_GSUB:NS_GSUB + 2] = np.asarray(inp["g_subln"][0], np.float32).reshape(2, 128).T
    s[:, NS_PAR + qh] = 1.0
    s[0:16, NS_BF] = np.asarray(inp["b_forget"][0], np.float32)
    s[0:32, NS_RB:NS_RB + 8] = np.asarray(inp["rel_bias"], np.float32)
    s[:, NS_RB15:NS_RB15 + 8] = np.asarray(inp["rel_bias"], np.float32)[15][None, :]
    lam = np.stack([inp["lambda_q1"][0], inp["lambda_k1"][0], inp["lambda_q2"][0], inp["lambda_k2"][0]])
    s[:, NS_LAM:NS_LAM + 512] = np.asarray(lam, np.float32).reshape(1, 512)
    s[:, NS_ID:NS_ID + 128] = np.eye(128, dtype=np.float32)
    s[127, NS_E127:NS_E127 + 128] = 1.0
    return s


def prep_core(inp, c, names):
    b, qh = c // 2, c % 2
    m = {}
    xT = None
    if "xa" in names or "xo" in names:
        xT = np.ascontiguousarray(np.asarray(inp["x"][b], np.float32).T).reshape(KC, 128, S)
    if "xa" in names:
        m["xa"] = xT
    if "xo" in names:
        m["xo"] = np.ascontiguousarray(xT[:, :, own_tokens(qh)])
    if "memT" in names:
        m["memT"] = np.ascontiguousarray(np.asarray(inp["mem"][b], np.float32).T).reshape(KC, 128, NMEM)
    if "smalls" in names:
        m["smalls"] = prep_smalls(inp, qh)
    if "maskF" in names:
        m["maskF"] = fox_mask_table(qh)
    if "oh" in names:
        m["oh"] = t5_onehot()
    return m


def shared_inputs(inp, names):
    m = {}
    for nm in ("w_in", "w_out", "wq_mem", "wk_mem", "wv_mem", "wo_mem"):
        if nm in names:
            m[nm] = np.asarray(inp[nm][0], np.float32)
    if "w_up0" in names:
        w = np.asarray(inp["w_up"][0], np.float32)
        m["w_up0"] = np.ascontiguousarray(w[:, :DFF // 2])
        m["w_up1"] = np.ascontiguousarray(w[:, DFF // 2:])
    if "w_dn0" in names:
        w = np.asarray(inp["w_down"][0], np.float32)
        m["w_dn0"] = w[:DFF // 2]
        m["w_dn1"] = w[DFF // 2:]
    return m


def kernel(**inputs):
    nc, cx = build()
    sh = shared_inputs(inputs, cx.inputs)
    in_maps = []
    for c in range(8):
        m = prep_core(inputs, c, cx.inputs)
        m.update(sh)
        in_maps.append(m)
    res = run_bass_kernel_spmd(nc, in_maps, core_ids=list(range(8)))
    out = np.empty((4, S, D), np.float32)
    for c in range(8):
        b, qh = c // 2, c % 2
        o = np.asarray(res.results[c]["outT"], np.float32).reshape(D, NOWN)
        out[b, own_tokens(qh), :] = o.T
    return out
```

```python
import math
from contextlib import ExitStack
import numpy as np
import concourse.bass as bass
import concourse.mybir as mybir
from concourse.bass_utils import run_bass_kernel_spmd

F32 = mybir.dt.float32
BF16 = mybir.dt.bfloat16
AF = mybir.ActivationFunctionType
ALU = mybir.AluOpType
AXX = mybir.AxisListType.X

D = 4096
S = 4096
NOWN = 2048
DFF = 16384
NMEM = 256
KC = 32
SCALE = 128.0 ** -0.5
SCALE_M = 1024.0 ** -0.5
NEG = -30000.0
LAMBDA_INIT = 0.8 - 0.6 * math.exp(0.0)
C_QF, C_KF, C_VF, C_F, C_QD, C_KD, C_VD = 0, 2048, 4096, 6144, 6160, 8208, 10256
SB_BASE = 20480
SB_LIMIT = 229376


class Ev:
    __slots__ = ("sem", "val")

    def __init__(self, sem, val):
        self.sem = sem
        self.val = val


class Sem:
    def __init__(self, h, name):
        self.h = h
        self.name = name
        self.n = 0


class Prog:
    def __init__(self, nc):
        self.nc = nc
        self.es = ExitStack()
        self.q = {e: [] for e in ("pe", "act", "dve", "pool", "sp")}
        self.waited = {}
        self.nsem = 0
        self.prog = {e: self.sem("prog_" + e) for e in ("pe", "act", "dve", "pool")}

    def sem(self, name):
        if not hasattr(self, "allsems"):
            self.allsems = []
            self.pool = []
            self.inuse = []
        if name.startswith("prog_"):
            self.nsem += 1
            name = f"{name}_{self.nsem}"
            sm = Sem(self.es.enter_context(self.nc.semaphore(name)), name)
            self.allsems.append(sm)
            return sm
        if self.pool:
            sm = self.pool.pop()
        else:
            self.nsem += 1
            name = f"{name}_{self.nsem}"
            sm = Sem(self.es.enter_context(self.nc.semaphore(name)), name)
            self.allsems.append(sm)
        self.inuse.append(sm)
        return sm

    def wait(self, eng, ev):
        if ev is None:
            return
        if isinstance(ev, (list, tuple)):
            for e in ev:
                self.wait(eng, e)
            return
        k = (eng, ev.sem.name)
        if self.waited.get(k, 0) >= ev.val:
            return
        self.waited[k] = ev.val
        self.q[eng].append(("w", ev.sem, ev.val))

    def op(self, eng, fn, waits=(), signal=True):
        self.wait(eng, waits)
        if signal:
            s = self.prog[eng]
            s.n += 1
            self.q[eng].append(("o", fn, s, 1))
            return Ev(s, s.n)
        self.q[eng].append(("o", fn, None, 0))
        return None

    def dma(self, eng, sem, fn, waits=()):
        self.wait(eng, waits)
        sem.n += 16
        self.q[eng].append(("o", fn, sem, 16))
        return Ev(sem, sem.n)

    def barrier(self):
        if not hasattr(self, "allsems"):
            self.allsems = []
        evs = [Ev(sm, sm.n) for sm in self.allsems if sm.n > 0]
        for eng in self.q:
            self.wait(eng, evs)
        self.pool.extend(self.inuse)
        self.inuse = []

    def emit(self):
        with self.nc.Block() as block:
            def mk(eng):
                def f(e):
                    for it in self.q[eng]:
                        if it[0] == "w":
                            e.wait_ge(it[1].h, it[2])
                        else:
                            ins = it[1](e)
                            if it[2] is not None:
                                ins.then_inc(it[2].h, it[3])
                return f
            block.tensor(mk("pe"))
            block.scalar(mk("act"))
            block.vector(mk("dve"))
            block.gpsimd(mk("pool"))
            block.sync(mk("sp"))


class SBAlloc:
    def __init__(self, nc):
        self.nc = nc
        self.off = SB_BASE
        self.cnt = 0

    def alloc(self, name, shape, dtype):
        self.cnt += 1
        nbytes = int(np.prod(shape[1:])) * (2 if dtype == BF16 else 4)
        nbytes = (nbytes + 63) // 64 * 64
        off = self.off
        assert off + nbytes <= SB_LIMIT, f"SBUF overflow at {name}: {off}+{nbytes}"
        self.off += nbytes
        return self.nc.alloc_sbuf_tensor_at(f"{name}_{self.cnt}", list(shape), dtype, offset=off)

    def mark(self):
        return self.off

    def release(self, m):
        self.off = m


class Ctx:
    pass


def gemm(cx, *, X, kc, wfn, ntok, colblocks, x_stream, Tt=1024, KP=16):
    P, sb = cx.P, cx.sb
    m0 = sb.mark()
    NW = 3
    Wt = [sb.alloc("gw", [128, KP, 512], BF16) for _ in range(NW)]
    wsem = [P.sem("gw") for _ in range(NW)]
    wfree = [cx.phase_ev] * NW
    nk = kc // KP
    if x_stream:
        NX = 3
        Xt = [sb.alloc("gx", [128, KP, Tt], BF16) for _ in range(NX)]
    else:
        NX = 1
        Xt = [sb.alloc("gx", [128, kc, Tt], BF16)]
    xsem = [P.sem("gx") for _ in range(NX)]
    xsem2 = P.sem("gx2")
    xfree = [cx.phase_ev] * NX
    for cb in colblocks:
        cb["epi"].setup(cx, Tt)
    ntt = ntok // Tt
    NT = min(512, Tt)
    cx.NT = NT
    pieces = [(tt, ci, kp) for tt in range(ntt) for ci in range(len(colblocks)) for kp in range(nk)]
    wload = {}
    xload = {}

    def load_w(i):
        tt, ci, kp = pieces[i]
        cb = colblocks[ci]
        slot = i % NW
        src = wfn(kp * KP, KP, cb["c0"], cb["cn"]).rearrange("(k p) c -> p k c", p=128)
        dst = Wt[slot][:, :, 0:cb["cn"]]
        wload[i] = P.dma("pool", wsem[slot], lambda e, d=dst, s=src: e.dma_start(out=d, in_=s),
                         waits=[wfree[slot]])

    def load_x(i):
        tt, ci, kp = pieces[i]
        if x_stream:
            slot = i % NX
            src = X[kp * KP:(kp + 1) * KP, :, tt * Tt:(tt + 1) * Tt].rearrange("k p t -> p k t")
            xload[i] = P.dma("pool", xsem[slot], lambda e, d=Xt[slot][:], s=src: e.dma_start(out=d, in_=s),
                             waits=[xfree[slot]])
        else:
            if ci == 0 and kp == 0:
                src = X[:, :, tt * Tt:(tt + 1) * Tt].rearrange("k p t -> p k t")
                half = kc // 2
                xload[(tt, 0)] = P.dma("pool", xsem[0], lambda e, d=Xt[0][:, 0:half, :], s=src[:, 0:half, :]: e.dma_start(out=d, in_=s),
                                       waits=[xfree[0]])
                xload[(tt, 1)] = P.dma("pool", xsem2, lambda e, d=Xt[0][:, half:kc, :], s=src[:, half:kc, :]: e.dma_start(out=d, in_=s))

    npieces = len(pieces)
    PRE = NW - 1
    if x_stream:
        for i in range(min(NX, npieces)):
            load_x(i)
    for i in range(min(PRE, npieces)):
        load_w(i)
    if not x_stream:
        load_x(0)
    ev = None
    for i, (tt, ci, kp) in enumerate(pieces):
        cb = colblocks[ci]
        epi = cb["epi"]
        cn = cb["cn"]
        if kp == 0:
            epi.begin(cx, tt, ci, cb)
        slot = i % NW
        P.wait("pe", wload[i])
        if x_stream:
            xs = i % NX
            P.wait("pe", xload[i])
            Xc = Xt[xs]
        else:
            P.wait("pe", xload[(tt, 0)])
            if (kp + 1) * KP > kc // 2:
                P.wait("pe", xload[(tt, 1)])
            Xc = Xt[0]
        if cb["mode"] == "fm":
            groups = [(cs, ts) for cs in range((cn + 127) // 128) for ts in range(Tt // NT)]
        else:
            groups = [(tb,) for tb in range(Tt // 128)]
        for gi, g in enumerate(groups):
            bank = gi
            if kp == 0:
                P.wait("pe", cx.bank_free[bank])
            for k in range(KP):
                first = (kp == 0 and k == 0)
                last = (kp == nk - 1 and k == KP - 1)
                endp = (gi == len(groups) - 1 and k == KP - 1)
                kk = k if x_stream else kp * KP + k
                if cb["mode"] == "fm":
                    cs, ts = g
                    m = min(128, cn - cs * 128)
                    out = cx.ps[bank][0:m, 0:NT]
                    lhsT = Wt[slot][:, k, cs * 128:cs * 128 + m]
                    rhs = Xc[:, kk, ts * NT:(ts + 1) * NT]
                else:
                    tb = g[0]
                    out = cx.ps[bank][:, 0:cn]
                    lhsT = Xc[:, kk, tb * 128:(tb + 1) * 128]
                    rhs = Wt[slot][:, k, 0:cn]
                ev = P.op("pe", lambda e, o=out, l=lhsT, r=rhs, st=first, sp=last:
                          e.matmul(o, lhsT=l, rhs=r, start=st, stop=sp), signal=(last or endp))
            if kp == nk - 1:
                cx.bank_free[bank] = epi.group(cx, cx.ps[bank], tt, ci, cb, g, ev)
        wfree[slot] = ev
        if x_stream:
            xfree[xs] = ev
            if i + NX < npieces:
                load_x(i + NX)
        else:
            if ci == len(colblocks) - 1 and kp == nk - 1:
                xfree[0] = ev
                if tt + 1 < ntt:
                    load_x(i + 1)
        if i + PRE < npieces:
            load_w(i + PRE)
        if kp == nk - 1:
            epi.end(cx, tt, ci, cb)
    cx.phase_ev = ev
    evs = [ev]
    for cb in colblocks:
        evs += cb["epi"].finish(cx)
    sb.release(m0)
    return evs


class EpiBase:
    def setup(self, cx, Tt):
        pass

    def begin(self, cx, tt, ci, cb):
        pass

    def end(self, cx, tt, ci, cb):
        pass

    def finish(self, cx):
        return []


class EpiCopyFM(EpiBase):
    def __init__(self, destfn, eng="act"):
        self.destfn = destfn
        self.eng = eng
        self.ready = False

    def setup(self, cx, Tt):
        if self.ready:
            return
        self.ready = True
        self.Tt = Tt
        self.stg = [cx.sb.alloc("stg", [128, 4, Tt], BF16) for _ in range(2)]
        self.ssem = [cx.P.sem("st") for _ in range(2)]
        self.sfree = [None, None]
        self.cnt = 0
        self.last = None

    def begin(self, cx, tt, ci, cb):
        self.buf = self.cnt % 2
        self.cnt += 1

    def group(self, cx, bank, tt, ci, cb, g, pe_ev):
        cs, ts = g
        m = min(128, cb["cn"] - cs * 128)
        NT = cx.NT
        o = self.stg[self.buf][0:m, cs, ts * NT:(ts + 1) * NT]
        i = bank[0:m, 0:NT]
        if self.eng == "act":
            fn = lambda e, o=o, i=i: e.activation(out=o, in_=i, func=AF.Copy)
        else:
            fn = lambda e, o=o, i=i: e.tensor_copy(out=o, in_=i)
        self.last = cx.P.op(self.eng, fn, waits=[pe_ev, self.sfree[self.buf]])
        return self.last

    def end(self, cx, tt, ci, cb):
        b = self.buf
        n = (cb["cn"] + 127) // 128
        dst = self.destfn(tt, ci, cb)
        src = self.stg[b][:, 0:n, :]
        self.sfree[b] = cx.P.dma("sp", self.ssem[b], lambda e, d=dst, s=src: e.dma_start(out=d, in_=s),
                                 waits=[self.last])

    def finish(self, cx):
        return [e for e in self.sfree if e is not None]


class EpiCopyTM(EpiBase):
    def __init__(self, destfn):
        self.destfn = destfn
        self.ready = False

    def setup(self, cx, Tt):
        if self.ready:
            return
        self.ready = True
        self.ntb = Tt // 128
        self.stg = [cx.sb.alloc("stgv", [128, 4, self.ntb, 128], BF16) for _ in range(2)]
        self.ssem = [cx.P.sem("stv") for _ in range(2)]
        self.sfree = [None, None]
        self.cnt = 0

    def begin(self, cx, tt, ci, cb):
        self.buf = self.cnt % 2
        self.cnt += 1

    def group(self, cx, bank, tt, ci, cb, g, pe_ev):
        tb = g[0]
        nh = cb["cn"] // 128
        o = self.stg[self.buf][:, 0:nh, tb, :]
        i = bank[:, 0:cb["cn"]].rearrange("p (h d) -> p h d", h=nh)
        self.last = cx.P.op("dve", lambda e, o=o, i=i: e.tensor_copy(out=o, in_=i),
                            waits=[pe_ev, self.sfree[self.buf]])
        return self.last

    def end(self, cx, tt, ci, cb):
        b = self.buf
        dst = self.destfn(tt, ci, cb)
        src = self.stg[b][:].rearrange("p h t d -> p h (t d)")
        self.sfree[b] = cx.P.dma("sp", self.ssem[b], lambda e, d=dst, s=src: e.dma_start(out=d, in_=s),
                                 waits=[self.last])

    def finish(self, cx):
        return [e for e in self.sfree if e is not None]


class EpiResid(EpiBase):
    def __init__(self, residfn, destfn):
        self.residfn = residfn
        self.destfn = destfn
        self.ready = False

    def setup(self, cx, Tt):
        if self.ready:
            return
        self.ready = True
        self.res = [cx.sb.alloc("res", [128, 4, Tt], F32) for _ in range(2)]
        self.rsem = [cx.P.sem("rs") for _ in range(2)]
        self.ssem = [cx.P.sem("str") for _ in range(2)]
        self.rfree = [None, None]
        self.rload = [None, None]
        self.cnt = 0

    def begin(self, cx, tt, ci, cb):
        b = self.cnt % 2
        self.buf = b
        self.cnt += 1
        src = self.residfn(tt, ci, cb)
        self.rload[b] = cx.P.dma("sp", self.rsem[b], lambda e, d=self.res[b][:], s=src: e.dma_start(out=d, in_=s),
                                 waits=[self.rfree[b]])

    def group(self, cx, bank, tt, ci, cb, g, pe_ev):
        cs, ts = g
        b = self.buf
        r = self.res[b][:, cs, ts * 512:(ts + 1) * 512]
        self.last = cx.P.op("dve", lambda e, i=bank, r=r: e.tensor_tensor(out=r, in0=i, in1=r, op=ALU.add),
                            waits=[pe_ev, self.rload[b]])
        return self.last

    def end(self, cx, tt, ci, cb):
        b = self.buf
        dst = self.destfn(tt, ci, cb)
        self.rfree[b] = cx.P.dma("sp", self.ssem[b], lambda e, d=dst, s=self.res[b][:]: e.dma_start(out=d, in_=s),
                                 waits=[self.last])

    def finish(self, cx):
        return [e for e in self.rfree if e is not None]


class EpiRelu2(EpiBase):
    def __init__(self, destfn):
        self.destfn = destfn
        self.ready = False

    def setup(self, cx, Tt):
        if self.ready:
            return
        self.ready = True
        self.tmp = [cx.sb.alloc("rtmp", [128, 512], F32) for _ in range(2)]
        self.tfree = [None, None]
        self.stg = [cx.sb.alloc("stgu", [128, 4, Tt], BF16) for _ in range(2)]
        self.ssem = [cx.P.sem("stu") for _ in range(2)]
        self.sfree = [None, None]
        self.cnt = 0
        self.gc = 0

    def begin(self, cx, tt, ci, cb):
        self.buf = self.cnt % 2
        self.cnt += 1

    def group(self, cx, bank, tt, ci, cb, g, pe_ev):
        cs, ts = g
        b = self.buf
        t = self.gc % 2
        self.gc += 1
        tm = self.tmp[t][:]
        a_ev = cx.P.op("act", lambda e, o=tm, i=bank: e.activation(out=o, in_=i, func=AF.Relu),
                       waits=[pe_ev, self.tfree[t]])
        o = self.stg[b][:, cs, ts * 512:(ts + 1) * 512]
        self.last = cx.P.op("dve", lambda e, o=o, i=tm: e.tensor_tensor(out=o, in0=i, in1=i, op=ALU.mult),
                            waits=[a_ev, self.sfree[b]])
        self.tfree[t] = self.last
        return a_ev

    def end(self, cx, tt, ci, cb):
        b = self.buf
        dst = self.destfn(tt, ci, cb)
        self.sfree[b] = cx.P.dma("sp", self.ssem[b], lambda e, d=dst, s=self.stg[b][:]: e.dma_start(out=d, in_=s),
                                 waits=[self.last])

    def finish(self, cx):
        return [e for e in self.sfree if e is not None]


class EpiSig(EpiBase):
    def __init__(self, sigT, bias):
        self.sigT = sigT
        self.bias = bias
        self.last = None

    def group(self, cx, bank, tt, ci, cb, g, pe_ev):
        cs, ts = g
        t0 = tt * 1024 + ts * 512
        o = self.sigT[0:16, t0:t0 + 512]
        self.last = cx.P.op("act", lambda e, o=o, i=bank[0:16, :], b=self.bias: e.activation(
            out=o, in_=i, func=AF.Sigmoid, bias=b, scale=1.0), waits=[pe_ev])
        return self.last

    def finish(self, cx):
        return [self.last]


def norm_pass(cx, src, dst, ntok, gi, out_dtype, TT=256, waits=()):
    P, sb = cx.P, cx.sb
    m0 = sb.mark()
    NXB = 4 if TT == 128 else (3 if out_dtype == BF16 else 2)
    xin = [sb.alloc("nx", [128, KC, TT], F32) for _ in range(NXB)]
    xsem = [P.sem("nx") for _ in range(NXB)]
    xfree = [None] * NXB
    sq = [sb.alloc("nsq", [128, KC, TT], BF16) for _ in range(2)]
    sqfree = [None, None]
    lnv = sb.alloc("nln", [128, TT], F32)
    rstd = [sb.alloc("nrstd", [128, TT], F32) for _ in range(2)]
    rfree = [None, None]
    ot = [sb.alloc("no", [128, KC, TT], out_dtype) for _ in range(2)]
    osem = [P.sem("no") for _ in range(2)]
    ofree = [None, None]
    nt = ntok // TT
    ld = {}
    sqev = {}
    KD = 32

    def load(t):
        b = t % NXB
        s_ = src[:, :, t * TT:(t + 1) * TT].rearrange("k p t -> p k t")
        ld[t] = P.dma("sp", xsem[b], lambda e, d=xin[b][:], s=s_: e.dma_start(out=d, in_=s),
                      waits=[xfree[b]] + list(waits))

    def square(t):
        b = t % NXB
        q = t % 2
        sqev[t] = P.op("act", lambda e, o=sq[q][:], i=xin[b][:]: e.activation(out=o, in_=i, func=AF.Square),
                       waits=[ld[t], sqfree[q]])

    for t in range(min(NXB - 1, nt)):
        load(t)
    square(0)
    for t in range(nt):
        if t + NXB - 1 < nt:
            load(t + NXB - 1)
        if t + 1 < nt:
            square(t + 1)
        b = t % NXB
        q = t % 2
        bank = t % 2
        P.wait("pe", [sqev[t], cx.bank_free[bank]])
        for k in range(KC):
            pe = P.op("pe", lambda e, o=cx.ps[bank][:, 0:TT], r=sq[q][:, k, :], st=(k == 0), sp=(k == KC - 1):
                      e.matmul(o, lhsT=cx.ones_b[:], rhs=r, start=st, stop=sp), signal=(k == KC - 1))
        sqfree[q] = pe
        a2 = P.op("act", lambda e, i=cx.ps[bank][:, 0:TT]: e.activation(
            out=lnv[:], in_=i, func=AF.Ln, bias=cx.eps6[:, 0:1], scale=1.0 / D), waits=[pe])
        cx.bank_free[bank] = a2
        a3 = P.op("act", lambda e, o=rstd[q][:]: e.activation(out=o, in_=lnv[:], func=AF.Exp, scale=-0.5),
                  waits=[a2, rfree[q]])
        last_d = last_p = None
        for k in range(KC):
            eng = "dve" if k < KD else "pool"
            ev = P.op(eng, lambda e, o=ot[t % 2][:, k, :], i=xin[b][:, k, :], g=cx.gvec[:, gi, k:k + 1],
                      r=rstd[q][:]: e.scalar_tensor_tensor(out=o, in0=i, scalar=g, in1=r, op0=ALU.mult, op1=ALU.mult),
                      waits=[a3, ofree[t % 2]])
            if eng == "dve":
                last_d = ev
            else:
                last_p = ev
        xfree[b] = [last_d, last_p]
        rfree[q] = [last_d, last_p]
        dd = dst[:, :, t * TT:(t + 1) * TT].rearrange("k p t -> p k t")
        ofree[t % 2] = P.dma("sp", osem[t % 2], lambda e, d=dd, s=ot[t % 2][:]: e.dma_start(out=d, in_=s),
                             waits=[last_d, last_p])
    sb.release(m0)
    return [e for e in ofree if e is not None]


def attn_prep(cx, dq_d, oh_d):
    P, sb = cx.P, cx.sb
    sm = cx.smalls
    sigT = cx.sigT
    cx.ctm = sb.alloc("ctm", [128, 32, 16], F32)
    cx.biask = sb.alloc("biask", [128, 16, 4, 32], F32)
    cx.lam = sb.alloc("lam", [128, 4], F32)
    cx.t5 = sb.alloc("t5", [128, 4, 8, 128], F32)
    cx.rb15s = sb.alloc("rb15s", [128, 8], F32)
    cx.gsub8 = sb.alloc("gsub8", [128, 2], F32)
    m0 = sb.mark()
    ones16 = sb.alloc("ones16", [16, S], F32)
    cT = sb.alloc("cT", [16, S], F32)
    dqf = sb.alloc("dqf", [16, S], F32)
    tmp2 = sb.alloc("tmp2", [16, NOWN], F32)
    dqo = sb.alloc("dqo", [16, NOWN], F32)
    hib = sb.alloc("hib", [16, NOWN], BF16)
    lob = sb.alloc("lob", [16, NOWN], BF16)
    crbc = sb.alloc("crbc", [128, 4, 16], F32)
    lamt = sb.alloc("lamt", [128, 2, 128], F32)
    rbext = sb.alloc("rbext", [64, 8], F32)
    ohc = [sb.alloc("ohc", [33, 32, 128], F32) for _ in range(2)]
    ones_f = sb.alloc("ones_f", [128, 128], F32)

    a0 = P.op("act", lambda e: e.activation(out=sigT[:], in_=sigT[:], func=AF.Ln))
    d0 = P.op("dve", lambda e: e.memset(ones16[:], 1.0))
    prev = [a0, d0]
    for sg in range(4):
        ini = 0.0 if sg == 0 else cT[:, sg * 1024 - 1:sg * 1024]
        pv = P.op("dve", lambda e, sg=sg, ini=ini: e.tensor_tensor_scan(
            out=cT[:, sg * 1024:(sg + 1) * 1024], data0=ones16[:, sg * 1024:(sg + 1) * 1024],
            data1=sigT[:, sg * 1024:(sg + 1) * 1024], initial=ini, op0=ALU.mult, op1=ALU.add), waits=prev)
        prev = [pv]
    dscan = prev[0]
    P.wait("pe", [dscan, cx.bank_free[0]])
    for kb in range(32):
        pe = P.op("pe", lambda e, kb=kb: e.matmul(cx.ps[0][:, kb * 16:(kb + 1) * 16],
                                                 lhsT=cT[0:16, kb * 128:(kb + 1) * 128],
                                                 rhs=cx.ident_f[0:16, 0:16], start=True, stop=True),
                  signal=(kb == 31))
    dctm = P.op("dve", lambda e: e.tensor_copy(out=cx.ctm[:].rearrange("p k h -> p (k h)"), in_=cx.ps[0]), waits=[pe])
    cx.bank_free[0] = dctm
    dz = P.op("dve", lambda e: e.memset(crbc[:, 0, :], 0.0))
    P.wait("pe", [dctm, cx.bank_free[1]])
    for j in range(1, 4):
        pe = P.op("pe", lambda e, j=j: e.matmul(cx.ps[1][:, j * 16:(j + 1) * 16], lhsT=cx.e127,
                                               rhs=cx.ctm[:, 8 * j - 1, :], start=True, stop=True),
                  signal=(j == 3))
    dcr = P.op("dve", lambda e: e.tensor_copy(out=crbc[:, 1:4, :].rearrange("p j h -> p (j h)"),
                                              in_=cx.ps[1][:, 16:64]), waits=[pe, dz])
    cx.bank_free[1] = dcr
    for h in range(16):
        for j in range(4):
            P.op("dve", lambda e, h=h, j=j: e.tensor_scalar(
                out=cx.biask[:, h, j, :], in0=cx.ctm[:, :, h], scalar1=crbc[:, j, h:h + 1], scalar2=-1.0,
                op0=ALU.subtract, op1=ALU.mult), waits=[dcr])
    evs = []
    for j in range(4):
        sc1 = 0.0 if j == 0 else cT[:, 1024 * j - 1:1024 * j]
        evs.append(P.op("dve", lambda e, j=j, sc1=sc1: e.tensor_scalar(
            out=dqf[:, 1024 * j:1024 * (j + 1)], in0=cT[:, 1024 * j:1024 * (j + 1)], scalar1=sc1,
            scalar2=1.0 / SCALE, op0=ALU.subtract, op1=ALU.mult), waits=[dscan]))
    v = dqf[:].rearrange("p (a r t) -> p a r t", r=2, t=128)
    t2v = tmp2[:].rearrange("p (a t) -> p a t", t=128)
    dqv = dqo[:].rearrange("p (a t) -> p a t", t=128)
    e1 = P.op("dve", lambda e: e.tensor_scalar(out=t2v, in0=v[:, :, 0, :], scalar1=cx.par[0:16, 0:1], scalar2=None,
                                               op0=ALU.mult), waits=evs)
    e2 = P.op("dve", lambda e: e.scalar_tensor_tensor(out=dqv, in0=v[:, :, 1, :], scalar=cx.par[0:16, 1:2], in1=t2v,
                                                      op0=ALU.mult, op1=ALU.add), waits=[e1])
    e3 = P.op("dve", lambda e: e.tensor_copy(out=hib[:], in_=dqo[:]), waits=[e2])
    e4 = P.op("dve", lambda e: e.tensor_copy(out=tmp2[:], in_=hib[:]), waits=[e3])
    e5 = P.op("dve", lambda e: e.tensor_tensor(out=tmp2[:], in0=dqo[:], in1=tmp2[:], op=ALU.subtract), waits=[e4])
    e6 = P.op("dve", lambda e: e.tensor_copy(out=lob[:], in_=tmp2[:]), waits=[e5])
    dsem = P.sem("dq")
    P.dma("sp", dsem, lambda e: e.dma_start(out=dq_d[:, 0, :], in_=hib[:]), waits=[e3])
    P.dma("sp", dsem, lambda e: e.dma_start(out=dq_d[:, 1, :], in_=lob[:]), waits=[e6])
    lv = sm[:, NS_LAM:NS_LAM + 512].rearrange("p (a d) -> p a d", a=4)
    l1 = P.op("dve", lambda e: e.tensor_tensor(out=lamt[:, 0, :], in0=lv[:, 0, :], in1=lv[:, 1, :], op=ALU.mult))
    l2 = P.op("dve", lambda e: e.tensor_tensor(out=lamt[:, 1, :], in0=lv[:, 2, :], in1=lv[:, 3, :], op=ALU.mult))
    l3 = P.op("dve", lambda e: e.tensor_reduce(out=cx.lam[:, 0:2], in_=lamt[:], axis=AXX, op=ALU.add), waits=[l1, l2])
    l4 = P.op("act", lambda e: e.activation(out=cx.lam[:, 2:4], in_=cx.lam[:, 0:2], func=AF.Exp), waits=[l3])
    l5 = P.op("dve", lambda e: e.tensor_tensor(out=cx.lam[:, 0:1], in0=cx.lam[:, 2:3], in1=cx.lam[:, 3:4],
                                               op=ALU.subtract), waits=[l4])
    l6 = P.op("dve", lambda e: e.tensor_scalar(out=cx.lam[:, 0:1], in0=cx.lam[:, 0:1], scalar1=LAMBDA_INIT, scalar2=None,
                                               op0=ALU.add), waits=[l5])
    P.op("dve", lambda e: e.tensor_scalar(out=cx.lam[:, 1:2], in0=cx.lam[:, 0:1], scalar1=-1.0, scalar2=None,
                                          op0=ALU.mult), waits=[l6])
    P.op("dve", lambda e: e.tensor_scalar(out=cx.gsub8[:], in0=sm[:, NS_GSUB:NS_GSUB + 2],
                                          scalar1=1.0 - LAMBDA_INIT, scalar2=None, op0=ALU.mult))
    r0 = P.op("dve", lambda e: e.memset(rbext[32:33, :], NEG))
    r1 = P.op("dve", lambda e: e.tensor_scalar(out=rbext[0:32, :], in0=sm[0:32, NS_RB:NS_RB + 8], scalar1=1.0 / SCALE,
                                               scalar2=None, op0=ALU.mult))
    r2 = P.op("dve", lambda e: e.tensor_scalar(out=cx.rb15s[:], in0=sm[:, NS_RB15:NS_RB15 + 8], scalar1=1.0 / SCALE,
                                               scalar2=None, op0=ALU.mult))
    r3 = P.op("dve", lambda e: e.memset(ones_f[:], 1.0))
    P.op("dve", lambda e: e.memset(cx.t5[:, 3, 0, :], NEG))
    for h in range(8):
        P.op("dve", lambda e, h=h: e.tensor_scalar(out=cx.t5[:, 2, h, :], in0=ones_f[:], scalar1=cx.rb15s[:, h:h + 1],
                                                   scalar2=None, op0=ALU.mult), waits=[r2, r3])
    osem = [P.sem("oh") for _ in range(2)]
    ofree = [None, None]
    ohv = oh_d.rearrange("b (t q k) -> b t q k", t=2, q=128)
    n = 0
    for ty in range(2):
        P.wait("pe", [cx.bank_free[2], cx.bank_free[3], r0, r1])
        for qc in range(4):
            b = n % 2
            n += 1
            ld = P.dma("sp", osem[b], lambda e, b=b, ty=ty, qc=qc: e.dma_start(
                out=ohc[b][:], in_=ohv[:, ty, qc * 32:(qc + 1) * 32, :]), waits=[ofree[b]])
            P.wait("pe", ld)
            for ql in range(32):
                q = qc * 32 + ql
                bank = 2 + (q * 8) // 512
                col = (q * 8) % 512
                pe = P.op("pe", lambda e, b=b, ql=ql, bank=bank, col=col: e.matmul(
                    cx.ps[bank][:, col:col + 8], lhsT=ohc[b][0:33, ql, :], rhs=rbext[0:33, 0:8], start=True, stop=True),
                    signal=(ql == 31))
            ofree[b] = pe
        dd = None
        for hb in range(2):
            dd = P.op("dve", lambda e, ty=ty, hb=hb: e.tensor_copy(
                out=cx.t5[:, ty, :, hb * 64:(hb + 1) * 64],
                in_=cx.ps[2 + hb].rearrange("p (q h) -> p h q", h=8)), waits=[pe])
            cx.bank_free[2 + hb] = dd
    sb.release(m0)


def attn_loads(cx, kinds, nheads, done_evs):
    pass


def fox_attention(cx, KT, QT, Vs, dq_d, mixT, maskF_d, after_mask=None):
    P, sb = cx.P, cx.sb
    m0 = sb.mark()
    kt = [sb.alloc("kt", [128, S], BF16) for _ in range(2)]
    vt = [sb.alloc("vt", [128, 32, 128], BF16) for _ in range(2)]
    qt = [sb.alloc("qt", [128, NOWN], BF16) for _ in range(2)]
    dqt = [sb.alloc("dqt", [2, NOWN], BF16) for _ in range(2)]
    hsem = [P.sem("fh") for _ in range(2)]
    maskF = sb.alloc("maskF", [128, 8, 512], BF16)
    NPT = 3
    pt = [sb.alloc("pt", [128, 512], BF16) for _ in range(NPT)]
    ptfree = [None] * NPT
    rl = sb.alloc("rl", [128, 512], F32)
    ostg = [sb.alloc("ostg", [128, 512], BF16) for _ in range(2)]
    osem = [P.sem("fo") for _ in range(2)]
    ofree = [None, None]
    msem = P.sem("mk")
    mld = P.dma("pool", msem, lambda e: e.dma_start(out=maskF[:].rearrange("p r q -> p (r q)"), in_=maskF_d))
    if after_mask is not None:
        after_mask()
    hload = {}
    hdone = {}

    def load_head(h):
        sl = h % 2
        w = [hdone.get(h - 2)]
        P.dma("sp", hsem[sl], lambda e: e.dma_start(out=kt[sl][:], in_=KT[h]), waits=w)
        P.dma("sp", hsem[sl], lambda e: e.dma_start(out=vt[sl][:], in_=Vs[h]))
        P.dma("sp", hsem[sl], lambda e: e.dma_start(out=qt[sl][:], in_=QT[h]))
        hload[h] = P.dma("sp", hsem[sl], lambda e: e.dma_start(out=dqt[sl][:], in_=dq_d[h]))

    items = [(h, j, kb) for h in range(16) for j in range(4) for kb in range(8 * j + 8)]
    st_fin = [None]
    LA = 2
    exp_ev = {}
    load_head(0)
    load_head(1)

    def emit_S(t):
        h, j, kb = items[t]
        sl = h % 2
        sbank = t % 4
        P.wait("pe", [hload[h], cx.bank_free[sbank], mld])
        diag = kb >= 8 * j
        P.op("pe", lambda e: e.matmul(cx.ps[sbank], lhsT=kt[sl][:, kb * 128:(kb + 1) * 128],
                                      rhs=qt[sl][:, j * 512:(j + 1) * 512], start=True, stop=False), signal=False)
        pe = P.op("pe", lambda e: e.matmul(cx.ps[sbank], lhsT=cx.ones_b[0:2, :], rhs=dqt[sl][0:2, j * 512:(j + 1) * 512],
                                           start=False, stop=(not diag)), signal=(not diag))
        if diag:
            pe = P.op("pe", lambda e: e.matmul(cx.ps[sbank], lhsT=cx.ident_b[:], rhs=maskF[:, kb - 8 * j, :],
                                               start=False, stop=True))
        p = t % NPT
        ev = P.op("act", lambda e: e.activation(out=pt[p][:], in_=cx.ps[sbank], func=AF.Exp,
                                                bias=cx.biask[:, h, j, kb:kb + 1], scale=SCALE),
                  waits=[pe, ptfree[p]])
        exp_ev[t] = ev
        cx.bank_free[sbank] = ev

    def emit_PV(t):
        h, j, kb = items[t]
        sl = h % 2
        hj = h * 4 + j
        ob = 4 + hj % 2
        lb = 6 + hj % 2
        last = (kb == 8 * j + 7)
        if kb == 0:
            P.wait("pe", [cx.bank_free[ob], cx.bank_free[lb]])
        P.wait("pe", exp_ev[t])
        p = t % NPT
        P.op("pe", lambda e: e.matmul(cx.ps[ob], lhsT=vt[sl][:, kb, :], rhs=pt[p][:], start=(kb == 0), stop=last),
             signal=False)
        pe = P.op("pe", lambda e: e.matmul(cx.ps[lb], lhsT=cx.ones_b[:], rhs=pt[p][:], start=(kb == 0), stop=last))
        ptfree[p] = pe
        if last:
            o = hj % 2
            a1 = P.op("act", lambda e: e.activation(out=rl[:], in_=cx.ps[lb], func=AF.Ln), waits=[pe, st_fin[0]])
            d1 = P.op("act", lambda e: e.activation(out=rl[:], in_=rl[:], func=AF.Exp, scale=-1.0), waits=[a1])
            d2 = P.op("dve", lambda e: e.tensor_tensor(out=ostg[o][:], in0=cx.ps[ob], in1=rl[:], op=ALU.mult),
                      waits=[d1, ofree[o]])
            st_fin[0] = d2
            cx.bank_free[ob] = d2
            cx.bank_free[lb] = d2
            ofree[o] = P.dma("sp", osem[o], lambda e: e.dma_start(out=mixT[h, :, j * 512:(j + 1) * 512], in_=ostg[o][:]),
                             waits=[d2])
            if j == 3:
                hdone[h] = pe
                if h + 2 < 16:
                    load_head(h + 2)

    for t in range(len(items) + LA):
        if t < len(items):
            emit_S(t)
        if t >= LA:
            emit_PV(t - LA)
    sb.release(m0)


def diff_attention(cx, KT, QT, Vs, mixT):
    P, sb = cx.P, cx.sb
    sm = cx.smalls
    m0 = sb.mark()
    kt = [sb.alloc("dkt", [128, 2, S], BF16) for _ in range(2)]
    qt = [sb.alloc("dqt", [128, 2, NOWN], BF16) for _ in range(2)]
    vt = [sb.alloc("dvt", [128, 2, 32, 128], BF16) for _ in range(2)]
    bd = [sb.alloc("bd", [128, 9, 512], BF16) for _ in range(2)]
    tmpb = [sb.alloc("tmpb", [128, 128], F32) for _ in range(2)]
    hsem = [P.sem("dh") for _ in range(2)]
    NPT = 3
    pt = [sb.alloc("dpt", [128, 512], BF16) for _ in range(NPT)]
    ptfree = [None] * NPT
    r1 = sb.alloc("r1", [128, 512], F32)
    r2 = sb.alloc("r2", [128, 512], F32)
    t2 = sb.alloc("t2", [128, 512], F32)
    dfe = [sb.alloc("dfe", [128, 512], F32) for _ in range(2)]
    sqb = [sb.alloc("sqb", [128, 512], BF16) for _ in range(2)]
    lnv = sb.alloc("lnv", [128, 512], F32)
    rstd = sb.alloc("rstdd", [128, 512], F32)
    ostg = [sb.alloc("dostg", [128, 2, 512], BF16) for _ in range(2)]
    osem = [P.sem("do") for _ in range(2)]
    ofree = [None, None]
    hload = {}
    hdone = {}
    bdready = {}
    st = {"sc": 0, "fin": None, "tb": 0}

    def load_head(h):
        sl = h % 2
        w = [hdone.get(h - 2)]
        P.dma("sp", hsem[sl], lambda e: e.dma_start(out=kt[sl][:], in_=KT[16 + 2 * h:18 + 2 * h].rearrange("c p t -> p c t")),
              waits=w)
        P.dma("sp", hsem[sl], lambda e: e.dma_start(out=vt[sl][:], in_=Vs[16 + 2 * h:18 + 2 * h].rearrange("c p k d -> p c k d")))
        hload[h] = P.dma("sp", hsem[sl], lambda e: e.dma_start(
            out=qt[sl][:], in_=QT[16 + 2 * h:18 + 2 * h].rearrange("c p t -> p c t")))
        def base(t):
            if t < -1:
                return cx.t5[:, 2, h, :]
            if t == -1:
                return cx.t5[:, 1, h, :]
            if t == 0:
                return cx.t5[:, 0, h, :]
            return cx.t5[:, 3, 0, :]
        ev = None
        for r in range(-1, 8):
            for i in range(4):
                tb = st["tb"] % 2
                st["tb"] += 1
                b0 = base(r - 2 * i)
                b1 = base(r - 2 * i - 1)
                ea = P.op("dve", lambda e, tb=tb, b0=b0: e.tensor_scalar(out=tmpb[tb][:], in0=b0, scalar1=cx.par[:, 0:1],
                                                                        scalar2=None, op0=ALU.mult), waits=w)
                ev = P.op("dve", lambda e, tb=tb, b1=b1, r=r, i=i: e.scalar_tensor_tensor(
                    out=bd[sl][:, r + 1, i * 128:(i + 1) * 128], in0=b1, scalar=cx.par[:, 1:2], in1=tmpb[tb][:],
                    op0=ALU.mult, op1=ALU.add), waits=[ea])
        bdready[h] = ev

    items = [(h, j, c, kb) for h in range(8) for j in range(4) for c in range(2) for kb in range(8 * j + 8)]
    LA = 1
    DEFER = 10
    exp_ev = {}
    dfa = [sb.alloc("dfa", [128, 512], F32) for _ in range(2)]
    load_head(0)
    load_head(1)
    st.update(r1free=None, r2free=None, sqfree=None, pend=None, since=0)

    def emit_S(t):
        h, j, c, kb = items[t]
        sl = h % 2
        sbank = st["sc"] % 2
        st["sc"] += 1
        diag = kb >= 8 * j - 1
        P.wait("pe", [hload[h], cx.bank_free[sbank]])
        pe = P.op("pe", lambda e: e.matmul(cx.ps[sbank], lhsT=kt[sl][:, c, kb * 128:(kb + 1) * 128],
                                           rhs=qt[sl][:, c, j * 512:(j + 1) * 512], start=True, stop=(not diag)),
                  signal=(not diag))
        if diag:
            P.wait("pe", bdready[h])
            pe = P.op("pe", lambda e: e.matmul(cx.ps[sbank], lhsT=cx.ident_b[:], rhs=bd[sl][:, kb - 8 * j + 1, :],
                                               start=False, stop=True))
        p = t % NPT
        bias = sm[:, NS_RB15 + h:NS_RB15 + h + 1] if not diag else cx.eps6[:, 2:3]
        ev = P.op("act", lambda e: e.activation(out=pt[p][:], in_=cx.ps[sbank], func=AF.Exp, bias=bias, scale=SCALE),
                  waits=[pe, ptfree[p]])
        exp_ev[t] = ev
        cx.bank_free[sbank] = ev

    def emit_ss():
        h, j, sq_evs = st["pend"]
        st["pend"] = None
        o = (h * 4 + j) % 2
        sbank = st["sc"] % 2
        st["sc"] += 1
        P.wait("pe", [cx.bank_free[sbank]] + sq_evs)
        P.op("pe", lambda e: e.matmul(cx.ps[sbank], lhsT=cx.ones_b[:], rhs=sqb[0][:], start=True, stop=False), signal=False)
        pss = P.op("pe", lambda e: e.matmul(cx.ps[sbank], lhsT=cx.ones_b[:], rhs=sqb[1][:], start=False, stop=True))
        st["sqfree"] = pss
        a1 = P.op("act", lambda e: e.activation(out=lnv[:], in_=cx.ps[sbank], func=AF.Ln, bias=cx.eps6[:, 1:2],
                                                scale=1.0 / 256.0), waits=[pss, st["fin"]])
        cx.bank_free[sbank] = a1
        a2 = P.op("act", lambda e: e.activation(out=rstd[:], in_=lnv[:], func=AF.Exp, scale=-0.5), waits=[a1])
        fin = None
        for e_ in range(2):
            fin = P.op("dve", lambda e, e_=e_: e.scalar_tensor_tensor(
                out=ostg[o][:, e_, :], in0=dfe[e_][:], scalar=cx.gsub8[:, e_:e_ + 1], in1=rstd[:],
                op0=ALU.mult, op1=ALU.mult), waits=[a2, ofree[o]])
        st["fin"] = fin
        ofree[o] = P.dma("sp", osem[o], lambda e: e.dma_start(
            out=mixT[16 + 2 * h:18 + 2 * h, :, j * 512:(j + 1) * 512].rearrange("c p t -> p c t"), in_=ostg[o][:]),
            waits=[fin])
        if j == 3:
            hdone[h] = pss
            if h + 2 < 8:
                load_head(h + 2)

    def emit_PV(t):
        h, j, c, kb = items[t]
        sl = h % 2
        u = (h * 4 + j) * 2 + c
        sset = u % 2
        ob = 2 + 3 * sset
        lb = 4 + 3 * sset
        last = (kb == 8 * j + 7)
        if kb == 0:
            P.wait("pe", [cx.bank_free[ob], cx.bank_free[ob + 1], cx.bank_free[lb]])
        P.wait("pe", exp_ev[t])
        p = t % NPT
        for e_ in range(2):
            P.op("pe", lambda e, e_=e_: e.matmul(cx.ps[ob + e_], lhsT=vt[sl][:, e_, kb, :], rhs=pt[p][:],
                                                 start=(kb == 0), stop=last), signal=False)
        pe = P.op("pe", lambda e: e.matmul(cx.ps[lb], lhsT=cx.ones_b[:], rhs=pt[p][:], start=(kb == 0), stop=last))
        ptfree[p] = pe
        st["since"] += 1
        if st["pend"] is not None and st["since"] >= DEFER:
            emit_ss()
        if last and c == 0:
            a1 = P.op("act", lambda e: e.activation(out=r1[:], in_=cx.ps[lb], func=AF.Ln), waits=[pe, st["r1free"]])
            a2 = P.op("act", lambda e: e.activation(out=r1[:], in_=r1[:], func=AF.Exp, scale=-1.0), waits=[a1])
            dl = None
            for e_ in range(2):
                dl = P.op("dve", lambda e, e_=e_: e.tensor_tensor(out=dfa[e_][:], in0=cx.ps[ob + e_], in1=r1[:], op=ALU.mult),
                          waits=[a2])
            st["r1free"] = dl
            for b in (ob, ob + 1, lb):
                cx.bank_free[b] = dl
        if last and c == 1:
            if st["pend"] is not None:
                emit_ss()
            a1 = P.op("act", lambda e: e.activation(out=r2[:], in_=cx.ps[lb], func=AF.Ln), waits=[pe, st["r2free"]])
            a2 = P.op("act", lambda e: e.activation(out=r2[:], in_=r2[:], func=AF.Exp, scale=-1.0), waits=[a1])
            sq_evs = []
            dl = None
            for e_ in range(2):
                d5 = P.op("dve", lambda e, e_=e_: e.tensor_tensor(out=t2[:], in0=cx.ps[ob + e_], in1=r2[:], op=ALU.mult),
                          waits=[a2])
                d6 = P.op("dve", lambda e, e_=e_: e.scalar_tensor_tensor(
                    out=dfe[e_][:], in0=t2[:], scalar=cx.lam[:, 1:2], in1=dfa[e_][:], op0=ALU.mult, op1=ALU.add),
                    waits=[d5, st["fin"]])
                d7 = P.op("dve", lambda e, e_=e_: e.tensor_tensor(out=sqb[e_][:], in0=dfe[e_][:], in1=dfe[e_][:], op=ALU.mult),
                          waits=[d6, st["sqfree"]])
                sq_evs.append(d7)
                dl = d5
            st["r2free"] = dl
            for b in (ob, ob + 1, lb):
                cx.bank_free[b] = dl
            st["pend"] = (h, j, sq_evs)
            st["since"] = 0

    for t in range(len(items) + LA):
        if t < len(items):
            emit_S(t)
        if t >= LA:
            emit_PV(t - LA)
    if st["pend"] is not None:
        emit_ss()
    sb.release(m0)


def cross_attention(cx, qmT, kmT, VM, omT):
    P, sb = cx.P, cx.sb
    m0 = sb.mark()
    kmt = sb.alloc("kmt", [128, KC, NMEM], BF16)
    vmt = sb.alloc("vmt", [128, KC, 2, 128], BF16)
    qm = [sb.alloc("qm", [128, KC, 512], BF16) for _ in range(2)]
    qsem = [P.sem("cq") for _ in range(2)]
    qfree = [None, None]
    ksem = P.sem("ck")
    P.dma("sp", ksem, lambda e: e.dma_start(out=kmt[:], in_=kmT.rearrange("k p t -> p k t")))
    kld = P.dma("sp", ksem, lambda e: e.dma_start(out=vmt[:], in_=VM.rearrange("c p k d -> p c k d")))
    ptm = [sb.alloc("ptm", [128, 512], BF16) for _ in range(4)]
    ptfree = [None] * 4
    rl = sb.alloc("crl", [128, 512], F32)
    ostg = [sb.alloc("costg", [128, 8, 512], BF16) for _ in range(2)]
    osem = [P.sem("co") for _ in range(2)]
    ofree = [None, None]
    st = {"b": 0}

    def nbank():
        b = st["b"] % 8
        st["b"] += 1
        return b

    qld = {}

    def loadq(t):
        b = t % 2
        qld[t] = P.dma("sp", qsem[b], lambda e: e.dma_start(
            out=qm[b][:], in_=qmT[:, :, t * 512:(t + 1) * 512].rearrange("k p t -> p k t")), waits=[qfree[b]])

    loadq(0)
    n = 0
    rl_free = None
    for t in range(4):
        if t + 1 < 4:
            loadq(t + 1)
        qb = t % 2
        for hm in range(4):
            o = n % 2
            pts = []
            for mb in range(2):
                bk = nbank()
                P.wait("pe", [qld[t], kld, cx.bank_free[bk]])
                for ch in range(8):
                    pe = P.op("pe", lambda e, ch=ch, mb=mb, bk=bk, hm=hm, qb=qb: e.matmul(
                        cx.ps[bk], lhsT=kmt[:, 8 * hm + ch, mb * 128:(mb + 1) * 128], rhs=qm[qb][:, 8 * hm + ch, :],
                        start=(ch == 0), stop=(ch == 7)), signal=(ch == 7))
                p = (2 * n + mb) % 4
                ev = P.op("act", lambda e, p=p, bk=bk: e.activation(out=ptm[p][:], in_=cx.ps[bk], func=AF.Exp, scale=SCALE_M),
                          waits=[pe, ptfree[p]])
                cx.bank_free[bk] = ev
                pts.append((p, ev))
            if hm == 3:
                qfree[qb] = pe
            bl = nbank()
            P.wait("pe", [cx.bank_free[bl], pts[0][1], pts[1][1]])
            P.op("pe", lambda e, bl=bl, p=pts[0][0]: e.matmul(cx.ps[bl], lhsT=cx.ones_b[:], rhs=ptm[p][:], start=True, stop=False),
                 signal=False)
            pl = P.op("pe", lambda e, bl=bl, p=pts[1][0]: e.matmul(cx.ps[bl], lhsT=cx.ones_b[:], rhs=ptm[p][:], start=False, stop=True))
            d1 = P.op("dve", lambda e, bl=bl: e.reciprocal(out=rl[:], in_=cx.ps[bl]), waits=[pl, rl_free])
            cx.bank_free[bl] = d1
            dlast = None
            for e_ in range(8):
                bo = nbank()
                P.wait("pe", [cx.bank_free[bo]])
                P.op("pe", lambda e, bo=bo, e_=e_, p=pts[0][0], hm=hm: e.matmul(cx.ps[bo], lhsT=vmt[:, 8 * hm + e_, 0, :], rhs=ptm[p][:],
                                                                       start=True, stop=False), signal=False)
                po = P.op("pe", lambda e, bo=bo, e_=e_, p=pts[1][0], hm=hm: e.matmul(cx.ps[bo], lhsT=vmt[:, 8 * hm + e_, 1, :], rhs=ptm[p][:],
                                                                            start=False, stop=True))
                dlast = P.op("dve", lambda e, bo=bo, e_=e_, o=o: e.tensor_tensor(out=ostg[o][:, e_, :], in0=cx.ps[bo], in1=rl[:],
                                                                                 op=ALU.mult), waits=[po, d1, ofree[o]])
                cx.bank_free[bo] = dlast
            ptfree[pts[0][0]] = po
            ptfree[pts[1][0]] = po
            rl_free = dlast
            ofree[o] = P.dma("sp", osem[o], lambda e, o=o, hm=hm, t=t: e.dma_start(
                out=omT[8 * hm:8 * hm + 8, :, t * 512:(t + 1) * 512].rearrange("c p t -> p c t"), in_=ostg[o][:]),
                waits=[dlast])
            n += 1
    sb.release(m0)


NS_G, NS_GSUB, NS_PAR, NS_BF, NS_RB, NS_RB15, NS_LAM, NS_ID, NS_E127 = 0, 160, 162, 164, 165, 173, 181, 693, 821
NS = 949


def build(upto=99, debug=()):
    nc = bass.Bass("TRN2", target_bir_lowering=False)
    cx = Ctx()
    cx.nc = nc
    cx.P = P = Prog(nc)
    cx.sb = sb = SBAlloc(nc)
    cx.phase_ev = None
    cx.inputs = []
    cx.outputs = []

    def inp(name, shape, dt=F32):
        cx.inputs.append(name)
        return nc.dram_tensor(name, list(shape), dt, kind="ExternalInput").ap()

    def outp(name, shape, dt=F32):
        cx.outputs.append(name)
        return nc.dram_tensor(name, list(shape), dt, kind="ExternalOutput").ap()

    def scratch(name, shape, dt):
        return nc.dram_tensor(name, list(shape), dt).ap()

    psum = cx.es_ps = P.es.enter_context(nc.psum_tensor("ps", [128, 8, 512], F32))
    cx.ps = [psum[:, b, :] for b in range(8)]
    cx.bank_free = [None] * 8

    smalls_d = inp("smalls", [128, NS])
    smalls = sb.alloc("smalls", [128, NS], F32)
    cx.gvec = smalls[:, NS_G:NS_G + 160].rearrange("p (g k) -> p g k", g=5)
    cx.ident_f = smalls[:, NS_ID:NS_ID + 128]
    cx.e127 = smalls[:, NS_E127:NS_E127 + 128]
    cx.par = smalls[:, NS_PAR:NS_PAR + 2]
    cx.smalls = smalls
    cx.ones_b = sb.alloc("ones_b", [128, 128], BF16)
    cx.ident_b = sb.alloc("ident_b", [128, 128], BF16)
    cx.eps6 = sb.alloc("eps6", [128, 4], F32)
    csem = P.sem("const")
    ld = P.dma("sp", csem, lambda e: e.dma_start(out=smalls[:], in_=smalls_d))
    P.op("dve", lambda e: e.memset(cx.ones_b[:], 1.0))
    P.op("dve", lambda e: e.memset(cx.eps6[:, 0:1], 1e-6))
    P.op("dve", lambda e: e.memset(cx.eps6[:, 1:2], 1e-5))
    P.op("dve", lambda e: e.memset(cx.eps6[:, 2:4], 0.0))
    P.op("dve", lambda e: e.tensor_copy(out=cx.ident_b[:], in_=cx.ident_f), waits=[ld])
    P.barrier()

    dbg = {}

    def finish():
        P.barrier()
        dsem = P.sem("dbg")
        for name, (ap, shape, dt) in dbg.items():
            if name in debug:
                o = outp("dbg_" + name, shape, dt)
                P.dma("sp", dsem, lambda e, o=o, a=ap: e.dma_start(out=o, in_=a))
        P.barrier()
        P.emit()
        return nc, cx

    xa = inp("xa", [KC, 128, S])
    xo = inp("xo", [KC, 128, NOWN])
    aT_all = scratch("aT_all", [KC, 128, S], BF16)
    aT_own = scratch("aT_own", [KC, 128, NOWN], BF16)
    dbg["aT_all"] = (aT_all, [KC, 128, S], BF16)
    dbg["aT_own"] = (aT_own, [KC, 128, NOWN], BF16)
    norm_pass(cx, xa, aT_all, S, 0, BF16)
    norm_pass(cx, xo, aT_own, NOWN, 0, BF16)
    P.barrier()
    if upto <= 1:
        return finish()

    w_in = inp("w_in", [D, 12304])
    KT = scratch("KT", [32, 128, S], BF16)
    QT = scratch("QT", [32, 128, NOWN], BF16)
    Vs = scratch("Vs", [32, 128, 32, 128], BF16)
    dbg["KT"] = (KT, [32, 128, S], BF16)
    dbg["QT"] = (QT, [32, 128, NOWN], BF16)
    dbg["Vs"] = (Vs, [32, 128, 32, 128], BF16)
    attn_mark = sb.mark()
    sigT = sb.alloc("sigT", [16, S], F32)
    cx.sigT = sigT

    def w_in_fn(k0, kn, c0, cn):
        return w_in[k0 * 128:(k0 + kn) * 128, c0:c0 + cn]

    def kt_dest(base):
        def f(tt, ci, cb):
            h0 = base + (cb["c0"] - cb["cbase"]) // 128
            return KT[h0:h0 + 4, :, tt * 1024:(tt + 1) * 1024].rearrange("h p t -> p h t")
        return f

    def v_dest(base):
        def f(tt, ci, cb):
            h0 = base + (cb["c0"] - cb["cbase"]) // 128
            return Vs[h0:h0 + 4, :, tt * 8:(tt + 1) * 8, :].rearrange("h p k d -> p h (k d)")
        return f

    def q_dest(base):
        def f(tt, ci, cb):
            h0 = base + (cb["c0"] - cb["cbase"]) // 128
            return QT[h0:h0 + 4, :, tt * 1024:(tt + 1) * 1024].rearrange("h p t -> p h t")
        return f

    ekf = EpiCopyFM(lambda tt, ci, cb: kt_dest(cb["hb"])(tt, ci, cb))
    ekd = ekf
    evf = EpiCopyTM(lambda tt, ci, cb: v_dest(cb["hb"])(tt, ci, cb))
    evd = evf
    esg = EpiSig(sigT, smalls[0:16, NS_BF:NS_BF + 1])
    cbs = []
    for c in range(4):
        cbs.append(dict(c0=C_KF + 512 * c, cn=512, cbase=C_KF, hb=0, mode="fm", epi=ekf))
    for c in range(4):
        cbs.append(dict(c0=C_VF + 512 * c, cn=512, cbase=C_VF, hb=0, mode="tm", epi=evf))
    cbs.append(dict(c0=C_F, cn=16, cbase=C_F, hb=0, mode="fm", epi=esg))
    for c in range(4):
        cbs.append(dict(c0=C_KD + 512 * c, cn=512, cbase=C_KD, hb=16, mode="fm", epi=ekd))
    for c in range(4):
        cbs.append(dict(c0=C_VD + 512 * c, cn=512, cbase=C_VD, hb=16, mode="tm", epi=evd))
    if upto == 2:
        cbs = [cbs[0], cbs[4], cbs[8], cbs[9], cbs[13]]
    gemm(cx, X=aT_all, kc=KC, wfn=w_in_fn, ntok=S, colblocks=cbs, x_stream=False)
    P.barrier()
    if upto <= 2:
        return finish()
    eqf = EpiCopyFM(lambda tt, ci, cb: q_dest(cb["hb"])(tt, ci, cb))
    eqd = eqf
    cbs = []
    for c in range(4):
        cbs.append(dict(c0=C_QF + 512 * c, cn=512, cbase=C_QF, hb=0, mode="fm", epi=eqf))
    for c in range(4):
        cbs.append(dict(c0=C_QD + 512 * c, cn=512, cbase=C_QD, hb=16, mode="fm", epi=eqd))
    gemm(cx, X=aT_own, kc=KC, wfn=w_in_fn, ntok=NOWN, colblocks=cbs, x_stream=False)
    P.barrier()
    if upto <= 3:
        return finish()

    dq_d = scratch("dq_d", [16, 2, NOWN], BF16)
    oh_d = inp("oh", [33, 2 * 128 * 128])
    maskF_d = inp("maskF", [128, 8 * 512])
    mixT = scratch("mixT", [KC, 128, NOWN], BF16)
    dbg["mixT"] = (mixT, [KC, 128, NOWN], BF16)
    dbg["dq_d"] = (dq_d, [16, 2, NOWN], BF16)
    attn_prep(cx, dq_d, oh_d)
    P.barrier()
    if upto <= 4:
        return finish()
    w_dn0 = inp("w_dn0", [DFF // 2, D])
    w_dn1 = inp("w_dn1", [DFF // 2, D])
    wdn_bf = scratch("wdn_bf", [DFF, D], BF16)
    def precast():
        pcsem = P.sem("precast")
        for r in range(32):
            wsrc = (w_dn0 if r < 16 else w_dn1)[(r % 16) * 512:(r % 16 + 1) * 512, :]
            P.dma("pool", pcsem, lambda e, d=wdn_bf[r * 512:(r + 1) * 512, :], s_=wsrc: e.dma_start(out=d, in_=s_))
    fox_attention(cx, KT, QT, Vs, dq_d, mixT, maskF_d, after_mask=precast)
    P.barrier()
    if upto <= 5:
        return finish()
    diff_attention(cx, KT, QT, Vs, mixT)
    P.barrier()
    sb.release(attn_mark)
    if upto <= 6:
        return finish()

    def fm_dest(Y, Tt=1024):
        def f(tt, ci, cb):
            h0 = cb["c0"] // 128
            n = (cb["cn"] + 127) // 128
            return Y[h0:h0 + n, :, tt * Tt:(tt + 1) * Tt].rearrange("h p t -> p h t")
        return f

    def simple_w(w):
        def f(k0, kn, c0, cn):
            return w[k0 * 128:(k0 + kn) * 128, c0:c0 + cn]
        return f

    def cblocks(n, epi, mode="fm"):
        return [dict(c0=512 * c, cn=512, cbase=0, hb=0, mode=mode, epi=epi) for c in range(n)]

    w_out = inp("w_out", [D, D])
    h1T = scratch("h1T", [KC, 128, NOWN], F32)
    dbg["h1T"] = (h1T, [KC, 128, NOWN], F32)
    gemm(cx, X=mixT, kc=KC, wfn=simple_w(w_out), ntok=NOWN, colblocks=cblocks(8, EpiResid(fm_dest(xo), fm_dest(h1T))),
         x_stream=False)
    P.barrier()
    if upto <= 7:
        return finish()
    memT = inp("memT", [KC, 128, NMEM])
    cT_d = scratch("cT_d", [KC, 128, NOWN], BF16)
    mT_d = scratch("mT_d", [KC, 128, NMEM], BF16)
    norm_pass(cx, h1T, cT_d, NOWN, 1, BF16)
    norm_pass(cx, memT, mT_d, NMEM, 2, BF16)
    P.barrier()
    wk = inp("wk_mem", [D, D])
    wv = inp("wv_mem", [D, D])
    kmT = scratch("kmT", [KC, 128, NMEM], BF16)
    VM = scratch("VM", [KC, 128, 2, 128], BF16)
    gemm(cx, X=mT_d, kc=KC, wfn=simple_w(wk), ntok=NMEM, colblocks=cblocks(8, EpiCopyFM(fm_dest(kmT, 256))),
         x_stream=False, Tt=256)
    P.barrier()

    def vm_dest(tt, ci, cb):
        h0 = cb["c0"] // 128
        return VM[h0:h0 + 4, :, :, :].rearrange("h p k d -> p h (k d)")
    gemm(cx, X=mT_d, kc=KC, wfn=simple_w(wv), ntok=NMEM, colblocks=cblocks(8, EpiCopyTM(vm_dest), mode="tm"),
         x_stream=False, Tt=256)
    P.barrier()
    wq = inp("wq_mem", [D, D])
    qmT = scratch("qmT", [KC, 128, NOWN], BF16)
    gemm(cx, X=cT_d, kc=KC, wfn=simple_w(wq), ntok=NOWN, colblocks=cblocks(8, EpiCopyFM(fm_dest(qmT))), x_stream=False)
    P.barrier()
    omT = scratch("omT", [KC, 128, NOWN], BF16)
    dbg["omT"] = (omT, [KC, 128, NOWN], BF16)
    cross_attention(cx, qmT, kmT, VM, omT)
    P.barrier()
    if upto <= 11:
        return finish()
    wo = inp("wo_mem", [D, D])
    h2T = scratch("h2T", [KC, 128, NOWN], F32)
    dbg["h2T"] = (h2T, [KC, 128, NOWN], F32)
    gemm(cx, X=omT, kc=KC, wfn=simple_w(wo), ntok=NOWN, colblocks=cblocks(8, EpiResid(fm_dest(h1T), fm_dest(h2T))),
         x_stream=False)
    P.barrier()
    if upto <= 12:
        return finish()
    nT_d = scratch("nT_d", [KC, 128, NOWN], BF16)
    norm_pass(cx, h2T, nT_d, NOWN, 3, BF16)
    P.barrier()
    w_up0 = inp("w_up0", [D, DFF // 2])
    w_up1 = inp("w_up1", [D, DFF // 2])
    actT = scratch("actT", [DFF // 128, 128, NOWN], BF16)

    def wup_fn(k0, kn, c0, cn):
        w, c = (w_up0, c0) if c0 < DFF // 2 else (w_up1, c0 - DFF // 2)
        return w[k0 * 128:(k0 + kn) * 128, c:c + cn]
    gemm(cx, X=nT_d, kc=KC, wfn=wup_fn, ntok=NOWN, colblocks=cblocks(32, EpiRelu2(fm_dest(actT))), x_stream=False)
    P.barrier()
    h3T = scratch("h3T", [KC, 128, NOWN], F32)
    dbg["h3T"] = (h3T, [KC, 128, NOWN], F32)

    def wdn_fn(k0, kn, c0, cn):
        return wdn_bf[k0 * 128:(k0 + kn) * 128, c0:c0 + cn]
    gemm(cx, X=actT, kc=DFF // 128, wfn=wdn_fn, ntok=NOWN, colblocks=cblocks(8, EpiResid(fm_dest(h2T), fm_dest(h3T))),
         x_stream=True)
    P.barrier()
    outT = outp("outT", [KC, 128, NOWN])
    norm_pass(cx, h3T, outT, NOWN, 4, F32, TT=128)
    return finish()


def own_tokens(qh):
    blocks = [8 * j + 2 * i + qh for j in range(4) for i in range(4)]
    return np.concatenate([np.arange(bk * 128, (bk + 1) * 128) for bk in blocks])


def t5_bucket_np(rel):
    rel = np.asarray(rel, np.int32)
    ret = np.where(rel > 0, 16, 0).astype(np.int32)
    n = np.abs(rel)
    nf = np.maximum(n, 1).astype(np.float32)
    large = 8 + (np.log(nf / np.float32(8)) / np.float32(math.log(128 / 8)) * np.float32(8)).astype(np.int32)
    large = np.minimum(large, 15)
    return ret + np.where(n < 8, n, large)


def fox_mask_table(qh):
    m = np.zeros((128, 8, 4, 128), np.float32)
    kk = np.arange(128)[:, None]
    qq = np.arange(128)[None, :]
    diag = np.where(kk <= qq, 0.0, NEG).astype(np.float32)
    for r in range(8):
        for i in range(4):
            t = r - (2 * i + qh)
            if t == 0:
                m[:, r, i, :] = diag
            elif t > 0:
                m[:, r, i, :] = NEG
    return m.reshape(128, 8 * 512)


def t5_onehot():
    oh = np.zeros((33, 2, 128, 128), np.float32)
    q = np.arange(128)[:, None]
    k = np.arange(128)[None, :]
    bd = t5_bucket_np(k - q)
    allowed = (k // 64) <= (q // 64)
    bd = np.where(allowed, bd, 32)
    bn = t5_bucket_np(k - q - 128)
    for b in range(33):
        oh[b, 0] = (bd == b)
        oh[b, 1] = (bn == b)
    return oh.reshape(33, 2 * 128 * 128)


def prep_smalls(inp, qh):
    s = np.zeros((128, NS), np.float32)
    gs = [inp["g_mix"][0], inp["g_cross"][0], inp["g_mem"][0], inp["g_mlp"][0], inp["g_final"]]
    for gi, g in enumerate(gs):
        s[:, NS_G + gi * 32:NS_G + (gi + 1) * 32] = np.asarray(g, np.float32).reshape(32, 128).T
    s[:, NS_GSUB:NS_GSUB + 2] = np.asarray(inp["g_subln"][0], np.float32).reshape(2, 128).T
    s[:, NS_PAR + qh] = 1.0
    s[0:16, NS_BF] = np.asarray(inp["b_forget"][0], np.float32)
    s[0:32, NS_RB:NS_RB + 8] = np.asarray(inp["rel_bias"], np.float32)
    s[:, NS_RB15:NS_RB15 + 8] = np.asarray(inp["rel_bias"], np.float32)[15][None, :]
    lam = np.stack([inp["lambda_q1"][0], inp["lambda_k1"][0], inp["lambda_q2"][0], inp["lambda_k2"][0]])
    s[:, NS_LAM:NS_LAM + 512] = np.asarray(lam, np.float32).reshape(1, 512)
    s[:, NS_ID:NS_ID + 128] = np.eye(128, dtype=np.float32)
    s[127, NS_E127:NS_E127 + 128] = 1.0
    return s


def prep_core(inp, c, names):
    b, qh = c // 2, c % 2
    m = {}
    xT = None
    if "xa" in names or "xo" in names:
        xT = np.ascontiguousarray(np.asarray(inp["x"][b], np.float32).T).reshape(KC, 128, S)
    if "xa" in names:
        m["xa"] = xT
    if "xo" in names:
        m["xo"] = np.ascontiguousarray(xT[:, :, own_tokens(qh)])
    if "memT" in names:
        m["memT"] = np.ascontiguousarray(np.asarray(inp["mem"][b], np.float32).T).reshape(KC, 128, NMEM)
    if "smalls" in names:
        m["smalls"] = prep_smalls(inp, qh)
    if "maskF" in names:
        m["maskF"] = fox_mask_table(qh)
    if "oh" in names:
        m["oh"] = t5_onehot()
    return m


def shared_inputs(inp, names):
    m = {}
    for nm in ("w_in", "w_out", "wq_mem", "wk_mem", "wv_mem", "wo_mem"):
        if nm in names:
            m[nm] = np.asarray(inp[nm][0], np.float32)
    if "w_up0" in names:
        w = np.asarray(inp["w_up"][0], np.float32)
        m["w_up0"] = np.ascontiguousarray(w[:, :DFF // 2])
        m["w_up1"] = np.ascontiguousarray(w[:, DFF // 2:])
    if "w_dn0" in names:
        w = np.asarray(inp["w_down"][0], np.float32)
        m["w_dn0"] = w[:DFF // 2]
        m["w_dn1"] = w[DFF // 2:]
    return m


def kernel(**inputs):
    nc, cx = build()
    sh = shared_inputs(inputs, cx.inputs)
    in_maps = []
    for c in range(8):
        m = prep_core(inputs, c, cx.inputs)
        m.update(sh)
        in_maps.append(m)
    res = run_bass_kernel_spmd(nc, in_maps, core_ids=list(range(8)))
    out = np.empty((4, S, D), np.float32)
    for c in range(8):
        b, qh = c // 2, c % 2
        o = np.asarray(res.results[c]["outT"], np.float32).reshape(D, NOWN)
        out[b, own_tokens(qh), :] = o.T
    return out
```

```python
import math
from contextlib import ExitStack
import numpy as np
import concourse.bass as bass
import concourse.mybir as mybir
from concourse.bass_utils import run_bass_kernel_spmd

F32 = mybir.dt.float32
BF16 = mybir.dt.bfloat16
AF = mybir.ActivationFunctionType
ALU = mybir.AluOpType
AXX = mybir.AxisListType.X

D = 4096
S = 4096
NOWN = 2048
DFF = 16384
NMEM = 256
KC = 32
SCALE = 128.0 ** -0.5
SCALE_M = 1024.0 ** -0.5
NEG = -30000.0
LAMBDA_INIT = 0.8 - 0.6 * math.exp(0.0)
C_QF, C_KF, C_VF, C_F, C_QD, C_KD, C_VD = 0, 2048, 4096, 6144, 6160, 8208, 10256
SB_BASE = 20480
SB_LIMIT = 229376


class Ev:
    __slots__ = ("sem", "val")

    def __init__(self, sem, val):
        self.sem = sem
        self.val = val


class Sem:
    def __init__(self, h, name):
        self.h = h
        self.name = name
        self.n = 0


class Prog:
    def __init__(self, nc):
        self.nc = nc
        self.es = ExitStack()
        self.q = {e: [] for e in ("pe", "act", "dve", "pool", "sp")}
        self.waited = {}
        self.nsem = 0
        self.prog = {e: self.sem("prog_" + e) for e in ("pe", "act", "dve", "pool")}

    def sem(self, name):
        if not hasattr(self, "allsems"):
            self.allsems = []
            self.pool = []
            self.inuse = []
        if name.startswith("prog_"):
            self.nsem += 1
            name = f"{name}_{self.nsem}"
            sm = Sem(self.es.enter_context(self.nc.semaphore(name)), name)
            self.allsems.append(sm)
            return sm
        if self.pool:
            sm = self.pool.pop()
        else:
            self.nsem += 1
            name = f"{name}_{self.nsem}"
            sm = Sem(self.es.enter_context(self.nc.semaphore(name)), name)
            self.allsems.append(sm)
        self.inuse.append(sm)
        return sm

    def wait(self, eng, ev):
        if ev is None:
            return
        if isinstance(ev, (list, tuple)):
            for e in ev:
                self.wait(eng, e)
            return
        k = (eng, ev.sem.name)
        if self.waited.get(k, 0) >= ev.val:
            return
        self.waited[k] = ev.val
        self.q[eng].append(("w", ev.sem, ev.val))

    def op(self, eng, fn, waits=(), signal=True):
        self.wait(eng, waits)
        if signal:
            s = self.prog[eng]
            s.n += 1
            self.q[eng].append(("o", fn, s, 1))
            return Ev(s, s.n)
        self.q[eng].append(("o", fn, None, 0))
        return None

    def dma(self, eng, sem, fn, waits=()):
        self.wait(eng, waits)
        sem.n += 16
        self.q[eng].append(("o", fn, sem, 16))
        return Ev(sem, sem.n)

    def barrier(self):
        if not hasattr(self, "allsems"):
            self.allsems = []
        evs = [Ev(sm, sm.n) for sm in self.allsems if sm.n > 0]
        for eng in self.q:
            self.wait(eng, evs)
        self.pool.extend(self.inuse)
        self.inuse = []

    def emit(self):
        with self.nc.Block() as block:
            def mk(eng):
                def f(e):
                    for it in self.q[eng]:
                        if it[0] == "w":
                            e.wait_ge(it[1].h, it[2])
                        else:
                            ins = it[1](e)
                            if it[2] is not None:
                                ins.then_inc(it[2].h, it[3])
                return f
            block.tensor(mk("pe"))
            block.scalar(mk("act"))
            block.vector(mk("dve"))
            block.gpsimd(mk("pool"))
            block.sync(mk("sp"))


class SBAlloc:
    def __init__(self, nc):
        self.nc = nc
        self.off = SB_BASE
        self.cnt = 0

    def alloc(self, name, shape, dtype):
        self.cnt += 1
        nbytes = int(np.prod(shape[1:])) * (2 if dtype == BF16 else 4)
        nbytes = (nbytes + 63) // 64 * 64
        off = self.off
        assert off + nbytes <= SB_LIMIT, f"SBUF overflow at {name}: {off}+{nbytes}"
        self.off += nbytes
        return self.nc.alloc_sbuf_tensor_at(f"{name}_{self.cnt}", list(shape), dtype, offset=off)

    def mark(self):
        return self.off

    def release(self, m):
        self.off = m


class Ctx:
    pass


def gemm(cx, *, X, kc, wfn, ntok, colblocks, x_stream, Tt=1024, KP=16, bg=None):
    P, sb = cx.P, cx.sb
    m0 = sb.mark()
    NW = 3
    Wt = [sb.alloc("gw", [128, KP, 512], BF16) for _ in range(NW)]
    wsem = [P.sem("gw") for _ in range(NW)]
    wfree = [cx.phase_ev] * NW
    nk = kc // KP
    if x_stream:
        NX = 3
        Xt = [sb.alloc("gx", [128, KP, Tt], BF16) for _ in range(NX)]
    else:
        NX = 1
        Xt = [sb.alloc("gx", [128, kc, Tt], BF16)]
    xsem = [P.sem("gx") for _ in range(NX)]
    xsem2 = P.sem("gx2")
    xfree = [cx.phase_ev] * NX
    for cb in colblocks:
        cb["epi"].setup(cx, Tt)
    ntt = ntok // Tt
    NT = min(512, Tt)
    cx.NT = NT
    pieces = [(tt, ci, kp) for tt in range(ntt) for ci in range(len(colblocks)) for kp in range(nk)]
    wload = {}
    xload = {}

    def load_w(i):
        tt, ci, kp = pieces[i]
        cb = colblocks[ci]
        slot = i % NW
        src = wfn(kp * KP, KP, cb["c0"], cb["cn"]).rearrange("(k p) c -> p k c", p=128)
        dst = Wt[slot][:, :, 0:cb["cn"]]
        wload[i] = P.dma("pool", wsem[slot], lambda e, d=dst, s=src: e.dma_start(out=d, in_=s),
                         waits=[wfree[slot]])
        if bg:
            bg.pop(0)()

    def load_x(i):
        tt, ci, kp = pieces[i]
        if x_stream:
            slot = i % NX
            src = X[kp * KP:(kp + 1) * KP, :, tt * Tt:(tt + 1) * Tt].rearrange("k p t -> p k t")
            xload[i] = P.dma("pool", xsem[slot], lambda e, d=Xt[slot][:], s=src: e.dma_start(out=d, in_=s),
                             waits=[xfree[slot]])
        else:
            if ci == 0 and kp == 0:
                src = X[:, :, tt * Tt:(tt + 1) * Tt].rearrange("k p t -> p k t")
                half = kc // 2
                xload[(tt, 0)] = P.dma("pool", xsem[0], lambda e, d=Xt[0][:, 0:half, :], s=src[:, 0:half, :]: e.dma_start(out=d, in_=s),
                                       waits=[xfree[0]])
                xload[(tt, 1)] = P.dma("pool", xsem2, lambda e, d=Xt[0][:, half:kc, :], s=src[:, half:kc, :]: e.dma_start(out=d, in_=s))

    npieces = len(pieces)
    PRE = NW - 1
    if x_stream:
        for i in range(min(NX, npieces)):
            load_x(i)
    for i in range(min(PRE, npieces)):
        load_w(i)
    if not x_stream:
        load_x(0)
    ev = None
    for i, (tt, ci, kp) in enumerate(pieces):
        cb = colblocks[ci]
        epi = cb["epi"]
        cn = cb["cn"]
        if kp == 0:
            epi.begin(cx, tt, ci, cb)
        slot = i % NW
        P.wait("pe", wload[i])
        if x_stream:
            xs = i % NX
            P.wait("pe", xload[i])
            Xc = Xt[xs]
        else:
            P.wait("pe", xload[(tt, 0)])
            if (kp + 1) * KP > kc // 2:
                P.wait("pe", xload[(tt, 1)])
            Xc = Xt[0]
        if cb["mode"] == "fm":
            groups = [(cs, ts) for cs in range((cn + 127) // 128) for ts in range(Tt // NT)]
        else:
            groups = [(tb,) for tb in range(Tt // 128)]
        for gi, g in enumerate(groups):
            bank = gi
            if kp == 0:
                P.wait("pe", cx.bank_free[bank])
            for k in range(KP):
                first = (kp == 0 and k == 0)
                last = (kp == nk - 1 and k == KP - 1)
                endp = (gi == len(groups) - 1 and k == KP - 1)
                kk = k if x_stream else kp * KP + k
                if cb["mode"] == "fm":
                    cs, ts = g
                    m = min(128, cn - cs * 128)
                    out = cx.ps[bank][0:m, 0:NT]
                    lhsT = Wt[slot][:, k, cs * 128:cs * 128 + m]
                    rhs = Xc[:, kk, ts * NT:(ts + 1) * NT]
                else:
                    tb = g[0]
                    out = cx.ps[bank][:, 0:cn]
                    lhsT = Xc[:, kk, tb * 128:(tb + 1) * 128]
                    rhs = Wt[slot][:, k, 0:cn]
                ev = P.op("pe", lambda e, o=out, l=lhsT, r=rhs, st=first, sp=last:
                          e.matmul(o, lhsT=l, rhs=r, start=st, stop=sp), signal=(last or endp))
            if kp == nk - 1:
                cx.bank_free[bank] = epi.group(cx, cx.ps[bank], tt, ci, cb, g, ev)
        wfree[slot] = ev
        if x_stream:
            xfree[xs] = ev
            if i + NX < npieces:
                load_x(i + NX)
        else:
            if ci == len(colblocks) - 1 and kp == nk - 1:
                xfree[0] = ev
                if tt + 1 < ntt:
                    load_x(i + 1)
        if i + PRE < npieces:
            load_w(i + PRE)
        if kp == nk - 1:
            epi.end(cx, tt, ci, cb)
    cx.phase_ev = ev
    evs = [ev]
    for cb in colblocks:
        evs += cb["epi"].finish(cx)
    sb.release(m0)
    return evs


class EpiBase:
    def setup(self, cx, Tt):
        pass

    def begin(self, cx, tt, ci, cb):
        pass

    def end(self, cx, tt, ci, cb):
        pass

    def finish(self, cx):
        return []


class EpiCopyFM(EpiBase):
    def __init__(self, destfn, eng="act"):
        self.destfn = destfn
        self.eng = eng
        self.ready = False

    def setup(self, cx, Tt):
        if self.ready:
            return
        self.ready = True
        self.Tt = Tt
        self.stg = [cx.sb.alloc("stg", [128, 4, Tt], BF16) for _ in range(2)]
        self.ssem = [cx.P.sem("st") for _ in range(2)]
        self.sfree = [None, None]
        self.cnt = 0
        self.last = None

    def begin(self, cx, tt, ci, cb):
        self.buf = self.cnt % 2
        self.cnt += 1

    def group(self, cx, bank, tt, ci, cb, g, pe_ev):
        cs, ts = g
        m = min(128, cb["cn"] - cs * 128)
        NT = cx.NT
        o = self.stg[self.buf][0:m, cs, ts * NT:(ts + 1) * NT]
        i = bank[0:m, 0:NT]
        if self.eng == "act":
            fn = lambda e, o=o, i=i: e.activation(out=o, in_=i, func=AF.Copy)
        else:
            fn = lambda e, o=o, i=i: e.tensor_copy(out=o, in_=i)
        self.last = cx.P.op(self.eng, fn, waits=[pe_ev, self.sfree[self.buf]])
        return self.last

    def end(self, cx, tt, ci, cb):
        b = self.buf
        n = (cb["cn"] + 127) // 128
        dst = self.destfn(tt, ci, cb)
        src = self.stg[b][:, 0:n, :]
        self.sfree[b] = cx.P.dma("sp", self.ssem[b], lambda e, d=dst, s=src: e.dma_start(out=d, in_=s),
                                 waits=[self.last])

    def finish(self, cx):
        return [e for e in self.sfree if e is not None]


class EpiCopyTM(EpiBase):
    def __init__(self, destfn):
        self.destfn = destfn
        self.ready = False

    def setup(self, cx, Tt):
        if self.ready:
            return
        self.ready = True
        self.ntb = Tt // 128
        self.stg = [cx.sb.alloc("stgv", [128, 4, self.ntb, 128], BF16) for _ in range(2)]
        self.ssem = [cx.P.sem("stv") for _ in range(2)]
        self.sfree = [None, None]
        self.cnt = 0

    def begin(self, cx, tt, ci, cb):
        self.buf = self.cnt % 2
        self.cnt += 1

    def group(self, cx, bank, tt, ci, cb, g, pe_ev):
        tb = g[0]
        nh = cb["cn"] // 128
        o = self.stg[self.buf][:, 0:nh, tb, :]
        i = bank[:, 0:cb["cn"]].rearrange("p (h d) -> p h d", h=nh)
        self.last = cx.P.op("dve", lambda e, o=o, i=i: e.tensor_copy(out=o, in_=i),
                            waits=[pe_ev, self.sfree[self.buf]])
        return self.last

    def end(self, cx, tt, ci, cb):
        b = self.buf
        dst = self.destfn(tt, ci, cb)
        src = self.stg[b][:].rearrange("p h t d -> p h (t d)")
        self.sfree[b] = cx.P.dma("sp", self.ssem[b], lambda e, d=dst, s=src: e.dma_start(out=d, in_=s),
                                 waits=[self.last])

    def finish(self, cx):
        return [e for e in self.sfree if e is not None]


class EpiResid(EpiBase):
    def __init__(self, residfn, destfn):
        self.residfn = residfn
        self.destfn = destfn
        self.ready = False

    def setup(self, cx, Tt):
        if self.ready:
            return
        self.ready = True
        self.res = [cx.sb.alloc("res", [128, 4, Tt], F32) for _ in range(2)]
        self.rsem = [cx.P.sem("rs") for _ in range(2)]
        self.ssem = [cx.P.sem("str") for _ in range(2)]
        self.rfree = [None, None]
        self.rload = [None, None]
        self.cnt = 0

    def begin(self, cx, tt, ci, cb):
        b = self.cnt % 2
        self.buf = b
        self.cnt += 1
        src = self.residfn(tt, ci, cb)
        self.rload[b] = cx.P.dma("sp", self.rsem[b], lambda e, d=self.res[b][:], s=src: e.dma_start(out=d, in_=s),
                                 waits=[self.rfree[b]])

    def group(self, cx, bank, tt, ci, cb, g, pe_ev):
        cs, ts = g
        b = self.buf
        r = self.res[b][:, cs, ts * 512:(ts + 1) * 512]
        self.last = cx.P.op("dve", lambda e, i=bank, r=r: e.tensor_tensor(out=r, in0=i, in1=r, op=ALU.add),
                            waits=[pe_ev, self.rload[b]])
        return self.last

    def end(self, cx, tt, ci, cb):
        b = self.buf
        dst = self.destfn(tt, ci, cb)
        self.rfree[b] = cx.P.dma("sp", self.ssem[b], lambda e, d=dst, s=self.res[b][:]: e.dma_start(out=d, in_=s),
                                 waits=[self.last])

    def finish(self, cx):
        return [e for e in self.rfree if e is not None]


class EpiRelu2(EpiBase):
    def __init__(self, destfn):
        self.destfn = destfn
        self.ready = False

    def setup(self, cx, Tt):
        if self.ready:
            return
        self.ready = True
        self.tmp = [cx.sb.alloc("rtmp", [128, 512], F32) for _ in range(2)]
        self.tfree = [None, None]
        self.stg = [cx.sb.alloc("stgu", [128, 4, Tt], BF16) for _ in range(2)]
        self.ssem = [cx.P.sem("stu") for _ in range(2)]
        self.sfree = [None, None]
        self.cnt = 0
        self.gc = 0

    def begin(self, cx, tt, ci, cb):
        self.buf = self.cnt % 2
        self.cnt += 1

    def group(self, cx, bank, tt, ci, cb, g, pe_ev):
        cs, ts = g
        b = self.buf
        t = self.gc % 2
        self.gc += 1
        tm = self.tmp[t][:]
        a_ev = cx.P.op("act", lambda e, o=tm, i=bank: e.activation(out=o, in_=i, func=AF.Relu),
                       waits=[pe_ev, self.tfree[t]])
        o = self.stg[b][:, cs, ts * 512:(ts + 1) * 512]
        self.last = cx.P.op("dve", lambda e, o=o, i=tm: e.tensor_tensor(out=o, in0=i, in1=i, op=ALU.mult),
                            waits=[a_ev, self.sfree[b]])
        self.tfree[t] = self.last
        return a_ev

    def end(self, cx, tt, ci, cb):
        b = self.buf
        dst = self.destfn(tt, ci, cb)
        self.sfree[b] = cx.P.dma("sp", self.ssem[b], lambda e, d=dst, s=self.stg[b][:]: e.dma_start(out=d, in_=s),
                                 waits=[self.last])

    def finish(self, cx):
        return [e for e in self.sfree if e is not None]


class EpiSig(EpiBase):
    def __init__(self, sigT, bias):
        self.sigT = sigT
        self.bias = bias
        self.last = None

    def group(self, cx, bank, tt, ci, cb, g, pe_ev):
        cs, ts = g
        t0 = tt * 1024 + ts * 512
        o = self.sigT[0:16, t0:t0 + 512]
        self.last = cx.P.op("act", lambda e, o=o, i=bank[0:16, :], b=self.bias: e.activation(
            out=o, in_=i, func=AF.Sigmoid, bias=b, scale=1.0), waits=[pe_ev])
        return self.last

    def finish(self, cx):
        return [self.last]


def norm_pass(cx, src, dst, ntok, gi, out_dtype, TT=256, waits=()):
    P, sb = cx.P, cx.sb
    m0 = sb.mark()
    NXB = 4 if TT == 128 else (3 if out_dtype == BF16 else 2)
    xin = [sb.alloc("nx", [128, KC, TT], F32) for _ in range(NXB)]
    xsem = [P.sem("nx") for _ in range(NXB)]
    xfree = [None] * NXB
    sq = [sb.alloc("nsq", [128, KC, TT], BF16) for _ in range(2)]
    sqfree = [None, None]
    lnv = sb.alloc("nln", [128, TT], F32)
    rstd = [sb.alloc("nrstd", [128, TT], F32) for _ in range(2)]
    rfree = [None, None]
    ot = [sb.alloc("no", [128, KC, TT], out_dtype) for _ in range(2)]
    osem = [P.sem("no") for _ in range(2)]
    ofree = [None, None]
    nt = ntok // TT
    ld = {}
    sqev = {}
    KD = 32

    def load(t):
        b = t % NXB
        s_ = src[:, :, t * TT:(t + 1) * TT].rearrange("k p t -> p k t")
        ld[t] = P.dma("sp", xsem[b], lambda e, d=xin[b][:], s=s_: e.dma_start(out=d, in_=s),
                      waits=[xfree[b]] + list(waits))

    def square(t):
        b = t % NXB
        q = t % 2
        sqev[t] = P.op("act", lambda e, o=sq[q][:], i=xin[b][:]: e.activation(out=o, in_=i, func=AF.Square),
                       waits=[ld[t], sqfree[q]])

    for t in range(min(NXB - 1, nt)):
        load(t)
    square(0)
    for t in range(nt):
        if t + NXB - 1 < nt:
            load(t + NXB - 1)
        if t + 1 < nt:
            square(t + 1)
        b = t % NXB
        q = t % 2
        bank = t % 2
        P.wait("pe", [sqev[t], cx.bank_free[bank]])
        for k in range(KC):
            pe = P.op("pe", lambda e, o=cx.ps[bank][:, 0:TT], r=sq[q][:, k, :], st=(k == 0), sp=(k == KC - 1):
                      e.matmul(o, lhsT=cx.ones_b[:], rhs=r, start=st, stop=sp), signal=(k == KC - 1))
        sqfree[q] = pe
        a2 = P.op("act", lambda e, i=cx.ps[bank][:, 0:TT]: e.activation(
            out=lnv[:], in_=i, func=AF.Ln, bias=cx.eps6[:, 0:1], scale=1.0 / D), waits=[pe])
        cx.bank_free[bank] = a2
        a3 = P.op("act", lambda e, o=rstd[q][:]: e.activation(out=o, in_=lnv[:], func=AF.Exp, scale=-0.5),
                  waits=[a2, rfree[q]])
        last_d = last_p = None
        for k in range(KC):
            eng = "dve" if k < KD else "pool"
            ev = P.op(eng, lambda e, o=ot[t % 2][:, k, :], i=xin[b][:, k, :], g=cx.gvec[:, gi, k:k + 1],
                      r=rstd[q][:]: e.scalar_tensor_tensor(out=o, in0=i, scalar=g, in1=r, op0=ALU.mult, op1=ALU.mult),
                      waits=[a3, ofree[t % 2]])
            if eng == "dve":
                last_d = ev
            else:
                last_p = ev
        xfree[b] = [last_d, last_p]
        rfree[q] = [last_d, last_p]
        dd = dst[:, :, t * TT:(t + 1) * TT].rearrange("k p t -> p k t")
        ofree[t % 2] = P.dma("sp", osem[t % 2], lambda e, d=dd, s=ot[t % 2][:]: e.dma_start(out=d, in_=s),
                             waits=[last_d, last_p])
    sb.release(m0)
    return [e for e in ofree if e is not None]


def attn_prep(cx, dq_d, oh_d):
    P, sb = cx.P, cx.sb
    sm = cx.smalls
    sigT = cx.sigT
    cx.ctm = sb.alloc("ctm", [128, 32, 16], F32)
    cx.biask = sb.alloc("biask", [128, 16, 4, 32], F32)
    cx.lam = sb.alloc("lam", [128, 4], F32)
    cx.t5 = sb.alloc("t5", [128, 4, 8, 128], F32)
    cx.rb15s = sb.alloc("rb15s", [128, 8], F32)
    cx.gsub8 = sb.alloc("gsub8", [128, 2], F32)
    m0 = sb.mark()
    ones16 = sb.alloc("ones16", [16, S], F32)
    cT = sb.alloc("cT", [16, S], F32)
    dqf = sb.alloc("dqf", [16, S], F32)
    tmp2 = sb.alloc("tmp2", [16, NOWN], F32)
    dqo = sb.alloc("dqo", [16, NOWN], F32)
    hib = sb.alloc("hib", [16, NOWN], BF16)
    lob = sb.alloc("lob", [16, NOWN], BF16)
    crbc = sb.alloc("crbc", [128, 4, 16], F32)
    lamt = sb.alloc("lamt", [128, 2, 128], F32)
    rbext = sb.alloc("rbext", [64, 8], F32)
    ohc = [sb.alloc("ohc", [33, 32, 128], F32) for _ in range(2)]
    ones_f = sb.alloc("ones_f", [128, 128], F32)

    a0 = P.op("act", lambda e: e.activation(out=sigT[:], in_=sigT[:], func=AF.Ln))
    d0 = P.op("dve", lambda e: e.memset(ones16[:], 1.0))
    prev = [a0, d0]
    for sg in range(4):
        ini = 0.0 if sg == 0 else cT[:, sg * 1024 - 1:sg * 1024]
        pv = P.op("dve", lambda e, sg=sg, ini=ini: e.tensor_tensor_scan(
            out=cT[:, sg * 1024:(sg + 1) * 1024], data0=ones16[:, sg * 1024:(sg + 1) * 1024],
            data1=sigT[:, sg * 1024:(sg + 1) * 1024], initial=ini, op0=ALU.mult, op1=ALU.add), waits=prev)
        prev = [pv]
    dscan = prev[0]
    P.wait("pe", [dscan, cx.bank_free[0]])
    for kb in range(32):
        pe = P.op("pe", lambda e, kb=kb: e.matmul(cx.ps[0][:, kb * 16:(kb + 1) * 16],
                                                 lhsT=cT[0:16, kb * 128:(kb + 1) * 128],
                                                 rhs=cx.ident_f[0:16, 0:16], start=True, stop=True),
                  signal=(kb == 31))
    dctm = P.op("dve", lambda e: e.tensor_copy(out=cx.ctm[:].rearrange("p k h -> p (k h)"), in_=cx.ps[0]), waits=[pe])
    cx.bank_free[0] = dctm
    dz = P.op("dve", lambda e: e.memset(crbc[:, 0, :], 0.0))
    P.wait("pe", [dctm, cx.bank_free[1]])
    for j in range(1, 4):
        pe = P.op("pe", lambda e, j=j: e.matmul(cx.ps[1][:, j * 16:(j + 1) * 16], lhsT=cx.e127,
                                               rhs=cx.ctm[:, 8 * j - 1, :], start=True, stop=True),
                  signal=(j == 3))
    dcr = P.op("dve", lambda e: e.tensor_copy(out=crbc[:, 1:4, :].rearrange("p j h -> p (j h)"),
                                              in_=cx.ps[1][:, 16:64]), waits=[pe, dz])
    cx.bank_free[1] = dcr
    for h in range(16):
        for j in range(4):
            P.op("dve", lambda e, h=h, j=j: e.tensor_scalar(
                out=cx.biask[:, h, j, :], in0=cx.ctm[:, :, h], scalar1=crbc[:, j, h:h + 1], scalar2=-1.0,
                op0=ALU.subtract, op1=ALU.mult), waits=[dcr])
    evs = []
    for j in range(4):
        sc1 = 0.0 if j == 0 else cT[:, 1024 * j - 1:1024 * j]
        evs.append(P.op("dve", lambda e, j=j, sc1=sc1: e.tensor_scalar(
            out=dqf[:, 1024 * j:1024 * (j + 1)], in0=cT[:, 1024 * j:1024 * (j + 1)], scalar1=sc1,
            scalar2=1.0 / SCALE, op0=ALU.subtract, op1=ALU.mult), waits=[dscan]))
    v = dqf[:].rearrange("p (a r t) -> p a r t", r=2, t=128)
    t2v = tmp2[:].rearrange("p (a t) -> p a t", t=128)
    dqv = dqo[:].rearrange("p (a t) -> p a t", t=128)
    e1 = P.op("dve", lambda e: e.tensor_scalar(out=t2v, in0=v[:, :, 0, :], scalar1=cx.par[0:16, 0:1], scalar2=None,
                                               op0=ALU.mult), waits=evs)
    e2 = P.op("dve", lambda e: e.scalar_tensor_tensor(out=dqv, in0=v[:, :, 1, :], scalar=cx.par[0:16, 1:2], in1=t2v,
                                                      op0=ALU.mult, op1=ALU.add), waits=[e1])
    e3 = P.op("dve", lambda e: e.tensor_copy(out=hib[:], in_=dqo[:]), waits=[e2])
    e4 = P.op("dve", lambda e: e.tensor_copy(out=tmp2[:], in_=hib[:]), waits=[e3])
    e5 = P.op("dve", lambda e: e.tensor_tensor(out=tmp2[:], in0=dqo[:], in1=tmp2[:], op=ALU.subtract), waits=[e4])
    e6 = P.op("dve", lambda e: e.tensor_copy(out=lob[:], in_=tmp2[:]), waits=[e5])
    dsem = P.sem("dq")
    P.dma("sp", dsem, lambda e: e.dma_start(out=dq_d[:, 0, :], in_=hib[:]), waits=[e3])
    P.dma("sp", dsem, lambda e: e.dma_start(out=dq_d[:, 1, :], in_=lob[:]), waits=[e6])
    lv = sm[:, NS_LAM:NS_LAM + 512].rearrange("p (a d) -> p a d", a=4)
    l1 = P.op("dve", lambda e: e.tensor_tensor(out=lamt[:, 0, :], in0=lv[:, 0, :], in1=lv[:, 1, :], op=ALU.mult))
    l2 = P.op("dve", lambda e: e.tensor_tensor(out=lamt[:, 1, :], in0=lv[:, 2, :], in1=lv[:, 3, :], op=ALU.mult))
    l3 = P.op("dve", lambda e: e.tensor_reduce(out=cx.lam[:, 0:2], in_=lamt[:], axis=AXX, op=ALU.add), waits=[l1, l2])
    l4 = P.op("act", lambda e: e.activation(out=cx.lam[:, 2:4], in_=cx.lam[:, 0:2], func=AF.Exp), waits=[l3])
    l5 = P.op("dve", lambda e: e.tensor_tensor(out=cx.lam[:, 0:1], in0=cx.lam[:, 2:3], in1=cx.lam[:, 3:4],
                                               op=ALU.subtract), waits=[l4])
    l6 = P.op("dve", lambda e: e.tensor_scalar(out=cx.lam[:, 0:1], in0=cx.lam[:, 0:1], scalar1=LAMBDA_INIT, scalar2=None,
                                               op0=ALU.add), waits=[l5])
    P.op("dve", lambda e: e.tensor_scalar(out=cx.lam[:, 1:2], in0=cx.lam[:, 0:1], scalar1=-1.0, scalar2=None,
                                          op0=ALU.mult), waits=[l6])
    P.op("dve", lambda e: e.tensor_scalar(out=cx.gsub8[:], in0=sm[:, NS_GSUB:NS_GSUB + 2],
                                          scalar1=1.0 - LAMBDA_INIT, scalar2=None, op0=ALU.mult))
    r0 = P.op("dve", lambda e: e.memset(rbext[32:33, :], NEG))
    r1 = P.op("dve", lambda e: e.tensor_scalar(out=rbext[0:32, :], in0=sm[0:32, NS_RB:NS_RB + 8], scalar1=1.0 / SCALE,
                                               scalar2=None, op0=ALU.mult))
    r2 = P.op("dve", lambda e: e.tensor_scalar(out=cx.rb15s[:], in0=sm[:, NS_RB15:NS_RB15 + 8], scalar1=1.0 / SCALE,
                                               scalar2=None, op0=ALU.mult))
    r3 = P.op("dve", lambda e: e.memset(ones_f[:], 1.0))
    P.op("dve", lambda e: e.memset(cx.t5[:, 3, 0, :], NEG))
    for h in range(8):
        P.op("dve", lambda e, h=h: e.tensor_scalar(out=cx.t5[:, 2, h, :], in0=ones_f[:], scalar1=cx.rb15s[:, h:h + 1],
                                                   scalar2=None, op0=ALU.mult), waits=[r2, r3])
    osem = [P.sem("oh") for _ in range(2)]
    ofree = [None, None]
    ohv = oh_d.rearrange("b (t q k) -> b t q k", t=2, q=128)
    n = 0
    for ty in range(2):
        P.wait("pe", [cx.bank_free[2], cx.bank_free[3], r0, r1])
        for qc in range(4):
            b = n % 2
            n += 1
            ld = P.dma("sp", osem[b], lambda e, b=b, ty=ty, qc=qc: e.dma_start(
                out=ohc[b][:], in_=ohv[:, ty, qc * 32:(qc + 1) * 32, :]), waits=[ofree[b]])
            P.wait("pe", ld)
            for ql in range(32):
                q = qc * 32 + ql
                bank = 2 + (q * 8) // 512
                col = (q * 8) % 512
                pe = P.op("pe", lambda e, b=b, ql=ql, bank=bank, col=col: e.matmul(
                    cx.ps[bank][:, col:col + 8], lhsT=ohc[b][0:33, ql, :], rhs=rbext[0:33, 0:8], start=True, stop=True),
                    signal=(ql == 31))
            ofree[b] = pe
        dd = None
        for hb in range(2):
            dd = P.op("dve", lambda e, ty=ty, hb=hb: e.tensor_copy(
                out=cx.t5[:, ty, :, hb * 64:(hb + 1) * 64],
                in_=cx.ps[2 + hb].rearrange("p (q h) -> p h q", h=8)), waits=[pe])
            cx.bank_free[2 + hb] = dd
    sb.release(m0)


def attn_loads(cx, kinds, nheads, done_evs):
    pass


def fox_attention(cx, KT, QT, Vs, dq_d, mixT, maskF_d, after_mask=None):
    P, sb = cx.P, cx.sb
    m0 = sb.mark()
    kt = [sb.alloc("kt", [128, S], BF16) for _ in range(2)]
    vt = [sb.alloc("vt", [128, 32, 128], BF16) for _ in range(2)]
    qt = [sb.alloc("qt", [128, NOWN], BF16) for _ in range(2)]
    dqt = [sb.alloc("dqt", [2, NOWN], BF16) for _ in range(2)]
    hsem = [P.sem("fh") for _ in range(2)]
    maskF = sb.alloc("maskF", [128, 8, 512], BF16)
    NPT = 3
    pt = [sb.alloc("pt", [128, 512], BF16) for _ in range(NPT)]
    ptfree = [None] * NPT
    rl = sb.alloc("rl", [128, 512], F32)
    ostg = [sb.alloc("ostg", [128, 512], BF16) for _ in range(2)]
    osem = [P.sem("fo") for _ in range(2)]
    ofree = [None, None]
    msem = P.sem("mk")
    mld = P.dma("pool", msem, lambda e: e.dma_start(out=maskF[:].rearrange("p r q -> p (r q)"), in_=maskF_d))
    if after_mask is not None:
        after_mask()
    hload = {}
    hdone = {}

    def load_head(h):
        sl = h % 2
        w = [hdone.get(h - 2)]
        P.dma("sp", hsem[sl], lambda e: e.dma_start(out=kt[sl][:], in_=KT[h]), waits=w)
        P.dma("sp", hsem[sl], lambda e: e.dma_start(out=vt[sl][:], in_=Vs[h]))
        P.dma("sp", hsem[sl], lambda e: e.dma_start(out=qt[sl][:], in_=QT[h]))
        hload[h] = P.dma("sp", hsem[sl], lambda e: e.dma_start(out=dqt[sl][:], in_=dq_d[h]))

    items = [(h, j, kb) for h in range(16) for j in range(4) for kb in range(8 * j + 8)]
    st_fin = [None]
    LA = 2
    exp_ev = {}
    load_head(0)
    load_head(1)

    def emit_S(t):
        h, j, kb = items[t]
        sl = h % 2
        sbank = t % 4
        P.wait("pe", [hload[h], cx.bank_free[sbank], mld])
        diag = kb >= 8 * j
        P.op("pe", lambda e: e.matmul(cx.ps[sbank], lhsT=kt[sl][:, kb * 128:(kb + 1) * 128],
                                      rhs=qt[sl][:, j * 512:(j + 1) * 512], start=True, stop=False), signal=False)
        pe = P.op("pe", lambda e: e.matmul(cx.ps[sbank], lhsT=cx.ones_b[0:2, :], rhs=dqt[sl][0:2, j * 512:(j + 1) * 512],
                                           start=False, stop=(not diag)), signal=(not diag))
        if diag:
            pe = P.op("pe", lambda e: e.matmul(cx.ps[sbank], lhsT=cx.ident_b[:], rhs=maskF[:, kb - 8 * j, :],
                                               start=False, stop=True))
        p = t % NPT
        ev = P.op("act", lambda e: e.activation(out=pt[p][:], in_=cx.ps[sbank], func=AF.Exp,
                                                bias=cx.biask[:, h, j, kb:kb + 1], scale=SCALE),
                  waits=[pe, ptfree[p]])
        exp_ev[t] = ev
        cx.bank_free[sbank] = ev

    def emit_PV(t):
        h, j, kb = items[t]
        sl = h % 2
        hj = h * 4 + j
        ob = 4 + hj % 2
        lb = 6 + hj % 2
        last = (kb == 8 * j + 7)
        if kb == 0:
            P.wait("pe", [cx.bank_free[ob], cx.bank_free[lb]])
        P.wait("pe", exp_ev[t])
        p = t % NPT
        P.op("pe", lambda e: e.matmul(cx.ps[ob], lhsT=vt[sl][:, kb, :], rhs=pt[p][:], start=(kb == 0), stop=last),
             signal=False)
        pe = P.op("pe", lambda e: e.matmul(cx.ps[lb], lhsT=cx.ones_b[:], rhs=pt[p][:], start=(kb == 0), stop=last))
        ptfree[p] = pe
        if last:
            o = hj % 2
            a1 = P.op("act", lambda e: e.activation(out=rl[:], in_=cx.ps[lb], func=AF.Ln), waits=[pe, st_fin[0]])
            d1 = P.op("act", lambda e: e.activation(out=rl[:], in_=rl[:], func=AF.Exp, scale=-1.0), waits=[a1])
            d2 = P.op("dve", lambda e: e.tensor_tensor(out=ostg[o][:], in0=cx.ps[ob], in1=rl[:], op=ALU.mult),
                      waits=[d1, ofree[o]])
            st_fin[0] = d2
            cx.bank_free[ob] = d2
            cx.bank_free[lb] = d2
            ofree[o] = P.dma("sp", osem[o], lambda e: e.dma_start(out=mixT[h, :, j * 512:(j + 1) * 512], in_=ostg[o][:]),
                             waits=[d2])
            if j == 3:
                hdone[h] = pe
                if h + 2 < 16:
                    load_head(h + 2)

    for t in range(len(items) + LA):
        if t < len(items):
            emit_S(t)
        if t >= LA:
            emit_PV(t - LA)
    sb.release(m0)


def diff_attention(cx, KT, QT, Vs, mixT):
    P, sb = cx.P, cx.sb
    sm = cx.smalls
    m0 = sb.mark()
    kt = [sb.alloc("dkt", [128, 2, S], BF16) for _ in range(2)]
    qt = [sb.alloc("dqt", [128, 2, NOWN], BF16) for _ in range(2)]
    vt = [sb.alloc("dvt", [128, 2, 32, 128], BF16) for _ in range(2)]
    bd = [sb.alloc("bd", [128, 9, 512], BF16) for _ in range(2)]
    tmpb = [sb.alloc("tmpb", [128, 128], F32) for _ in range(2)]
    hsem = [P.sem("dh") for _ in range(2)]
    NPT = 3
    pt = [sb.alloc("dpt", [128, 512], BF16) for _ in range(NPT)]
    ptfree = [None] * NPT
    r1 = sb.alloc("r1", [128, 512], F32)
    r2 = sb.alloc("r2", [128, 512], F32)
    t2 = sb.alloc("t2", [128, 512], F32)
    dfe = [sb.alloc("dfe", [128, 512], F32) for _ in range(2)]
    sqb = [sb.alloc("sqb", [128, 512], BF16) for _ in range(2)]
    lnv = sb.alloc("lnv", [128, 512], F32)
    rstd = sb.alloc("rstdd", [128, 512], F32)
    ostg = [sb.alloc("dostg", [128, 2, 512], BF16) for _ in range(2)]
    osem = [P.sem("do") for _ in range(2)]
    ofree = [None, None]
    hload = {}
    hdone = {}
    bdready = {}
    st = {"sc": 0, "fin": None, "tb": 0}

    def load_head(h):
        sl = h % 2
        w = [hdone.get(h - 2)]
        P.dma("sp", hsem[sl], lambda e: e.dma_start(out=kt[sl][:], in_=KT[16 + 2 * h:18 + 2 * h].rearrange("c p t -> p c t")),
              waits=w)
        P.dma("sp", hsem[sl], lambda e: e.dma_start(out=vt[sl][:], in_=Vs[16 + 2 * h:18 + 2 * h].rearrange("c p k d -> p c k d")))
        hload[h] = P.dma("sp", hsem[sl], lambda e: e.dma_start(
            out=qt[sl][:], in_=QT[16 + 2 * h:18 + 2 * h].rearrange("c p t -> p c t")))
        def base(t):
            if t < -1:
                return cx.t5[:, 2, h, :]
            if t == -1:
                return cx.t5[:, 1, h, :]
            if t == 0:
                return cx.t5[:, 0, h, :]
            return cx.t5[:, 3, 0, :]
        ev = None
        for r in range(-1, 8):
            for i in range(4):
                tb = st["tb"] % 2
                st["tb"] += 1
                b0 = base(r - 2 * i)
                b1 = base(r - 2 * i - 1)
                ea = P.op("dve", lambda e, tb=tb, b0=b0: e.tensor_scalar(out=tmpb[tb][:], in0=b0, scalar1=cx.par[:, 0:1],
                                                                        scalar2=None, op0=ALU.mult), waits=w)
                ev = P.op("dve", lambda e, tb=tb, b1=b1, r=r, i=i: e.scalar_tensor_tensor(
                    out=bd[sl][:, r + 1, i * 128:(i + 1) * 128], in0=b1, scalar=cx.par[:, 1:2], in1=tmpb[tb][:],
                    op0=ALU.mult, op1=ALU.add), waits=[ea])
        bdready[h] = ev

    items = [(h, j, c, kb) for h in range(8) for j in range(4) for c in range(2) for kb in range(8 * j + 8)]
    LA = 1
    DEFER = 10
    exp_ev = {}
    dfa = [sb.alloc("dfa", [128, 512], F32) for _ in range(2)]
    load_head(0)
    load_head(1)
    st.update(r1free=None, r2free=None, sqfree=None, pend=None, since=0)

    def emit_S(t):
        h, j, c, kb = items[t]
        sl = h % 2
        sbank = st["sc"] % 2
        st["sc"] += 1
        diag = kb >= 8 * j - 1
        P.wait("pe", [hload[h], cx.bank_free[sbank]])
        pe = P.op("pe", lambda e: e.matmul(cx.ps[sbank], lhsT=kt[sl][:, c, kb * 128:(kb + 1) * 128],
                                           rhs=qt[sl][:, c, j * 512:(j + 1) * 512], start=True, stop=(not diag)),
                  signal=(not diag))
        if diag:
            P.wait("pe", bdready[h])
            pe = P.op("pe", lambda e: e.matmul(cx.ps[sbank], lhsT=cx.ident_b[:], rhs=bd[sl][:, kb - 8 * j + 1, :],
                                               start=False, stop=True))
        p = t % NPT
        bias = sm[:, NS_RB15 + h:NS_RB15 + h + 1] if not diag else cx.eps6[:, 2:3]
        ev = P.op("act", lambda e: e.activation(out=pt[p][:], in_=cx.ps[sbank], func=AF.Exp, bias=bias, scale=SCALE),
                  waits=[pe, ptfree[p]])
        exp_ev[t] = ev
        cx.bank_free[sbank] = ev

    def emit_ss():
        h, j, sq_evs = st["pend"]
        st["pend"] = None
        o = (h * 4 + j) % 2
        sbank = st["sc"] % 2
        st["sc"] += 1
        P.wait("pe", [cx.bank_free[sbank]] + sq_evs)
        P.op("pe", lambda e: e.matmul(cx.ps[sbank], lhsT=cx.ones_b[:], rhs=sqb[0][:], start=True, stop=False), signal=False)
        pss = P.op("pe", lambda e: e.matmul(cx.ps[sbank], lhsT=cx.ones_b[:], rhs=sqb[1][:], start=False, stop=True))
        st["sqfree"] = pss
        a1 = P.op("act", lambda e: e.activation(out=lnv[:], in_=cx.ps[sbank], func=AF.Ln, bias=cx.eps6[:, 1:2],
                                                scale=1.0 / 256.0), waits=[pss, st["fin"]])
        cx.bank_free[sbank] = a1
        a2 = P.op("act", lambda e: e.activation(out=rstd[:], in_=lnv[:], func=AF.Exp, scale=-0.5), waits=[a1])
        fin = None
        for e_ in range(2):
            fin = P.op("dve", lambda e, e_=e_: e.scalar_tensor_tensor(
                out=ostg[o][:, e_, :], in0=dfe[e_][:], scalar=cx.gsub8[:, e_:e_ + 1], in1=rstd[:],
                op0=ALU.mult, op1=ALU.mult), waits=[a2, ofree[o]])
        st["fin"] = fin
        ofree[o] = P.dma("sp", osem[o], lambda e: e.dma_start(
            out=mixT[16 + 2 * h:18 + 2 * h, :, j * 512:(j + 1) * 512].rearrange("c p t -> p c t"), in_=ostg[o][:]),
            waits=[fin])
        if j == 3:
            hdone[h] = pss
            if h + 2 < 8:
                load_head(h + 2)

    def emit_PV(t):
        h, j, c, kb = items[t]
        sl = h % 2
        u = (h * 4 + j) * 2 + c
        sset = u % 2
        ob = 2 + 3 * sset
        lb = 4 + 3 * sset
        last = (kb == 8 * j + 7)
        if kb == 0:
            P.wait("pe", [cx.bank_free[ob], cx.bank_free[ob + 1], cx.bank_free[lb]])
        P.wait("pe", exp_ev[t])
        p = t % NPT
        for e_ in range(2):
            P.op("pe", lambda e, e_=e_: e.matmul(cx.ps[ob + e_], lhsT=vt[sl][:, e_, kb, :], rhs=pt[p][:],
                                                 start=(kb == 0), stop=last), signal=False)
        pe = P.op("pe", lambda e: e.matmul(cx.ps[lb], lhsT=cx.ones_b[:], rhs=pt[p][:], start=(kb == 0), stop=last))
        ptfree[p] = pe
        st["since"] += 1
        if st["pend"] is not None and st["since"] >= DEFER:
            emit_ss()
        if last and c == 0:
            a1 = P.op("act", lambda e: e.activation(out=r1[:], in_=cx.ps[lb], func=AF.Ln), waits=[pe, st["r1free"]])
            a2 = P.op("act", lambda e: e.activation(out=r1[:], in_=r1[:], func=AF.Exp, scale=-1.0), waits=[a1])
            dl = None
            for e_ in range(2):
                dl = P.op("dve", lambda e, e_=e_: e.tensor_tensor(out=dfa[e_][:], in0=cx.ps[ob + e_], in1=r1[:], op=ALU.mult),
                          waits=[a2])
            st["r1free"] = dl
            for b in (ob, ob + 1, lb):
                cx.bank_free[b] = dl
        if last and c == 1:
            if st["pend"] is not None:
                emit_ss()
            a1 = P.op("act", lambda e: e.activation(out=r2[:], in_=cx.ps[lb], func=AF.Ln), waits=[pe, st["r2free"]])
            a2 = P.op("act", lambda e: e.activation(out=r2[:], in_=r2[:], func=AF.Exp, scale=-1.0), waits=[a1])
            sq_evs = []
            dl = None
            for e_ in range(2):
                d5 = P.op("dve", lambda e, e_=e_: e.tensor_tensor(out=t2[:], in0=cx.ps[ob + e_], in1=r2[:], op=ALU.mult),
                          waits=[a2])
                d6 = P.op("dve", lambda e, e_=e_: e.scalar_tensor_tensor(
                    out=dfe[e_][:], in0=t2[:], scalar=cx.lam[:, 1:2], in1=dfa[e_][:], op0=ALU.mult, op1=ALU.add),
                    waits=[d5, st["fin"]])
                d7 = P.op("dve", lambda e, e_=e_: e.tensor_tensor(out=sqb[e_][:], in0=dfe[e_][:], in1=dfe[e_][:], op=ALU.mult),
                          waits=[d6, st["sqfree"]])
                sq_evs.append(d7)
                dl = d5
            st["r2free"] = dl
            for b in (ob, ob + 1, lb):
                cx.bank_free[b] = dl
            st["pend"] = (h, j, sq_evs)
            st["since"] = 0

    for t in range(len(items) + LA):
        if t < len(items):
            emit_S(t)
        if t >= LA:
            emit_PV(t - LA)
    if st["pend"] is not None:
        emit_ss()
    sb.release(m0)


def cross_attention(cx, qmT, kmT, VM, omT):
    P, sb = cx.P, cx.sb
    m0 = sb.mark()
    kmt = sb.alloc("kmt", [128, KC, NMEM], BF16)
    vmt = sb.alloc("vmt", [128, KC, 2, 128], BF16)
    qm = [sb.alloc("qm", [128, KC, 512], BF16) for _ in range(2)]
    qsem = [P.sem("cq") for _ in range(2)]
    qfree = [None, None]
    ksem = P.sem("ck")
    P.dma("sp", ksem, lambda e: e.dma_start(out=kmt[:], in_=kmT.rearrange("k p t -> p k t")))
    kld = P.dma("sp", ksem, lambda e: e.dma_start(out=vmt[:], in_=VM.rearrange("c p k d -> p c k d")))
    ptm = [sb.alloc("ptm", [128, 512], BF16) for _ in range(4)]
    ptfree = [None] * 4
    rl = sb.alloc("crl", [128, 512], F32)
    ostg = [sb.alloc("costg", [128, 8, 512], BF16) for _ in range(2)]
    osem = [P.sem("co") for _ in range(2)]
    ofree = [None, None]
    st = {"b": 0}

    def nbank():
        b = st["b"] % 8
        st["b"] += 1
        return b

    qld = {}

    def loadq(t):
        b = t % 2
        qld[t] = P.dma("sp", qsem[b], lambda e: e.dma_start(
            out=qm[b][:], in_=qmT[:, :, t * 512:(t + 1) * 512].rearrange("k p t -> p k t")), waits=[qfree[b]])

    loadq(0)
    n = 0
    rl_free = None
    for t in range(4):
        if t + 1 < 4:
            loadq(t + 1)
        qb = t % 2
        for hm in range(4):
            o = n % 2
            pts = []
            for mb in range(2):
                bk = nbank()
                P.wait("pe", [qld[t], kld, cx.bank_free[bk]])
                for ch in range(8):
                    pe = P.op("pe", lambda e, ch=ch, mb=mb, bk=bk, hm=hm, qb=qb: e.matmul(
                        cx.ps[bk], lhsT=kmt[:, 8 * hm + ch, mb * 128:(mb + 1) * 128], rhs=qm[qb][:, 8 * hm + ch, :],
                        start=(ch == 0), stop=(ch == 7)), signal=(ch == 7))
                p = (2 * n + mb) % 4
                ev = P.op("act", lambda e, p=p, bk=bk: e.activation(out=ptm[p][:], in_=cx.ps[bk], func=AF.Exp, scale=SCALE_M),
                          waits=[pe, ptfree[p]])
                cx.bank_free[bk] = ev
                pts.append((p, ev))
            if hm == 3:
                qfree[qb] = pe
            bl = nbank()
            P.wait("pe", [cx.bank_free[bl], pts[0][1], pts[1][1]])
            P.op("pe", lambda e, bl=bl, p=pts[0][0]: e.matmul(cx.ps[bl], lhsT=cx.ones_b[:], rhs=ptm[p][:], start=True, stop=False),
                 signal=False)
            pl = P.op("pe", lambda e, bl=bl, p=pts[1][0]: e.matmul(cx.ps[bl], lhsT=cx.ones_b[:], rhs=ptm[p][:], start=False, stop=True))
            d1 = P.op("dve", lambda e, bl=bl: e.reciprocal(out=rl[:], in_=cx.ps[bl]), waits=[pl, rl_free])
            cx.bank_free[bl] = d1
            dlast = None
            for e_ in range(8):
                bo = nbank()
                P.wait("pe", [cx.bank_free[bo]])
                P.op("pe", lambda e, bo=bo, e_=e_, p=pts[0][0], hm=hm: e.matmul(cx.ps[bo], lhsT=vmt[:, 8 * hm + e_, 0, :], rhs=ptm[p][:],
                                                                       start=True, stop=False), signal=False)
                po = P.op("pe", lambda e, bo=bo, e_=e_, p=pts[1][0], hm=hm: e.matmul(cx.ps[bo], lhsT=vmt[:, 8 * hm + e_, 1, :], rhs=ptm[p][:],
                                                                            start=False, stop=True))
                dlast = P.op("dve", lambda e, bo=bo, e_=e_, o=o: e.tensor_tensor(out=ostg[o][:, e_, :], in0=cx.ps[bo], in1=rl[:],
                                                                                 op=ALU.mult), waits=[po, d1, ofree[o]])
                cx.bank_free[bo] = dlast
            ptfree[pts[0][0]] = po
            ptfree[pts[1][0]] = po
            rl_free = dlast
            ofree[o] = P.dma("sp", osem[o], lambda e, o=o, hm=hm, t=t: e.dma_start(
                out=omT[8 * hm:8 * hm + 8, :, t * 512:(t + 1) * 512].rearrange("c p t -> p c t"), in_=ostg[o][:]),
                waits=[dlast])
            n += 1
    sb.release(m0)


NS_G, NS_GSUB, NS_PAR, NS_BF, NS_RB, NS_RB15, NS_LAM, NS_ID, NS_E127 = 0, 160, 162, 164, 165, 173, 181, 693, 821
NS = 949


def build(upto=99, debug=()):
    nc = bass.Bass("TRN2", target_bir_lowering=False)
    cx = Ctx()
    cx.nc = nc
    cx.P = P = Prog(nc)
    cx.sb = sb = SBAlloc(nc)
    cx.phase_ev = None
    cx.inputs = []
    cx.outputs = []

    def inp(name, shape, dt=F32):
        cx.inputs.append(name)
        return nc.dram_tensor(name, list(shape), dt, kind="ExternalInput").ap()

    def outp(name, shape, dt=F32):
        cx.outputs.append(name)
        return nc.dram_tensor(name, list(shape), dt, kind="ExternalOutput").ap()

    def scratch(name, shape, dt):
        return nc.dram_tensor(name, list(shape), dt).ap()

    psum = cx.es_ps = P.es.enter_context(nc.psum_tensor("ps", [128, 8, 512], F32))
    cx.ps = [psum[:, b, :] for b in range(8)]
    cx.bank_free = [None] * 8

    smalls_d = inp("smalls", [128, NS])
    smalls = sb.alloc("smalls", [128, NS], F32)
    cx.gvec = smalls[:, NS_G:NS_G + 160].rearrange("p (g k) -> p g k", g=5)
    cx.ident_f = smalls[:, NS_ID:NS_ID + 128]
    cx.e127 = smalls[:, NS_E127:NS_E127 + 128]
    cx.par = smalls[:, NS_PAR:NS_PAR + 2]
    cx.smalls = smalls
    cx.ones_b = sb.alloc("ones_b", [128, 128], BF16)
    cx.ident_b = sb.alloc("ident_b", [128, 128], BF16)
    cx.eps6 = sb.alloc("eps6", [128, 4], F32)
    csem = P.sem("const")
    ld = P.dma("sp", csem, lambda e: e.dma_start(out=smalls[:], in_=smalls_d))
    P.op("dve", lambda e: e.memset(cx.ones_b[:], 1.0))
    P.op("dve", lambda e: e.memset(cx.eps6[:, 0:1], 1e-6))
    P.op("dve", lambda e: e.memset(cx.eps6[:, 1:2], 1e-5))
    P.op("dve", lambda e: e.memset(cx.eps6[:, 2:4], 0.0))
    P.op("dve", lambda e: e.tensor_copy(out=cx.ident_b[:], in_=cx.ident_f), waits=[ld])
    P.barrier()

    dbg = {}

    def finish():
        P.barrier()
        dsem = P.sem("dbg")
        for name, (ap, shape, dt) in dbg.items():
            if name in debug:
                o = outp("dbg_" + name, shape, dt)
                P.dma("sp", dsem, lambda e, o=o, a=ap: e.dma_start(out=o, in_=a))
        P.barrier()
        P.emit()
        return nc, cx

    xa = inp("xa", [KC, 128, S])
    xo = inp("xo", [KC, 128, NOWN])
    aT_all = scratch("aT_all", [KC, 128, S], BF16)
    aT_own = scratch("aT_own", [KC, 128, NOWN], BF16)
    dbg["aT_all"] = (aT_all, [KC, 128, S], BF16)
    dbg["aT_own"] = (aT_own, [KC, 128, NOWN], BF16)
    norm_pass(cx, xa, aT_all, S, 0, BF16)
    norm_pass(cx, xo, aT_own, NOWN, 0, BF16)
    P.barrier()
    if upto <= 1:
        return finish()

    w_in = inp("w_in", [D, 12304])
    KT = scratch("KT", [32, 128, S], BF16)
    QT = scratch("QT", [32, 128, NOWN], BF16)
    Vs = scratch("Vs", [32, 128, 32, 128], BF16)
    dbg["KT"] = (KT, [32, 128, S], BF16)
    dbg["QT"] = (QT, [32, 128, NOWN], BF16)
    dbg["Vs"] = (Vs, [32, 128, 32, 128], BF16)
    attn_mark = sb.mark()
    sigT = sb.alloc("sigT", [16, S], F32)
    cx.sigT = sigT

    def w_in_fn(k0, kn, c0, cn):
        return w_in[k0 * 128:(k0 + kn) * 128, c0:c0 + cn]

    def kt_dest(base):
        def f(tt, ci, cb):
            h0 = base + (cb["c0"] - cb["cbase"]) // 128
            return KT[h0:h0 + 4, :, tt * 1024:(tt + 1) * 1024].rearrange("h p t -> p h t")
        return f

    def v_dest(base):
        def f(tt, ci, cb):
            h0 = base + (cb["c0"] - cb["cbase"]) // 128
            return Vs[h0:h0 + 4, :, tt * 8:(tt + 1) * 8, :].rearrange("h p k d -> p h (k d)")
        return f

    def q_dest(base):
        def f(tt, ci, cb):
            h0 = base + (cb["c0"] - cb["cbase"]) // 128
            return QT[h0:h0 + 4, :, tt * 1024:(tt + 1) * 1024].rearrange("h p t -> p h t")
        return f

    ekf = EpiCopyFM(lambda tt, ci, cb: kt_dest(cb["hb"])(tt, ci, cb))
    ekd = ekf
    evf = EpiCopyTM(lambda tt, ci, cb: v_dest(cb["hb"])(tt, ci, cb))
    evd = evf
    esg = EpiSig(sigT, smalls[0:16, NS_BF:NS_BF + 1])
    cbs = []
    for c in range(4):
        cbs.append(dict(c0=C_KF + 512 * c, cn=512, cbase=C_KF, hb=0, mode="fm", epi=ekf))
    for c in range(4):
        cbs.append(dict(c0=C_VF + 512 * c, cn=512, cbase=C_VF, hb=0, mode="tm", epi=evf))
    cbs.append(dict(c0=C_F, cn=16, cbase=C_F, hb=0, mode="fm", epi=esg))
    for c in range(4):
        cbs.append(dict(c0=C_KD + 512 * c, cn=512, cbase=C_KD, hb=16, mode="fm", epi=ekd))
    for c in range(4):
        cbs.append(dict(c0=C_VD + 512 * c, cn=512, cbase=C_VD, hb=16, mode="tm", epi=evd))
    if upto == 2:
        cbs = [cbs[0], cbs[4], cbs[8], cbs[9], cbs[13]]
    w_dn0 = inp("w_dn0", [DFF // 2, D])
    w_dn1 = inp("w_dn1", [DFF // 2, D])
    wdn_bf = scratch("wdn_bf", [DFF, D], BF16)
    pcsem = P.sem("precast")
    bg = []
    for r in range(128):
        wsrc = (w_dn0 if r < 64 else w_dn1)[(r % 64) * 128:(r % 64 + 1) * 128, :]
        bg.append(lambda d=wdn_bf[r * 128:(r + 1) * 128, :], s_=wsrc: P.dma(
            "pool", pcsem, lambda e, d=d, s_=s_: e.dma_start(out=d, in_=s_)))
    gemm(cx, X=aT_all, kc=KC, wfn=w_in_fn, ntok=S, colblocks=cbs, x_stream=False, bg=bg)
    while bg:
        bg.pop(0)()
    P.barrier()
    if upto <= 2:
        return finish()
    eqf = EpiCopyFM(lambda tt, ci, cb: q_dest(cb["hb"])(tt, ci, cb))
    eqd = eqf
    cbs = []
    for c in range(4):
        cbs.append(dict(c0=C_QF + 512 * c, cn=512, cbase=C_QF, hb=0, mode="fm", epi=eqf))
    for c in range(4):
        cbs.append(dict(c0=C_QD + 512 * c, cn=512, cbase=C_QD, hb=16, mode="fm", epi=eqd))
    gemm(cx, X=aT_own, kc=KC, wfn=w_in_fn, ntok=NOWN, colblocks=cbs, x_stream=False)
    P.barrier()
    if upto <= 3:
        return finish()

    dq_d = scratch("dq_d", [16, 2, NOWN], BF16)
    oh_d = inp("oh", [33, 2 * 128 * 128])
    maskF_d = inp("maskF", [128, 8 * 512])
    mixT = scratch("mixT", [KC, 128, NOWN], BF16)
    dbg["mixT"] = (mixT, [KC, 128, NOWN], BF16)
    dbg["dq_d"] = (dq_d, [16, 2, NOWN], BF16)
    attn_prep(cx, dq_d, oh_d)
    P.barrier()
    if upto <= 4:
        return finish()
    fox_attention(cx, KT, QT, Vs, dq_d, mixT, maskF_d)
    P.barrier()
    if upto <= 5:
        return finish()
    diff_attention(cx, KT, QT, Vs, mixT)
    P.barrier()
    sb.release(attn_mark)
    if upto <= 6:
        return finish()

    def fm_dest(Y, Tt=1024):
        def f(tt, ci, cb):
            h0 = cb["c0"] // 128
            n = (cb["cn"] + 127) // 128
            return Y[h0:h0 + n, :, tt * Tt:(tt + 1) * Tt].rearrange("h p t -> p h t")
        return f

    def simple_w(w):
        def f(k0, kn, c0, cn):
            return w[k0 * 128:(k0 + kn) * 128, c0:c0 + cn]
        return f

    def cblocks(n, epi, mode="fm"):
        return [dict(c0=512 * c, cn=512, cbase=0, hb=0, mode=mode, epi=epi) for c in range(n)]

    w_out = inp("w_out", [D, D])
    h1T = scratch("h1T", [KC, 128, NOWN], F32)
    dbg["h1T"] = (h1T, [KC, 128, NOWN], F32)
    gemm(cx, X=mixT, kc=KC, wfn=simple_w(w_out), ntok=NOWN, colblocks=cblocks(8, EpiResid(fm_dest(xo), fm_dest(h1T))),
         x_stream=False)
    P.barrier()
    if upto <= 7:
        return finish()
    memT = inp("memT", [KC, 128, NMEM])
    cT_d = scratch("cT_d", [KC, 128, NOWN], BF16)
    mT_d = scratch("mT_d", [KC, 128, NMEM], BF16)
    norm_pass(cx, h1T, cT_d, NOWN, 1, BF16)
    norm_pass(cx, memT, mT_d, NMEM, 2, BF16)
    P.barrier()
    wk = inp("wk_mem", [D, D])
    wv = inp("wv_mem", [D, D])
    kmT = scratch("kmT", [KC, 128, NMEM], BF16)
    VM = scratch("VM", [KC, 128, 2, 128], BF16)
    gemm(cx, X=mT_d, kc=KC, wfn=simple_w(wk), ntok=NMEM, colblocks=cblocks(8, EpiCopyFM(fm_dest(kmT, 256))),
         x_stream=False, Tt=256)
    P.barrier()

    def vm_dest(tt, ci, cb):
        h0 = cb["c0"] // 128
        return VM[h0:h0 + 4, :, :, :].rearrange("h p k d -> p h (k d)")
    gemm(cx, X=mT_d, kc=KC, wfn=simple_w(wv), ntok=NMEM, colblocks=cblocks(8, EpiCopyTM(vm_dest), mode="tm"),
         x_stream=False, Tt=256)
    P.barrier()
    wq = inp("wq_mem", [D, D])
    qmT = scratch("qmT", [KC, 128, NOWN], BF16)
    gemm(cx, X=cT_d, kc=KC, wfn=simple_w(wq), ntok=NOWN, colblocks=cblocks(8, EpiCopyFM(fm_dest(qmT))), x_stream=False)
    P.barrier()
    omT = scratch("omT", [KC, 128, NOWN], BF16)
    dbg["omT"] = (omT, [KC, 128, NOWN], BF16)
    cross_attention(cx, qmT, kmT, VM, omT)
    P.barrier()
    if upto <= 11:
        return finish()
    wo = inp("wo_mem", [D, D])
    h2T = scratch("h2T", [KC, 128, NOWN], F32)
    dbg["h2T"] = (h2T, [KC, 128, NOWN], F32)
    gemm(cx, X=omT, kc=KC, wfn=simple_w(wo), ntok=NOWN, colblocks=cblocks(8, EpiResid(fm_dest(h1T), fm_dest(h2T))),
         x_stream=False)
    P.barrier()
    if upto <= 12:
        return finish()
    nT_d = scratch("nT_d", [KC, 128, NOWN], BF16)
    norm_pass(cx, h2T, nT_d, NOWN, 3, BF16)
    P.barrier()
    w_up0 = inp("w_up0", [D, DFF // 2])
    w_up1 = inp("w_up1", [D, DFF // 2])
    actT = scratch("actT", [DFF // 128, 128, NOWN], BF16)

    def wup_fn(k0, kn, c0, cn):
        w, c = (w_up0, c0) if c0 < DFF // 2 else (w_up1, c0 - DFF // 2)
        return w[k0 * 128:(k0 + kn) * 128, c:c + cn]
    gemm(cx, X=nT_d, kc=KC, wfn=wup_fn, ntok=NOWN, colblocks=cblocks(32, EpiRelu2(fm_dest(actT))), x_stream=False)
    P.barrier()
    h3T = scratch("h3T", [KC, 128, NOWN], F32)
    dbg["h3T"] = (h3T, [KC, 128, NOWN], F32)

    def wdn_fn(k0, kn, c0, cn):
        return wdn_bf[k0 * 128:(k0 + kn) * 128, c0:c0 + cn]
    gemm(cx, X=actT, kc=DFF // 128, wfn=wdn_fn, ntok=NOWN, colblocks=cblocks(8, EpiResid(fm_dest(h2T), fm_dest(h3T))),
         x_stream=True)
    P.barrier()
    outT = outp("outT", [KC, 128, NOWN])
    norm_pass(cx, h3T, outT, NOWN, 4, F32, TT=128)
    return finish()


def own_tokens(qh):
    blocks = [8 * j + 2 * i + qh for j in range(4) for i in range(4)]
    return np.concatenate([np.arange(bk * 128, (bk + 1) * 128) for bk in blocks])


def t5_bucket_np(rel):
    rel = np.asarray(rel, np.int32)
    ret = np.where(rel > 0, 16, 0).astype(np.int32)
    n = np.abs(rel)
    nf = np.maximum(n, 1).astype(np.float32)
    large = 8 + (np.log(nf / np.float32(8)) / np.float32(math.log(128 / 8)) * np.float32(8)).astype(np.int32)
    large = np.minimum(large, 15)
    return ret + np.where(n < 8, n, large)


def fox_mask_table(qh):
    m = np.zeros((128, 8, 4, 128), np.float32)
    kk = np.arange(128)[:, None]
    qq = np.arange(128)[None, :]
    diag = np.where(kk <= qq, 0.0, NEG).astype(np.float32)
    for r in range(8):
        for i in range(4):
            t = r - (2 * i + qh)
            if t == 0:
                m[:, r, i, :] = diag
            elif t > 0:
                m[:, r, i, :] = NEG
    return m.reshape(128, 8 * 512)


def t5_onehot():
    oh = np.zeros((33, 2, 128, 128), np.float32)
    q = np.arange(128)[:, None]
    k = np.arange(128)[None, :]
    bd = t5_bucket_np(k - q)
    allowed = (k // 64) <= (q // 64)
    bd = np.where(allowed, bd, 32)
    bn = t5_bucket_np(k - q - 128)
    for b in range(33):
        oh[b, 0] = (bd == b)
        oh[b, 1] = (bn == b)
    return oh.reshape(33, 2 * 128 * 128)


def prep_smalls(inp, qh):
    s = np.zeros((128, NS), np.float32)
    gs = [inp["g_mix"][0], inp["g_cross"][0], inp["g_mem"][0], inp["g_mlp"][0], inp["g_final"]]
    for gi, g in enumerate(gs):
        s[:, NS_G + gi * 32:NS_G + (gi + 1) * 32] = np.asarray(g, np.float32).reshape(32, 128).T
    s[:, NS_GSUB:NS_GSUB + 2] = np.asarray(inp["g_subln"][0], np.float32).reshape(2, 128).T
    s[:, NS_PAR + qh] = 1.0
    s[0:16, NS_BF] = np.asarray(inp["b_forget"][0], np.float32)
    s[0:32, NS_RB:NS_RB + 8] = np.asarray(inp["rel_bias"], np.float32)
    s[:, NS_RB15:NS_RB15 + 8] = np.asarray(inp["rel_bias"], np.float32)[15][None, :]
    lam = np.stack([inp["lambda_q1"][0], inp["lambda_k1"][0], inp["lambda_q2"][0], inp["lambda_k2"][0]])
    s[:, NS_LAM:NS_LAM + 512] = np.asarray(lam, np.float32).reshape(1, 512)
    s[:, NS_ID:NS_ID + 128] = np.eye(128, dtype=np.float32)
    s[127, NS_E127:NS_E127 + 128] = 1.0
    return s


def prep_core(inp, c, names):
    b, qh = c // 2, c % 2
    m = {}
    xT = None
    if "xa" in names or "xo" in names:
        xT = np.ascontiguousarray(np.asarray(inp["x"][b], np.float32).T).reshape(KC, 128, S)
    if "xa" in names:
        m["xa"] = xT
    if "xo" in names:
        m["xo"] = np.ascontiguousarray(xT[:, :, own_tokens(qh)])
    if "memT" in names:
        m["memT"] = np.ascontiguousarray(np.asarray(inp["mem"][b], np.float32).T).reshape(KC, 128, NMEM)
    if "smalls" in names:
        m["smalls"] = prep_smalls(inp, qh)
    if "maskF" in names:
        m["maskF"] = fox_mask_table(qh)
    if "oh" in names:
        m["oh"] = t5_onehot()
    return m


def shared_inputs(inp, names):
    m = {}
    for nm in ("w_in", "w_out", "wq_mem", "wk_mem", "wv_mem", "wo_mem"):
        if nm in names:
            m[nm] = np.asarray(inp[nm][0], np.float32)
    if "w_up0" in names:
        w = np.asarray(inp["w_up"][0], np.float32)
        m["w_up0"] = np.ascontiguousarray(w[:, :DFF // 2])
        m["w_up1"] = np.ascontiguousarray(w[:, DFF // 2:])
    if "w_dn0" in names:
        w = np.asarray(inp["w_down"][0], np.float32)
        m["w_dn0"] = w[:DFF // 2]
        m["w_dn1"] = w[DFF // 2:]
    return m


def kernel(**inputs):
    nc, cx = build()
    sh = shared_inputs(inputs, cx.inputs)
    in_maps = []
    for c in range(8):
        m = prep_core(inputs, c, cx.inputs)
        m.update(sh)
        in_maps.append(m)
    res = run_bass_kernel_spmd(nc, in_maps, core_ids=list(range(8)))
    out = np.empty((4, S, D), np.float32)
    for c in range(8):
        b, qh = c // 2, c % 2
        o = np.asarray(res.results[c]["outT"], np.float32).reshape(D, NOWN)
        out[b, own_tokens(qh), :] = o.T
    return out
```

```python
import math
from contextlib import ExitStack
import numpy as np
import concourse.bass as bass
import concourse.mybir as mybir
from concourse.bass_utils import run_bass_kernel_spmd

F32 = mybir.dt.float32
BF16 = mybir.dt.bfloat16
AF = mybir.ActivationFunctionType
ALU = mybir.AluOpType
AXX = mybir.AxisListType.X

D = 4096
S = 4096
NOWN = 2048
DFF = 16384
NMEM = 256
KC = 32
SCALE = 128.0 ** -0.5
SCALE_M = 1024.0 ** -0.5
NEG = -30000.0
LAMBDA_INIT = 0.8 - 0.6 * math.exp(0.0)
C_QF, C_KF, C_VF, C_F, C_QD, C_KD, C_VD = 0, 2048, 4096, 6144, 6160, 8208, 10256
SB_BASE = 20480
SB_LIMIT = 229376


class Ev:
    __slots__ = ("sem", "val")

    def __init__(self, sem, val):
        self.sem = sem
        self.val = val


class Sem:
    def __init__(self, h, name):
        self.h = h
        self.name = name
        self.n = 0


class Prog:
    def __init__(self, nc):
        self.nc = nc
        self.es = ExitStack()
        self.q = {e: [] for e in ("pe", "act", "dve", "pool", "sp")}
        self.waited = {}
        self.nsem = 0
        self.prog = {e: self.sem("prog_" + e) for e in ("pe", "act", "dve", "pool")}

    def sem(self, name):
        if not hasattr(self, "allsems"):
            self.allsems = []
            self.pool = []
            self.inuse = []
        if name.startswith("prog_"):
            self.nsem += 1
            name = f"{name}_{self.nsem}"
            sm = Sem(self.es.enter_context(self.nc.semaphore(name)), name)
            self.allsems.append(sm)
            return sm
        if self.pool:
            sm = self.pool.pop()
        else:
            self.nsem += 1
            name = f"{name}_{self.nsem}"
            sm = Sem(self.es.enter_context(self.nc.semaphore(name)), name)
            self.allsems.append(sm)
        self.inuse.append(sm)
        return sm

    def wait(self, eng, ev):
        if ev is None:
            return
        if isinstance(ev, (list, tuple)):
            for e in ev:
                self.wait(eng, e)
            return
        k = (eng, ev.sem.name)
        if self.waited.get(k, 0) >= ev.val:
            return
        self.waited[k] = ev.val
        self.q[eng].append(("w", ev.sem, ev.val))

    def op(self, eng, fn, waits=(), signal=True):
        self.wait(eng, waits)
        if signal:
            s = self.prog[eng]
            s.n += 1
            self.q[eng].append(("o", fn, s, 1))
            return Ev(s, s.n)
        self.q[eng].append(("o", fn, None, 0))
        return None

    def dma(self, eng, sem, fn, waits=()):
        self.wait(eng, waits)
        sem.n += 16
        self.q[eng].append(("o", fn, sem, 16))
        return Ev(sem, sem.n)

    def barrier(self):
        if not hasattr(self, "allsems"):
            self.allsems = []
        evs = [Ev(sm, sm.n) for sm in self.allsems if sm.n > 0]
        for eng in self.q:
            self.wait(eng, evs)
        self.pool.extend(self.inuse)
        self.inuse = []

    def emit(self):
        with self.nc.Block() as block:
            def mk(eng):
                def f(e):
                    for it in self.q[eng]:
                        if it[0] == "w":
                            e.wait_ge(it[1].h, it[2])
                        else:
                            ins = it[1](e)
                            if it[2] is not None:
                                ins.then_inc(it[2].h, it[3])
                return f
            block.tensor(mk("pe"))
            block.scalar(mk("act"))
            block.vector(mk("dve"))
            block.gpsimd(mk("pool"))
            block.sync(mk("sp"))


class SBAlloc:
    def __init__(self, nc):
        self.nc = nc
        self.off = SB_BASE
        self.cnt = 0

    def alloc(self, name, shape, dtype):
        self.cnt += 1
        nbytes = int(np.prod(shape[1:])) * (2 if dtype == BF16 else 4)
        nbytes = (nbytes + 63) // 64 * 64
        off = self.off
        assert off + nbytes <= SB_LIMIT, f"SBUF overflow at {name}: {off}+{nbytes}"
        self.off += nbytes
        return self.nc.alloc_sbuf_tensor_at(f"{name}_{self.cnt}", list(shape), dtype, offset=off)

    def mark(self):
        return self.off

    def release(self, m):
        self.off = m


class Ctx:
    pass


def gemm(cx, *, X, kc, wfn, ntok, colblocks, x_stream, Tt=1024, KP=16, bg=None, NXS=3):
    P, sb = cx.P, cx.sb
    m0 = sb.mark()
    NW = 3
    Wt = [sb.alloc("gw", [128, KP, 512], BF16) for _ in range(NW)]
    wsem = [P.sem("gw") for _ in range(NW)]
    wfree = [cx.phase_ev] * NW
    nk = kc // KP
    if x_stream:
        NX = NXS
        Xt = [sb.alloc("gx", [128, KP, Tt], BF16) for _ in range(NX)]
    else:
        NX = 1
        Xt = [sb.alloc("gx", [128, kc, Tt], BF16)]
    xsem = [P.sem("gx") for _ in range(NX)]
    xsem2 = P.sem("gx2")
    xfree = [cx.phase_ev] * NX
    for cb in colblocks:
        cb["epi"].setup(cx, Tt)
    ntt = ntok // Tt
    NT = min(512, Tt)
    cx.NT = NT
    pieces = [(tt, ci, kp) for tt in range(ntt) for ci in range(len(colblocks)) for kp in range(nk)]
    wload = {}
    xload = {}

    def load_w(i):
        tt, ci, kp = pieces[i]
        cb = colblocks[ci]
        slot = i % NW
        src = wfn(kp * KP, KP, cb["c0"], cb["cn"]).rearrange("(k p) c -> p k c", p=128)
        dst = Wt[slot][:, :, 0:cb["cn"]]
        wload[i] = P.dma("pool", wsem[slot], lambda e, d=dst, s=src: e.dma_start(out=d, in_=s),
                         waits=[wfree[slot]])
        if bg:
            bg.pop(0)()

    def load_x(i):
        tt, ci, kp = pieces[i]
        if x_stream:
            slot = i % NX
            src = X[kp * KP:(kp + 1) * KP, :, tt * Tt:(tt + 1) * Tt].rearrange("k p t -> p k t")
            xload[i] = P.dma("pool", xsem[slot], lambda e, d=Xt[slot][:], s=src: e.dma_start(out=d, in_=s),
                             waits=[xfree[slot]])
        else:
            if ci == 0 and kp == 0:
                src = X[:, :, tt * Tt:(tt + 1) * Tt].rearrange("k p t -> p k t")
                half = kc // 2
                xload[(tt, 0)] = P.dma("pool", xsem[0], lambda e, d=Xt[0][:, 0:half, :], s=src[:, 0:half, :]: e.dma_start(out=d, in_=s),
                                       waits=[xfree[0]])
                xload[(tt, 1)] = P.dma("pool", xsem2, lambda e, d=Xt[0][:, half:kc, :], s=src[:, half:kc, :]: e.dma_start(out=d, in_=s))

    npieces = len(pieces)
    PRE = NW - 1
    if x_stream:
        for i in range(min(NX, npieces)):
            load_x(i)
    for i in range(min(PRE, npieces)):
        load_w(i)
    if not x_stream:
        load_x(0)
    ev = None
    for i, (tt, ci, kp) in enumerate(pieces):
        cb = colblocks[ci]
        epi = cb["epi"]
        cn = cb["cn"]
        if kp == 0:
            epi.begin(cx, tt, ci, cb)
        slot = i % NW
        P.wait("pe", wload[i])
        if x_stream:
            xs = i % NX
            P.wait("pe", xload[i])
            Xc = Xt[xs]
        else:
            P.wait("pe", xload[(tt, 0)])
            if (kp + 1) * KP > kc // 2:
                P.wait("pe", xload[(tt, 1)])
            Xc = Xt[0]
        if cb["mode"] == "fm":
            groups = [(cs, ts) for cs in range((cn + 127) // 128) for ts in range(Tt // NT)]
        else:
            groups = [(tb,) for tb in range(Tt // 128)]
        for gi, g in enumerate(groups):
            bank = gi
            if kp == 0:
                P.wait("pe", cx.bank_free[bank])
            for k in range(KP):
                first = (kp == 0 and k == 0)
                last = (kp == nk - 1 and k == KP - 1)
                endp = (gi == len(groups) - 1 and k == KP - 1)
                kk = k if x_stream else kp * KP + k
                if cb["mode"] == "fm":
                    cs, ts = g
                    m = min(128, cn - cs * 128)
                    out = cx.ps[bank][0:m, 0:NT]
                    lhsT = Wt[slot][:, k, cs * 128:cs * 128 + m]
                    rhs = Xc[:, kk, ts * NT:(ts + 1) * NT]
                else:
                    tb = g[0]
                    out = cx.ps[bank][:, 0:cn]
                    lhsT = Xc[:, kk, tb * 128:(tb + 1) * 128]
                    rhs = Wt[slot][:, k, 0:cn]
                ev = P.op("pe", lambda e, o=out, l=lhsT, r=rhs, st=first, sp=last:
                          e.matmul(o, lhsT=l, rhs=r, start=st, stop=sp), signal=(last or endp))
            if kp == nk - 1:
                cx.bank_free[bank] = epi.group(cx, cx.ps[bank], tt, ci, cb, g, ev)
        wfree[slot] = ev
        if x_stream:
            xfree[xs] = ev
            if i + NX < npieces:
                load_x(i + NX)
        else:
            if ci == len(colblocks) - 1 and kp == nk - 1:
                xfree[0] = ev
                if tt + 1 < ntt:
                    load_x(i + 1)
        if i + PRE < npieces:
            load_w(i + PRE)
        if kp == nk - 1:
            epi.end(cx, tt, ci, cb)
    cx.phase_ev = ev
    evs = [ev]
    for cb in colblocks:
        evs += cb["epi"].finish(cx)
    sb.release(m0)
    return evs


class EpiBase:
    def setup(self, cx, Tt):
        pass

    def begin(self, cx, tt, ci, cb):
        pass

    def end(self, cx, tt, ci, cb):
        pass

    def finish(self, cx):
        return []


class EpiCopyFM(EpiBase):
    def __init__(self, destfn, eng="act", rstd=None):
        self.destfn = destfn
        self.eng = eng
        self.rstd = rstd
        self.ready = False

    def setup(self, cx, Tt):
        if self.ready:
            return
        self.ready = True
        self.Tt = Tt
        self.stg = [cx.sb.alloc("stg", [128, 4, Tt], BF16) for _ in range(2)]
        self.ssem = [cx.P.sem("st") for _ in range(2)]
        self.sfree = [None, None]
        self.cnt = 0
        self.last = None

    def begin(self, cx, tt, ci, cb):
        self.buf = self.cnt % 2
        self.cnt += 1

    def group(self, cx, bank, tt, ci, cb, g, pe_ev):
        cs, ts = g
        m = min(128, cb["cn"] - cs * 128)
        NT = cx.NT
        o = self.stg[self.buf][0:m, cs, ts * NT:(ts + 1) * NT]
        i = bank[0:m, 0:NT]
        if self.rstd is not None:
            t0 = tt * self.Tt + ts * NT
            rs = self.rstd[0:m, t0:t0 + NT]
            fn = lambda e, o=o, i=i, rs=rs: e.tensor_tensor(out=o, in0=i, in1=rs, op=ALU.mult)
            self.last = cx.P.op("dve", fn, waits=[pe_ev, self.sfree[self.buf]])
            return self.last
        if self.eng == "act":
            fn = lambda e, o=o, i=i: e.activation(out=o, in_=i, func=AF.Copy)
        else:
            fn = lambda e, o=o, i=i: e.tensor_copy(out=o, in_=i)
        self.last = cx.P.op(self.eng, fn, waits=[pe_ev, self.sfree[self.buf]])
        return self.last

    def end(self, cx, tt, ci, cb):
        b = self.buf
        n = (cb["cn"] + 127) // 128
        dst = self.destfn(tt, ci, cb)
        src = self.stg[b][:, 0:n, :]
        self.sfree[b] = cx.P.dma("sp", self.ssem[b], lambda e, d=dst, s=src: e.dma_start(out=d, in_=s),
                                 waits=[self.last])

    def finish(self, cx):
        return [e for e in self.sfree if e is not None]


class EpiCopyTM(EpiBase):
    def __init__(self, destfn):
        self.destfn = destfn
        self.ready = False

    def setup(self, cx, Tt):
        if self.ready:
            return
        self.ready = True
        self.ntb = Tt // 128
        self.stg = [cx.sb.alloc("stgv", [128, 4, self.ntb, 128], BF16) for _ in range(2)]
        self.ssem = [cx.P.sem("stv") for _ in range(2)]
        self.sfree = [None, None]
        self.cnt = 0

    def begin(self, cx, tt, ci, cb):
        self.buf = self.cnt % 2
        self.cnt += 1

    def group(self, cx, bank, tt, ci, cb, g, pe_ev):
        tb = g[0]
        nh = cb["cn"] // 128
        o = self.stg[self.buf][:, 0:nh, tb, :]
        i = bank[:, 0:cb["cn"]].rearrange("p (h d) -> p h d", h=nh)
        self.last = cx.P.op("dve", lambda e, o=o, i=i: e.tensor_copy(out=o, in_=i),
                            waits=[pe_ev, self.sfree[self.buf]])
        return self.last

    def end(self, cx, tt, ci, cb):
        b = self.buf
        dst = self.destfn(tt, ci, cb)
        src = self.stg[b][:].rearrange("p h t d -> p h (t d)")
        self.sfree[b] = cx.P.dma("sp", self.ssem[b], lambda e, d=dst, s=src: e.dma_start(out=d, in_=s),
                                 waits=[self.last])

    def finish(self, cx):
        return [e for e in self.sfree if e is not None]


class EpiResid(EpiBase):
    def __init__(self, residfn, destfn, norm=None):
        self.residfn = residfn
        self.destfn = destfn
        self.norm = norm
        self.ready = False

    def setup(self, cx, Tt):
        if self.ready:
            return
        self.ready = True
        self.Tt = Tt
        self.res = [cx.sb.alloc("res", [128, 4, Tt], F32) for _ in range(2)]
        self.rsem = [cx.P.sem("rs") for _ in range(2)]
        self.ssem = [cx.P.sem("str") for _ in range(2)]
        self.rfree = [None, None]
        self.rload = [None, None]
        self.cnt = 0
        self.extra = []
        if self.norm:
            self.sqt = [cx.sb.alloc("sqt", [128, 512], F32) for _ in range(2)]
            self.sqfree = [None, None]
            self.sqc = 0
            self.lnt = cx.sb.alloc("lnt", [128, 512], F32)
            self.accev = None
            self.rstd_evs = []
            if self.norm.get("hg_dest"):
                self.hg = [cx.sb.alloc("hgs", [128, 4, Tt], BF16) for _ in range(2)]
                self.hsem = [cx.P.sem("hg") for _ in range(2)]
                self.hfree = [None, None]

    def begin(self, cx, tt, ci, cb):
        b = self.cnt % 2
        self.buf = b
        self.cnt += 1
        self.extra = []
        src = self.residfn(tt, ci, cb)
        self.rload[b] = cx.P.dma("sp", self.rsem[b], lambda e, d=self.res[b][:], s=src: e.dma_start(out=d, in_=s),
                                 waits=[self.rfree[b]])

    def group(self, cx, bank, tt, ci, cb, g, pe_ev):
        P = cx.P
        cs, ts = g
        b = self.buf
        r = self.res[b][:, cs, ts * 512:(ts + 1) * 512]
        self.last = P.op("dve", lambda e, i=bank, r=r: e.tensor_tensor(out=r, in0=i, in1=r, op=ALU.add),
                         waits=[pe_ev, self.rload[b]])
        if self.norm:
            nm = self.norm
            if nm.get("hg_dest"):
                chunk = cb["c0"] // 128 + cs
                gap = cx.gvec[:, nm["gi"], chunk:chunk + 1]
                o = self.hg[b][:, cs, ts * 512:(ts + 1) * 512]
                ah = P.op("act", lambda e, o=o, r=r, gap=gap: e.activation(out=o, in_=r, func=AF.Copy, scale=gap),
                          waits=[self.last, self.hfree[b]])
                self.extra.append(ah)
            k = self.sqc % 2
            self.sqc += 1
            asq = P.op("act", lambda e, o=self.sqt[k][:], r=r: e.activation(out=o, in_=r, func=AF.Square),
                       waits=[self.last, self.sqfree[k]])
            t0 = tt * self.Tt + ts * 512
            acs = cx.acc[:, t0:t0 + 512]
            if ci == 0 and cs == 0:
                d = P.op("dve", lambda e, o=acs, i=self.sqt[k][:]: e.tensor_copy(out=o, in_=i),
                         waits=[asq] + self.rstd_evs)
            else:
                d = P.op("dve", lambda e, o=acs, i=self.sqt[k][:]: e.tensor_tensor(out=o, in0=o, in1=i, op=ALU.add),
                         waits=[asq])
            self.sqfree[k] = d
            self.accev = d
            self.extra.append(asq)
        return self.last

    def end(self, cx, tt, ci, cb):
        P = cx.P
        b = self.buf
        dst = self.destfn(tt, ci, cb)
        self.rfree[b] = P.dma("sp", self.ssem[b], lambda e, d=dst, s=self.res[b][:]: e.dma_start(out=d, in_=s),
                              waits=[self.last] + self.extra)
        if self.norm:
            nm = self.norm
            if nm.get("hg_dest"):
                hd = nm["hg_dest"](tt, ci, cb)
                self.hfree[b] = P.dma("sp", self.hsem[b], lambda e, d=hd, s=self.hg[b][:]: e.dma_start(out=d, in_=s),
                                      waits=self.extra)
            if ci == nm["ncb"] - 1:
                self.rstd_evs = []
                for ts in range(self.Tt // 512):
                    bank = ts
                    t0 = tt * self.Tt + ts * 512
                    P.wait("pe", [cx.bank_free[bank], self.accev])
                    pe = P.op("pe", lambda e, bank=bank, t0=t0: e.matmul(cx.ps[bank], lhsT=cx.ones_f[:], rhs=cx.acc[:, t0:t0 + 512],
                                                                       start=True, stop=True))
                    a1 = P.op("act", lambda e, bank=bank: e.activation(out=self.lnt[:], in_=cx.ps[bank], func=AF.Ln,
                                                                      bias=cx.eps6[:, 0:1], scale=1.0 / D),
                              waits=[pe] + self.rstd_evs)
                    cx.bank_free[bank] = a1
                    a2 = P.op("act", lambda e, t0=t0: e.activation(out=cx.rstd_a[:, t0:t0 + 512], in_=self.lnt[:], func=AF.Exp,
                                                                  scale=-0.5), waits=[a1])
                    d2 = P.op("dve", lambda e, t0=t0: e.tensor_tensor(out=cx.r2_a[:, t0:t0 + 512], in0=cx.rstd_a[:, t0:t0 + 512],
                                                                     in1=cx.rstd_a[:, t0:t0 + 512], op=ALU.mult), waits=[a2])
                    self.rstd_evs = [d2]

    def finish(self, cx):
        return [e for e in self.rfree if e is not None]


class EpiRelu2(EpiBase):
    def __init__(self, destfn, r2=None):
        self.destfn = destfn
        self.r2 = r2
        self.ready = False

    def setup(self, cx, Tt):
        if self.ready:
            return
        self.ready = True
        self.Tt = Tt
        self.tmp = [cx.sb.alloc("rtmp", [128, 512], F32) for _ in range(2)]
        self.tfree = [None, None]
        self.stg = [cx.sb.alloc("stgu", [128, 4, Tt], BF16) for _ in range(2)]
        self.ssem = [cx.P.sem("stu") for _ in range(2)]
        self.sfree = [None, None]
        self.cnt = 0
        self.gc = 0

    def begin(self, cx, tt, ci, cb):
        self.buf = self.cnt % 2
        self.cnt += 1

    def group(self, cx, bank, tt, ci, cb, g, pe_ev):
        cs, ts = g
        b = self.buf
        t = self.gc % 2
        self.gc += 1
        tm = self.tmp[t][:]
        a_ev = cx.P.op("act", lambda e, o=tm, i=bank: e.activation(out=o, in_=i, func=AF.Relu),
                       waits=[pe_ev, self.tfree[t]])
        o = self.stg[b][:, cs, ts * 512:(ts + 1) * 512]
        if self.r2 is not None:
            t0 = tt * self.Tt + ts * 512
            d1 = cx.P.op("dve", lambda e, i=tm: e.tensor_tensor(out=i, in0=i, in1=i, op=ALU.mult), waits=[a_ev])
            self.last = cx.P.op("dve", lambda e, o=o, i=tm, r=self.r2[:, t0:t0 + 512]: e.tensor_tensor(
                out=o, in0=i, in1=r, op=ALU.mult), waits=[d1, self.sfree[b]])
        else:
            self.last = cx.P.op("dve", lambda e, o=o, i=tm: e.tensor_tensor(out=o, in0=i, in1=i, op=ALU.mult),
                                waits=[a_ev, self.sfree[b]])
        self.tfree[t] = self.last
        return a_ev

    def end(self, cx, tt, ci, cb):
        b = self.buf
        dst = self.destfn(tt, ci, cb)
        self.sfree[b] = cx.P.dma("sp", self.ssem[b], lambda e, d=dst, s=self.stg[b][:]: e.dma_start(out=d, in_=s),
                                 waits=[self.last])

    def finish(self, cx):
        return [e for e in self.sfree if e is not None]


class EpiSig(EpiBase):
    def __init__(self, sigT, bias):
        self.sigT = sigT
        self.bias = bias
        self.last = None

    def group(self, cx, bank, tt, ci, cb, g, pe_ev):
        cs, ts = g
        t0 = tt * 1024 + ts * 512
        o = self.sigT[0:16, t0:t0 + 512]
        self.last = cx.P.op("act", lambda e, o=o, i=bank[0:16, :], b=self.bias: e.activation(
            out=o, in_=i, func=AF.Sigmoid, bias=b, scale=1.0), waits=[pe_ev])
        return self.last

    def finish(self, cx):
        return [self.last]


def norm_pass(cx, src, dst, ntok, gi, out_dtype, TT=256, waits=(), rstd_src=None):
    P, sb = cx.P, cx.sb
    m0 = sb.mark()
    NXB = 4 if TT == 128 else (3 if out_dtype == BF16 else 2)
    xin = [sb.alloc("nx", [128, KC, TT], F32) for _ in range(NXB)]
    xsem = [P.sem("nx") for _ in range(NXB)]
    xfree = [None] * NXB
    sq = [sb.alloc("nsq", [128, KC, TT], BF16) for _ in range(2)]
    sqfree = [None, None]
    lnv = sb.alloc("nln", [128, TT], F32)
    rstd = [sb.alloc("nrstd", [128, TT], F32) for _ in range(2)]
    rfree = [None, None]
    ot = [sb.alloc("no", [128, KC, TT], out_dtype) for _ in range(2)]
    osem = [P.sem("no") for _ in range(2)]
    ofree = [None, None]
    nt = ntok // TT
    ld = {}
    sqev = {}
    KD = 32

    def load(t):
        b = t % NXB
        s_ = src[:, :, t * TT:(t + 1) * TT].rearrange("k p t -> p k t")
        ld[t] = P.dma("sp", xsem[b], lambda e, d=xin[b][:], s=s_: e.dma_start(out=d, in_=s),
                      waits=[xfree[b]] + list(waits))

    def square(t):
        if rstd_src is not None:
            sqev[t] = None
            return
        b = t % NXB
        q = t % 2
        sqev[t] = P.op("act", lambda e, o=sq[q][:], i=xin[b][:]: e.activation(out=o, in_=i, func=AF.Square),
                       waits=[ld[t], sqfree[q]])

    for t in range(min(NXB - 1, nt)):
        load(t)
    square(0)
    for t in range(nt):
        if t + NXB - 1 < nt:
            load(t + NXB - 1)
        if t + 1 < nt:
            square(t + 1)
        b = t % NXB
        q = t % 2
        bank = t % 2
        if rstd_src is None:
            P.wait("pe", [sqev[t], cx.bank_free[bank]])
            for k in range(KC):
                pe = P.op("pe", lambda e, o=cx.ps[bank][:, 0:TT], r=sq[q][:, k, :], st=(k == 0), sp=(k == KC - 1):
                          e.matmul(o, lhsT=cx.ones_b[:], rhs=r, start=st, stop=sp), signal=(k == KC - 1))
            sqfree[q] = pe
            a2 = P.op("act", lambda e, i=cx.ps[bank][:, 0:TT]: e.activation(
                out=lnv[:], in_=i, func=AF.Ln, bias=cx.eps6[:, 0:1], scale=1.0 / D), waits=[pe])
            cx.bank_free[bank] = a2
            a3 = P.op("act", lambda e, o=rstd[q][:]: e.activation(out=o, in_=lnv[:], func=AF.Exp, scale=-0.5),
                      waits=[a2, rfree[q]])
            rs_ap = rstd[q][:]
        else:
            a3 = ld[t]
            rs_ap = rstd_src[:, t * TT:(t + 1) * TT]
        last_d = last_p = None
        for k in range(KC):
            eng = "dve" if k < KD else "pool"
            ev = P.op(eng, lambda e, o=ot[t % 2][:, k, :], i=xin[b][:, k, :], g=cx.gvec[:, gi, k:k + 1],
                      r=rs_ap: e.scalar_tensor_tensor(out=o, in0=i, scalar=g, in1=r, op0=ALU.mult, op1=ALU.mult),
                      waits=[a3, ofree[t % 2]])
            if eng == "dve":
                last_d = ev
            else:
                last_p = ev
        xfree[b] = [last_d, last_p]
        rfree[q] = [last_d, last_p]
        dd = dst[:, :, t * TT:(t + 1) * TT].rearrange("k p t -> p k t")
        ofree[t % 2] = P.dma("sp", osem[t % 2], lambda e, d=dd, s=ot[t % 2][:]: e.dma_start(out=d, in_=s),
                             waits=[last_d, last_p])
    sb.release(m0)
    return [e for e in ofree if e is not None]


def attn_prep(cx, dq_d, oh_d):
    P, sb = cx.P, cx.sb
    sm = cx.smalls
    sigT = cx.sigT
    cx.ctm = sb.alloc("ctm", [128, 32, 16], F32)
    cx.biask = sb.alloc("biask", [128, 16, 4, 32], F32)
    cx.lam = sb.alloc("lam", [128, 4], F32)
    cx.t5 = sb.alloc("t5", [128, 4, 8, 128], F32)
    cx.rb15s = sb.alloc("rb15s", [128, 8], F32)
    cx.gsub8 = sb.alloc("gsub8", [128, 2], F32)
    m0 = sb.mark()
    ones16 = sb.alloc("ones16", [16, S], F32)
    cT = sb.alloc("cT", [16, S], F32)
    dqf = sb.alloc("dqf", [16, S], F32)
    tmp2 = sb.alloc("tmp2", [16, NOWN], F32)
    dqo = sb.alloc("dqo", [16, NOWN], F32)
    hib = sb.alloc("hib", [16, NOWN], BF16)
    lob = sb.alloc("lob", [16, NOWN], BF16)
    crbc = sb.alloc("crbc", [128, 4, 16], F32)
    lamt = sb.alloc("lamt", [128, 2, 128], F32)
    rbext = sb.alloc("rbext", [64, 8], F32)
    ohc = [sb.alloc("ohc", [33, 32, 128], F32) for _ in range(2)]
    ones_f = sb.alloc("ones_f", [128, 128], F32)

    a0 = P.op("act", lambda e: e.activation(out=sigT[:], in_=sigT[:], func=AF.Ln))
    d0 = P.op("dve", lambda e: e.memset(ones16[:], 1.0))
    prev = [a0, d0]
    for sg in range(4):
        ini = 0.0 if sg == 0 else cT[:, sg * 1024 - 1:sg * 1024]
        pv = P.op("dve", lambda e, sg=sg, ini=ini: e.tensor_tensor_scan(
            out=cT[:, sg * 1024:(sg + 1) * 1024], data0=ones16[:, sg * 1024:(sg + 1) * 1024],
            data1=sigT[:, sg * 1024:(sg + 1) * 1024], initial=ini, op0=ALU.mult, op1=ALU.add), waits=prev)
        prev = [pv]
    dscan = prev[0]
    P.wait("pe", [dscan, cx.bank_free[0]])
    for kb in range(32):
        pe = P.op("pe", lambda e, kb=kb: e.matmul(cx.ps[0][:, kb * 16:(kb + 1) * 16],
                                                 lhsT=cT[0:16, kb * 128:(kb + 1) * 128],
                                                 rhs=cx.ident_f[0:16, 0:16], start=True, stop=True),
                  signal=(kb == 31))
    dctm = P.op("dve", lambda e: e.tensor_copy(out=cx.ctm[:].rearrange("p k h -> p (k h)"), in_=cx.ps[0]), waits=[pe])
    cx.bank_free[0] = dctm
    dz = P.op("dve", lambda e: e.memset(crbc[:, 0, :], 0.0))
    P.wait("pe", [dctm, cx.bank_free[1]])
    for j in range(1, 4):
        pe = P.op("pe", lambda e, j=j: e.matmul(cx.ps[1][:, j * 16:(j + 1) * 16], lhsT=cx.e127,
                                               rhs=cx.ctm[:, 8 * j - 1, :], start=True, stop=True),
                  signal=(j == 3))
    dcr = P.op("dve", lambda e: e.tensor_copy(out=crbc[:, 1:4, :].rearrange("p j h -> p (j h)"),
                                              in_=cx.ps[1][:, 16:64]), waits=[pe, dz])
    cx.bank_free[1] = dcr
    for h in range(16):
        for j in range(4):
            P.op("dve", lambda e, h=h, j=j: e.tensor_scalar(
                out=cx.biask[:, h, j, :], in0=cx.ctm[:, :, h], scalar1=crbc[:, j, h:h + 1], scalar2=-1.0,
                op0=ALU.subtract, op1=ALU.mult), waits=[dcr])
    evs = []
    for j in range(4):
        sc1 = 0.0 if j == 0 else cT[:, 1024 * j - 1:1024 * j]
        evs.append(P.op("dve", lambda e, j=j, sc1=sc1: e.tensor_scalar(
            out=dqf[:, 1024 * j:1024 * (j + 1)], in0=cT[:, 1024 * j:1024 * (j + 1)], scalar1=sc1,
            scalar2=1.0 / SCALE, op0=ALU.subtract, op1=ALU.mult), waits=[dscan]))
    v = dqf[:].rearrange("p (a r t) -> p a r t", r=2, t=128)
    t2v = tmp2[:].rearrange("p (a t) -> p a t", t=128)
    dqv = dqo[:].rearrange("p (a t) -> p a t", t=128)
    e1 = P.op("dve", lambda e: e.tensor_scalar(out=t2v, in0=v[:, :, 0, :], scalar1=cx.par[0:16, 0:1], scalar2=None,
                                               op0=ALU.mult), waits=evs)
    e2 = P.op("dve", lambda e: e.scalar_tensor_tensor(out=dqv, in0=v[:, :, 1, :], scalar=cx.par[0:16, 1:2], in1=t2v,
                                                      op0=ALU.mult, op1=ALU.add), waits=[e1])
    e3 = P.op("dve", lambda e: e.tensor_copy(out=hib[:], in_=dqo[:]), waits=[e2])
    e4 = P.op("dve", lambda e: e.tensor_copy(out=tmp2[:], in_=hib[:]), waits=[e3])
    e5 = P.op("dve", lambda e: e.tensor_tensor(out=tmp2[:], in0=dqo[:], in1=tmp2[:], op=ALU.subtract), waits=[e4])
    e6 = P.op("dve", lambda e: e.tensor_copy(out=lob[:], in_=tmp2[:]), waits=[e5])
    dsem = P.sem("dq")
    P.dma("sp", dsem, lambda e: e.dma_start(out=dq_d[:, 0, :], in_=hib[:]), waits=[e3])
    P.dma("sp", dsem, lambda e: e.dma_start(out=dq_d[:, 1, :], in_=lob[:]), waits=[e6])
    lv = sm[:, NS_LAM:NS_LAM + 512].rearrange("p (a d) -> p a d", a=4)
    l1 = P.op("dve", lambda e: e.tensor_tensor(out=lamt[:, 0, :], in0=lv[:, 0, :], in1=lv[:, 1, :], op=ALU.mult))
    l2 = P.op("dve", lambda e: e.tensor_tensor(out=lamt[:, 1, :], in0=lv[:, 2, :], in1=lv[:, 3, :], op=ALU.mult))
    l3 = P.op("dve", lambda e: e.tensor_reduce(out=cx.lam[:, 0:2], in_=lamt[:], axis=AXX, op=ALU.add), waits=[l1, l2])
    l4 = P.op("act", lambda e: e.activation(out=cx.lam[:, 2:4], in_=cx.lam[:, 0:2], func=AF.Exp), waits=[l3])
    l5 = P.op("dve", lambda e: e.tensor_tensor(out=cx.lam[:, 0:1], in0=cx.lam[:, 2:3], in1=cx.lam[:, 3:4],
                                               op=ALU.subtract), waits=[l4])
    l6 = P.op("dve", lambda e: e.tensor_scalar(out=cx.lam[:, 0:1], in0=cx.lam[:, 0:1], scalar1=LAMBDA_INIT, scalar2=None,
                                               op0=ALU.add), waits=[l5])
    P.op("dve", lambda e: e.tensor_scalar(out=cx.lam[:, 1:2], in0=cx.lam[:, 0:1], scalar1=-1.0, scalar2=None,
                                          op0=ALU.mult), waits=[l6])
    P.op("dve", lambda e: e.tensor_scalar(out=cx.gsub8[:], in0=sm[:, NS_GSUB:NS_GSUB + 2],
                                          scalar1=1.0 - LAMBDA_INIT, scalar2=None, op0=ALU.mult))
    r0 = P.op("dve", lambda e: e.memset(rbext[32:33, :], NEG))
    r1 = P.op("dve", lambda e: e.tensor_scalar(out=rbext[0:32, :], in0=sm[0:32, NS_RB:NS_RB + 8], scalar1=1.0 / SCALE,
                                               scalar2=None, op0=ALU.mult))
    r2 = P.op("dve", lambda e: e.tensor_scalar(out=cx.rb15s[:], in0=sm[:, NS_RB15:NS_RB15 + 8], scalar1=1.0 / SCALE,
                                               scalar2=None, op0=ALU.mult))
    r3 = P.op("dve", lambda e: e.memset(ones_f[:], 1.0))
    P.op("dve", lambda e: e.memset(cx.t5[:, 3, 0, :], NEG))
    for h in range(8):
        P.op("dve", lambda e, h=h: e.tensor_scalar(out=cx.t5[:, 2, h, :], in0=ones_f[:], scalar1=cx.rb15s[:, h:h + 1],
                                                   scalar2=None, op0=ALU.mult), waits=[r2, r3])
    osem = [P.sem("oh") for _ in range(2)]
    ofree = [None, None]
    ohv = oh_d.rearrange("b (t q k) -> b t q k", t=2, q=128)
    n = 0
    for ty in range(2):
        P.wait("pe", [cx.bank_free[2], cx.bank_free[3], r0, r1])
        for qc in range(4):
            b = n % 2
            n += 1
            ld = P.dma("sp", osem[b], lambda e, b=b, ty=ty, qc=qc: e.dma_start(
                out=ohc[b][:], in_=ohv[:, ty, qc * 32:(qc + 1) * 32, :]), waits=[ofree[b]])
            P.wait("pe", ld)
            for ql in range(32):
                q = qc * 32 + ql
                bank = 2 + (q * 8) // 512
                col = (q * 8) % 512
                pe = P.op("pe", lambda e, b=b, ql=ql, bank=bank, col=col: e.matmul(
                    cx.ps[bank][:, col:col + 8], lhsT=ohc[b][0:33, ql, :], rhs=rbext[0:33, 0:8], start=True, stop=True),
                    signal=(ql == 31))
            ofree[b] = pe
        dd = None
        for hb in range(2):
            dd = P.op("dve", lambda e, ty=ty, hb=hb: e.tensor_copy(
                out=cx.t5[:, ty, :, hb * 64:(hb + 1) * 64],
                in_=cx.ps[2 + hb].rearrange("p (q h) -> p h q", h=8)), waits=[pe])
            cx.bank_free[2 + hb] = dd
    sb.release(m0)


def attn_loads(cx, kinds, nheads, done_evs):
    pass


def fox_attention(cx, KT, QT, Vs, dq_d, mixT, maskF_d, after_mask=None):
    P, sb = cx.P, cx.sb
    m0 = sb.mark()
    kt = [sb.alloc("kt", [128, S], BF16) for _ in range(2)]
    vt = [sb.alloc("vt", [128, 32, 128], BF16) for _ in range(2)]
    qt = [sb.alloc("qt", [128, NOWN], BF16) for _ in range(2)]
    dqt = [sb.alloc("dqt", [2, NOWN], BF16) for _ in range(2)]
    hsem = [P.sem("fh") for _ in range(2)]
    maskF = sb.alloc("maskF", [128, 8, 512], BF16)
    NPT = 3
    pt = [sb.alloc("pt", [128, 512], BF16) for _ in range(NPT)]
    ptfree = [None] * NPT
    rl = sb.alloc("rl", [128, 512], F32)
    ostg = [sb.alloc("ostg", [128, 512], BF16) for _ in range(2)]
    osem = [P.sem("fo") for _ in range(2)]
    ofree = [None, None]
    msem = P.sem("mk")
    mld = P.dma("pool", msem, lambda e: e.dma_start(out=maskF[:].rearrange("p r q -> p (r q)"), in_=maskF_d))
    if after_mask is not None:
        after_mask()
    hload = {}
    hdone = {}

    def load_head(h):
        sl = h % 2
        w = [hdone.get(h - 2)]
        P.dma("sp", hsem[sl], lambda e: e.dma_start(out=kt[sl][:], in_=KT[h]), waits=w)
        P.dma("sp", hsem[sl], lambda e: e.dma_start(out=vt[sl][:], in_=Vs[h]))
        P.dma("sp", hsem[sl], lambda e: e.dma_start(out=qt[sl][:], in_=QT[h]))
        hload[h] = P.dma("sp", hsem[sl], lambda e: e.dma_start(out=dqt[sl][:], in_=dq_d[h]))

    items = [(h, j, kb) for h in range(16) for j in range(4) for kb in range(8 * j + 8)]
    st_fin = [None]
    LA = 2
    exp_ev = {}
    load_head(0)
    load_head(1)

    def emit_S(t):
        h, j, kb = items[t]
        sl = h % 2
        sbank = t % 4
        P.wait("pe", [hload[h], cx.bank_free[sbank], mld])
        diag = kb >= 8 * j
        P.op("pe", lambda e: e.matmul(cx.ps[sbank], lhsT=kt[sl][:, kb * 128:(kb + 1) * 128],
                                      rhs=qt[sl][:, j * 512:(j + 1) * 512], start=True, stop=False), signal=False)
        pe = P.op("pe", lambda e: e.matmul(cx.ps[sbank], lhsT=cx.ones_b[0:2, :], rhs=dqt[sl][0:2, j * 512:(j + 1) * 512],
                                           start=False, stop=(not diag)), signal=(not diag))
        if diag:
            pe = P.op("pe", lambda e: e.matmul(cx.ps[sbank], lhsT=cx.ident_b[:], rhs=maskF[:, kb - 8 * j, :],
                                               start=False, stop=True))
        p = t % NPT
        ev = P.op("act", lambda e: e.activation(out=pt[p][:], in_=cx.ps[sbank], func=AF.Exp,
                                                bias=cx.biask[:, h, j, kb:kb + 1], scale=SCALE),
                  waits=[pe, ptfree[p]])
        exp_ev[t] = ev
        cx.bank_free[sbank] = ev

    def emit_PV(t):
        h, j, kb = items[t]
        sl = h % 2
        hj = h * 4 + j
        ob = 4 + hj % 2
        lb = 6 + hj % 2
        last = (kb == 8 * j + 7)
        if kb == 0:
            P.wait("pe", [cx.bank_free[ob], cx.bank_free[lb]])
        P.wait("pe", exp_ev[t])
        p = t % NPT
        P.op("pe", lambda e: e.matmul(cx.ps[ob], lhsT=vt[sl][:, kb, :], rhs=pt[p][:], start=(kb == 0), stop=last),
             signal=False)
        pe = P.op("pe", lambda e: e.matmul(cx.ps[lb], lhsT=cx.ones_b[:], rhs=pt[p][:], start=(kb == 0), stop=last))
        ptfree[p] = pe
        if last:
            o = hj % 2
            a1 = P.op("act", lambda e: e.activation(out=rl[:], in_=cx.ps[lb], func=AF.Ln), waits=[pe, st_fin[0]])
            d1 = P.op("act", lambda e: e.activation(out=rl[:], in_=rl[:], func=AF.Exp, scale=-1.0), waits=[a1])
            d2 = P.op("dve", lambda e: e.tensor_tensor(out=ostg[o][:], in0=cx.ps[ob], in1=rl[:], op=ALU.mult),
                      waits=[d1, ofree[o]])
            st_fin[0] = d2
            cx.bank_free[ob] = d2
            cx.bank_free[lb] = d2
            ofree[o] = P.dma("sp", osem[o], lambda e: e.dma_start(out=mixT[h, :, j * 512:(j + 1) * 512], in_=ostg[o][:]),
                             waits=[d2])
            if j == 3:
                hdone[h] = pe
                if h + 2 < 16:
                    load_head(h + 2)

    for t in range(len(items) + LA):
        if t < len(items):
            emit_S(t)
        if t >= LA:
            emit_PV(t - LA)
    sb.release(m0)


def diff_attention(cx, KT, QT, Vs, mixT):
    P, sb = cx.P, cx.sb
    sm = cx.smalls
    m0 = sb.mark()
    kt = [sb.alloc("dkt", [128, 2, S], BF16) for _ in range(2)]
    qt = [sb.alloc("dqt", [128, 2, NOWN], BF16) for _ in range(2)]
    vt = [sb.alloc("dvt", [128, 2, 32, 128], BF16) for _ in range(2)]
    bd = [sb.alloc("bd", [128, 9, 512], BF16) for _ in range(2)]
    tmpb = [sb.alloc("tmpb", [128, 128], F32) for _ in range(2)]
    hsem = [P.sem("dh") for _ in range(2)]
    NPT = 3
    pt = [sb.alloc("dpt", [128, 512], BF16) for _ in range(NPT)]
    ptfree = [None] * NPT
    r1 = sb.alloc("r1", [128, 512], F32)
    r2 = sb.alloc("r2", [128, 512], F32)
    t2 = sb.alloc("t2", [128, 512], F32)
    dfe = [sb.alloc("dfe", [128, 512], F32) for _ in range(2)]
    sqb = [sb.alloc("sqb", [128, 512], BF16) for _ in range(2)]
    lnv = sb.alloc("lnv", [128, 512], F32)
    rstd = sb.alloc("rstdd", [128, 512], F32)
    ostg = [sb.alloc("dostg", [128, 2, 512], BF16) for _ in range(2)]
    osem = [P.sem("do") for _ in range(2)]
    ofree = [None, None]
    hload = {}
    hdone = {}
    bdready = {}
    st = {"sc": 0, "fin": None, "tb": 0}

    def load_head(h):
        sl = h % 2
        w = [hdone.get(h - 2)]
        P.dma("sp", hsem[sl], lambda e: e.dma_start(out=kt[sl][:], in_=KT[16 + 2 * h:18 + 2 * h].rearrange("c p t -> p c t")),
              waits=w)
        P.dma("sp", hsem[sl], lambda e: e.dma_start(out=vt[sl][:], in_=Vs[16 + 2 * h:18 + 2 * h].rearrange("c p k d -> p c k d")))
        hload[h] = P.dma("sp", hsem[sl], lambda e: e.dma_start(
            out=qt[sl][:], in_=QT[16 + 2 * h:18 + 2 * h].rearrange("c p t -> p c t")))
        def base(t):
            if t < -1:
                return cx.t5[:, 2, h, :]
            if t == -1:
                return cx.t5[:, 1, h, :]
            if t == 0:
                return cx.t5[:, 0, h, :]
            return cx.t5[:, 3, 0, :]
        ev = None
        for r in range(-1, 8):
            for i in range(4):
                tb = st["tb"] % 2
                st["tb"] += 1
                b0 = base(r - 2 * i)
                b1 = base(r - 2 * i - 1)
                ea = P.op("dve", lambda e, tb=tb, b0=b0: e.tensor_scalar(out=tmpb[tb][:], in0=b0, scalar1=cx.par[:, 0:1],
                                                                        scalar2=None, op0=ALU.mult), waits=w)
                ev = P.op("dve", lambda e, tb=tb, b1=b1, r=r, i=i: e.scalar_tensor_tensor(
                    out=bd[sl][:, r + 1, i * 128:(i + 1) * 128], in0=b1, scalar=cx.par[:, 1:2], in1=tmpb[tb][:],
                    op0=ALU.mult, op1=ALU.add), waits=[ea])
        bdready[h] = ev

    items = [(h, j, c, kb) for h in range(8) for j in range(4) for c in range(2) for kb in range(8 * j + 8)]
    LA = 1
    DEFER = 10
    exp_ev = {}
    dfa = [sb.alloc("dfa", [128, 512], F32) for _ in range(2)]
    load_head(0)
    load_head(1)
    st.update(r1free=None, r2free=None, sqfree=None, pend=None, since=0)

    def emit_S(t):
        h, j, c, kb = items[t]
        sl = h % 2
        sbank = st["sc"] % 2
        st["sc"] += 1
        diag = kb >= 8 * j - 1
        P.wait("pe", [hload[h], cx.bank_free[sbank]])
        pe = P.op("pe", lambda e: e.matmul(cx.ps[sbank], lhsT=kt[sl][:, c, kb * 128:(kb + 1) * 128],
                                           rhs=qt[sl][:, c, j * 512:(j + 1) * 512], start=True, stop=(not diag)),
                  signal=(not diag))
        if diag:
            P.wait("pe", bdready[h])
            pe = P.op("pe", lambda e: e.matmul(cx.ps[sbank], lhsT=cx.ident_b[:], rhs=bd[sl][:, kb - 8 * j + 1, :],
                                               start=False, stop=True))
        p = t % NPT
        bias = sm[:, NS_RB15 + h:NS_RB15 + h + 1] if not diag else cx.eps6[:, 2:3]
        ev = P.op("act", lambda e: e.activation(out=pt[p][:], in_=cx.ps[sbank], func=AF.Exp, bias=bias, scale=SCALE),
                  waits=[pe, ptfree[p]])
        exp_ev[t] = ev
        cx.bank_free[sbank] = ev

    def emit_ss():
        h, j, sq_evs = st["pend"]
        st["pend"] = None
        o = (h * 4 + j) % 2
        sbank = st["sc"] % 2
        st["sc"] += 1
        P.wait("pe", [cx.bank_free[sbank]] + sq_evs)
        P.op("pe", lambda e: e.matmul(cx.ps[sbank], lhsT=cx.ones_b[:], rhs=sqb[0][:], start=True, stop=False), signal=False)
        pss = P.op("pe", lambda e: e.matmul(cx.ps[sbank], lhsT=cx.ones_b[:], rhs=sqb[1][:], start=False, stop=True))
        st["sqfree"] = pss
        a1 = P.op("act", lambda e: e.activation(out=lnv[:], in_=cx.ps[sbank], func=AF.Ln, bias=cx.eps6[:, 1:2],
                                                scale=1.0 / 256.0), waits=[pss, st["fin"]])
        cx.bank_free[sbank] = a1
        a2 = P.op("act", lambda e: e.activation(out=rstd[:], in_=lnv[:], func=AF.Exp, scale=-0.5), waits=[a1])
        fin = None
        for e_ in range(2):
            fin = P.op("dve", lambda e, e_=e_: e.scalar_tensor_tensor(
                out=ostg[o][:, e_, :], in0=dfe[e_][:], scalar=cx.gsub8[:, e_:e_ + 1], in1=rstd[:],
                op0=ALU.mult, op1=ALU.mult), waits=[a2, ofree[o]])
        st["fin"] = fin
        ofree[o] = P.dma("sp", osem[o], lambda e: e.dma_start(
            out=mixT[16 + 2 * h:18 + 2 * h, :, j * 512:(j + 1) * 512].rearrange("c p t -> p c t"), in_=ostg[o][:]),
            waits=[fin])
        if j == 3:
            hdone[h] = pss
            if h + 2 < 8:
                load_head(h + 2)

    def emit_PV(t):
        h, j, c, kb = items[t]
        sl = h % 2
        u = (h * 4 + j) * 2 + c
        sset = u % 2
        ob = 2 + 3 * sset
        lb = 4 + 3 * sset
        last = (kb == 8 * j + 7)
        if kb == 0:
            P.wait("pe", [cx.bank_free[ob], cx.bank_free[ob + 1], cx.bank_free[lb]])
        P.wait("pe", exp_ev[t])
        p = t % NPT
        for e_ in range(2):
            P.op("pe", lambda e, e_=e_: e.matmul(cx.ps[ob + e_], lhsT=vt[sl][:, e_, kb, :], rhs=pt[p][:],
                                                 start=(kb == 0), stop=last), signal=False)
        pe = P.op("pe", lambda e: e.matmul(cx.ps[lb], lhsT=cx.ones_b[:], rhs=pt[p][:], start=(kb == 0), stop=last))
        ptfree[p] = pe
        st["since"] += 1
        if st["pend"] is not None and st["since"] >= DEFER:
            emit_ss()
        if last and c == 0:
            a1 = P.op("act", lambda e: e.activation(out=r1[:], in_=cx.ps[lb], func=AF.Ln), waits=[pe, st["r1free"]])
            a2 = P.op("act", lambda e: e.activation(out=r1[:], in_=r1[:], func=AF.Exp, scale=-1.0), waits=[a1])
            dl = None
            for e_ in range(2):
                dl = P.op("dve", lambda e, e_=e_: e.tensor_tensor(out=dfa[e_][:], in0=cx.ps[ob + e_], in1=r1[:], op=ALU.mult),
                          waits=[a2])
            st["r1free"] = dl
            for b in (ob, ob + 1, lb):
                cx.bank_free[b] = dl
        if last and c == 1:
            if st["pend"] is not None:
                emit_ss()
            a1 = P.op("act", lambda e: e.activation(out=r2[:], in_=cx.ps[lb], func=AF.Ln), waits=[pe, st["r2free"]])
            a2 = P.op("act", lambda e: e.activation(out=r2[:], in_=r2[:], func=AF.Exp, scale=-1.0), waits=[a1])
            sq_evs = []
            dl = None
            for e_ in range(2):
                d5 = P.op("dve", lambda e, e_=e_: e.tensor_tensor(out=t2[:], in0=cx.ps[ob + e_], in1=r2[:], op=ALU.mult),
                          waits=[a2])
                d6 = P.op("dve", lambda e, e_=e_: e.scalar_tensor_tensor(
                    out=dfe[e_][:], in0=t2[:], scalar=cx.lam[:, 1:2], in1=dfa[e_][:], op0=ALU.mult, op1=ALU.add),
                    waits=[d5, st["fin"]])
                d7 = P.op("dve", lambda e, e_=e_: e.tensor_tensor(out=sqb[e_][:], in0=dfe[e_][:], in1=dfe[e_][:], op=ALU.mult),
                          waits=[d6, st["sqfree"]])
                sq_evs.append(d7)
                dl = d5
            st["r2free"] = dl
            for b in (ob, ob + 1, lb):
                cx.bank_free[b] = dl
            st["pend"] = (h, j, sq_evs)
            st["since"] = 0

    for t in range(len(items) + LA):
        if t < len(items):
            emit_S(t)
        if t >= LA:
            emit_PV(t - LA)
    if st["pend"] is not None:
        emit_ss()
    sb.release(m0)


def cross_attention(cx, qmT, kmT, VM, omT):
    P, sb = cx.P, cx.sb
    m0 = sb.mark()
    kmt = sb.alloc("kmt", [128, KC, NMEM], BF16)
    vmt = sb.alloc("vmt", [128, KC, 2, 128], BF16)
    qm = [sb.alloc("qm", [128, KC, 512], BF16) for _ in range(2)]
    qsem = [P.sem("cq") for _ in range(2)]
    qfree = [None, None]
    ksem = P.sem("ck")
    P.dma("sp", ksem, lambda e: e.dma_start(out=kmt[:], in_=kmT.rearrange("k p t -> p k t")))
    kld = P.dma("sp", ksem, lambda e: e.dma_start(out=vmt[:], in_=VM.rearrange("c p k d -> p c k d")))
    ptm = [sb.alloc("ptm", [128, 512], BF16) for _ in range(4)]
    ptfree = [None] * 4
    rl = sb.alloc("crl", [128, 512], F32)
    ostg = [sb.alloc("costg", [128, 8, 512], BF16) for _ in range(2)]
    osem = [P.sem("co") for _ in range(2)]
    ofree = [None, None]
    st = {"b": 0}

    def nbank():
        b = st["b"] % 8
        st["b"] += 1
        return b

    qld = {}

    def loadq(t):
        b = t % 2
        qld[t] = P.dma("sp", qsem[b], lambda e: e.dma_start(
            out=qm[b][:], in_=qmT[:, :, t * 512:(t + 1) * 512].rearrange("k p t -> p k t")), waits=[qfree[b]])

    loadq(0)
    n = 0
    rl_free = None
    for t in range(4):
        if t + 1 < 4:
            loadq(t + 1)
        qb = t % 2
        for hm in range(4):
            o = n % 2
            pts = []
            for mb in range(2):
                bk = nbank()
                P.wait("pe", [qld[t], kld, cx.bank_free[bk]])
                for ch in range(8):
                    pe = P.op("pe", lambda e, ch=ch, mb=mb, bk=bk, hm=hm, qb=qb: e.matmul(
                        cx.ps[bk], lhsT=kmt[:, 8 * hm + ch, mb * 128:(mb + 1) * 128], rhs=qm[qb][:, 8 * hm + ch, :],
                        start=(ch == 0), stop=(ch == 7)), signal=(ch == 7))
                p = (2 * n + mb) % 4
                ev = P.op("act", lambda e, p=p, bk=bk: e.activation(out=ptm[p][:], in_=cx.ps[bk], func=AF.Exp, scale=SCALE_M),
                          waits=[pe, ptfree[p]])
                cx.bank_free[bk] = ev
                pts.append((p, ev))
            if hm == 3:
                qfree[qb] = pe
            bl = nbank()
            P.wait("pe", [cx.bank_free[bl], pts[0][1], pts[1][1]])
            P.op("pe", lambda e, bl=bl, p=pts[0][0]: e.matmul(cx.ps[bl], lhsT=cx.ones_b[:], rhs=ptm[p][:], start=True, stop=False),
                 signal=False)
            pl = P.op("pe", lambda e, bl=bl, p=pts[1][0]: e.matmul(cx.ps[bl], lhsT=cx.ones_b[:], rhs=ptm[p][:], start=False, stop=True))
            d1 = P.op("dve", lambda e, bl=bl: e.reciprocal(out=rl[:], in_=cx.ps[bl]), waits=[pl, rl_free])
            cx.bank_free[bl] = d1
            dlast = None
            for e_ in range(8):
                bo = nbank()
                P.wait("pe", [cx.bank_free[bo]])
                P.op("pe", lambda e, bo=bo, e_=e_, p=pts[0][0], hm=hm: e.matmul(cx.ps[bo], lhsT=vmt[:, 8 * hm + e_, 0, :], rhs=ptm[p][:],
                                                                       start=True, stop=False), signal=False)
                po = P.op("pe", lambda e, bo=bo, e_=e_, p=pts[1][0], hm=hm: e.matmul(cx.ps[bo], lhsT=vmt[:, 8 * hm + e_, 1, :], rhs=ptm[p][:],
                                                                            start=False, stop=True))
                dlast = P.op("dve", lambda e, bo=bo, e_=e_, o=o: e.tensor_tensor(out=ostg[o][:, e_, :], in0=cx.ps[bo], in1=rl[:],
                                                                                 op=ALU.mult), waits=[po, d1, ofree[o]])
                cx.bank_free[bo] = dlast
            ptfree[pts[0][0]] = po
            ptfree[pts[1][0]] = po
            rl_free = dlast
            ofree[o] = P.dma("sp", osem[o], lambda e, o=o, hm=hm, t=t: e.dma_start(
                out=omT[8 * hm:8 * hm + 8, :, t * 512:(t + 1) * 512].rearrange("c p t -> p c t"), in_=ostg[o][:]),
                waits=[dlast])
            n += 1
    sb.release(m0)


NS_G, NS_GSUB, NS_PAR, NS_BF, NS_RB, NS_RB15, NS_LAM, NS_ID, NS_E127 = 0, 160, 162, 164, 165, 173, 181, 693, 821
NS = 949


def build(upto=99, debug=()):
    nc = bass.Bass("TRN2", target_bir_lowering=False)
    cx = Ctx()
    cx.nc = nc
    cx.P = P = Prog(nc)
    cx.sb = sb = SBAlloc(nc)
    cx.phase_ev = None
    cx.inputs = []
    cx.outputs = []

    def inp(name, shape, dt=F32):
        cx.inputs.append(name)
        return nc.dram_tensor(name, list(shape), dt, kind="ExternalInput").ap()

    def outp(name, shape, dt=F32):
        cx.outputs.append(name)
        return nc.dram_tensor(name, list(shape), dt, kind="ExternalOutput").ap()

    def scratch(name, shape, dt):
        return nc.dram_tensor(name, list(shape), dt).ap()

    psum = cx.es_ps = P.es.enter_context(nc.psum_tensor("ps", [128, 8, 512], F32))
    cx.ps = [psum[:, b, :] for b in range(8)]
    cx.bank_free = [None] * 8

    smalls_d = inp("smalls", [128, NS])
    smalls = sb.alloc("smalls", [128, NS], F32)
    cx.gvec = smalls[:, NS_G:NS_G + 160].rearrange("p (g k) -> p g k", g=5)
    cx.ident_f = smalls[:, NS_ID:NS_ID + 128]
    cx.e127 = smalls[:, NS_E127:NS_E127 + 128]
    cx.par = smalls[:, NS_PAR:NS_PAR + 2]
    cx.smalls = smalls
    cx.ones_b = sb.alloc("ones_b", [128, 128], BF16)
    cx.ident_b = sb.alloc("ident_b", [128, 128], BF16)
    cx.eps6 = sb.alloc("eps6", [128, 4], F32)
    csem = P.sem("const")
    ld = P.dma("sp", csem, lambda e: e.dma_start(out=smalls[:], in_=smalls_d))
    P.op("dve", lambda e: e.memset(cx.ones_b[:], 1.0))
    P.op("dve", lambda e: e.memset(cx.eps6[:, 0:1], 1e-6))
    P.op("dve", lambda e: e.memset(cx.eps6[:, 1:2], 1e-5))
    P.op("dve", lambda e: e.memset(cx.eps6[:, 2:4], 0.0))
    P.op("dve", lambda e: e.tensor_copy(out=cx.ident_b[:], in_=cx.ident_f), waits=[ld])
    P.barrier()

    dbg = {}

    def finish():
        P.barrier()
        dsem = P.sem("dbg")
        for name, (ap, shape, dt) in dbg.items():
            if name in debug:
                o = outp("dbg_" + name, shape, dt)
                P.dma("sp", dsem, lambda e, o=o, a=ap: e.dma_start(out=o, in_=a))
        P.barrier()
        P.emit()
        return nc, cx

    xa = inp("xa", [KC, 128, S])
    xo = inp("xo", [KC, 128, NOWN])
    aT_all = scratch("aT_all", [KC, 128, S], BF16)
    aT_own = scratch("aT_own", [KC, 128, NOWN], BF16)
    dbg["aT_all"] = (aT_all, [KC, 128, S], BF16)
    dbg["aT_own"] = (aT_own, [KC, 128, NOWN], BF16)
    norm_pass(cx, xa, aT_all, S, 0, BF16)
    norm_pass(cx, xo, aT_own, NOWN, 0, BF16)
    P.barrier()
    if upto <= 1:
        return finish()

    w_in = inp("w_in", [D, 12304])
    KT = scratch("KT", [32, 128, S], BF16)
    QT = scratch("QT", [32, 128, NOWN], BF16)
    Vs = scratch("Vs", [32, 128, 32, 128], BF16)
    dbg["KT"] = (KT, [32, 128, S], BF16)
    dbg["QT"] = (QT, [32, 128, NOWN], BF16)
    dbg["Vs"] = (Vs, [32, 128, 32, 128], BF16)
    attn_mark = sb.mark()
    sigT = sb.alloc("sigT", [16, S], F32)
    cx.sigT = sigT

    def w_in_fn(k0, kn, c0, cn):
        return w_in[k0 * 128:(k0 + kn) * 128, c0:c0 + cn]

    def kt_dest(base):
        def f(tt, ci, cb):
            h0 = base + (cb["c0"] - cb["cbase"]) // 128
            return KT[h0:h0 + 4, :, tt * 1024:(tt + 1) * 1024].rearrange("h p t -> p h t")
        return f

    def v_dest(base):
        def f(tt, ci, cb):
            h0 = base + (cb["c0"] - cb["cbase"]) // 128
            return Vs[h0:h0 + 4, :, tt * 8:(tt + 1) * 8, :].rearrange("h p k d -> p h (k d)")
        return f

    def q_dest(base):
        def f(tt, ci, cb):
            h0 = base + (cb["c0"] - cb["cbase"]) // 128
            return QT[h0:h0 + 4, :, tt * 1024:(tt + 1) * 1024].rearrange("h p t -> p h t")
        return f

    ekf = EpiCopyFM(lambda tt, ci, cb: kt_dest(cb["hb"])(tt, ci, cb))
    ekd = ekf
    evf = EpiCopyTM(lambda tt, ci, cb: v_dest(cb["hb"])(tt, ci, cb))
    evd = evf
    esg = EpiSig(sigT, smalls[0:16, NS_BF:NS_BF + 1])
    cbs = []
    for c in range(4):
        cbs.append(dict(c0=C_KF + 512 * c, cn=512, cbase=C_KF, hb=0, mode="fm", epi=ekf))
    for c in range(4):
        cbs.append(dict(c0=C_VF + 512 * c, cn=512, cbase=C_VF, hb=0, mode="tm", epi=evf))
    cbs.append(dict(c0=C_F, cn=16, cbase=C_F, hb=0, mode="fm", epi=esg))
    for c in range(4):
        cbs.append(dict(c0=C_KD + 512 * c, cn=512, cbase=C_KD, hb=16, mode="fm", epi=ekd))
    for c in range(4):
        cbs.append(dict(c0=C_VD + 512 * c, cn=512, cbase=C_VD, hb=16, mode="tm", epi=evd))
    if upto == 2:
        cbs = [cbs[0], cbs[4], cbs[8], cbs[9], cbs[13]]
    w_dn0 = inp("w_dn0", [DFF // 2, D])
    w_dn1 = inp("w_dn1", [DFF // 2, D])
    wdn_bf = scratch("wdn_bf", [DFF, D], BF16)
    pcsem = P.sem("precast")
    bg = []
    for r in range(128):
        wsrc = (w_dn0 if r < 64 else w_dn1)[(r % 64) * 128:(r % 64 + 1) * 128, :]
        bg.append(lambda d=wdn_bf[r * 128:(r + 1) * 128, :], s_=wsrc: P.dma(
            "pool", pcsem, lambda e, d=d, s_=s_: e.dma_start(out=d, in_=s_)))
    gemm(cx, X=aT_all, kc=KC, wfn=w_in_fn, ntok=S, colblocks=cbs, x_stream=False, bg=bg)
    while bg:
        bg.pop(0)()
    P.barrier()
    if upto <= 2:
        return finish()
    eqf = EpiCopyFM(lambda tt, ci, cb: q_dest(cb["hb"])(tt, ci, cb))
    eqd = eqf
    cbs = []
    for c in range(4):
        cbs.append(dict(c0=C_QF + 512 * c, cn=512, cbase=C_QF, hb=0, mode="fm", epi=eqf))
    for c in range(4):
        cbs.append(dict(c0=C_QD + 512 * c, cn=512, cbase=C_QD, hb=16, mode="fm", epi=eqd))
    gemm(cx, X=aT_own, kc=KC, wfn=w_in_fn, ntok=NOWN, colblocks=cbs, x_stream=False)
    P.barrier()
    if upto <= 3:
        return finish()

    dq_d = scratch("dq_d", [16, 2, NOWN], BF16)
    oh_d = inp("oh", [33, 2 * 128 * 128])
    maskF_d = inp("maskF", [128, 8 * 512])
    mixT = scratch("mixT", [KC, 128, NOWN], BF16)
    dbg["mixT"] = (mixT, [KC, 128, NOWN], BF16)
    dbg["dq_d"] = (dq_d, [16, 2, NOWN], BF16)
    attn_prep(cx, dq_d, oh_d)
    P.barrier()
    if upto <= 4:
        return finish()
    fox_attention(cx, KT, QT, Vs, dq_d, mixT, maskF_d)
    P.barrier()
    if upto <= 5:
        return finish()
    diff_attention(cx, KT, QT, Vs, mixT)
    P.barrier()
    sb.release(attn_mark)
    if upto <= 6:
        return finish()

    def fm_dest(Y, Tt=1024):
        def f(tt, ci, cb):
            h0 = cb["c0"] // 128
            n = (cb["cn"] + 127) // 128
            return Y[h0:h0 + n, :, tt * Tt:(tt + 1) * Tt].rearrange("h p t -> p h t")
        return f

    def simple_w(w):
        def f(k0, kn, c0, cn):
            return w[k0 * 128:(k0 + kn) * 128, c0:c0 + cn]
        return f

    def cblocks(n, epi, mode="fm"):
        return [dict(c0=512 * c, cn=512, cbase=0, hb=0, mode=mode, epi=epi) for c in range(n)]

    cx.acc = sb.alloc("acc", [128, NOWN], F32)
    cx.rstd_a = sb.alloc("rstd_a", [128, NOWN], F32)
    cx.r2_a = sb.alloc("r2_a", [128, NOWN], F32)
    cx.ones_f = sb.alloc("ones_f2", [128, 128], F32)
    P.op("dve", lambda e: e.memset(cx.ones_f[:], 1.0))
    P.barrier()
    w_out = inp("w_out", [D, D])
    h1T = scratch("h1T", [KC, 128, NOWN], F32)
    dbg["h1T"] = (h1T, [KC, 128, NOWN], F32)
    cT_d = scratch("cT_d", [KC, 128, NOWN], BF16)
    gemm(cx, X=mixT, kc=KC, wfn=simple_w(w_out), ntok=NOWN,
         colblocks=cblocks(8, EpiResid(fm_dest(xo), fm_dest(h1T), norm=dict(gi=1, hg_dest=fm_dest(cT_d), ncb=8))),
         x_stream=False)
    P.barrier()
    if upto <= 7:
        return finish()
    memT = inp("memT", [KC, 128, NMEM])
    mT_d = scratch("mT_d", [KC, 128, NMEM], BF16)
    norm_pass(cx, memT, mT_d, NMEM, 2, BF16)
    P.barrier()
    wk = inp("wk_mem", [D, D])
    wv = inp("wv_mem", [D, D])
    kmT = scratch("kmT", [KC, 128, NMEM], BF16)
    VM = scratch("VM", [KC, 128, 2, 128], BF16)
    gemm(cx, X=mT_d, kc=KC, wfn=simple_w(wk), ntok=NMEM, colblocks=cblocks(8, EpiCopyFM(fm_dest(kmT, 256))),
         x_stream=False, Tt=256)
    P.barrier()

    def vm_dest(tt, ci, cb):
        h0 = cb["c0"] // 128
        return VM[h0:h0 + 4, :, :, :].rearrange("h p k d -> p h (k d)")
    gemm(cx, X=mT_d, kc=KC, wfn=simple_w(wv), ntok=NMEM, colblocks=cblocks(8, EpiCopyTM(vm_dest), mode="tm"),
         x_stream=False, Tt=256)
    P.barrier()
    wq = inp("wq_mem", [D, D])
    qmT = scratch("qmT", [KC, 128, NOWN], BF16)
    gemm(cx, X=cT_d, kc=KC, wfn=simple_w(wq), ntok=NOWN, colblocks=cblocks(8, EpiCopyFM(fm_dest(qmT), rstd=cx.rstd_a)),
         x_stream=False)
    P.barrier()
    omT = scratch("omT", [KC, 128, NOWN], BF16)
    dbg["omT"] = (omT, [KC, 128, NOWN], BF16)
    cross_attention(cx, qmT, kmT, VM, omT)
    P.barrier()
    if upto <= 11:
        return finish()
    wo = inp("wo_mem", [D, D])
    h2T = scratch("h2T", [KC, 128, NOWN], F32)
    dbg["h2T"] = (h2T, [KC, 128, NOWN], F32)
    nT_d = scratch("nT_d", [KC, 128, NOWN], BF16)
    gemm(cx, X=omT, kc=KC, wfn=simple_w(wo), ntok=NOWN,
         colblocks=cblocks(8, EpiResid(fm_dest(h1T), fm_dest(h2T), norm=dict(gi=3, hg_dest=fm_dest(nT_d), ncb=8))),
         x_stream=False)
    P.barrier()
    if upto <= 12:
        return finish()
    w_up0 = inp("w_up0", [D, DFF // 2])
    w_up1 = inp("w_up1", [D, DFF // 2])
    actT = scratch("actT", [DFF // 128, 128, NOWN], BF16)

    def wup_fn(k0, kn, c0, cn):
        w, c = (w_up0, c0) if c0 < DFF // 2 else (w_up1, c0 - DFF // 2)
        return w[k0 * 128:(k0 + kn) * 128, c:c + cn]
    gemm(cx, X=nT_d, kc=KC, wfn=wup_fn, ntok=NOWN, colblocks=cblocks(32, EpiRelu2(fm_dest(actT), r2=cx.r2_a)), x_stream=False)
    P.barrier()
    h3T = scratch("h3T", [KC, 128, NOWN], F32)
    dbg["h3T"] = (h3T, [KC, 128, NOWN], F32)

    def wdn_fn(k0, kn, c0, cn):
        return wdn_bf[k0 * 128:(k0 + kn) * 128, c0:c0 + cn]
    gemm(cx, X=actT, kc=DFF // 128, wfn=wdn_fn, ntok=NOWN,
         colblocks=cblocks(8, EpiResid(fm_dest(h2T), fm_dest(h3T), norm=dict(gi=4, hg_dest=None, ncb=8))),
         x_stream=True, NXS=2)
    P.barrier()
    outT = outp("outT", [KC, 128, NOWN])
    norm_pass(cx, h3T, outT, NOWN, 4, F32, TT=128, rstd_src=cx.rstd_a)
    return finish()


def own_tokens(qh):
    blocks = [8 * j + 2 * i + qh for j in range(4) for i in range(4)]
    return np.concatenate([np.arange(bk * 128, (bk + 1) * 128) for bk in blocks])


def t5_bucket_np(rel):
    rel = np.asarray(rel, np.int32)
    ret = np.where(rel > 0, 16, 0).astype(np.int32)
    n = np.abs(rel)
    nf = np.maximum(n, 1).astype(np.float32)
    large = 8 + (np.log(nf / np.float32(8)) / np.float32(math.log(128 / 8)) * np.float32(8)).astype(np.int32)
    large = np.minimum(large, 15)
    return ret + np.where(n < 8, n, large)


def fox_mask_table(qh):
    m = np.zeros((128, 8, 4, 128), np.float32)
    kk = np.arange(128)[:, None]
    qq = np.arange(128)[None, :]
    diag = np.where(kk <= qq, 0.0, NEG).astype(np.float32)
    for r in range(8):
        for i in range(4):
            t = r - (2 * i + qh)
            if t == 0:
                m[:, r, i, :] = diag
            elif t > 0:
                m[:, r, i, :] = NEG
    return m.reshape(128, 8 * 512)


def t5_onehot():
    oh = np.zeros((33, 2, 128, 128), np.float32)
    q = np.arange(128)[:, None]
    k = np.arange(128)[None, :]
    bd = t5_bucket_np(k - q)
    allowed = (k // 64) <= (q // 64)
    bd = np.where(allowed, bd, 32)
    bn = t5_bucket_np(k - q - 128)
    for b in range(33):
        oh[b, 0] = (bd == b)
        oh[b, 1] = (bn == b)
    return oh.reshape(33, 2 * 128 * 128)


def prep_smalls(inp, qh):
    s = np.zeros((128, NS), np.float32)
    gs = [inp["g_mix"][0], inp["g_cross"][0], inp["g_mem"][0], inp["g_mlp"][0], inp["g_final"]]
    for gi, g in enumerate(gs):
        s[:, NS_G + gi * 32:NS_G + (gi + 1) * 32] = np.asarray(g, np.float32).reshape(32, 128).T
    s[:, NS_GSUB:NS_GSUB + 2] = np.asarray(inp["g_subln"][0], np.float32).reshape(2, 128).T
    s[:, NS_PAR + qh] = 1.0
    s[0:16, NS_BF] = np.asarray(inp["b_forget"][0], np.float32)
    s[0:32, NS_RB:NS_RB + 8] = np.asarray(inp["rel_bias"], np.float32)
    s[:, NS_RB15:NS_RB15 + 8] = np.asarray(inp["rel_bias"], np.float32)[15][None, :]
    lam = np.stack([inp["lambda_q1"][0], inp["lambda_k1"][0], inp["lambda_q2"][0], inp["lambda_k2"][0]])
    s[:, NS_LAM:NS_LAM + 512] = np.asarray(lam, np.float32).reshape(1, 512)
    s[:, NS_ID:NS_ID + 128] = np.eye(128, dtype=np.float32)
    s[127, NS_E127:NS_E127 + 128] = 1.0
    return s


def prep_core(inp, c, names):
    b, qh = c // 2, c % 2
    m = {}
    xT = None
    if "xa" in names or "xo" in names:
        xT = np.ascontiguousarray(np.asarray(inp["x"][b], np.float32).T).reshape(KC, 128, S)
    if "xa" in names:
        m["xa"] = xT
    if "xo" in names:
        m["xo"] = np.ascontiguousarray(xT[:, :, own_tokens(qh)])
    if "memT" in names:
        m["memT"] = np.ascontiguousarray(np.asarray(inp["mem"][b], np.float32).T).reshape(KC, 128, NMEM)
    if "smalls" in names:
        m["smalls"] = prep_smalls(inp, qh)
    if "maskF" in names:
        m["maskF"] = fox_mask_table(qh)
    if "oh" in names:
        m["oh"] = t5_onehot()
    return m


def shared_inputs(inp, names):
    m = {}
    for nm in ("w_in", "w_out", "wq_mem", "wk_mem", "wv_mem", "wo_mem"):
        if nm in names:
            m[nm] = np.asarray(inp[nm][0], np.float32)
    if "w_up0" in names:
        w = np.asarray(inp["w_up"][0], np.float32)
        m["w_up0"] = np.ascontiguousarray(w[:, :DFF // 2])
        m["w_up1"] = np.ascontiguousarray(w[:, DFF // 2:])
    if "w_dn0" in names:
        w = np.asarray(inp["w_down"][0], np.float32)
        m["w_dn0"] = w[:DFF // 2]
        m["w_dn1"] = w[DFF // 2:]
    return m


def kernel(**inputs):
    nc, cx = build()
    sh = shared_inputs(inputs, cx.inputs)
    in_maps = []
    for c in range(8):
        m = prep_core(inputs, c, cx.inputs)
        m.update(sh)
        in_maps.append(m)
    res = run_bass_kernel_spmd(nc, in_maps, core_ids=list(range(8)))
    out = np.empty((4, S, D), np.float32)
    for c in range(8):
        b, qh = c // 2, c % 2
        o = np.asarray(res.results[c]["outT"], np.float32).reshape(D, NOWN)
        out[b, own_tokens(qh), :] = o.T
    return out
```

```python
import math
from contextlib import ExitStack
import numpy as np
import concourse.bass as bass
import concourse.mybir as mybir
from concourse.bass_utils import run_bass_kernel_spmd

F32 = mybir.dt.float32
BF16 = mybir.dt.bfloat16
AF = mybir.ActivationFunctionType
ALU = mybir.AluOpType
AXX = mybir.AxisListType.X

D = 4096
S = 4096
NOWN = 2048
DFF = 16384
NMEM = 256
KC = 32
SCALE = 128.0 ** -0.5
SCALE_M = 1024.0 ** -0.5
NEG = -30000.0
LAMBDA_INIT = 0.8 - 0.6 * math.exp(0.0)
C_QF, C_KF, C_VF, C_F, C_QD, C_KD, C_VD = 0, 2048, 4096, 6144, 6160, 8208, 10256
SB_BASE = 20480
SB_LIMIT = 229376


class Ev:
    __slots__ = ("sem", "val")

    def __init__(self, sem, val):
        self.sem = sem
        self.val = val


class Sem:
    def __init__(self, h, name):
        self.h = h
        self.name = name
        self.n = 0


class Prog:
    def __init__(self, nc):
        self.nc = nc
        self.es = ExitStack()
        self.q = {e: [] for e in ("pe", "act", "dve", "pool", "sp")}
        self.waited = {}
        self.nsem = 0
        self.prog = {e: self.sem("prog_" + e) for e in ("pe", "act", "dve", "pool")}

    def sem(self, name):
        if not hasattr(self, "allsems"):
            self.allsems = []
            self.pool = []
            self.inuse = []
        if name.startswith("prog_"):
            self.nsem += 1
            name = f"{name}_{self.nsem}"
            sm = Sem(self.es.enter_context(self.nc.semaphore(name)), name)
            self.allsems.append(sm)
            return sm
        if self.pool:
            sm = self.pool.pop()
        else:
            self.nsem += 1
            name = f"{name}_{self.nsem}"
            sm = Sem(self.es.enter_context(self.nc.semaphore(name)), name)
            self.allsems.append(sm)
        self.inuse.append(sm)
        return sm

    def wait(self, eng, ev):
        if ev is None:
            return
        if isinstance(ev, (list, tuple)):
            for e in ev:
                self.wait(eng, e)
            return
        k = (eng, ev.sem.name)
        if self.waited.get(k, 0) >= ev.val:
            return
        self.waited[k] = ev.val
        self.q[eng].append(("w", ev.sem, ev.val))

    def op(self, eng, fn, waits=(), signal=True):
        self.wait(eng, waits)
        if signal:
            s = self.prog[eng]
            s.n += 1
            self.q[eng].append(("o", fn, s, 1))
            return Ev(s, s.n)
        self.q[eng].append(("o", fn, None, 0))
        return None

    def dma(self, eng, sem, fn, waits=()):
        self.wait(eng, waits)
        sem.n += 16
        self.q[eng].append(("o", fn, sem, 16))
        return Ev(sem, sem.n)

    def barrier(self):
        if not hasattr(self, "allsems"):
            self.allsems = []
        evs = [Ev(sm, sm.n) for sm in self.allsems if sm.n > 0]
        for eng in self.q:
            self.wait(eng, evs)
        self.pool.extend(self.inuse)
        self.inuse = []

    def emit(self):
        with self.nc.Block() as block:
            def mk(eng):
                def f(e):
                    for it in self.q[eng]:
                        if it[0] == "w":
                            e.wait_ge(it[1].h, it[2])
                        else:
                            ins = it[1](e)
                            if it[2] is not None:
                                ins.then_inc(it[2].h, it[3])
                return f
            block.tensor(mk("pe"))
            block.scalar(mk("act"))
            block.vector(mk("dve"))
            block.gpsimd(mk("pool"))
            block.sync(mk("sp"))


class SBAlloc:
    def __init__(self, nc):
        self.nc = nc
        self.off = SB_BASE
        self.cnt = 0

    def alloc(self, name, shape, dtype):
        self.cnt += 1
        nbytes = int(np.prod(shape[1:])) * (2 if dtype == BF16 else 4)
        nbytes = (nbytes + 63) // 64 * 64
        off = self.off
        assert off + nbytes <= SB_LIMIT, f"SBUF overflow at {name}: {off}+{nbytes}"
        self.off += nbytes
        return self.nc.alloc_sbuf_tensor_at(f"{name}_{self.cnt}", list(shape), dtype, offset=off)

    def mark(self):
        return self.off

    def release(self, m):
        self.off = m


class Ctx:
    pass


def gemm(cx, *, X, kc, wfn, ntok, colblocks, x_stream, Tt=1024, KP=16, bg=None, NXS=3, x_tiled=False):
    P, sb = cx.P, cx.sb
    m0 = sb.mark()
    NW = 3
    Wt = [sb.alloc("gw", [128, KP, 512], BF16) for _ in range(NW)]
    wsem = [P.sem("gw") for _ in range(NW)]
    wfree = [cx.phase_ev] * NW
    nk = kc // KP
    if x_stream:
        NX = NXS
        Xt = [sb.alloc("gx", [128, KP, Tt], BF16) for _ in range(NX)]
    else:
        NX = 1
        Xt = [sb.alloc("gx", [128, kc, Tt], BF16)]
    xsem = [P.sem("gx") for _ in range(NX)]
    xsem2 = P.sem("gx2")
    xfree = [cx.phase_ev] * NX
    for cb in colblocks:
        cb["epi"].setup(cx, Tt)
    ntt = ntok // Tt
    NT = min(512, Tt)
    cx.NT = NT
    pieces = [(tt, ci, kp) for tt in range(ntt) for ci in range(len(colblocks)) for kp in range(nk)]
    wload = {}
    xload = {}

    def load_w(i):
        tt, ci, kp = pieces[i]
        cb = colblocks[ci]
        slot = i % NW
        src = wfn(kp * KP, KP, cb["c0"], cb["cn"]).rearrange("(k p) c -> p k c", p=128)
        dst = Wt[slot][:, :, 0:cb["cn"]]
        wload[i] = P.dma("pool", wsem[slot], lambda e, d=dst, s=src: e.dma_start(out=d, in_=s),
                         waits=[wfree[slot]])
        if bg:
            bg.pop(0)()

    def load_x(i):
        tt, ci, kp = pieces[i]
        if x_stream:
            slot = i % NX
            src = X[kp * KP:(kp + 1) * KP, :, tt * Tt:(tt + 1) * Tt].rearrange("k p t -> p k t")
            xload[i] = P.dma("pool", xsem[slot], lambda e, d=Xt[slot][:], s=src: e.dma_start(out=d, in_=s),
                             waits=[xfree[slot]])
        else:
            if ci == 0 and kp == 0 and x_tiled:
                half = kc // 2
                nq = Tt // 256
                for hh, sem_ in ((0, xsem[0]), (1, xsem2)):
                    k0, k1 = hh * half, (hh + 1) * half
                    for q in range(nq):
                        xload[(tt, hh)] = P.dma("pool", sem_, lambda e, d=Xt[0][:, k0:k1, q * 256:(q + 1) * 256],
                                                s=X[tt * nq + q, :, k0:k1, :]: e.dma_start(out=d, in_=s),
                                                waits=[xfree[0]])
            elif ci == 0 and kp == 0:
                src = X[:, :, tt * Tt:(tt + 1) * Tt].rearrange("k p t -> p k t")
                half = kc // 2
                xload[(tt, 0)] = P.dma("pool", xsem[0], lambda e, d=Xt[0][:, 0:half, :], s=src[:, 0:half, :]: e.dma_start(out=d, in_=s),
                                       waits=[xfree[0]])
                xload[(tt, 1)] = P.dma("pool", xsem2, lambda e, d=Xt[0][:, half:kc, :], s=src[:, half:kc, :]: e.dma_start(out=d, in_=s))

    npieces = len(pieces)
    PRE = NW - 1
    if x_stream:
        for i in range(min(NX, npieces)):
            load_x(i)
    for i in range(min(PRE, npieces)):
        load_w(i)
    if not x_stream:
        load_x(0)
    ev = None
    for i, (tt, ci, kp) in enumerate(pieces):
        cb = colblocks[ci]
        epi = cb["epi"]
        cn = cb["cn"]
        if kp == 0:
            epi.begin(cx, tt, ci, cb)
        slot = i % NW
        P.wait("pe", wload[i])
        if x_stream:
            xs = i % NX
            P.wait("pe", xload[i])
            Xc = Xt[xs]
        else:
            P.wait("pe", xload[(tt, 0)])
            if (kp + 1) * KP > kc // 2:
                P.wait("pe", xload[(tt, 1)])
            Xc = Xt[0]
        if cb["mode"] == "fm":
            groups = [(cs, ts) for cs in range((cn + 127) // 128) for ts in range(Tt // NT)]
        else:
            groups = [(tb,) for tb in range(Tt // 128)]
        for gi, g in enumerate(groups):
            bank = gi
            if kp == 0:
                P.wait("pe", cx.bank_free[bank])
            for k in range(KP):
                first = (kp == 0 and k == 0)
                last = (kp == nk - 1 and k == KP - 1)
                endp = (gi == len(groups) - 1 and k == KP - 1)
                kk = k if x_stream else kp * KP + k
                if cb["mode"] == "fm":
                    cs, ts = g
                    m = min(128, cn - cs * 128)
                    out = cx.ps[bank][0:m, 0:NT]
                    lhsT = Wt[slot][:, k, cs * 128:cs * 128 + m]
                    rhs = Xc[:, kk, ts * NT:(ts + 1) * NT]
                else:
                    tb = g[0]
                    out = cx.ps[bank][:, 0:cn]
                    lhsT = Xc[:, kk, tb * 128:(tb + 1) * 128]
                    rhs = Wt[slot][:, k, 0:cn]
                ev = P.op("pe", lambda e, o=out, l=lhsT, r=rhs, st=first, sp=last:
                          e.matmul(o, lhsT=l, rhs=r, start=st, stop=sp), signal=(last or endp))
            if kp == nk - 1:
                cx.bank_free[bank] = epi.group(cx, cx.ps[bank], tt, ci, cb, g, ev)
        wfree[slot] = ev
        if x_stream:
            xfree[xs] = ev
            if i + NX < npieces:
                load_x(i + NX)
        else:
            if ci == len(colblocks) - 1 and kp == nk - 1:
                xfree[0] = ev
                if tt + 1 < ntt:
                    load_x(i + 1)
        if i + PRE < npieces:
            load_w(i + PRE)
        if kp == nk - 1:
            epi.end(cx, tt, ci, cb)
    cx.phase_ev = ev
    evs = [ev]
    for cb in colblocks:
        evs += cb["epi"].finish(cx)
    sb.release(m0)
    return evs


class EpiBase:
    def setup(self, cx, Tt):
        pass

    def begin(self, cx, tt, ci, cb):
        pass

    def end(self, cx, tt, ci, cb):
        pass

    def finish(self, cx):
        return []


class EpiCopyFM(EpiBase):
    def __init__(self, destfn, eng="act", rstd=None):
        self.destfn = destfn
        self.eng = eng
        self.rstd = rstd
        self.ready = False

    def setup(self, cx, Tt):
        if self.ready:
            return
        self.ready = True
        self.Tt = Tt
        self.stg = [cx.sb.alloc("stg", [128, 4, Tt], BF16) for _ in range(2)]
        self.ssem = [cx.P.sem("st") for _ in range(2)]
        self.sfree = [None, None]
        self.cnt = 0
        self.last = None

    def begin(self, cx, tt, ci, cb):
        self.buf = self.cnt % 2
        self.cnt += 1

    def group(self, cx, bank, tt, ci, cb, g, pe_ev):
        cs, ts = g
        m = min(128, cb["cn"] - cs * 128)
        NT = cx.NT
        o = self.stg[self.buf][0:m, cs, ts * NT:(ts + 1) * NT]
        i = bank[0:m, 0:NT]
        if self.rstd is not None:
            t0 = tt * self.Tt + ts * NT
            rs = self.rstd[0:m, t0:t0 + NT]
            fn = lambda e, o=o, i=i, rs=rs: e.tensor_tensor(out=o, in0=i, in1=rs, op=ALU.mult)
            self.last = cx.P.op("dve", fn, waits=[pe_ev, self.sfree[self.buf]])
            return self.last
        if self.eng == "act":
            fn = lambda e, o=o, i=i: e.activation(out=o, in_=i, func=AF.Copy)
        else:
            fn = lambda e, o=o, i=i: e.tensor_copy(out=o, in_=i)
        self.last = cx.P.op(self.eng, fn, waits=[pe_ev, self.sfree[self.buf]])
        return self.last

    def end(self, cx, tt, ci, cb):
        b = self.buf
        n = (cb["cn"] + 127) // 128
        dst = self.destfn(tt, ci, cb)
        src = self.stg[b][:, 0:n, :]
        self.sfree[b] = cx.P.dma("sp", self.ssem[b], lambda e, d=dst, s=src: e.dma_start(out=d, in_=s),
                                 waits=[self.last])

    def finish(self, cx):
        return [e for e in self.sfree if e is not None]


class EpiCopyTM(EpiBase):
    def __init__(self, destfn):
        self.destfn = destfn
        self.ready = False

    def setup(self, cx, Tt):
        if self.ready:
            return
        self.ready = True
        self.ntb = Tt // 128
        self.stg = [cx.sb.alloc("stgv", [128, 4, self.ntb, 128], BF16) for _ in range(2)]
        self.ssem = [cx.P.sem("stv") for _ in range(2)]
        self.sfree = [None, None]
        self.cnt = 0

    def begin(self, cx, tt, ci, cb):
        self.buf = self.cnt % 2
        self.cnt += 1

    def group(self, cx, bank, tt, ci, cb, g, pe_ev):
        tb = g[0]
        nh = cb["cn"] // 128
        o = self.stg[self.buf][:, 0:nh, tb, :]
        i = bank[:, 0:cb["cn"]].rearrange("p (h d) -> p h d", h=nh)
        self.last = cx.P.op("dve", lambda e, o=o, i=i: e.tensor_copy(out=o, in_=i),
                            waits=[pe_ev, self.sfree[self.buf]])
        return self.last

    def end(self, cx, tt, ci, cb):
        b = self.buf
        dst = self.destfn(tt, ci, cb)
        src = self.stg[b][:].rearrange("p h t d -> p h (t d)")
        self.sfree[b] = cx.P.dma("sp", self.ssem[b], lambda e, d=dst, s=src: e.dma_start(out=d, in_=s),
                                 waits=[self.last])

    def finish(self, cx):
        return [e for e in self.sfree if e is not None]


class EpiResid(EpiBase):
    def __init__(self, residfn, destfn, norm=None):
        self.residfn = residfn
        self.destfn = destfn
        self.norm = norm
        self.ready = False

    def setup(self, cx, Tt):
        if self.ready:
            return
        self.ready = True
        self.Tt = Tt
        self.res = [cx.sb.alloc("res", [128, 4, Tt], F32) for _ in range(2)]
        self.rsem = [cx.P.sem("rs") for _ in range(2)]
        self.ssem = [cx.P.sem("str") for _ in range(2)]
        self.rfree = [None, None]
        self.rload = [None, None]
        self.cnt = 0
        self.extra = []
        if self.norm:
            self.sqt = [cx.sb.alloc("sqt", [128, 512], F32) for _ in range(2)]
            self.sqfree = [None, None]
            self.sqc = 0
            self.lnt = cx.sb.alloc("lnt", [128, 512], F32)
            self.accev = None
            self.rstd_evs = []
            if self.norm.get("hg_dest"):
                self.hg = [cx.sb.alloc("hgs", [128, 4, Tt], BF16) for _ in range(2)]
                self.hsem = [cx.P.sem("hg") for _ in range(2)]
                self.hfree = [None, None]

    def begin(self, cx, tt, ci, cb):
        b = self.cnt % 2
        self.buf = b
        self.cnt += 1
        self.extra = []
        src = self.residfn(tt, ci, cb)
        self.rload[b] = cx.P.dma("sp", self.rsem[b], lambda e, d=self.res[b][:], s=src: e.dma_start(out=d, in_=s),
                                 waits=[self.rfree[b]])

    def group(self, cx, bank, tt, ci, cb, g, pe_ev):
        P = cx.P
        cs, ts = g
        b = self.buf
        r = self.res[b][:, cs, ts * 512:(ts + 1) * 512]
        self.last = P.op("dve", lambda e, i=bank, r=r: e.tensor_tensor(out=r, in0=i, in1=r, op=ALU.add),
                         waits=[pe_ev, self.rload[b]])
        if self.norm:
            nm = self.norm
            if nm.get("hg_dest"):
                chunk = cb["c0"] // 128 + cs
                gap = cx.gvec[:, nm["gi"], chunk:chunk + 1]
                o = self.hg[b][:, cs, ts * 512:(ts + 1) * 512]
                ah = P.op("act", lambda e, o=o, r=r, gap=gap: e.activation(out=o, in_=r, func=AF.Copy, scale=gap),
                          waits=[self.last, self.hfree[b]])
                self.extra.append(ah)
            k = self.sqc % 2
            self.sqc += 1
            asq = P.op("act", lambda e, o=self.sqt[k][:], r=r: e.activation(out=o, in_=r, func=AF.Square),
                       waits=[self.last, self.sqfree[k]])
            t0 = tt * self.Tt + ts * 512
            acs = cx.acc[:, t0:t0 + 512]
            if ci == 0 and cs == 0:
                d = P.op("dve", lambda e, o=acs, i=self.sqt[k][:]: e.tensor_copy(out=o, in_=i),
                         waits=[asq] + self.rstd_evs)
            else:
                d = P.op("dve", lambda e, o=acs, i=self.sqt[k][:]: e.tensor_tensor(out=o, in0=o, in1=i, op=ALU.add),
                         waits=[asq])
            self.sqfree[k] = d
            self.accev = d
            self.extra.append(asq)
        return self.last

    def end(self, cx, tt, ci, cb):
        P = cx.P
        b = self.buf
        dst = self.destfn(tt, ci, cb)
        self.rfree[b] = P.dma("sp", self.ssem[b], lambda e, d=dst, s=self.res[b][:]: e.dma_start(out=d, in_=s),
                              waits=[self.last] + self.extra)
        if self.norm:
            nm = self.norm
            if nm.get("hg_dest"):
                hd = nm["hg_dest"](tt, ci, cb)
                self.hfree[b] = P.dma("sp", self.hsem[b], lambda e, d=hd, s=self.hg[b][:]: e.dma_start(out=d, in_=s),
                                      waits=self.extra)
            if ci == nm["ncb"] - 1:
                self.rstd_evs = []
                for ts in range(self.Tt // 512):
                    bank = ts
                    t0 = tt * self.Tt + ts * 512
                    P.wait("pe", [cx.bank_free[bank], self.accev])
                    pe = P.op("pe", lambda e, bank=bank, t0=t0: e.matmul(cx.ps[bank], lhsT=cx.ones_f[:], rhs=cx.acc[:, t0:t0 + 512],
                                                                       start=True, stop=True))
                    a1 = P.op("act", lambda e, bank=bank: e.activation(out=self.lnt[:], in_=cx.ps[bank], func=AF.Ln,
                                                                      bias=cx.eps6[:, 0:1], scale=1.0 / D),
                              waits=[pe] + self.rstd_evs)
                    cx.bank_free[bank] = a1
                    a2 = P.op("act", lambda e, t0=t0: e.activation(out=cx.rstd_a[:, t0:t0 + 512], in_=self.lnt[:], func=AF.Exp,
                                                                  scale=-0.5), waits=[a1])
                    d2 = P.op("dve", lambda e, t0=t0: e.tensor_tensor(out=cx.r2_a[:, t0:t0 + 512], in0=cx.rstd_a[:, t0:t0 + 512],
                                                                     in1=cx.rstd_a[:, t0:t0 + 512], op=ALU.mult), waits=[a2])
                    self.rstd_evs = [d2]

    def finish(self, cx):
        return [e for e in self.rfree if e is not None]


class EpiRelu2(EpiBase):
    def __init__(self, destfn, r2=None):
        self.destfn = destfn
        self.r2 = r2
        self.ready = False

    def setup(self, cx, Tt):
        if self.ready:
            return
        self.ready = True
        self.Tt = Tt
        self.tmp = [cx.sb.alloc("rtmp", [128, 512], F32) for _ in range(2)]
        self.tfree = [None, None]
        self.stg = [cx.sb.alloc("stgu", [128, 4, Tt], BF16) for _ in range(2)]
        self.ssem = [cx.P.sem("stu") for _ in range(2)]
        self.sfree = [None, None]
        self.cnt = 0
        self.gc = 0

    def begin(self, cx, tt, ci, cb):
        self.buf = self.cnt % 2
        self.cnt += 1

    def group(self, cx, bank, tt, ci, cb, g, pe_ev):
        cs, ts = g
        b = self.buf
        t = self.gc % 2
        self.gc += 1
        tm = self.tmp[t][:]
        a_ev = cx.P.op("act", lambda e, o=tm, i=bank: e.activation(out=o, in_=i, func=AF.Relu),
                       waits=[pe_ev, self.tfree[t]])
        o = self.stg[b][:, cs, ts * 512:(ts + 1) * 512]
        if self.r2 is not None:
            t0 = tt * self.Tt + ts * 512
            d1 = cx.P.op("dve", lambda e, i=tm: e.tensor_tensor(out=i, in0=i, in1=i, op=ALU.mult), waits=[a_ev])
            self.last = cx.P.op("dve", lambda e, o=o, i=tm, r=self.r2[:, t0:t0 + 512]: e.tensor_tensor(
                out=o, in0=i, in1=r, op=ALU.mult), waits=[d1, self.sfree[b]])
        else:
            self.last = cx.P.op("dve", lambda e, o=o, i=tm: e.tensor_tensor(out=o, in0=i, in1=i, op=ALU.mult),
                                waits=[a_ev, self.sfree[b]])
        self.tfree[t] = self.last
        return a_ev

    def end(self, cx, tt, ci, cb):
        b = self.buf
        dst = self.destfn(tt, ci, cb)
        self.sfree[b] = cx.P.dma("sp", self.ssem[b], lambda e, d=dst, s=self.stg[b][:]: e.dma_start(out=d, in_=s),
                                 waits=[self.last])

    def finish(self, cx):
        return [e for e in self.sfree if e is not None]


class EpiSig(EpiBase):
    def __init__(self, sigT, bias):
        self.sigT = sigT
        self.bias = bias
        self.last = None

    def group(self, cx, bank, tt, ci, cb, g, pe_ev):
        cs, ts = g
        t0 = tt * 1024 + ts * 512
        o = self.sigT[0:16, t0:t0 + 512]
        self.last = cx.P.op("act", lambda e, o=o, i=bank[0:16, :], b=self.bias: e.activation(
            out=o, in_=i, func=AF.Sigmoid, bias=b, scale=1.0), waits=[pe_ev])
        return self.last

    def finish(self, cx):
        return [self.last]


def norm_pass(cx, src, dst, ntok, gi, out_dtype, TT=256, waits=(), rstd_src=None, src_tiled=False, dst_tiled=False):
    P, sb = cx.P, cx.sb
    m0 = sb.mark()
    NXB = 4 if TT == 128 else (3 if out_dtype == BF16 else 2)
    xin = [sb.alloc("nx", [128, KC, TT], F32) for _ in range(NXB)]
    xsem = [P.sem("nx") for _ in range(NXB)]
    xfree = [None] * NXB
    sq = [sb.alloc("nsq", [128, KC, TT] if rstd_src is None else [128, 2], BF16) for _ in range(2)]
    sqfree = [None, None]
    lnv = sb.alloc("nln", [128, TT], F32)
    rstd = [sb.alloc("nrstd", [128, TT], F32) for _ in range(2)]
    rfree = [None, None]
    ot = [sb.alloc("no", [128, KC, TT], out_dtype) for _ in range(2)]
    osem = [P.sem("no") for _ in range(2)]
    ofree = [None, None]
    nt = ntok // TT
    ld = {}
    sqev = {}
    KD = 32

    def load(t):
        b = t % NXB
        s_ = src[t] if src_tiled else src[:, :, t * TT:(t + 1) * TT].rearrange("k p t -> p k t")
        ld[t] = P.dma("sp", xsem[b], lambda e, d=xin[b][:], s=s_: e.dma_start(out=d, in_=s),
                      waits=[xfree[b]] + list(waits))

    def square(t):
        if rstd_src is not None:
            sqev[t] = None
            return
        b = t % NXB
        q = t % 2
        sqev[t] = P.op("act", lambda e, o=sq[q][:], i=xin[b][:]: e.activation(out=o, in_=i, func=AF.Square),
                       waits=[ld[t], sqfree[q]])

    for t in range(min(NXB - 1, nt)):
        load(t)
    square(0)
    for t in range(nt):
        if t + NXB - 1 < nt:
            load(t + NXB - 1)
        if t + 1 < nt:
            square(t + 1)
        b = t % NXB
        q = t % 2
        bank = t % 2
        if rstd_src is None:
            P.wait("pe", [sqev[t], cx.bank_free[bank]])
            for k in range(KC):
                pe = P.op("pe", lambda e, o=cx.ps[bank][:, 0:TT], r=sq[q][:, k, :], st=(k == 0), sp=(k == KC - 1):
                          e.matmul(o, lhsT=cx.ones_b[:], rhs=r, start=st, stop=sp), signal=(k == KC - 1))
            sqfree[q] = pe
            a2 = P.op("act", lambda e, i=cx.ps[bank][:, 0:TT]: e.activation(
                out=lnv[:], in_=i, func=AF.Ln, bias=cx.eps6[:, 0:1], scale=1.0 / D), waits=[pe])
            cx.bank_free[bank] = a2
            a3 = P.op("act", lambda e, o=rstd[q][:]: e.activation(out=o, in_=lnv[:], func=AF.Exp, scale=-0.5),
                      waits=[a2, rfree[q]])
            rs_ap = rstd[q][:]
        else:
            a3 = ld[t]
            rs_ap = rstd_src[:, t * TT:(t + 1) * TT]
        last_d = last_p = None
        for k in range(KC):
            eng = "dve" if k < KD else "pool"
            ev = P.op(eng, lambda e, o=ot[t % 2][:, k, :], i=xin[b][:, k, :], g=cx.gvec[:, gi, k:k + 1],
                      r=rs_ap: e.scalar_tensor_tensor(out=o, in0=i, scalar=g, in1=r, op0=ALU.mult, op1=ALU.mult),
                      waits=[a3, ofree[t % 2]])
            if eng == "dve":
                last_d = ev
            else:
                last_p = ev
        xfree[b] = [last_d, last_p]
        rfree[q] = [last_d, last_p]
        dd = dst[t] if dst_tiled else dst[:, :, t * TT:(t + 1) * TT].rearrange("k p t -> p k t")
        ofree[t % 2] = P.dma("sp", osem[t % 2], lambda e, d=dd, s=ot[t % 2][:]: e.dma_start(out=d, in_=s),
                             waits=[last_d, last_p])
    sb.release(m0)
    return [e for e in ofree if e is not None]


def attn_prep(cx, dq_d, oh_d):
    P, sb = cx.P, cx.sb
    sm = cx.smalls
    sigT = cx.sigT
    cx.ctm = sb.alloc("ctm", [128, 32, 16], F32)
    cx.biask = sb.alloc("biask", [128, 16, 4, 32], F32)
    cx.lam = sb.alloc("lam", [128, 4], F32)
    cx.t5 = sb.alloc("t5", [128, 4, 8, 128], F32)
    cx.rb15s = sb.alloc("rb15s", [128, 8], F32)
    cx.gsub8 = sb.alloc("gsub8", [128, 2], F32)
    m0 = sb.mark()
    ones16 = sb.alloc("ones16", [16, S], F32)
    cT = sb.alloc("cT", [16, S], F32)
    dqf = sb.alloc("dqf", [16, S], F32)
    tmp2 = sb.alloc("tmp2", [16, NOWN], F32)
    dqo = sb.alloc("dqo", [16, NOWN], F32)
    hib = sb.alloc("hib", [16, NOWN], BF16)
    lob = sb.alloc("lob", [16, NOWN], BF16)
    crbc = sb.alloc("crbc", [128, 4, 16], F32)
    lamt = sb.alloc("lamt", [128, 2, 128], F32)
    rbext = sb.alloc("rbext", [64, 8], F32)
    ohc = [sb.alloc("ohc", [33, 32, 128], F32) for _ in range(2)]
    ones_f = sb.alloc("ones_f", [128, 128], F32)

    a0 = P.op("act", lambda e: e.activation(out=sigT[:], in_=sigT[:], func=AF.Ln))
    d0 = P.op("dve", lambda e: e.memset(ones16[:], 1.0))
    prev = [a0, d0]
    for sg in range(4):
        ini = 0.0 if sg == 0 else cT[:, sg * 1024 - 1:sg * 1024]
        pv = P.op("dve", lambda e, sg=sg, ini=ini: e.tensor_tensor_scan(
            out=cT[:, sg * 1024:(sg + 1) * 1024], data0=ones16[:, sg * 1024:(sg + 1) * 1024],
            data1=sigT[:, sg * 1024:(sg + 1) * 1024], initial=ini, op0=ALU.mult, op1=ALU.add), waits=prev)
        prev = [pv]
    dscan = prev[0]
    P.wait("pe", [dscan, cx.bank_free[0]])
    for kb in range(32):
        pe = P.op("pe", lambda e, kb=kb: e.matmul(cx.ps[0][:, kb * 16:(kb + 1) * 16],
                                                 lhsT=cT[0:16, kb * 128:(kb + 1) * 128],
                                                 rhs=cx.ident_f[0:16, 0:16], start=True, stop=True),
                  signal=(kb == 31))
    dctm = P.op("dve", lambda e: e.tensor_copy(out=cx.ctm[:].rearrange("p k h -> p (k h)"), in_=cx.ps[0]), waits=[pe])
    cx.bank_free[0] = dctm
    dz = P.op("dve", lambda e: e.memset(crbc[:, 0, :], 0.0))
    P.wait("pe", [dctm, cx.bank_free[1]])
    for j in range(1, 4):
        pe = P.op("pe", lambda e, j=j: e.matmul(cx.ps[1][:, j * 16:(j + 1) * 16], lhsT=cx.e127,
                                               rhs=cx.ctm[:, 8 * j - 1, :], start=True, stop=True),
                  signal=(j == 3))
    dcr = P.op("dve", lambda e: e.tensor_copy(out=crbc[:, 1:4, :].rearrange("p j h -> p (j h)"),
                                              in_=cx.ps[1][:, 16:64]), waits=[pe, dz])
    cx.bank_free[1] = dcr
    for h in range(16):
        for j in range(4):
            P.op("dve", lambda e, h=h, j=j: e.tensor_scalar(
                out=cx.biask[:, h, j, :], in0=cx.ctm[:, :, h], scalar1=crbc[:, j, h:h + 1], scalar2=-1.0,
                op0=ALU.subtract, op1=ALU.mult), waits=[dcr])
    evs = []
    for j in range(4):
        sc1 = 0.0 if j == 0 else cT[:, 1024 * j - 1:1024 * j]
        evs.append(P.op("dve", lambda e, j=j, sc1=sc1: e.tensor_scalar(
            out=dqf[:, 1024 * j:1024 * (j + 1)], in0=cT[:, 1024 * j:1024 * (j + 1)], scalar1=sc1,
            scalar2=1.0 / SCALE, op0=ALU.subtract, op1=ALU.mult), waits=[dscan]))
    v = dqf[:].rearrange("p (a r t) -> p a r t", r=2, t=128)
    t2v = tmp2[:].rearrange("p (a t) -> p a t", t=128)
    dqv = dqo[:].rearrange("p (a t) -> p a t", t=128)
    e1 = P.op("dve", lambda e: e.tensor_scalar(out=t2v, in0=v[:, :, 0, :], scalar1=cx.par[0:16, 0:1], scalar2=None,
                                               op0=ALU.mult), waits=evs)
    e2 = P.op("dve", lambda e: e.scalar_tensor_tensor(out=dqv, in0=v[:, :, 1, :], scalar=cx.par[0:16, 1:2], in1=t2v,
                                                      op0=ALU.mult, op1=ALU.add), waits=[e1])
    e3 = P.op("dve", lambda e: e.tensor_copy(out=hib[:], in_=dqo[:]), waits=[e2])
    e4 = P.op("dve", lambda e: e.tensor_copy(out=tmp2[:], in_=hib[:]), waits=[e3])
    e5 = P.op("dve", lambda e: e.tensor_tensor(out=tmp2[:], in0=dqo[:], in1=tmp2[:], op=ALU.subtract), waits=[e4])
    e6 = P.op("dve", lambda e: e.tensor_copy(out=lob[:], in_=tmp2[:]), waits=[e5])
    dsem = P.sem("dq")
    P.dma("sp", dsem, lambda e: e.dma_start(out=dq_d[:, 0, :], in_=hib[:]), waits=[e3])
    P.dma("sp", dsem, lambda e: e.dma_start(out=dq_d[:, 1, :], in_=lob[:]), waits=[e6])
    lv = sm[:, NS_LAM:NS_LAM + 512].rearrange("p (a d) -> p a d", a=4)
    l1 = P.op("dve", lambda e: e.tensor_tensor(out=lamt[:, 0, :], in0=lv[:, 0, :], in1=lv[:, 1, :], op=ALU.mult))
    l2 = P.op("dve", lambda e: e.tensor_tensor(out=lamt[:, 1, :], in0=lv[:, 2, :], in1=lv[:, 3, :], op=ALU.mult))
    l3 = P.op("dve", lambda e: e.tensor_reduce(out=cx.lam[:, 0:2], in_=lamt[:], axis=AXX, op=ALU.add), waits=[l1, l2])
    l4 = P.op("act", lambda e: e.activation(out=cx.lam[:, 2:4], in_=cx.lam[:, 0:2], func=AF.Exp), waits=[l3])
    l5 = P.op("dve", lambda e: e.tensor_tensor(out=cx.lam[:, 0:1], in0=cx.lam[:, 2:3], in1=cx.lam[:, 3:4],
                                               op=ALU.subtract), waits=[l4])
    l6 = P.op("dve", lambda e: e.tensor_scalar(out=cx.lam[:, 0:1], in0=cx.lam[:, 0:1], scalar1=LAMBDA_INIT, scalar2=None,
                                               op0=ALU.add), waits=[l5])
    P.op("dve", lambda e: e.tensor_scalar(out=cx.lam[:, 1:2], in0=cx.lam[:, 0:1], scalar1=-1.0, scalar2=None,
                                          op0=ALU.mult), waits=[l6])
    P.op("dve", lambda e: e.tensor_scalar(out=cx.gsub8[:], in0=sm[:, NS_GSUB:NS_GSUB + 2],
                                          scalar1=1.0 - LAMBDA_INIT, scalar2=None, op0=ALU.mult))
    r0 = P.op("dve", lambda e: e.memset(rbext[32:33, :], NEG))
    r1 = P.op("dve", lambda e: e.tensor_scalar(out=rbext[0:32, :], in0=sm[0:32, NS_RB:NS_RB + 8], scalar1=1.0 / SCALE,
                                               scalar2=None, op0=ALU.mult))
    r2 = P.op("dve", lambda e: e.tensor_scalar(out=cx.rb15s[:], in0=sm[:, NS_RB15:NS_RB15 + 8], scalar1=1.0 / SCALE,
                                               scalar2=None, op0=ALU.mult))
    r3 = P.op("dve", lambda e: e.memset(ones_f[:], 1.0))
    P.op("dve", lambda e: e.memset(cx.t5[:, 3, 0, :], NEG))
    for h in range(8):
        P.op("dve", lambda e, h=h: e.tensor_scalar(out=cx.t5[:, 2, h, :], in0=ones_f[:], scalar1=cx.rb15s[:, h:h + 1],
                                                   scalar2=None, op0=ALU.mult), waits=[r2, r3])
    osem = [P.sem("oh") for _ in range(2)]
    ofree = [None, None]
    ohv = oh_d.rearrange("b (t q k) -> b t q k", t=2, q=128)
    n = 0
    for ty in range(2):
        P.wait("pe", [cx.bank_free[2], cx.bank_free[3], r0, r1])
        for qc in range(4):
            b = n % 2
            n += 1
            ld = P.dma("sp", osem[b], lambda e, b=b, ty=ty, qc=qc: e.dma_start(
                out=ohc[b][:], in_=ohv[:, ty, qc * 32:(qc + 1) * 32, :]), waits=[ofree[b]])
            P.wait("pe", ld)
            for ql in range(32):
                q = qc * 32 + ql
                bank = 2 + (q * 8) // 512
                col = (q * 8) % 512
                pe = P.op("pe", lambda e, b=b, ql=ql, bank=bank, col=col: e.matmul(
                    cx.ps[bank][:, col:col + 8], lhsT=ohc[b][0:33, ql, :], rhs=rbext[0:33, 0:8], start=True, stop=True),
                    signal=(ql == 31))
            ofree[b] = pe
        dd = None
        for hb in range(2):
            dd = P.op("dve", lambda e, ty=ty, hb=hb: e.tensor_copy(
                out=cx.t5[:, ty, :, hb * 64:(hb + 1) * 64],
                in_=cx.ps[2 + hb].rearrange("p (q h) -> p h q", h=8)), waits=[pe])
            cx.bank_free[2 + hb] = dd
    sb.release(m0)


def attn_loads(cx, kinds, nheads, done_evs):
    pass


def fox_attention(cx, KT, QT, Vs, dq_d, mixT, maskF_d, after_mask=None):
    P, sb = cx.P, cx.sb
    m0 = sb.mark()
    kt = [sb.alloc("kt", [128, S], BF16) for _ in range(2)]
    vt = [sb.alloc("vt", [128, 32, 128], BF16) for _ in range(2)]
    qt = [sb.alloc("qt", [128, NOWN], BF16) for _ in range(2)]
    dqt = [sb.alloc("dqt", [2, NOWN], BF16) for _ in range(2)]
    hsem = [P.sem("fh") for _ in range(2)]
    maskF = sb.alloc("maskF", [128, 8, 512], BF16)
    NPT = 3
    pt = [sb.alloc("pt", [128, 512], BF16) for _ in range(NPT)]
    ptfree = [None] * NPT
    rl = sb.alloc("rl", [128, 512], F32)
    ostg = [sb.alloc("ostg", [128, 512], BF16) for _ in range(2)]
    osem = [P.sem("fo") for _ in range(2)]
    ofree = [None, None]
    msem = P.sem("mk")
    mld = P.dma("pool", msem, lambda e: e.dma_start(out=maskF[:].rearrange("p r q -> p (r q)"), in_=maskF_d))
    if after_mask is not None:
        after_mask()
    hload = {}
    hdone = {}

    def load_head(h):
        sl = h % 2
        w = [hdone.get(h - 2)]
        P.dma("sp", hsem[sl], lambda e: e.dma_start(out=kt[sl][:], in_=KT[h]), waits=w)
        P.dma("sp", hsem[sl], lambda e: e.dma_start(out=vt[sl][:], in_=Vs[h]))
        P.dma("sp", hsem[sl], lambda e: e.dma_start(out=qt[sl][:], in_=QT[h]))
        hload[h] = P.dma("sp", hsem[sl], lambda e: e.dma_start(out=dqt[sl][:], in_=dq_d[h]))

    items = [(h, j, kb) for h in range(16) for j in range(4) for kb in range(8 * j + 8)]
    st_fin = [None]
    LA = 2
    exp_ev = {}
    load_head(0)
    load_head(1)

    def emit_S(t):
        h, j, kb = items[t]
        sl = h % 2
        sbank = t % 4
        P.wait("pe", [hload[h], cx.bank_free[sbank], mld])
        diag = kb >= 8 * j
        P.op("pe", lambda e: e.matmul(cx.ps[sbank], lhsT=kt[sl][:, kb * 128:(kb + 1) * 128],
                                      rhs=qt[sl][:, j * 512:(j + 1) * 512], start=True, stop=False), signal=False)
        pe = P.op("pe", lambda e: e.matmul(cx.ps[sbank], lhsT=cx.ones_b[0:2, :], rhs=dqt[sl][0:2, j * 512:(j + 1) * 512],
                                           start=False, stop=(not diag)), signal=(not diag))
        if diag:
            pe = P.op("pe", lambda e: e.matmul(cx.ps[sbank], lhsT=cx.ident_b[:], rhs=maskF[:, kb - 8 * j, :],
                                               start=False, stop=True))
        p = t % NPT
        ev = P.op("act", lambda e: e.activation(out=pt[p][:], in_=cx.ps[sbank], func=AF.Exp,
                                                bias=cx.biask[:, h, j, kb:kb + 1], scale=SCALE),
                  waits=[pe, ptfree[p]])
        exp_ev[t] = ev
        cx.bank_free[sbank] = ev

    def emit_PV(t):
        h, j, kb = items[t]
        sl = h % 2
        hj = h * 4 + j
        ob = 4 + hj % 2
        lb = 6 + hj % 2
        last = (kb == 8 * j + 7)
        if kb == 0:
            P.wait("pe", [cx.bank_free[ob], cx.bank_free[lb]])
        P.wait("pe", exp_ev[t])
        p = t % NPT
        P.op("pe", lambda e: e.matmul(cx.ps[ob], lhsT=vt[sl][:, kb, :], rhs=pt[p][:], start=(kb == 0), stop=last),
             signal=False)
        pe = P.op("pe", lambda e: e.matmul(cx.ps[lb], lhsT=cx.ones_b[:], rhs=pt[p][:], start=(kb == 0), stop=last))
        ptfree[p] = pe
        if last:
            o = hj % 2
            a1 = P.op("act", lambda e: e.activation(out=rl[:], in_=cx.ps[lb], func=AF.Ln), waits=[pe, st_fin[0]])
            d1 = P.op("act", lambda e: e.activation(out=rl[:], in_=rl[:], func=AF.Exp, scale=-1.0), waits=[a1])
            d2 = P.op("dve", lambda e: e.tensor_tensor(out=ostg[o][:], in0=cx.ps[ob], in1=rl[:], op=ALU.mult),
                      waits=[d1, ofree[o]])
            st_fin[0] = d2
            cx.bank_free[ob] = d2
            cx.bank_free[lb] = d2
            ofree[o] = P.dma("sp", osem[o], lambda e: e.dma_start(out=mixT[h, :, j * 512:(j + 1) * 512], in_=ostg[o][:]),
                             waits=[d2])
            if j == 3:
                hdone[h] = pe
                if h + 2 < 16:
                    load_head(h + 2)

    for t in range(len(items) + LA):
        if t < len(items):
            emit_S(t)
        if t >= LA:
            emit_PV(t - LA)
    sb.release(m0)


def diff_attention(cx, KT, QT, Vs, mixT):
    P, sb = cx.P, cx.sb
    sm = cx.smalls
    m0 = sb.mark()
    kt = [sb.alloc("dkt", [128, 2, S], BF16) for _ in range(2)]
    qt = [sb.alloc("dqt", [128, 2, NOWN], BF16) for _ in range(2)]
    vt = [sb.alloc("dvt", [128, 2, 32, 128], BF16) for _ in range(2)]
    bd = [sb.alloc("bd", [128, 9, 512], BF16) for _ in range(2)]
    tmpb = [sb.alloc("tmpb", [128, 128], F32) for _ in range(2)]
    hsem = [P.sem("dh") for _ in range(2)]
    NPT = 3
    pt = [sb.alloc("dpt", [128, 512], BF16) for _ in range(NPT)]
    ptfree = [None] * NPT
    r1 = sb.alloc("r1", [128, 512], F32)
    r2 = sb.alloc("r2", [128, 512], F32)
    t2 = sb.alloc("t2", [128, 512], F32)
    dfe = [sb.alloc("dfe", [128, 512], F32) for _ in range(2)]
    sqb = [sb.alloc("sqb", [128, 512], BF16) for _ in range(2)]
    lnv = sb.alloc("lnv", [128, 512], F32)
    rstd = sb.alloc("rstdd", [128, 512], F32)
    ostg = [sb.alloc("dostg", [128, 2, 512], BF16) for _ in range(2)]
    osem = [P.sem("do") for _ in range(2)]
    ofree = [None, None]
    hload = {}
    hdone = {}
    bdready = {}
    st = {"sc": 0, "fin": None, "tb": 0}

    def load_head(h):
        sl = h % 2
        w = [hdone.get(h - 2)]
        P.dma("sp", hsem[sl], lambda e: e.dma_start(out=kt[sl][:], in_=KT[16 + 2 * h:18 + 2 * h].rearrange("c p t -> p c t")),
              waits=w)
        P.dma("sp", hsem[sl], lambda e: e.dma_start(out=vt[sl][:], in_=Vs[16 + 2 * h:18 + 2 * h].rearrange("c p k d -> p c k d")))
        hload[h] = P.dma("sp", hsem[sl], lambda e: e.dma_start(
            out=qt[sl][:], in_=QT[16 + 2 * h:18 + 2 * h].rearrange("c p t -> p c t")))
        def base(t):
            if t < -1:
                return cx.t5[:, 2, h, :]
            if t == -1:
                return cx.t5[:, 1, h, :]
            if t == 0:
                return cx.t5[:, 0, h, :]
            return cx.t5[:, 3, 0, :]
        ev = None
        for r in range(-1, 8):
            for i in range(4):
                tb = st["tb"] % 2
                st["tb"] += 1
                b0 = base(r - 2 * i)
                b1 = base(r - 2 * i - 1)
                ea = P.op("dve", lambda e, tb=tb, b0=b0: e.tensor_scalar(out=tmpb[tb][:], in0=b0, scalar1=cx.par[:, 0:1],
                                                                        scalar2=None, op0=ALU.mult), waits=w)
                ev = P.op("dve", lambda e, tb=tb, b1=b1, r=r, i=i: e.scalar_tensor_tensor(
                    out=bd[sl][:, r + 1, i * 128:(i + 1) * 128], in0=b1, scalar=cx.par[:, 1:2], in1=tmpb[tb][:],
                    op0=ALU.mult, op1=ALU.add), waits=[ea])
        bdready[h] = ev

    items = [(h, j, c, kb) for h in range(8) for j in range(4) for c in range(2) for kb in range(8 * j + 8)]
    LA = 1
    DEFER = 10
    exp_ev = {}
    dfa = [sb.alloc("dfa", [128, 512], F32) for _ in range(2)]
    load_head(0)
    load_head(1)
    st.update(r1free=None, r2free=None, sqfree=None, pend=None, since=0)

    def emit_S(t):
        h, j, c, kb = items[t]
        sl = h % 2
        sbank = st["sc"] % 2
        st["sc"] += 1
        diag = kb >= 8 * j - 1
        P.wait("pe", [hload[h], cx.bank_free[sbank]])
        pe = P.op("pe", lambda e: e.matmul(cx.ps[sbank], lhsT=kt[sl][:, c, kb * 128:(kb + 1) * 128],
                                           rhs=qt[sl][:, c, j * 512:(j + 1) * 512], start=True, stop=(not diag)),
                  signal=(not diag))
        if diag:
            P.wait("pe", bdready[h])
            pe = P.op("pe", lambda e: e.matmul(cx.ps[sbank], lhsT=cx.ident_b[:], rhs=bd[sl][:, kb - 8 * j + 1, :],
                                               start=False, stop=True))
        p = t % NPT
        bias = sm[:, NS_RB15 + h:NS_RB15 + h + 1] if not diag else cx.eps6[:, 2:3]
        ev = P.op("act", lambda e: e.activation(out=pt[p][:], in_=cx.ps[sbank], func=AF.Exp, bias=bias, scale=SCALE),
                  waits=[pe, ptfree[p]])
        exp_ev[t] = ev
        cx.bank_free[sbank] = ev

    def emit_ss():
        h, j, sq_evs = st["pend"]
        st["pend"] = None
        o = (h * 4 + j) % 2
        sbank = st["sc"] % 2
        st["sc"] += 1
        P.wait("pe", [cx.bank_free[sbank]] + sq_evs)
        P.op("pe", lambda e: e.matmul(cx.ps[sbank], lhsT=cx.ones_b[:], rhs=sqb[0][:], start=True, stop=False), signal=False)
        pss = P.op("pe", lambda e: e.matmul(cx.ps[sbank], lhsT=cx.ones_b[:], rhs=sqb[1][:], start=False, stop=True))
        st["sqfree"] = pss
        a1 = P.op("act", lambda e: e.activation(out=lnv[:], in_=cx.ps[sbank], func=AF.Ln, bias=cx.eps6[:, 1:2],
                                                scale=1.0 / 256.0), waits=[pss, st["fin"]])
        cx.bank_free[sbank] = a1
        a2 = P.op("act", lambda e: e.activation(out=rstd[:], in_=lnv[:], func=AF.Exp, scale=-0.5), waits=[a1])
        fin = None
        for e_ in range(2):
            fin = P.op("dve", lambda e, e_=e_: e.scalar_tensor_tensor(
                out=ostg[o][:, e_, :], in0=dfe[e_][:], scalar=cx.gsub8[:, e_:e_ + 1], in1=rstd[:],
                op0=ALU.mult, op1=ALU.mult), waits=[a2, ofree[o]])
        st["fin"] = fin
        ofree[o] = P.dma("sp", osem[o], lambda e: e.dma_start(
            out=mixT[16 + 2 * h:18 + 2 * h, :, j * 512:(j + 1) * 512].rearrange("c p t -> p c t"), in_=ostg[o][:]),
            waits=[fin])
        if j == 3:
            hdone[h] = pss
            if h + 2 < 8:
                load_head(h + 2)

    def emit_PV(t):
        h, j, c, kb = items[t]
        sl = h % 2
        u = (h * 4 + j) * 2 + c
        sset = u % 2
        ob = 2 + 3 * sset
        lb = 4 + 3 * sset
        last = (kb == 8 * j + 7)
        if kb == 0:
            P.wait("pe", [cx.bank_free[ob], cx.bank_free[ob + 1], cx.bank_free[lb]])
        P.wait("pe", exp_ev[t])
        p = t % NPT
        for e_ in range(2):
            P.op("pe", lambda e, e_=e_: e.matmul(cx.ps[ob + e_], lhsT=vt[sl][:, e_, kb, :], rhs=pt[p][:],
                                                 start=(kb == 0), stop=last), signal=False)
        pe = P.op("pe", lambda e: e.matmul(cx.ps[lb], lhsT=cx.ones_b[:], rhs=pt[p][:], start=(kb == 0), stop=last))
        ptfree[p] = pe
        st["since"] += 1
        if st["pend"] is not None and st["since"] >= DEFER:
            emit_ss()
        if last and c == 0:
            a1 = P.op("act", lambda e: e.activation(out=r1[:], in_=cx.ps[lb], func=AF.Ln), waits=[pe, st["r1free"]])
            a2 = P.op("act", lambda e: e.activation(out=r1[:], in_=r1[:], func=AF.Exp, scale=-1.0), waits=[a1])
            dl = None
            for e_ in range(2):
                dl = P.op("dve", lambda e, e_=e_: e.tensor_tensor(out=dfa[e_][:], in0=cx.ps[ob + e_], in1=r1[:], op=ALU.mult),
                          waits=[a2])
            st["r1free"] = dl
            for b in (ob, ob + 1, lb):
                cx.bank_free[b] = dl
        if last and c == 1:
            if st["pend"] is not None:
                emit_ss()
            a1 = P.op("act", lambda e: e.activation(out=r2[:], in_=cx.ps[lb], func=AF.Ln), waits=[pe, st["r2free"]])
            a2 = P.op("act", lambda e: e.activation(out=r2[:], in_=r2[:], func=AF.Exp, scale=-1.0), waits=[a1])
            sq_evs = []
            dl = None
            for e_ in range(2):
                d5 = P.op("dve", lambda e, e_=e_: e.tensor_tensor(out=t2[:], in0=cx.ps[ob + e_], in1=r2[:], op=ALU.mult),
                          waits=[a2])
                d6 = P.op("dve", lambda e, e_=e_: e.scalar_tensor_tensor(
                    out=dfe[e_][:], in0=t2[:], scalar=cx.lam[:, 1:2], in1=dfa[e_][:], op0=ALU.mult, op1=ALU.add),
                    waits=[d5, st["fin"]])
                d7 = P.op("dve", lambda e, e_=e_: e.tensor_tensor(out=sqb[e_][:], in0=dfe[e_][:], in1=dfe[e_][:], op=ALU.mult),
                          waits=[d6, st["sqfree"]])
                sq_evs.append(d7)
                dl = d5
            st["r2free"] = dl
            for b in (ob, ob + 1, lb):
                cx.bank_free[b] = dl
            st["pend"] = (h, j, sq_evs)
            st["since"] = 0

    for t in range(len(items) + LA):
        if t < len(items):
            emit_S(t)
        if t >= LA:
            emit_PV(t - LA)
    if st["pend"] is not None:
        emit_ss()
    sb.release(m0)


def cross_attention(cx, qmT, kmT, VM, omT):
    P, sb = cx.P, cx.sb
    m0 = sb.mark()
    kmt = sb.alloc("kmt", [128, KC, NMEM], BF16)
    vmt = sb.alloc("vmt", [128, KC, 2, 128], BF16)
    qm = [sb.alloc("qm", [128, KC, 512], BF16) for _ in range(2)]
    qsem = [P.sem("cq") for _ in range(2)]
    qfree = [None, None]
    ksem = P.sem("ck")
    P.dma("sp", ksem, lambda e: e.dma_start(out=kmt[:], in_=kmT.rearrange("k p t -> p k t")))
    kld = P.dma("sp", ksem, lambda e: e.dma_start(out=vmt[:], in_=VM.rearrange("c p k d -> p c k d")))
    ptm = [sb.alloc("ptm", [128, 512], BF16) for _ in range(4)]
    ptfree = [None] * 4
    rl = sb.alloc("crl", [128, 512], F32)
    ostg = [sb.alloc("costg", [128, 8, 512], BF16) for _ in range(2)]
    osem = [P.sem("co") for _ in range(2)]
    ofree = [None, None]
    st = {"b": 0}

    def nbank():
        b = st["b"] % 8
        st["b"] += 1
        return b

    qld = {}

    def loadq(t):
        b = t % 2
        qld[t] = P.dma("sp", qsem[b], lambda e: e.dma_start(
            out=qm[b][:], in_=qmT[:, :, t * 512:(t + 1) * 512].rearrange("k p t -> p k t")), waits=[qfree[b]])

    loadq(0)
    n = 0
    rl_free = None
    for t in range(4):
        if t + 1 < 4:
            loadq(t + 1)
        qb = t % 2
        for hm in range(4):
            o = n % 2
            pts = []
            for mb in range(2):
                bk = nbank()
                P.wait("pe", [qld[t], kld, cx.bank_free[bk]])
                for ch in range(8):
                    pe = P.op("pe", lambda e, ch=ch, mb=mb, bk=bk, hm=hm, qb=qb: e.matmul(
                        cx.ps[bk], lhsT=kmt[:, 8 * hm + ch, mb * 128:(mb + 1) * 128], rhs=qm[qb][:, 8 * hm + ch, :],
                        start=(ch == 0), stop=(ch == 7)), signal=(ch == 7))
                p = (2 * n + mb) % 4
                ev = P.op("act", lambda e, p=p, bk=bk: e.activation(out=ptm[p][:], in_=cx.ps[bk], func=AF.Exp, scale=SCALE_M),
                          waits=[pe, ptfree[p]])
                cx.bank_free[bk] = ev
                pts.append((p, ev))
            if hm == 3:
                qfree[qb] = pe
            bl = nbank()
            P.wait("pe", [cx.bank_free[bl], pts[0][1], pts[1][1]])
            P.op("pe", lambda e, bl=bl, p=pts[0][0]: e.matmul(cx.ps[bl], lhsT=cx.ones_b[:], rhs=ptm[p][:], start=True, stop=False),
                 signal=False)
            pl = P.op("pe", lambda e, bl=bl, p=pts[1][0]: e.matmul(cx.ps[bl], lhsT=cx.ones_b[:], rhs=ptm[p][:], start=False, stop=True))
            d1 = P.op("dve", lambda e, bl=bl: e.reciprocal(out=rl[:], in_=cx.ps[bl]), waits=[pl, rl_free])
            cx.bank_free[bl] = d1
            dlast = None
            for e_ in range(8):
                bo = nbank()
                P.wait("pe", [cx.bank_free[bo]])
                P.op("pe", lambda e, bo=bo, e_=e_, p=pts[0][0], hm=hm: e.matmul(cx.ps[bo], lhsT=vmt[:, 8 * hm + e_, 0, :], rhs=ptm[p][:],
                                                                       start=True, stop=False), signal=False)
                po = P.op("pe", lambda e, bo=bo, e_=e_, p=pts[1][0], hm=hm: e.matmul(cx.ps[bo], lhsT=vmt[:, 8 * hm + e_, 1, :], rhs=ptm[p][:],
                                                                            start=False, stop=True))
                dlast = P.op("dve", lambda e, bo=bo, e_=e_, o=o: e.tensor_tensor(out=ostg[o][:, e_, :], in0=cx.ps[bo], in1=rl[:],
                                                                                 op=ALU.mult), waits=[po, d1, ofree[o]])
                cx.bank_free[bo] = dlast
            ptfree[pts[0][0]] = po
            ptfree[pts[1][0]] = po
            rl_free = dlast
            ofree[o] = P.dma("sp", osem[o], lambda e, o=o, hm=hm, t=t: e.dma_start(
                out=omT[8 * hm:8 * hm + 8, :, t * 512:(t + 1) * 512].rearrange("c p t -> p c t"), in_=ostg[o][:]),
                waits=[dlast])
            n += 1
    sb.release(m0)


NS_G, NS_GSUB, NS_PAR, NS_BF, NS_RB, NS_RB15, NS_LAM, NS_ID, NS_E127 = 0, 160, 162, 164, 165, 173, 181, 693, 821
NS = 949


def build(upto=99, debug=()):
    nc = bass.Bass("TRN2", target_bir_lowering=False)
    cx = Ctx()
    cx.nc = nc
    cx.P = P = Prog(nc)
    cx.sb = sb = SBAlloc(nc)
    cx.phase_ev = None
    cx.inputs = []
    cx.outputs = []

    def inp(name, shape, dt=F32):
        cx.inputs.append(name)
        return nc.dram_tensor(name, list(shape), dt, kind="ExternalInput").ap()

    def outp(name, shape, dt=F32):
        cx.outputs.append(name)
        return nc.dram_tensor(name, list(shape), dt, kind="ExternalOutput").ap()

    def scratch(name, shape, dt):
        return nc.dram_tensor(name, list(shape), dt).ap()

    psum = cx.es_ps = P.es.enter_context(nc.psum_tensor("ps", [128, 8, 512], F32))
    cx.ps = [psum[:, b, :] for b in range(8)]
    cx.bank_free = [None] * 8

    smalls_d = inp("smalls", [128, NS])
    smalls = sb.alloc("smalls", [128, NS], F32)
    cx.gvec = smalls[:, NS_G:NS_G + 160].rearrange("p (g k) -> p g k", g=5)
    cx.ident_f = smalls[:, NS_ID:NS_ID + 128]
    cx.e127 = smalls[:, NS_E127:NS_E127 + 128]
    cx.par = smalls[:, NS_PAR:NS_PAR + 2]
    cx.smalls = smalls
    cx.ones_b = sb.alloc("ones_b", [128, 128], BF16)
    cx.ident_b = sb.alloc("ident_b", [128, 128], BF16)
    cx.eps6 = sb.alloc("eps6", [128, 4], F32)
    csem = P.sem("const")
    ld = P.dma("sp", csem, lambda e: e.dma_start(out=smalls[:], in_=smalls_d))
    P.op("dve", lambda e: e.memset(cx.ones_b[:], 1.0))
    P.op("dve", lambda e: e.memset(cx.eps6[:, 0:1], 1e-6))
    P.op("dve", lambda e: e.memset(cx.eps6[:, 1:2], 1e-5))
    P.op("dve", lambda e: e.memset(cx.eps6[:, 2:4], 0.0))
    P.op("dve", lambda e: e.tensor_copy(out=cx.ident_b[:], in_=cx.ident_f), waits=[ld])
    P.barrier()

    dbg = {}

    def finish():
        P.barrier()
        dsem = P.sem("dbg")
        for name, (ap, shape, dt) in dbg.items():
            if name in debug:
                o = outp("dbg_" + name, shape, dt)
                P.dma("sp", dsem, lambda e, o=o, a=ap: e.dma_start(out=o, in_=a))
        P.barrier()
        P.emit()
        return nc, cx

    xa = inp("xa", [S // 256, 128, KC, 256])
    xo = inp("xo", [KC, 128, NOWN])
    xot = inp("xot", [NOWN // 256, 128, KC, 256])
    aT_all = scratch("aT_all", [KC, 128, S], BF16)
    aT_own = scratch("aT_own", [KC, 128, NOWN], BF16)
    norm_pass(cx, xa, aT_all, S, 0, BF16, src_tiled=True)
    norm_pass(cx, xot, aT_own, NOWN, 0, BF16, src_tiled=True)
    P.barrier()
    if upto <= 1:
        return finish()

    w_in = inp("w_in", [D, 12304])
    KT = scratch("KT", [32, 128, S], BF16)
    QT = scratch("QT", [32, 128, NOWN], BF16)
    Vs = scratch("Vs", [32, 128, 32, 128], BF16)
    dbg["KT"] = (KT, [32, 128, S], BF16)
    dbg["QT"] = (QT, [32, 128, NOWN], BF16)
    dbg["Vs"] = (Vs, [32, 128, 32, 128], BF16)
    attn_mark = sb.mark()
    sigT = sb.alloc("sigT", [16, S], F32)
    cx.sigT = sigT

    def w_in_fn(k0, kn, c0, cn):
        return w_in[k0 * 128:(k0 + kn) * 128, c0:c0 + cn]

    def kt_dest(base):
        def f(tt, ci, cb):
            h0 = base + (cb["c0"] - cb["cbase"]) // 128
            return KT[h0:h0 + 4, :, tt * 1024:(tt + 1) * 1024].rearrange("h p t -> p h t")
        return f

    def v_dest(base):
        def f(tt, ci, cb):
            h0 = base + (cb["c0"] - cb["cbase"]) // 128
            return Vs[h0:h0 + 4, :, tt * 8:(tt + 1) * 8, :].rearrange("h p k d -> p h (k d)")
        return f

    def q_dest(base):
        def f(tt, ci, cb):
            h0 = base + (cb["c0"] - cb["cbase"]) // 128
            return QT[h0:h0 + 4, :, tt * 1024:(tt + 1) * 1024].rearrange("h p t -> p h t")
        return f

    ekf = EpiCopyFM(lambda tt, ci, cb: kt_dest(cb["hb"])(tt, ci, cb))
    ekd = ekf
    evf = EpiCopyTM(lambda tt, ci, cb: v_dest(cb["hb"])(tt, ci, cb))
    evd = evf
    esg = EpiSig(sigT, smalls[0:16, NS_BF:NS_BF + 1])
    cbs = []
    for c in range(4):
        cbs.append(dict(c0=C_KF + 512 * c, cn=512, cbase=C_KF, hb=0, mode="fm", epi=ekf))
    for c in range(4):
        cbs.append(dict(c0=C_VF + 512 * c, cn=512, cbase=C_VF, hb=0, mode="tm", epi=evf))
    cbs.append(dict(c0=C_F, cn=16, cbase=C_F, hb=0, mode="fm", epi=esg))
    for c in range(4):
        cbs.append(dict(c0=C_KD + 512 * c, cn=512, cbase=C_KD, hb=16, mode="fm", epi=ekd))
    for c in range(4):
        cbs.append(dict(c0=C_VD + 512 * c, cn=512, cbase=C_VD, hb=16, mode="tm", epi=evd))
    if upto == 2:
        cbs = [cbs[0], cbs[4], cbs[8], cbs[9], cbs[13]]
    w_dn0 = inp("w_dn0", [DFF // 2, D])
    w_dn1 = inp("w_dn1", [DFF // 2, D])
    wdn_bf = scratch("wdn_bf", [DFF, D], BF16)
    pcsem = P.sem("precast")
    bg = []
    for r in range(128):
        wsrc = (w_dn0 if r < 64 else w_dn1)[(r % 64) * 128:(r % 64 + 1) * 128, :]
        bg.append(lambda d=wdn_bf[r * 128:(r + 1) * 128, :], s_=wsrc: P.dma(
            "pool", pcsem, lambda e, d=d, s_=s_: e.dma_start(out=d, in_=s_)))
    gemm(cx, X=aT_all, kc=KC, wfn=w_in_fn, ntok=S, colblocks=cbs, x_stream=False, bg=bg)
    while bg:
        bg.pop(0)()
    P.barrier()
    if upto <= 2:
        return finish()
    eqf = EpiCopyFM(lambda tt, ci, cb: q_dest(cb["hb"])(tt, ci, cb))
    eqd = eqf
    cbs = []
    for c in range(4):
        cbs.append(dict(c0=C_QF + 512 * c, cn=512, cbase=C_QF, hb=0, mode="fm", epi=eqf))
    for c in range(4):
        cbs.append(dict(c0=C_QD + 512 * c, cn=512, cbase=C_QD, hb=16, mode="fm", epi=eqd))
    gemm(cx, X=aT_own, kc=KC, wfn=w_in_fn, ntok=NOWN, colblocks=cbs, x_stream=False)
    P.barrier()
    if upto <= 3:
        return finish()

    dq_d = scratch("dq_d", [16, 2, NOWN], BF16)
    oh_d = inp("oh", [33, 2 * 128 * 128])
    maskF_d = inp("maskF", [128, 8 * 512])
    mixT = scratch("mixT", [KC, 128, NOWN], BF16)
    dbg["mixT"] = (mixT, [KC, 128, NOWN], BF16)
    dbg["dq_d"] = (dq_d, [16, 2, NOWN], BF16)
    attn_prep(cx, dq_d, oh_d)
    P.barrier()
    if upto <= 4:
        return finish()
    fox_attention(cx, KT, QT, Vs, dq_d, mixT, maskF_d)
    P.barrier()
    if upto <= 5:
        return finish()
    diff_attention(cx, KT, QT, Vs, mixT)
    P.barrier()
    sb.release(attn_mark)
    if upto <= 6:
        return finish()

    def fm_dest(Y, Tt=1024):
        def f(tt, ci, cb):
            h0 = cb["c0"] // 128
            n = (cb["cn"] + 127) // 128
            return Y[h0:h0 + n, :, tt * Tt:(tt + 1) * Tt].rearrange("h p t -> p h t")
        return f

    def simple_w(w):
        def f(k0, kn, c0, cn):
            return w[k0 * 128:(k0 + kn) * 128, c0:c0 + cn]
        return f

    def cblocks(n, epi, mode="fm"):
        return [dict(c0=512 * c, cn=512, cbase=0, hb=0, mode=mode, epi=epi) for c in range(n)]

    cx.acc = sb.alloc("acc", [128, NOWN], F32)
    cx.rstd_a = sb.alloc("rstd_a", [128, NOWN], F32)
    cx.r2_a = sb.alloc("r2_a", [128, NOWN], F32)
    cx.ones_f = sb.alloc("ones_f2", [128, 128], F32)
    P.op("dve", lambda e: e.memset(cx.ones_f[:], 1.0))
    P.barrier()
    w_out = inp("w_out", [D, D])
    h1T = scratch("h1T", [KC, 128, NOWN], F32)
    dbg["h1T"] = (h1T, [KC, 128, NOWN], F32)
    cT_d = scratch("cT_d", [KC, 128, NOWN], BF16)
    gemm(cx, X=mixT, kc=KC, wfn=simple_w(w_out), ntok=NOWN,
         colblocks=cblocks(8, EpiResid(fm_dest(xo), fm_dest(h1T), norm=dict(gi=1, hg_dest=fm_dest(cT_d), ncb=8))),
         x_stream=False)
    P.barrier()
    if upto <= 7:
        return finish()
    memT = inp("memT", [KC, 128, NMEM])
    mT_d = scratch("mT_d", [KC, 128, NMEM], BF16)
    norm_pass(cx, memT, mT_d, NMEM, 2, BF16)
    P.barrier()
    wk = inp("wk_mem", [D, D])
    wv = inp("wv_mem", [D, D])
    kmT = scratch("kmT", [KC, 128, NMEM], BF16)
    VM = scratch("VM", [KC, 128, 2, 128], BF16)
    def vm_dest(tt, ci, cb):
        h0 = cb["c0"] // 128
        return VM[h0:h0 + 4, :, :, :].rearrange("h p k d -> p h (k d)")

    def wkv_fn(k0, kn, c0, cn):
        w, c = (wk, c0) if c0 < D else (wv, c0 - D)
        return w[k0 * 128:(k0 + kn) * 128, c:c + cn]
    ekm = EpiCopyFM(fm_dest(kmT, 256))
    evm = EpiCopyTM(lambda tt, ci, cb: vm_dest(tt, ci, dict(cb, c0=cb["c0"] - D)))
    cbs = [dict(c0=512 * c, cn=512, cbase=0, hb=0, mode="fm", epi=ekm) for c in range(8)]
    cbs += [dict(c0=D + 512 * c, cn=512, cbase=0, hb=0, mode="tm", epi=evm) for c in range(8)]
    gemm(cx, X=mT_d, kc=KC, wfn=wkv_fn, ntok=NMEM, colblocks=cbs, x_stream=False, Tt=256)
    P.barrier()
    wq = inp("wq_mem", [D, D])
    qmT = scratch("qmT", [KC, 128, NOWN], BF16)
    gemm(cx, X=cT_d, kc=KC, wfn=simple_w(wq), ntok=NOWN, colblocks=cblocks(8, EpiCopyFM(fm_dest(qmT), rstd=cx.rstd_a)),
         x_stream=False)
    P.barrier()
    omT = scratch("omT", [KC, 128, NOWN], BF16)
    dbg["omT"] = (omT, [KC, 128, NOWN], BF16)
    cross_attention(cx, qmT, kmT, VM, omT)
    P.barrier()
    if upto <= 11:
        return finish()
    wo = inp("wo_mem", [D, D])
    h2T = scratch("h2T", [KC, 128, NOWN], F32)
    dbg["h2T"] = (h2T, [KC, 128, NOWN], F32)
    nT_d = scratch("nT_d", [KC, 128, NOWN], BF16)
    gemm(cx, X=omT, kc=KC, wfn=simple_w(wo), ntok=NOWN,
         colblocks=cblocks(8, EpiResid(fm_dest(h1T), fm_dest(h2T), norm=dict(gi=3, hg_dest=fm_dest(nT_d), ncb=8))),
         x_stream=False)
    P.barrier()
    if upto <= 12:
        return finish()
    w_up0 = inp("w_up0", [D, DFF // 2])
    w_up1 = inp("w_up1", [D, DFF // 2])
    actT = scratch("actT", [DFF // 128, 128, NOWN], BF16)

    def wup_fn(k0, kn, c0, cn):
        w, c = (w_up0, c0) if c0 < DFF // 2 else (w_up1, c0 - DFF // 2)
        return w[k0 * 128:(k0 + kn) * 128, c:c + cn]
    gemm(cx, X=nT_d, kc=KC, wfn=wup_fn, ntok=NOWN, colblocks=cblocks(32, EpiRelu2(fm_dest(actT), r2=cx.r2_a)), x_stream=False)
    P.barrier()
    h3T = scratch("h3T", [KC, 128, NOWN], F32)
    dbg["h3T"] = (h3T, [KC, 128, NOWN], F32)

    def wdn_fn(k0, kn, c0, cn):
        return wdn_bf[k0 * 128:(k0 + kn) * 128, c0:c0 + cn]
    gemm(cx, X=actT, kc=DFF // 128, wfn=wdn_fn, ntok=NOWN,
         colblocks=cblocks(8, EpiResid(fm_dest(h2T), fm_dest(h3T), norm=dict(gi=4, hg_dest=None, ncb=8))),
         x_stream=True, NXS=2)
    P.barrier()
    outT = outp("outT", [NOWN // 256, 128, KC, 256])
    norm_pass(cx, h3T, outT, NOWN, 4, F32, TT=256, rstd_src=cx.rstd_a, dst_tiled=True)
    return finish()


def own_tokens(qh):
    blocks = [8 * j + 2 * i + qh for j in range(4) for i in range(4)]
    return np.concatenate([np.arange(bk * 128, (bk + 1) * 128) for bk in blocks])


def t5_bucket_np(rel):
    rel = np.asarray(rel, np.int32)
    ret = np.where(rel > 0, 16, 0).astype(np.int32)
    n = np.abs(rel)
    nf = np.maximum(n, 1).astype(np.float32)
    large = 8 + (np.log(nf / np.float32(8)) / np.float32(math.log(128 / 8)) * np.float32(8)).astype(np.int32)
    large = np.minimum(large, 15)
    return ret + np.where(n < 8, n, large)


def fox_mask_table(qh):
    m = np.zeros((128, 8, 4, 128), np.float32)
    kk = np.arange(128)[:, None]
    qq = np.arange(128)[None, :]
    diag = np.where(kk <= qq, 0.0, NEG).astype(np.float32)
    for r in range(8):
        for i in range(4):
            t = r - (2 * i + qh)
            if t == 0:
                m[:, r, i, :] = diag
            elif t > 0:
                m[:, r, i, :] = NEG
    return m.reshape(128, 8 * 512)


def t5_onehot():
    oh = np.zeros((33, 2, 128, 128), np.float32)
    q = np.arange(128)[:, None]
    k = np.arange(128)[None, :]
    bd = t5_bucket_np(k - q)
    allowed = (k // 64) <= (q // 64)
    bd = np.where(allowed, bd, 32)
    bn = t5_bucket_np(k - q - 128)
    for b in range(33):
        oh[b, 0] = (bd == b)
        oh[b, 1] = (bn == b)
    return oh.reshape(33, 2 * 128 * 128)


def prep_smalls(inp, qh):
    s = np.zeros((128, NS), np.float32)
    gs = [inp["g_mix"][0], inp["g_cross"][0], inp["g_mem"][0], inp["g_mlp"][0], inp["g_final"]]
    for gi, g in enumerate(gs):
        s[:, NS_G + gi * 32:NS_G + (gi + 1) * 32] = np.asarray(g, np.float32).reshape(32, 128).T
    s[:, NS_GSUB:NS_GSUB + 2] = np.asarray(inp["g_subln"][0], np.float32).reshape(2, 128).T
    s[:, NS_PAR + qh] = 1.0
    s[0:16, NS_BF] = np.asarray(inp["b_forget"][0], np.float32)
    s[0:32, NS_RB:NS_RB + 8] = np.asarray(inp["rel_bias"], np.float32)
    s[:, NS_RB15:NS_RB15 + 8] = np.asarray(inp["rel_bias"], np.float32)[15][None, :]
    lam = np.stack([inp["lambda_q1"][0], inp["lambda_k1"][0], inp["lambda_q2"][0], inp["lambda_k2"][0]])
    s[:, NS_LAM:NS_LAM + 512] = np.asarray(lam, np.float32).reshape(1, 512)
    s[:, NS_ID:NS_ID + 128] = np.eye(128, dtype=np.float32)
    s[127, NS_E127:NS_E127 + 128] = 1.0
    return s


def prep_core(inp, c, names):
    b, qh = c // 2, c % 2
    m = {}
    xT = None
    if "xa" in names or "xo" in names or "xot" in names:
        xT = np.ascontiguousarray(np.asarray(inp["x"][b], np.float32).T).reshape(KC, 128, S)
    if "xa" in names:
        m["xa"] = xT
    if "xo" in names:
        m["xo"] = np.ascontiguousarray(xT[:, :, own_tokens(qh)])
    if "xot" in names:
        m["xot"] = np.ascontiguousarray(m["xo"].reshape(KC, 128, NOWN // 256, 256).transpose(2, 1, 0, 3))
    if "xa" in names:
        m["xa"] = np.ascontiguousarray(xT.reshape(KC, 128, S // 256, 256).transpose(2, 1, 0, 3))
    if "memT" in names:
        m["memT"] = np.ascontiguousarray(np.asarray(inp["mem"][b], np.float32).T).reshape(KC, 128, NMEM)
    if "smalls" in names:
        m["smalls"] = prep_smalls(inp, qh)
    if "maskF" in names:
        m["maskF"] = fox_mask_table(qh)
    if "oh" in names:
        m["oh"] = t5_onehot()
    return m


def shared_inputs(inp, names):
    m = {}
    for nm in ("w_in", "w_out", "wq_mem", "wk_mem", "wv_mem", "wo_mem"):
        if nm in names:
            m[nm] = np.asarray(inp[nm][0], np.float32)
    if "w_up0" in names:
        w = np.asarray(inp["w_up"][0], np.float32)
        m["w_up0"] = np.ascontiguousarray(w[:, :DFF // 2])
        m["w_up1"] = np.ascontiguousarray(w[:, DFF // 2:])
    if "w_dn0" in names:
        w = np.asarray(inp["w_down"][0], np.float32)
        m["w_dn0"] = w[:DFF // 2]
        m["w_dn1"] = w[DFF // 2:]
    return m


def kernel(**inputs):
    nc, cx = build()
    sh = shared_inputs(inputs, cx.inputs)
    in_maps = []
    for c in range(8):
        m = prep_core(inputs, c, cx.inputs)
        m.update(sh)
        in_maps.append(m)
    res = run_bass_kernel_spmd(nc, in_maps, core_ids=list(range(8)))
    out = np.empty((4, S, D), np.float32)
    for c in range(8):
        b, qh = c // 2, c % 2
        o = np.asarray(res.results[c]["outT"], np.float32).reshape(NOWN // 256, 128, KC, 256)
        out[b, own_tokens(qh), :] = o.transpose(0, 3, 2, 1).reshape(NOWN, D)
    return out
```

```python
import math
from contextlib import ExitStack
import numpy as np
import concourse.bass as bass
import concourse.mybir as mybir
from concourse.bass_utils import run_bass_kernel_spmd

F32 = mybir.dt.float32
BF16 = mybir.dt.bfloat16
AF = mybir.ActivationFunctionType
ALU = mybir.AluOpType
AXX = mybir.AxisListType.X

D = 4096
S = 4096
NOWN = 2048
DFF = 16384
NMEM = 256
KC = 32
SCALE = 128.0 ** -0.5
SCALE_M = 1024.0 ** -0.5
NEG = -30000.0
LAMBDA_INIT = 0.8 - 0.6 * math.exp(0.0)
C_QF, C_KF, C_VF, C_F, C_QD, C_KD, C_VD = 0, 2048, 4096, 6144, 6160, 8208, 10256
SB_BASE = 20480
SB_LIMIT = 229376


class Ev:
    __slots__ = ("sem", "val")

    def __init__(self, sem, val):
        self.sem = sem
        self.val = val


class Sem:
    def __init__(self, h, name):
        self.h = h
        self.name = name
        self.n = 0


class Prog:
    def __init__(self, nc):
        self.nc = nc
        self.es = ExitStack()
        self.q = {e: [] for e in ("pe", "act", "dve", "pool", "sp")}
        self.waited = {}
        self.nsem = 0
        self.prog = {e: self.sem("prog_" + e) for e in ("pe", "act", "dve", "pool")}

    def sem(self, name):
        if not hasattr(self, "allsems"):
            self.allsems = []
            self.pool = []
            self.inuse = []
        if name.startswith("prog_"):
            self.nsem += 1
            name = f"{name}_{self.nsem}"
            sm = Sem(self.es.enter_context(self.nc.semaphore(name)), name)
            self.allsems.append(sm)
            return sm
        if self.pool:
            sm = self.pool.pop()
        else:
            self.nsem += 1
            name = f"{name}_{self.nsem}"
            sm = Sem(self.es.enter_context(self.nc.semaphore(name)), name)
            self.allsems.append(sm)
        self.inuse.append(sm)
        return sm

    def wait(self, eng, ev):
        if ev is None:
            return
        if isinstance(ev, (list, tuple)):
            for e in ev:
                self.wait(eng, e)
            return
        k = (eng, ev.sem.name)
        if self.waited.get(k, 0) >= ev.val:
            return
        self.waited[k] = ev.val
        self.q[eng].append(("w", ev.sem, ev.val))

    def op(self, eng, fn, waits=(), signal=True):
        self.wait(eng, waits)
        if signal:
            s = self.prog[eng]
            s.n += 1
            self.q[eng].append(("o", fn, s, 1))
            return Ev(s, s.n)
        self.q[eng].append(("o", fn, None, 0))
        return None

    def dma(self, eng, sem, fn, waits=()):
        self.wait(eng, waits)
        sem.n += 16
        self.q[eng].append(("o", fn, sem, 16))
        return Ev(sem, sem.n)

    def barrier(self):
        if not hasattr(self, "allsems"):
            self.allsems = []
        evs = [Ev(sm, sm.n) for sm in self.allsems if sm.n > 0]
        for eng in self.q:
            self.wait(eng, evs)
        self.pool.extend(self.inuse)
        self.inuse = []

    def emit(self):
        with self.nc.Block() as block:
            def mk(eng):
                def f(e):
                    for it in self.q[eng]:
                        if it[0] == "w":
                            e.wait_ge(it[1].h, it[2])
                        else:
                            ins = it[1](e)
                            if it[2] is not None:
                                ins.then_inc(it[2].h, it[3])
                return f
            block.tensor(mk("pe"))
            block.scalar(mk("act"))
            block.vector(mk("dve"))
            block.gpsimd(mk("pool"))
            block.sync(mk("sp"))


class SBAlloc:
    def __init__(self, nc):
        self.nc = nc
        self.off = SB_BASE
        self.cnt = 0

    def alloc(self, name, shape, dtype):
        self.cnt += 1
        nbytes = int(np.prod(shape[1:])) * (2 if dtype == BF16 else 4)
        nbytes = (nbytes + 63) // 64 * 64
        off = self.off
        assert off + nbytes <= SB_LIMIT, f"SBUF overflow at {name}: {off}+{nbytes}"
        self.off += nbytes
        return self.nc.alloc_sbuf_tensor_at(f"{name}_{self.cnt}", list(shape), dtype, offset=off)

    def mark(self):
        return self.off

    def release(self, m):
        self.off = m


class Ctx:
    pass


def gemm(cx, *, X, kc, wfn, ntok, colblocks, x_stream, Tt=1024, KP=16, bg=None, NXS=3, x_tiled=False):
    P, sb = cx.P, cx.sb
    m0 = sb.mark()
    NW = 3
    Wt = [sb.alloc("gw", [128, KP, 512], BF16) for _ in range(NW)]
    wsem = [P.sem("gw") for _ in range(NW)]
    wfree = [cx.phase_ev] * NW
    nk = kc // KP
    if x_stream:
        NX = NXS
        Xt = [sb.alloc("gx", [128, KP, Tt], BF16) for _ in range(NX)]
    else:
        NX = 1
        Xt = [sb.alloc("gx", [128, kc, Tt], BF16)]
    xsem = [P.sem("gx") for _ in range(NX)]
    xsem2 = P.sem("gx2")
    xfree = [cx.phase_ev] * NX
    for cb in colblocks:
        cb["epi"].setup(cx, Tt)
    ntt = ntok // Tt
    NT = min(512, Tt)
    cx.NT = NT
    pieces = [(tt, ci, kp) for tt in range(ntt) for ci in range(len(colblocks)) for kp in range(nk)]
    wload = {}
    xload = {}

    def load_w(i):
        tt, ci, kp = pieces[i]
        cb = colblocks[ci]
        slot = i % NW
        src = wfn(kp * KP, KP, cb["c0"], cb["cn"]).rearrange("(k p) c -> p k c", p=128)
        dst = Wt[slot][:, :, 0:cb["cn"]]
        wload[i] = P.dma("pool", wsem[slot], lambda e, d=dst, s=src: e.dma_start(out=d, in_=s),
                         waits=[wfree[slot]])
        if bg:
            bg.pop(0)()

    def load_x(i):
        tt, ci, kp = pieces[i]
        if x_stream:
            slot = i % NX
            src = X[kp * KP:(kp + 1) * KP, :, tt * Tt:(tt + 1) * Tt].rearrange("k p t -> p k t")
            xload[i] = P.dma("pool", xsem[slot], lambda e, d=Xt[slot][:], s=src: e.dma_start(out=d, in_=s),
                             waits=[xfree[slot]])
        else:
            if ci == 0 and kp == 0 and x_tiled:
                half = kc // 2
                nq = Tt // 256
                for hh, sem_ in ((0, xsem[0]), (1, xsem2)):
                    k0, k1 = hh * half, (hh + 1) * half
                    for q in range(nq):
                        xload[(tt, hh)] = P.dma("pool", sem_, lambda e, d=Xt[0][:, k0:k1, q * 256:(q + 1) * 256],
                                                s=X[tt * nq + q, :, k0:k1, :]: e.dma_start(out=d, in_=s),
                                                waits=[xfree[0]])
            elif ci == 0 and kp == 0:
                src = X[:, :, tt * Tt:(tt + 1) * Tt].rearrange("k p t -> p k t")
                half = kc // 2
                xload[(tt, 0)] = P.dma("pool", xsem[0], lambda e, d=Xt[0][:, 0:half, :], s=src[:, 0:half, :]: e.dma_start(out=d, in_=s),
                                       waits=[xfree[0]])
                xload[(tt, 1)] = P.dma("pool", xsem2, lambda e, d=Xt[0][:, half:kc, :], s=src[:, half:kc, :]: e.dma_start(out=d, in_=s))

    npieces = len(pieces)
    PRE = NW - 1
    if x_stream:
        for i in range(min(NX, npieces)):
            load_x(i)
    for i in range(min(PRE, npieces)):
        load_w(i)
    if not x_stream:
        load_x(0)
    ev = None
    for i, (tt, ci, kp) in enumerate(pieces):
        cb = colblocks[ci]
        epi = cb["epi"]
        cn = cb["cn"]
        if kp == 0:
            epi.begin(cx, tt, ci, cb)
        slot = i % NW
        P.wait("pe", wload[i])
        if x_stream:
            xs = i % NX
            P.wait("pe", xload[i])
            Xc = Xt[xs]
        else:
            P.wait("pe", xload[(tt, 0)])
            if (kp + 1) * KP > kc // 2:
                P.wait("pe", xload[(tt, 1)])
            Xc = Xt[0]
        if cb["mode"] == "fm":
            groups = [(cs, ts) for cs in range((cn + 127) // 128) for ts in range(Tt // NT)]
        else:
            groups = [(tb,) for tb in range(Tt // 128)]
        for gi, g in enumerate(groups):
            bank = gi
            if kp == 0:
                P.wait("pe", cx.bank_free[bank])
            for k in range(KP):
                first = (kp == 0 and k == 0)
                last = (kp == nk - 1 and k == KP - 1)
                endp = (gi == len(groups) - 1 and k == KP - 1)
                kk = k if x_stream else kp * KP + k
                if cb["mode"] == "fm":
                    cs, ts = g
                    m = min(128, cn - cs * 128)
                    out = cx.ps[bank][0:m, 0:NT]
                    lhsT = Wt[slot][:, k, cs * 128:cs * 128 + m]
                    rhs = Xc[:, kk, ts * NT:(ts + 1) * NT]
                else:
                    tb = g[0]
                    out = cx.ps[bank][:, 0:cn]
                    lhsT = Xc[:, kk, tb * 128:(tb + 1) * 128]
                    rhs = Wt[slot][:, k, 0:cn]
                ev = P.op("pe", lambda e, o=out, l=lhsT, r=rhs, st=first, sp=last:
                          e.matmul(o, lhsT=l, rhs=r, start=st, stop=sp), signal=(last or endp))
            if kp == nk - 1:
                cx.bank_free[bank] = epi.group(cx, cx.ps[bank], tt, ci, cb, g, ev)
        wfree[slot] = ev
        if x_stream:
            xfree[xs] = ev
            if i + NX < npieces:
                load_x(i + NX)
        else:
            if ci == len(colblocks) - 1 and kp == nk - 1:
                xfree[0] = ev
                if tt + 1 < ntt:
                    load_x(i + 1)
        if i + PRE < npieces:
            load_w(i + PRE)
        if kp == nk - 1:
            epi.end(cx, tt, ci, cb)
    cx.phase_ev = ev
    evs = [ev]
    for cb in colblocks:
        evs += cb["epi"].finish(cx)
    sb.release(m0)
    return evs


class EpiBase:
    def setup(self, cx, Tt):
        pass

    def begin(self, cx, tt, ci, cb):
        pass

    def end(self, cx, tt, ci, cb):
        pass

    def finish(self, cx):
        return []


class EpiCopyFM(EpiBase):
    def __init__(self, destfn, eng="act", rstd=None):
        self.destfn = destfn
        self.eng = eng
        self.rstd = rstd
        self.ready = False

    def setup(self, cx, Tt):
        if self.ready:
            return
        self.ready = True
        self.Tt = Tt
        self.stg = [cx.sb.alloc("stg", [128, 4, Tt], BF16) for _ in range(2)]
        self.ssem = [cx.P.sem("st") for _ in range(2)]
        self.sfree = [None, None]
        self.cnt = 0
        self.last = None

    def begin(self, cx, tt, ci, cb):
        self.buf = self.cnt % 2
        self.cnt += 1

    def group(self, cx, bank, tt, ci, cb, g, pe_ev):
        cs, ts = g
        m = min(128, cb["cn"] - cs * 128)
        NT = cx.NT
        o = self.stg[self.buf][0:m, cs, ts * NT:(ts + 1) * NT]
        i = bank[0:m, 0:NT]
        if self.rstd is not None:
            t0 = tt * self.Tt + ts * NT
            rs = self.rstd[0:m, t0:t0 + NT]
            fn = lambda e, o=o, i=i, rs=rs: e.tensor_tensor(out=o, in0=i, in1=rs, op=ALU.mult)
            self.last = cx.P.op("dve", fn, waits=[pe_ev, self.sfree[self.buf]])
            return self.last
        if self.eng == "act":
            fn = lambda e, o=o, i=i: e.activation(out=o, in_=i, func=AF.Copy)
        else:
            fn = lambda e, o=o, i=i: e.tensor_copy(out=o, in_=i)
        self.last = cx.P.op(self.eng, fn, waits=[pe_ev, self.sfree[self.buf]])
        return self.last

    def end(self, cx, tt, ci, cb):
        b = self.buf
        n = (cb["cn"] + 127) // 128
        dst = self.destfn(tt, ci, cb)
        src = self.stg[b][:, 0:n, :]
        self.sfree[b] = cx.P.dma("sp", self.ssem[b], lambda e, d=dst, s=src: e.dma_start(out=d, in_=s),
                                 waits=[self.last])

    def finish(self, cx):
        return [e for e in self.sfree if e is not None]


class EpiCopyTM(EpiBase):
    def __init__(self, destfn):
        self.destfn = destfn
        self.ready = False

    def setup(self, cx, Tt):
        if self.ready:
            return
        self.ready = True
        self.ntb = Tt // 128
        self.stg = [cx.sb.alloc("stgv", [128, 4, self.ntb, 128], BF16) for _ in range(2)]
        self.ssem = [cx.P.sem("stv") for _ in range(2)]
        self.sfree = [None, None]
        self.cnt = 0

    def begin(self, cx, tt, ci, cb):
        self.buf = self.cnt % 2
        self.cnt += 1

    def group(self, cx, bank, tt, ci, cb, g, pe_ev):
        tb = g[0]
        nh = cb["cn"] // 128
        o = self.stg[self.buf][:, 0:nh, tb, :]
        i = bank[:, 0:cb["cn"]].rearrange("p (h d) -> p h d", h=nh)
        self.last = cx.P.op("dve", lambda e, o=o, i=i: e.tensor_copy(out=o, in_=i),
                            waits=[pe_ev, self.sfree[self.buf]])
        return self.last

    def end(self, cx, tt, ci, cb):
        b = self.buf
        dst = self.destfn(tt, ci, cb)
        src = self.stg[b][:].rearrange("p h t d -> p h (t d)")
        self.sfree[b] = cx.P.dma("sp", self.ssem[b], lambda e, d=dst, s=src: e.dma_start(out=d, in_=s),
                                 waits=[self.last])

    def finish(self, cx):
        return [e for e in self.sfree if e is not None]


class EpiResid(EpiBase):
    def __init__(self, residfn, destfn, norm=None):
        self.residfn = residfn
        self.destfn = destfn
        self.norm = norm
        self.ready = False

    def setup(self, cx, Tt):
        if self.ready:
            return
        self.ready = True
        self.Tt = Tt
        self.res = [cx.sb.alloc("res", [128, 4, Tt], F32) for _ in range(2)]
        self.rsem = [cx.P.sem("rs") for _ in range(2)]
        self.ssem = [cx.P.sem("str") for _ in range(2)]
        self.rfree = [None, None]
        self.rload = [None, None]
        self.cnt = 0
        self.extra = []
        if self.norm:
            self.sqt = [cx.sb.alloc("sqt", [128, 512], F32) for _ in range(2)]
            self.sqfree = [None, None]
            self.sqc = 0
            self.lnt = cx.sb.alloc("lnt", [128, 512], F32)
            self.accev = None
            self.rstd_evs = []
            if self.norm.get("hg_dest"):
                self.hg = [cx.sb.alloc("hgs", [128, 4, Tt], BF16) for _ in range(2)]
                self.hsem = [cx.P.sem("hg") for _ in range(2)]
                self.hfree = [None, None]

    def begin(self, cx, tt, ci, cb):
        b = self.cnt % 2
        self.buf = b
        self.cnt += 1
        self.extra = []
        src = self.residfn(tt, ci, cb)
        self.rload[b] = cx.P.dma("sp", self.rsem[b], lambda e, d=self.res[b][:], s=src: e.dma_start(out=d, in_=s),
                                 waits=[self.rfree[b]])

    def group(self, cx, bank, tt, ci, cb, g, pe_ev):
        P = cx.P
        cs, ts = g
        b = self.buf
        r = self.res[b][:, cs, ts * 512:(ts + 1) * 512]
        self.last = P.op("dve", lambda e, i=bank, r=r: e.tensor_tensor(out=r, in0=i, in1=r, op=ALU.add),
                         waits=[pe_ev, self.rload[b]])
        if self.norm:
            nm = self.norm
            if nm.get("hg_dest"):
                chunk = cb["c0"] // 128 + cs
                gap = cx.gvec[:, nm["gi"], chunk:chunk + 1]
                o = self.hg[b][:, cs, ts * 512:(ts + 1) * 512]
                ah = P.op("act", lambda e, o=o, r=r, gap=gap: e.activation(out=o, in_=r, func=AF.Copy, scale=gap),
                          waits=[self.last, self.hfree[b]])
                self.extra.append(ah)
            k = self.sqc % 2
            self.sqc += 1
            asq = P.op("act", lambda e, o=self.sqt[k][:], r=r: e.activation(out=o, in_=r, func=AF.Square),
                       waits=[self.last, self.sqfree[k]])
            t0 = tt * self.Tt + ts * 512
            acs = cx.acc[:, t0:t0 + 512]
            if ci == 0 and cs == 0:
                d = P.op("dve", lambda e, o=acs, i=self.sqt[k][:]: e.tensor_copy(out=o, in_=i),
                         waits=[asq] + self.rstd_evs)
            else:
                d = P.op("dve", lambda e, o=acs, i=self.sqt[k][:]: e.tensor_tensor(out=o, in0=o, in1=i, op=ALU.add),
                         waits=[asq])
            self.sqfree[k] = d
            self.accev = d
            self.extra.append(asq)
        return self.last

    def end(self, cx, tt, ci, cb):
        P = cx.P
        b = self.buf
        dst = self.destfn(tt, ci, cb)
        self.rfree[b] = P.dma("sp", self.ssem[b], lambda e, d=dst, s=self.res[b][:]: e.dma_start(out=d, in_=s),
                              waits=[self.last] + self.extra)
        if self.norm:
            nm = self.norm
            if nm.get("hg_dest"):
                hd = nm["hg_dest"](tt, ci, cb)
                self.hfree[b] = P.dma("sp", self.hsem[b], lambda e, d=hd, s=self.hg[b][:]: e.dma_start(out=d, in_=s),
                                      waits=self.extra)
            if ci == nm["ncb"] - 1:
                self.rstd_evs = []
                for ts in range(self.Tt // 512):
                    bank = ts
                    t0 = tt * self.Tt + ts * 512
                    P.wait("pe", [cx.bank_free[bank], self.accev])
                    pe = P.op("pe", lambda e, bank=bank, t0=t0: e.matmul(cx.ps[bank], lhsT=cx.ones_f[:], rhs=cx.acc[:, t0:t0 + 512],
                                                                       start=True, stop=True))
                    a1 = P.op("act", lambda e, bank=bank: e.activation(out=self.lnt[:], in_=cx.ps[bank], func=AF.Ln,
                                                                      bias=cx.eps6[:, 0:1], scale=1.0 / D),
                              waits=[pe] + self.rstd_evs)
                    cx.bank_free[bank] = a1
                    a2 = P.op("act", lambda e, t0=t0: e.activation(out=cx.rstd_a[:, t0:t0 + 512], in_=self.lnt[:], func=AF.Exp,
                                                                  scale=-0.5), waits=[a1])
                    d2 = P.op("dve", lambda e, t0=t0: e.tensor_tensor(out=cx.r2_a[:, t0:t0 + 512], in0=cx.rstd_a[:, t0:t0 + 512],
                                                                     in1=cx.rstd_a[:, t0:t0 + 512], op=ALU.mult), waits=[a2])
                    self.rstd_evs = [d2]

    def finish(self, cx):
        return [e for e in self.rfree if e is not None]


class EpiRelu2(EpiBase):
    def __init__(self, destfn, r2=None):
        self.destfn = destfn
        self.r2 = r2
        self.ready = False

    def setup(self, cx, Tt):
        if self.ready:
            return
        self.ready = True
        self.Tt = Tt
        self.tmp = [cx.sb.alloc("rtmp", [128, 512], F32) for _ in range(2)]
        self.tfree = [None, None]
        self.stg = [cx.sb.alloc("stgu", [128, 4, Tt], BF16) for _ in range(2)]
        self.ssem = [cx.P.sem("stu") for _ in range(2)]
        self.sfree = [None, None]
        self.cnt = 0
        self.gc = 0

    def begin(self, cx, tt, ci, cb):
        self.buf = self.cnt % 2
        self.cnt += 1

    def group(self, cx, bank, tt, ci, cb, g, pe_ev):
        cs, ts = g
        b = self.buf
        t = self.gc % 2
        self.gc += 1
        tm = self.tmp[t][:]
        a_ev = cx.P.op("act", lambda e, o=tm, i=bank: e.activation(out=o, in_=i, func=AF.Relu),
                       waits=[pe_ev, self.tfree[t]])
        o = self.stg[b][:, cs, ts * 512:(ts + 1) * 512]
        if self.r2 is not None:
            t0 = tt * self.Tt + ts * 512
            d1 = cx.P.op("dve", lambda e, i=tm: e.tensor_tensor(out=i, in0=i, in1=i, op=ALU.mult), waits=[a_ev])
            self.last = cx.P.op("dve", lambda e, o=o, i=tm, r=self.r2[:, t0:t0 + 512]: e.tensor_tensor(
                out=o, in0=i, in1=r, op=ALU.mult), waits=[d1, self.sfree[b]])
        else:
            self.last = cx.P.op("dve", lambda e, o=o, i=tm: e.tensor_tensor(out=o, in0=i, in1=i, op=ALU.mult),
                                waits=[a_ev, self.sfree[b]])
        self.tfree[t] = self.last
        return a_ev

    def end(self, cx, tt, ci, cb):
        b = self.buf
        dst = self.destfn(tt, ci, cb)
        self.sfree[b] = cx.P.dma("sp", self.ssem[b], lambda e, d=dst, s=self.stg[b][:]: e.dma_start(out=d, in_=s),
                                 waits=[self.last])

    def finish(self, cx):
        return [e for e in self.sfree if e is not None]


class EpiSig(EpiBase):
    def __init__(self, sigT, bias):
        self.sigT = sigT
        self.bias = bias
        self.last = None

    def group(self, cx, bank, tt, ci, cb, g, pe_ev):
        cs, ts = g
        t0 = tt * 1024 + ts * 512
        o = self.sigT[0:16, t0:t0 + 512]
        self.last = cx.P.op("act", lambda e, o=o, i=bank[0:16, :], b=self.bias: e.activation(
            out=o, in_=i, func=AF.Sigmoid, bias=b, scale=1.0), waits=[pe_ev])
        return self.last

    def finish(self, cx):
        return [self.last]


def norm_pass(cx, src, dst, ntok, gi, out_dtype, TT=256, waits=(), rstd_src=None, src_tiled=False, dst_tiled=False):
    P, sb = cx.P, cx.sb
    m0 = sb.mark()
    NXB = 4 if TT == 128 else (3 if out_dtype == BF16 else 2)
    xin = [sb.alloc("nx", [128, KC, TT], F32) for _ in range(NXB)]
    xsem = [P.sem("nx") for _ in range(NXB)]
    xfree = [None] * NXB
    sq = [sb.alloc("nsq", [128, KC, TT] if rstd_src is None else [128, 2], BF16) for _ in range(2)]
    sqfree = [None, None]
    lnv = sb.alloc("nln", [128, TT], F32)
    rstd = [sb.alloc("nrstd", [128, TT], F32) for _ in range(2)]
    rfree = [None, None]
    ot = [sb.alloc("no", [128, KC, TT], out_dtype) for _ in range(2)]
    osem = [P.sem("no") for _ in range(2)]
    ofree = [None, None]
    nt = ntok // TT
    ld = {}
    sqev = {}
    KD = 32

    def load(t):
        b = t % NXB
        s_ = src[t] if src_tiled else src[:, :, t * TT:(t + 1) * TT].rearrange("k p t -> p k t")
        ld[t] = P.dma("sp", xsem[b], lambda e, d=xin[b][:], s=s_: e.dma_start(out=d, in_=s),
                      waits=[xfree[b]] + list(waits))

    def square(t):
        if rstd_src is not None:
            sqev[t] = None
            return
        b = t % NXB
        q = t % 2
        sqev[t] = P.op("act", lambda e, o=sq[q][:], i=xin[b][:]: e.activation(out=o, in_=i, func=AF.Square),
                       waits=[ld[t], sqfree[q]])

    for t in range(min(NXB - 1, nt)):
        load(t)
    square(0)
    for t in range(nt):
        if t + NXB - 1 < nt:
            load(t + NXB - 1)
        if t + 1 < nt:
            square(t + 1)
        b = t % NXB
        q = t % 2
        bank = t % 2
        if rstd_src is None:
            P.wait("pe", [sqev[t], cx.bank_free[bank]])
            for k in range(KC):
                pe = P.op("pe", lambda e, o=cx.ps[bank][:, 0:TT], r=sq[q][:, k, :], st=(k == 0), sp=(k == KC - 1):
                          e.matmul(o, lhsT=cx.ones_b[:], rhs=r, start=st, stop=sp), signal=(k == KC - 1))
            sqfree[q] = pe
            a2 = P.op("act", lambda e, i=cx.ps[bank][:, 0:TT]: e.activation(
                out=lnv[:], in_=i, func=AF.Ln, bias=cx.eps6[:, 0:1], scale=1.0 / D), waits=[pe])
            cx.bank_free[bank] = a2
            a3 = P.op("act", lambda e, o=rstd[q][:]: e.activation(out=o, in_=lnv[:], func=AF.Exp, scale=-0.5),
                      waits=[a2, rfree[q]])
            rs_ap = rstd[q][:]
        else:
            a3 = ld[t]
            rs_ap = rstd_src[:, t * TT:(t + 1) * TT]
        last_d = last_p = None
        for k in range(KC):
            eng = "dve" if k < KD else "pool"
            ev = P.op(eng, lambda e, o=ot[t % 2][:, k, :], i=xin[b][:, k, :], g=cx.gvec[:, gi, k:k + 1],
                      r=rs_ap: e.scalar_tensor_tensor(out=o, in0=i, scalar=g, in1=r, op0=ALU.mult, op1=ALU.mult),
                      waits=[a3, ofree[t % 2]])
            if eng == "dve":
                last_d = ev
            else:
                last_p = ev
        xfree[b] = [last_d, last_p]
        rfree[q] = [last_d, last_p]
        dd = dst[t] if dst_tiled else dst[:, :, t * TT:(t + 1) * TT].rearrange("k p t -> p k t")
        ofree[t % 2] = P.dma("sp", osem[t % 2], lambda e, d=dd, s=ot[t % 2][:]: e.dma_start(out=d, in_=s),
                             waits=[last_d, last_p])
    sb.release(m0)
    return [e for e in ofree if e is not None]


def attn_prep(cx, dq_d, oh_d):
    P, sb = cx.P, cx.sb
    sm = cx.smalls
    sigT = cx.sigT
    cx.ctm = sb.alloc("ctm", [128, 32, 16], F32)
    cx.biask = sb.alloc("biask", [128, 16, 4, 32], F32)
    cx.lam = sb.alloc("lam", [128, 4], F32)
    cx.t5 = sb.alloc("t5", [128, 4, 8, 128], F32)
    cx.rb15s = sb.alloc("rb15s", [128, 8], F32)
    cx.gsub8 = sb.alloc("gsub8", [128, 2], F32)
    m0 = sb.mark()
    ones16 = sb.alloc("ones16", [16, S], F32)
    cT = sb.alloc("cT", [16, S], F32)
    dqf = sb.alloc("dqf", [16, S], F32)
    tmp2 = sb.alloc("tmp2", [16, NOWN], F32)
    dqo = sb.alloc("dqo", [16, NOWN], F32)
    hib = sb.alloc("hib", [16, NOWN], BF16)
    lob = sb.alloc("lob", [16, NOWN], BF16)
    crbc = sb.alloc("crbc", [128, 4, 16], F32)
    lamt = sb.alloc("lamt", [128, 2, 128], F32)
    rbext = sb.alloc("rbext", [64, 8], F32)
    ohc = [sb.alloc("ohc", [33, 32, 128], F32) for _ in range(2)]
    ones_f = sb.alloc("ones_f", [128, 128], F32)

    a0 = P.op("act", lambda e: e.activation(out=sigT[:], in_=sigT[:], func=AF.Ln))
    d0 = P.op("dve", lambda e: e.memset(ones16[:], 1.0))
    prev = [a0, d0]
    for sg in range(4):
        ini = 0.0 if sg == 0 else cT[:, sg * 1024 - 1:sg * 1024]
        pv = P.op("dve", lambda e, sg=sg, ini=ini: e.tensor_tensor_scan(
            out=cT[:, sg * 1024:(sg + 1) * 1024], data0=ones16[:, sg * 1024:(sg + 1) * 1024],
            data1=sigT[:, sg * 1024:(sg + 1) * 1024], initial=ini, op0=ALU.mult, op1=ALU.add), waits=prev)
        prev = [pv]
    dscan = prev[0]
    P.wait("pe", [dscan, cx.bank_free[0]])
    for kb in range(32):
        pe = P.op("pe", lambda e, kb=kb: e.matmul(cx.ps[0][:, kb * 16:(kb + 1) * 16],
                                                 lhsT=cT[0:16, kb * 128:(kb + 1) * 128],
                                                 rhs=cx.ident_f[0:16, 0:16], start=True, stop=True),
                  signal=(kb == 31))
    dctm = P.op("dve", lambda e: e.tensor_copy(out=cx.ctm[:].rearrange("p k h -> p (k h)"), in_=cx.ps[0]), waits=[pe])
    cx.bank_free[0] = dctm
    dz = P.op("dve", lambda e: e.memset(crbc[:, 0, :], 0.0))
    P.wait("pe", [dctm, cx.bank_free[1]])
    for j in range(1, 4):
        pe = P.op("pe", lambda e, j=j: e.matmul(cx.ps[1][:, j * 16:(j + 1) * 16], lhsT=cx.e127,
                                               rhs=cx.ctm[:, 8 * j - 1, :], start=True, stop=True),
                  signal=(j == 3))
    dcr = P.op("dve", lambda e: e.tensor_copy(out=crbc[:, 1:4, :].rearrange("p j h -> p (j h)"),
                                              in_=cx.ps[1][:, 16:64]), waits=[pe, dz])
    cx.bank_free[1] = dcr
    for h in range(16):
        for j in range(4):
            P.op("dve", lambda e, h=h, j=j: e.tensor_scalar(
                out=cx.biask[:, h, j, :], in0=cx.ctm[:, :, h], scalar1=crbc[:, j, h:h + 1], scalar2=-1.0,
                op0=ALU.subtract, op1=ALU.mult), waits=[dcr])
    evs = []
    for j in range(4):
        sc1 = 0.0 if j == 0 else cT[:, 1024 * j - 1:1024 * j]
        evs.append(P.op("dve", lambda e, j=j, sc1=sc1: e.tensor_scalar(
            out=dqf[:, 1024 * j:1024 * (j + 1)], in0=cT[:, 1024 * j:1024 * (j + 1)], scalar1=sc1,
            scalar2=1.0 / SCALE, op0=ALU.subtract, op1=ALU.mult), waits=[dscan]))
    v = dqf[:].rearrange("p (a r t) -> p a r t", r=2, t=128)
    t2v = tmp2[:].rearrange("p (a t) -> p a t", t=128)
    dqv = dqo[:].rearrange("p (a t) -> p a t", t=128)
    e1 = P.op("dve", lambda e: e.tensor_scalar(out=t2v, in0=v[:, :, 0, :], scalar1=cx.par[0:16, 0:1], scalar2=None,
                                               op0=ALU.mult), waits=evs)
    e2 = P.op("dve", lambda e: e.scalar_tensor_tensor(out=dqv, in0=v[:, :, 1, :], scalar=cx.par[0:16, 1:2], in1=t2v,
                                                      op0=ALU.mult, op1=ALU.add), waits=[e1])
    e3 = P.op("dve", lambda e: e.tensor_copy(out=hib[:], in_=dqo[:]), waits=[e2])
    e4 = P.op("dve", lambda e: e.tensor_copy(out=tmp2[:], in_=hib[:]), waits=[e3])
    e5 = P.op("dve", lambda e: e.tensor_tensor(out=tmp2[:], in0=dqo[:], in1=tmp2[:], op=ALU.subtract), waits=[e4])
    e6 = P.op("dve", lambda e: e.tensor_copy(out=lob[:], in_=tmp2[:]), waits=[e5])
    dsem = P.sem("dq")
    P.dma("sp", dsem, lambda e: e.dma_start(out=dq_d[:, 0, :], in_=hib[:]), waits=[e3])
    P.dma("sp", dsem, lambda e: e.dma_start(out=dq_d[:, 1, :], in_=lob[:]), waits=[e6])
    lv = sm[:, NS_LAM:NS_LAM + 512].rearrange("p (a d) -> p a d", a=4)
    l1 = P.op("dve", lambda e: e.tensor_tensor(out=lamt[:, 0, :], in0=lv[:, 0, :], in1=lv[:, 1, :], op=ALU.mult))
    l2 = P.op("dve", lambda e: e.tensor_tensor(out=lamt[:, 1, :], in0=lv[:, 2, :], in1=lv[:, 3, :], op=ALU.mult))
    l3 = P.op("dve", lambda e: e.tensor_reduce(out=cx.lam[:, 0:2], in_=lamt[:], axis=AXX, op=ALU.add), waits=[l1, l2])
    l4 = P.op("act", lambda e: e.activation(out=cx.lam[:, 2:4], in_=cx.lam[:, 0:2], func=AF.Exp), waits=[l3])
    l5 = P.op("dve", lambda e: e.tensor_tensor(out=cx.lam[:, 0:1], in0=cx.lam[:, 2:3], in1=cx.lam[:, 3:4],
                                               op=ALU.subtract), waits=[l4])
    l6 = P.op("dve", lambda e: e.tensor_scalar(out=cx.lam[:, 0:1], in0=cx.lam[:, 0:1], scalar1=LAMBDA_INIT, scalar2=None,
                                               op0=ALU.add), waits=[l5])
    P.op("dve", lambda e: e.tensor_scalar(out=cx.lam[:, 1:2], in0=cx.lam[:, 0:1], scalar1=-1.0, scalar2=None,
                                          op0=ALU.mult), waits=[l6])
    P.op("dve", lambda e: e.tensor_scalar(out=cx.gsub8[:], in0=sm[:, NS_GSUB:NS_GSUB + 2],
                                          scalar1=1.0 - LAMBDA_INIT, scalar2=None, op0=ALU.mult))
    r0 = P.op("dve", lambda e: e.memset(rbext[32:33, :], NEG))
    r1 = P.op("dve", lambda e: e.tensor_scalar(out=rbext[0:32, :], in0=sm[0:32, NS_RB:NS_RB + 8], scalar1=1.0 / SCALE,
                                               scalar2=None, op0=ALU.mult))
    r2 = P.op("dve", lambda e: e.tensor_scalar(out=cx.rb15s[:], in0=sm[:, NS_RB15:NS_RB15 + 8], scalar1=1.0 / SCALE,
                                               scalar2=None, op0=ALU.mult))
    r3 = P.op("dve", lambda e: e.memset(ones_f[:], 1.0))
    P.op("dve", lambda e: e.memset(cx.t5[:, 3, 0, :], NEG))
    for h in range(8):
        P.op("dve", lambda e, h=h: e.tensor_scalar(out=cx.t5[:, 2, h, :], in0=ones_f[:], scalar1=cx.rb15s[:, h:h + 1],
                                                   scalar2=None, op0=ALU.mult), waits=[r2, r3])
    osem = [P.sem("oh") for _ in range(2)]
    ofree = [None, None]
    ohv = oh_d.rearrange("b (t q k) -> b t q k", t=2, q=128)
    n = 0
    for ty in range(2):
        P.wait("pe", [cx.bank_free[2], cx.bank_free[3], r0, r1])
        for qc in range(4):
            b = n % 2
            n += 1
            ld = P.dma("sp", osem[b], lambda e, b=b, ty=ty, qc=qc: e.dma_start(
                out=ohc[b][:], in_=ohv[:, ty, qc * 32:(qc + 1) * 32, :]), waits=[ofree[b]])
            P.wait("pe", ld)
            for ql in range(32):
                q = qc * 32 + ql
                bank = 2 + (q * 8) // 512
                col = (q * 8) % 512
                pe = P.op("pe", lambda e, b=b, ql=ql, bank=bank, col=col: e.matmul(
                    cx.ps[bank][:, col:col + 8], lhsT=ohc[b][0:33, ql, :], rhs=rbext[0:33, 0:8], start=True, stop=True),
                    signal=(ql == 31))
            ofree[b] = pe
        dd = None
        for hb in range(2):
            dd = P.op("dve", lambda e, ty=ty, hb=hb: e.tensor_copy(
                out=cx.t5[:, ty, :, hb * 64:(hb + 1) * 64],
                in_=cx.ps[2 + hb].rearrange("p (q h) -> p h q", h=8)), waits=[pe])
            cx.bank_free[2 + hb] = dd
    sb.release(m0)


def attn_loads(cx, kinds, nheads, done_evs):
    pass


def fox_attention(cx, KT, QT, Vs, dq_d, mixT, maskF_d, after_mask=None):
    P, sb = cx.P, cx.sb
    m0 = sb.mark()
    kt = [sb.alloc("kt", [128, S], BF16) for _ in range(2)]
    vt = [sb.alloc("vt", [128, 32, 128], BF16) for _ in range(2)]
    qt = [sb.alloc("qt", [128, NOWN], BF16) for _ in range(2)]
    dqt = [sb.alloc("dqt", [2, NOWN], BF16) for _ in range(2)]
    hsem = [P.sem("fh") for _ in range(2)]
    maskF = sb.alloc("maskF", [128, 8, 512], BF16)
    NPT = 3
    pt = [sb.alloc("pt", [128, 512], BF16) for _ in range(NPT)]
    ptfree = [None] * NPT
    rl = sb.alloc("rl", [128, 512], F32)
    ostg = [sb.alloc("ostg", [128, 512], BF16) for _ in range(2)]
    osem = [P.sem("fo") for _ in range(2)]
    ofree = [None, None]
    msem = P.sem("mk")
    mld = P.dma("pool", msem, lambda e: e.dma_start(out=maskF[:].rearrange("p r q -> p (r q)"), in_=maskF_d))
    if after_mask is not None:
        after_mask()
    hload = {}
    hdone = {}

    def load_head(h):
        sl = h % 2
        w = [hdone.get(h - 2)]
        P.dma("sp", hsem[sl], lambda e: e.dma_start(out=kt[sl][:], in_=KT[h]), waits=w)
        P.dma("sp", hsem[sl], lambda e: e.dma_start(out=vt[sl][:], in_=Vs[h]))
        P.dma("sp", hsem[sl], lambda e: e.dma_start(out=qt[sl][:], in_=QT[h]))
        hload[h] = P.dma("sp", hsem[sl], lambda e: e.dma_start(out=dqt[sl][:], in_=dq_d[h]))

    items = [(h, j, kb) for h in range(16) for j in range(4) for kb in range(8 * j + 8)]
    st_fin = [None]
    LA = 2
    exp_ev = {}
    load_head(0)
    load_head(1)

    def emit_S(t):
        h, j, kb = items[t]
        sl = h % 2
        sbank = t % 4
        P.wait("pe", [hload[h], cx.bank_free[sbank], mld])
        diag = kb >= 8 * j
        c0 = 128 * ((kb - 8 * j) // 2) if diag else 0
        P.op("pe", lambda e: e.matmul(cx.ps[sbank][:, c0:512], lhsT=kt[sl][:, kb * 128:(kb + 1) * 128],
                                      rhs=qt[sl][:, j * 512 + c0:(j + 1) * 512], start=True, stop=False), signal=False)
        pe = P.op("pe", lambda e: e.matmul(cx.ps[sbank][:, c0:512], lhsT=cx.ones_b[0:2, :],
                                           rhs=dqt[sl][0:2, j * 512 + c0:(j + 1) * 512],
                                           start=False, stop=(not diag)), signal=(not diag))
        if diag:
            pe = P.op("pe", lambda e: e.matmul(cx.ps[sbank][:, c0:512], lhsT=cx.ident_b[:], rhs=maskF[:, kb - 8 * j, c0:512],
                                               start=False, stop=True))
        p = t % NPT
        ev = P.op("act", lambda e: e.activation(out=pt[p][:, c0:512], in_=cx.ps[sbank][:, c0:512], func=AF.Exp,
                                                bias=cx.biask[:, h, j, kb:kb + 1], scale=SCALE),
                  waits=[pe, ptfree[p]])
        exp_ev[t] = ev
        cx.bank_free[sbank] = ev

    def emit_PV(t):
        h, j, kb = items[t]
        sl = h % 2
        hj = h * 4 + j
        ob = 4 + hj % 2
        lb = 6 + hj % 2
        last = (kb == 8 * j + 7)
        if kb == 0:
            P.wait("pe", [cx.bank_free[ob], cx.bank_free[lb]])
        P.wait("pe", exp_ev[t])
        p = t % NPT
        c0 = 128 * ((kb - 8 * j) // 2) if kb >= 8 * j else 0
        P.op("pe", lambda e: e.matmul(cx.ps[ob][:, c0:512], lhsT=vt[sl][:, kb, :], rhs=pt[p][:, c0:512],
                                      start=(kb == 0), stop=last), signal=False)
        pe = P.op("pe", lambda e: e.matmul(cx.ps[lb][:, c0:512], lhsT=cx.ones_b[:], rhs=pt[p][:, c0:512],
                                           start=(kb == 0), stop=last))
        ptfree[p] = pe
        if last:
            o = hj % 2
            a1 = P.op("act", lambda e: e.activation(out=rl[:], in_=cx.ps[lb], func=AF.Ln), waits=[pe, st_fin[0]])
            d1 = P.op("act", lambda e: e.activation(out=rl[:], in_=rl[:], func=AF.Exp, scale=-1.0), waits=[a1])
            d2 = P.op("dve", lambda e: e.tensor_tensor(out=ostg[o][:], in0=cx.ps[ob], in1=rl[:], op=ALU.mult),
                      waits=[d1, ofree[o]])
            st_fin[0] = d2
            cx.bank_free[ob] = d2
            cx.bank_free[lb] = d2
            ofree[o] = P.dma("sp", osem[o], lambda e: e.dma_start(out=mixT[h, :, j * 512:(j + 1) * 512], in_=ostg[o][:]),
                             waits=[d2])
            if j == 3:
                hdone[h] = pe
                if h + 2 < 16:
                    load_head(h + 2)

    for t in range(len(items) + LA):
        if t < len(items):
            emit_S(t)
        if t >= LA:
            emit_PV(t - LA)
    sb.release(m0)


def diff_attention(cx, KT, QT, Vs, mixT):
    P, sb = cx.P, cx.sb
    sm = cx.smalls
    m0 = sb.mark()
    kt = [sb.alloc("dkt", [128, 2, S], BF16) for _ in range(2)]
    qt = [sb.alloc("dqt", [128, 2, NOWN], BF16) for _ in range(2)]
    vt = [sb.alloc("dvt", [128, 2, 32, 128], BF16) for _ in range(2)]
    bd = [sb.alloc("bd", [128, 9, 512], BF16) for _ in range(2)]
    tmpb = [sb.alloc("tmpb", [128, 128], F32) for _ in range(2)]
    hsem = [P.sem("dh") for _ in range(2)]
    NPT = 3
    pt = [sb.alloc("dpt", [128, 512], BF16) for _ in range(NPT)]
    ptfree = [None] * NPT
    r1 = sb.alloc("r1", [128, 512], F32)
    r2 = sb.alloc("r2", [128, 512], F32)
    t2 = sb.alloc("t2", [128, 512], F32)
    dfe = [sb.alloc("dfe", [128, 512], F32) for _ in range(2)]
    sqb = [sb.alloc("sqb", [128, 512], BF16) for _ in range(2)]
    lnv = sb.alloc("lnv", [128, 512], F32)
    rstd = sb.alloc("rstdd", [128, 512], F32)
    ostg = [sb.alloc("dostg", [128, 2, 512], BF16) for _ in range(2)]
    osem = [P.sem("do") for _ in range(2)]
    ofree = [None, None]
    hload = {}
    hdone = {}
    bdready = {}
    st = {"sc": 0, "fin": None, "tb": 0}

    def load_head(h):
        sl = h % 2
        w = [hdone.get(h - 2)]
        P.dma("sp", hsem[sl], lambda e: e.dma_start(out=kt[sl][:], in_=KT[16 + 2 * h:18 + 2 * h].rearrange("c p t -> p c t")),
              waits=w)
        P.dma("sp", hsem[sl], lambda e: e.dma_start(out=vt[sl][:], in_=Vs[16 + 2 * h:18 + 2 * h].rearrange("c p k d -> p c k d")))
        hload[h] = P.dma("sp", hsem[sl], lambda e: e.dma_start(
            out=qt[sl][:], in_=QT[16 + 2 * h:18 + 2 * h].rearrange("c p t -> p c t")))
        def base(t):
            if t < -1:
                return cx.t5[:, 2, h, :]
            if t == -1:
                return cx.t5[:, 1, h, :]
            if t == 0:
                return cx.t5[:, 0, h, :]
            return cx.t5[:, 3, 0, :]
        ev = None
        for r in range(-1, 8):
            for i in range(4):
                tb = st["tb"] % 2
                st["tb"] += 1
                b0 = base(r - 2 * i)
                b1 = base(r - 2 * i - 1)
                ea = P.op("dve", lambda e, tb=tb, b0=b0: e.tensor_scalar(out=tmpb[tb][:], in0=b0, scalar1=cx.par[:, 0:1],
                                                                        scalar2=None, op0=ALU.mult), waits=w)
                ev = P.op("dve", lambda e, tb=tb, b1=b1, r=r, i=i: e.scalar_tensor_tensor(
                    out=bd[sl][:, r + 1, i * 128:(i + 1) * 128], in0=b1, scalar=cx.par[:, 1:2], in1=tmpb[tb][:],
                    op0=ALU.mult, op1=ALU.add), waits=[ea])
        bdready[h] = ev

    items = [(h, j, c, kb) for h in range(8) for j in range(4) for c in range(2) for kb in range(8 * j + 8)]
    LA = 1
    DEFER = 10
    exp_ev = {}
    dfa = [sb.alloc("dfa", [128, 512], F32) for _ in range(2)]
    load_head(0)
    load_head(1)
    st.update(r1free=None, r2free=None, sqfree=None, pend=None, since=0)

    def emit_S(t):
        h, j, c, kb = items[t]
        sl = h % 2
        sbank = st["sc"] % 2
        st["sc"] += 1
        diag = kb >= 8 * j - 1
        c0 = 128 * ((kb - 8 * j) // 2) if kb >= 8 * j else 0
        P.wait("pe", [hload[h], cx.bank_free[sbank]])
        pe = P.op("pe", lambda e: e.matmul(cx.ps[sbank][:, c0:512], lhsT=kt[sl][:, c, kb * 128:(kb + 1) * 128],
                                           rhs=qt[sl][:, c, j * 512 + c0:(j + 1) * 512], start=True, stop=(not diag)),
                  signal=(not diag))
        if diag:
            P.wait("pe", bdready[h])
            pe = P.op("pe", lambda e: e.matmul(cx.ps[sbank][:, c0:512], lhsT=cx.ident_b[:],
                                               rhs=bd[sl][:, kb - 8 * j + 1, c0:512], start=False, stop=True))
        p = t % NPT
        bias = sm[:, NS_RB15 + h:NS_RB15 + h + 1] if not diag else cx.eps6[:, 2:3]
        ev = P.op("act", lambda e: e.activation(out=pt[p][:, c0:512], in_=cx.ps[sbank][:, c0:512], func=AF.Exp,
                                                bias=bias, scale=SCALE),
                  waits=[pe, ptfree[p]])
        exp_ev[t] = ev
        cx.bank_free[sbank] = ev

    def emit_ss():
        h, j, sq_evs = st["pend"]
        st["pend"] = None
        o = (h * 4 + j) % 2
        sbank = st["sc"] % 2
        st["sc"] += 1
        P.wait("pe", [cx.bank_free[sbank]] + sq_evs)
        P.op("pe", lambda e: e.matmul(cx.ps[sbank], lhsT=cx.ones_b[:], rhs=sqb[0][:], start=True, stop=False), signal=False)
        pss = P.op("pe", lambda e: e.matmul(cx.ps[sbank], lhsT=cx.ones_b[:], rhs=sqb[1][:], start=False, stop=True))
        st["sqfree"] = pss
        a1 = P.op("act", lambda e: e.activation(out=lnv[:], in_=cx.ps[sbank], func=AF.Ln, bias=cx.eps6[:, 1:2],
                                                scale=1.0 / 256.0), waits=[pss, st["fin"]])
        cx.bank_free[sbank] = a1
        a2 = P.op("act", lambda e: e.activation(out=rstd[:], in_=lnv[:], func=AF.Exp, scale=-0.5), waits=[a1])
        fin = None
        for e_ in range(2):
            fin = P.op("dve", lambda e, e_=e_: e.scalar_tensor_tensor(
                out=ostg[o][:, e_, :], in0=dfe[e_][:], scalar=cx.gsub8[:, e_:e_ + 1], in1=rstd[:],
                op0=ALU.mult, op1=ALU.mult), waits=[a2, ofree[o]])
        st["fin"] = fin
        ofree[o] = P.dma("sp", osem[o], lambda e: e.dma_start(
            out=mixT[16 + 2 * h:18 + 2 * h, :, j * 512:(j + 1) * 512].rearrange("c p t -> p c t"), in_=ostg[o][:]),
            waits=[fin])
        if j == 3:
            hdone[h] = pss
            if h + 2 < 8:
                load_head(h + 2)

    def emit_PV(t):
        h, j, c, kb = items[t]
        sl = h % 2
        u = (h * 4 + j) * 2 + c
        sset = u % 2
        ob = 2 + 3 * sset
        lb = 4 + 3 * sset
        last = (kb == 8 * j + 7)
        if kb == 0:
            P.wait("pe", [cx.bank_free[ob], cx.bank_free[ob + 1], cx.bank_free[lb]])
        P.wait("pe", exp_ev[t])
        p = t % NPT
        c0 = 128 * ((kb - 8 * j) // 2) if kb >= 8 * j else 0
        for e_ in range(2):
            P.op("pe", lambda e, e_=e_: e.matmul(cx.ps[ob + e_][:, c0:512], lhsT=vt[sl][:, e_, kb, :], rhs=pt[p][:, c0:512],
                                                 start=(kb == 0), stop=last), signal=False)
        pe = P.op("pe", lambda e: e.matmul(cx.ps[lb][:, c0:512], lhsT=cx.ones_b[:], rhs=pt[p][:, c0:512],
                                           start=(kb == 0), stop=last))
        ptfree[p] = pe
        st["since"] += 1
        if st["pend"] is not None and st["since"] >= DEFER:
            emit_ss()
        if last and c == 0:
            a1 = P.op("act", lambda e: e.activation(out=r1[:], in_=cx.ps[lb], func=AF.Ln), waits=[pe, st["r1free"]])
            a2 = P.op("act", lambda e: e.activation(out=r1[:], in_=r1[:], func=AF.Exp, scale=-1.0), waits=[a1])
            dl = None
            for e_ in range(2):
                dl = P.op("dve", lambda e, e_=e_: e.tensor_tensor(out=dfa[e_][:], in0=cx.ps[ob + e_], in1=r1[:], op=ALU.mult),
                          waits=[a2])
            st["r1free"] = dl
            for b in (ob, ob + 1, lb):
                cx.bank_free[b] = dl
        if last and c == 1:
            if st["pend"] is not None:
                emit_ss()
            a1 = P.op("act", lambda e: e.activation(out=r2[:], in_=cx.ps[lb], func=AF.Ln), waits=[pe, st["r2free"]])
            a2 = P.op("act", lambda e: e.activation(out=r2[:], in_=r2[:], func=AF.Exp, scale=-1.0), waits=[a1])
            sq_evs = []
            dl = None
            for e_ in range(2):
                d5 = P.op("dve", lambda e, e_=e_: e.tensor_tensor(out=t2[:], in0=cx.ps[ob + e_], in1=r2[:], op=ALU.mult),
                          waits=[a2])
                d6 = P.op("dve", lambda e, e_=e_: e.scalar_tensor_tensor(
                    out=dfe[e_][:], in0=t2[:], scalar=cx.lam[:, 1:2], in1=dfa[e_][:], op0=ALU.mult, op1=ALU.add),
                    waits=[d5, st["fin"]])
                d7 = P.op("dve", lambda e, e_=e_: e.tensor_tensor(out=sqb[e_][:], in0=dfe[e_][:], in1=dfe[e_][:], op=ALU.mult),
                          waits=[d6, st["sqfree"]])
                sq_evs.append(d7)
                dl = d5
            st["r2free"] = dl
            for b in (ob, ob + 1, lb):
                cx.bank_free[b] = dl
            st["pend"] = (h, j, sq_evs)
            st["since"] = 0

    for t in range(len(items) + LA):
        if t < len(items):
            emit_S(t)
        if t >= LA:
            emit_PV(t - LA)
    if st["pend"] is not None:
        emit_ss()
    sb.release(m0)


def cross_attention(cx, qmT, kmT, VM, omT):
    P, sb = cx.P, cx.sb
    m0 = sb.mark()
    kmt = sb.alloc("kmt", [128, KC, NMEM], BF16)
    vmt = sb.alloc("vmt", [128, KC, 2, 128], BF16)
    qm = [sb.alloc("qm", [128, KC, 512], BF16) for _ in range(2)]
    qsem = [P.sem("cq") for _ in range(2)]
    qfree = [None, None]
    ksem = P.sem("ck")
    P.dma("sp", ksem, lambda e: e.dma_start(out=kmt[:], in_=kmT.rearrange("k p t -> p k t")))
    kld = P.dma("sp", ksem, lambda e: e.dma_start(out=vmt[:], in_=VM.rearrange("c p k d -> p c k d")))
    ptm = [sb.alloc("ptm", [128, 512], BF16) for _ in range(4)]
    ptfree = [None] * 4
    rl = sb.alloc("crl", [128, 512], F32)
    ostg = [sb.alloc("costg", [128, 8, 512], BF16) for _ in range(2)]
    osem = [P.sem("co") for _ in range(2)]
    ofree = [None, None]
    st = {"b": 0}

    def nbank():
        b = st["b"] % 8
        st["b"] += 1
        return b

    qld = {}

    def loadq(t):
        b = t % 2
        qld[t] = P.dma("sp", qsem[b], lambda e: e.dma_start(
            out=qm[b][:], in_=qmT[:, :, t * 512:(t + 1) * 512].rearrange("k p t -> p k t")), waits=[qfree[b]])

    loadq(0)
    n = 0
    rl_free = None
    for t in range(4):
        if t + 1 < 4:
            loadq(t + 1)
        qb = t % 2
        for hm in range(4):
            o = n % 2
            pts = []
            for mb in range(2):
                bk = nbank()
                P.wait("pe", [qld[t], kld, cx.bank_free[bk]])
                for ch in range(8):
                    pe = P.op("pe", lambda e, ch=ch, mb=mb, bk=bk, hm=hm, qb=qb: e.matmul(
                        cx.ps[bk], lhsT=kmt[:, 8 * hm + ch, mb * 128:(mb + 1) * 128], rhs=qm[qb][:, 8 * hm + ch, :],
                        start=(ch == 0), stop=(ch == 7)), signal=(ch == 7))
                p = (2 * n + mb) % 4
                ev = P.op("act", lambda e, p=p, bk=bk: e.activation(out=ptm[p][:], in_=cx.ps[bk], func=AF.Exp, scale=SCALE_M),
                          waits=[pe, ptfree[p]])
                cx.bank_free[bk] = ev
                pts.append((p, ev))
            if hm == 3:
                qfree[qb] = pe
            bl = nbank()
            P.wait("pe", [cx.bank_free[bl], pts[0][1], pts[1][1]])
            P.op("pe", lambda e, bl=bl, p=pts[0][0]: e.matmul(cx.ps[bl], lhsT=cx.ones_b[:], rhs=ptm[p][:], start=True, stop=False),
                 signal=False)
            pl = P.op("pe", lambda e, bl=bl, p=pts[1][0]: e.matmul(cx.ps[bl], lhsT=cx.ones_b[:], rhs=ptm[p][:], start=False, stop=True))
            d1 = P.op("dve", lambda e, bl=bl: e.reciprocal(out=rl[:], in_=cx.ps[bl]), waits=[pl, rl_free])
            cx.bank_free[bl] = d1
            dlast = None
            for e_ in range(8):
                bo = nbank()
                P.wait("pe", [cx.bank_free[bo]])
                P.op("pe", lambda e, bo=bo, e_=e_, p=pts[0][0], hm=hm: e.matmul(cx.ps[bo], lhsT=vmt[:, 8 * hm + e_, 0, :], rhs=ptm[p][:],
                                                                       start=True, stop=False), signal=False)
                po = P.op("pe", lambda e, bo=bo, e_=e_, p=pts[1][0], hm=hm: e.matmul(cx.ps[bo], lhsT=vmt[:, 8 * hm + e_, 1, :], rhs=ptm[p][:],
                                                                            start=False, stop=True))
                dlast = P.op("dve", lambda e, bo=bo, e_=e_, o=o: e.tensor_tensor(out=ostg[o][:, e_, :], in0=cx.ps[bo], in1=rl[:],
                                                                                 op=ALU.mult), waits=[po, d1, ofree[o]])
                cx.bank_free[bo] = dlast
            ptfree[pts[0][0]] = po
            ptfree[pts[1][0]] = po
            rl_free = dlast
            ofree[o] = P.dma("sp", osem[o], lambda e, o=o, hm=hm, t=t: e.dma_start(
                out=omT[8 * hm:8 * hm + 8, :, t * 512:(t + 1) * 512].rearrange("c p t -> p c t"), in_=ostg[o][:]),
                waits=[dlast])
            n += 1
    sb.release(m0)


NS_G, NS_GSUB, NS_PAR, NS_BF, NS_RB, NS_RB15, NS_LAM, NS_ID, NS_E127 = 0, 160, 162, 164, 165, 173, 181, 693, 821
NS = 949


def build(upto=99, debug=()):
    nc = bass.Bass("TRN2", target_bir_lowering=False)
    cx = Ctx()
    cx.nc = nc
    cx.P = P = Prog(nc)
    cx.sb = sb = SBAlloc(nc)
    cx.phase_ev = None
    cx.inputs = []
    cx.outputs = []

    def inp(name, shape, dt=F32):
        cx.inputs.append(name)
        return nc.dram_tensor(name, list(shape), dt, kind="ExternalInput").ap()

    def outp(name, shape, dt=F32):
        cx.outputs.append(name)
        return nc.dram_tensor(name, list(shape), dt, kind="ExternalOutput").ap()

    def scratch(name, shape, dt):
        return nc.dram_tensor(name, list(shape), dt).ap()

    psum = cx.es_ps = P.es.enter_context(nc.psum_tensor("ps", [128, 8, 512], F32))
    cx.ps = [psum[:, b, :] for b in range(8)]
    cx.bank_free = [None] * 8

    smalls_d = inp("smalls", [128, NS])
    smalls = sb.alloc("smalls", [128, NS], F32)
    cx.gvec = smalls[:, NS_G:NS_G + 160].rearrange("p (g k) -> p g k", g=5)
    cx.ident_f = smalls[:, NS_ID:NS_ID + 128]
    cx.e127 = smalls[:, NS_E127:NS_E127 + 128]
    cx.par = smalls[:, NS_PAR:NS_PAR + 2]
    cx.smalls = smalls
    cx.ones_b = sb.alloc("ones_b", [128, 128], BF16)
    cx.ident_b = sb.alloc("ident_b", [128, 128], BF16)
    cx.eps6 = sb.alloc("eps6", [128, 4], F32)
    csem = P.sem("const")
    ld = P.dma("sp", csem, lambda e: e.dma_start(out=smalls[:], in_=smalls_d))
    P.op("dve", lambda e: e.memset(cx.ones_b[:], 1.0))
    P.op("dve", lambda e: e.memset(cx.eps6[:, 0:1], 1e-6))
    P.op("dve", lambda e: e.memset(cx.eps6[:, 1:2], 1e-5))
    P.op("dve", lambda e: e.memset(cx.eps6[:, 2:4], 0.0))
    P.op("dve", lambda e: e.tensor_copy(out=cx.ident_b[:], in_=cx.ident_f), waits=[ld])
    P.barrier()

    dbg = {}

    def finish():
        P.barrier()
        dsem = P.sem("dbg")
        for name, (ap, shape, dt) in dbg.items():
            if name in debug:
                o = outp("dbg_" + name, shape, dt)
                P.dma("sp", dsem, lambda e, o=o, a=ap: e.dma_start(out=o, in_=a))
        P.barrier()
        P.emit()
        return nc, cx

    xa = inp("xa", [S // 256, 128, KC, 256])
    xo = inp("xo", [KC, 128, NOWN])
    xot = inp("xot", [NOWN // 256, 128, KC, 256])
    aT_all = scratch("aT_all", [KC, 128, S], BF16)
    aT_own = scratch("aT_own", [KC, 128, NOWN], BF16)
    norm_pass(cx, xa, aT_all, S, 0, BF16, src_tiled=True)
    norm_pass(cx, xot, aT_own, NOWN, 0, BF16, src_tiled=True)
    P.barrier()
    if upto <= 1:
        return finish()

    w_in = inp("w_in", [D, 12304])
    KT = scratch("KT", [32, 128, S], BF16)
    QT = scratch("QT", [32, 128, NOWN], BF16)
    Vs = scratch("Vs", [32, 128, 32, 128], BF16)
    dbg["KT"] = (KT, [32, 128, S], BF16)
    dbg["QT"] = (QT, [32, 128, NOWN], BF16)
    dbg["Vs"] = (Vs, [32, 128, 32, 128], BF16)
    attn_mark = sb.mark()
    sigT = sb.alloc("sigT", [16, S], F32)
    cx.sigT = sigT

    def w_in_fn(k0, kn, c0, cn):
        return w_in[k0 * 128:(k0 + kn) * 128, c0:c0 + cn]

    def kt_dest(base):
        def f(tt, ci, cb):
            h0 = base + (cb["c0"] - cb["cbase"]) // 128
            return KT[h0:h0 + 4, :, tt * 1024:(tt + 1) * 1024].rearrange("h p t -> p h t")
        return f

    def v_dest(base):
        def f(tt, ci, cb):
            h0 = base + (cb["c0"] - cb["cbase"]) // 128
            return Vs[h0:h0 + 4, :, tt * 8:(tt + 1) * 8, :].rearrange("h p k d -> p h (k d)")
        return f

    def q_dest(base):
        def f(tt, ci, cb):
            h0 = base + (cb["c0"] - cb["cbase"]) // 128
            return QT[h0:h0 + 4, :, tt * 1024:(tt + 1) * 1024].rearrange("h p t -> p h t")
        return f

    ekf = EpiCopyFM(lambda tt, ci, cb: kt_dest(cb["hb"])(tt, ci, cb))
    ekd = ekf
    evf = EpiCopyTM(lambda tt, ci, cb: v_dest(cb["hb"])(tt, ci, cb))
    evd = evf
    esg = EpiSig(sigT, smalls[0:16, NS_BF:NS_BF + 1])
    cbs = []
    for c in range(4):
        cbs.append(dict(c0=C_KF + 512 * c, cn=512, cbase=C_KF, hb=0, mode="fm", epi=ekf))
    for c in range(4):
        cbs.append(dict(c0=C_VF + 512 * c, cn=512, cbase=C_VF, hb=0, mode="tm", epi=evf))
    cbs.append(dict(c0=C_F, cn=16, cbase=C_F, hb=0, mode="fm", epi=esg))
    for c in range(4):
        cbs.append(dict(c0=C_KD + 512 * c, cn=512, cbase=C_KD, hb=16, mode="fm", epi=ekd))
    for c in range(4):
        cbs.append(dict(c0=C_VD + 512 * c, cn=512, cbase=C_VD, hb=16, mode="tm", epi=evd))
    if upto == 2:
        cbs = [cbs[0], cbs[4], cbs[8], cbs[9], cbs[13]]
    w_dn0 = inp("w_dn0", [DFF // 2, D])
    w_dn1 = inp("w_dn1", [DFF // 2, D])
    wdn_bf = scratch("wdn_bf", [DFF, D], BF16)
    pcsem = P.sem("precast")
    bg = []
    for r in range(128):
        wsrc = (w_dn0 if r < 64 else w_dn1)[(r % 64) * 128:(r % 64 + 1) * 128, :]
        bg.append(lambda d=wdn_bf[r * 128:(r + 1) * 128, :], s_=wsrc: P.dma(
            "pool", pcsem, lambda e, d=d, s_=s_: e.dma_start(out=d, in_=s_)))
    gemm(cx, X=aT_all, kc=KC, wfn=w_in_fn, ntok=S, colblocks=cbs, x_stream=False, bg=bg)
    while bg:
        bg.pop(0)()
    P.barrier()
    if upto <= 2:
        return finish()
    eqf = EpiCopyFM(lambda tt, ci, cb: q_dest(cb["hb"])(tt, ci, cb))
    eqd = eqf
    cbs = []
    for c in range(4):
        cbs.append(dict(c0=C_QF + 512 * c, cn=512, cbase=C_QF, hb=0, mode="fm", epi=eqf))
    for c in range(4):
        cbs.append(dict(c0=C_QD + 512 * c, cn=512, cbase=C_QD, hb=16, mode="fm", epi=eqd))
    gemm(cx, X=aT_own, kc=KC, wfn=w_in_fn, ntok=NOWN, colblocks=cbs, x_stream=False)
    P.barrier()
    if upto <= 3:
        return finish()

    dq_d = scratch("dq_d", [16, 2, NOWN], BF16)
    oh_d = inp("oh", [33, 2 * 128 * 128])
    maskF_d = inp("maskF", [128, 8 * 512])
    mixT = scratch("mixT", [KC, 128, NOWN], BF16)
    dbg["mixT"] = (mixT, [KC, 128, NOWN], BF16)
    dbg["dq_d"] = (dq_d, [16, 2, NOWN], BF16)
    attn_prep(cx, dq_d, oh_d)
    P.barrier()
    if upto <= 4:
        return finish()
    fox_attention(cx, KT, QT, Vs, dq_d, mixT, maskF_d)
    P.barrier()
    if upto <= 5:
        return finish()
    diff_attention(cx, KT, QT, Vs, mixT)
    P.barrier()
    sb.release(attn_mark)
    if upto <= 6:
        return finish()

    def fm_dest(Y, Tt=1024):
        def f(tt, ci, cb):
            h0 = cb["c0"] // 128
            n = (cb["cn"] + 127) // 128
            return Y[h0:h0 + n, :, tt * Tt:(tt + 1) * Tt].rearrange("h p t -> p h t")
        return f

    def simple_w(w):
        def f(k0, kn, c0, cn):
            return w[k0 * 128:(k0 + kn) * 128, c0:c0 + cn]
        return f

    def cblocks(n, epi, mode="fm"):
        return [dict(c0=512 * c, cn=512, cbase=0, hb=0, mode=mode, epi=epi) for c in range(n)]

    cx.acc = sb.alloc("acc", [128, NOWN], F32)
    cx.rstd_a = sb.alloc("rstd_a", [128, NOWN], F32)
    cx.r2_a = sb.alloc("r2_a", [128, NOWN], F32)
    cx.ones_f = sb.alloc("ones_f2", [128, 128], F32)
    P.op("dve", lambda e: e.memset(cx.ones_f[:], 1.0))
    P.barrier()
    w_out = inp("w_out", [D, D])
    h1T = scratch("h1T", [KC, 128, NOWN], F32)
    dbg["h1T"] = (h1T, [KC, 128, NOWN], F32)
    cT_d = scratch("cT_d", [KC, 128, NOWN], BF16)
    gemm(cx, X=mixT, kc=KC, wfn=simple_w(w_out), ntok=NOWN,
         colblocks=cblocks(8, EpiResid(fm_dest(xo), fm_dest(h1T), norm=dict(gi=1, hg_dest=fm_dest(cT_d), ncb=8))),
         x_stream=False)
    P.barrier()
    if upto <= 7:
        return finish()
    memT = inp("memT", [KC, 128, NMEM])
    mT_d = scratch("mT_d", [KC, 128, NMEM], BF16)
    norm_pass(cx, memT, mT_d, NMEM, 2, BF16)
    P.barrier()
    wk = inp("wk_mem", [D, D])
    wv = inp("wv_mem", [D, D])
    kmT = scratch("kmT", [KC, 128, NMEM], BF16)
    VM = scratch("VM", [KC, 128, 2, 128], BF16)
    def vm_dest(tt, ci, cb):
        h0 = cb["c0"] // 128
        return VM[h0:h0 + 4, :, :, :].rearrange("h p k d -> p h (k d)")

    def wkv_fn(k0, kn, c0, cn):
        w, c = (wk, c0) if c0 < D else (wv, c0 - D)
        return w[k0 * 128:(k0 + kn) * 128, c:c + cn]
    ekm = EpiCopyFM(fm_dest(kmT, 256))
    evm = EpiCopyTM(lambda tt, ci, cb: vm_dest(tt, ci, dict(cb, c0=cb["c0"] - D)))
    cbs = [dict(c0=512 * c, cn=512, cbase=0, hb=0, mode="fm", epi=ekm) for c in range(8)]
    cbs += [dict(c0=D + 512 * c, cn=512, cbase=0, hb=0, mode="tm", epi=evm) for c in range(8)]
    gemm(cx, X=mT_d, kc=KC, wfn=wkv_fn, ntok=NMEM, colblocks=cbs, x_stream=False, Tt=256)
    P.barrier()
    wq = inp("wq_mem", [D, D])
    qmT = scratch("qmT", [KC, 128, NOWN], BF16)
    gemm(cx, X=cT_d, kc=KC, wfn=simple_w(wq), ntok=NOWN, colblocks=cblocks(8, EpiCopyFM(fm_dest(qmT), rstd=cx.rstd_a)),
         x_stream=False)
    P.barrier()
    omT = scratch("omT", [KC, 128, NOWN], BF16)
    dbg["omT"] = (omT, [KC, 128, NOWN], BF16)
    cross_attention(cx, qmT, kmT, VM, omT)
    P.barrier()
    if upto <= 11:
        return finish()
    wo = inp("wo_mem", [D, D])
    h2T = scratch("h2T", [KC, 128, NOWN], F32)
    dbg["h2T"] = (h2T, [KC, 128, NOWN], F32)
    nT_d = scratch("nT_d", [KC, 128, NOWN], BF16)
    gemm(cx, X=omT, kc=KC, wfn=simple_w(wo), ntok=NOWN,
         colblocks=cblocks(8, EpiResid(fm_dest(h1T), fm_dest(h2T), norm=dict(gi=3, hg_dest=fm_dest(nT_d), ncb=8))),
         x_stream=False)
    P.barrier()
    if upto <= 12:
        return finish()
    w_up0 = inp("w_up0", [D, DFF // 2])
    w_up1 = inp("w_up1", [D, DFF // 2])
    actT = scratch("actT", [DFF // 128, 128, NOWN], BF16)

    def wup_fn(k0, kn, c0, cn):
        w, c = (w_up0, c0) if c0 < DFF // 2 else (w_up1, c0 - DFF // 2)
        return w[k0 * 128:(k0 + kn) * 128, c:c + cn]
    gemm(cx, X=nT_d, kc=KC, wfn=wup_fn, ntok=NOWN, colblocks=cblocks(32, EpiRelu2(fm_dest(actT), r2=cx.r2_a)), x_stream=False)
    P.barrier()
    h3T = scratch("h3T", [KC, 128, NOWN], F32)
    dbg["h3T"] = (h3T, [KC, 128, NOWN], F32)

    def wdn_fn(k0, kn, c0, cn):
        return wdn_bf[k0 * 128:(k0 + kn) * 128, c0:c0 + cn]
    gemm(cx, X=actT, kc=DFF // 128, wfn=wdn_fn, ntok=NOWN,
         colblocks=cblocks(8, EpiResid(fm_dest(h2T), fm_dest(h3T), norm=dict(gi=4, hg_dest=None, ncb=8))),
         x_stream=True, NXS=2)
    P.barrier()
    outT = outp("outT", [NOWN // 256, 128, KC, 256])
    norm_pass(cx, h3T, outT, NOWN, 4, F32, TT=256, rstd_src=cx.rstd_a, dst_tiled=True)
    return finish()


def own_tokens(qh):
    blocks = [8 * j + 2 * i + qh for j in range(4) for i in range(4)]
    return np.concatenate([np.arange(bk * 128, (bk + 1) * 128) for bk in blocks])


def t5_bucket_np(rel):
    rel = np.asarray(rel, np.int32)
    ret = np.where(rel > 0, 16, 0).astype(np.int32)
    n = np.abs(rel)
    nf = np.maximum(n, 1).astype(np.float32)
    large = 8 + (np.log(nf / np.float32(8)) / np.float32(math.log(128 / 8)) * np.float32(8)).astype(np.int32)
    large = np.minimum(large, 15)
    return ret + np.where(n < 8, n, large)


def fox_mask_table(qh):
    m = np.zeros((128, 8, 4, 128), np.float32)
    kk = np.arange(128)[:, None]
    qq = np.arange(128)[None, :]
    diag = np.where(kk <= qq, 0.0, NEG).astype(np.float32)
    for r in range(8):
        for i in range(4):
            t = r - (2 * i + qh)
            if t == 0:
                m[:, r, i, :] = diag
            elif t > 0:
                m[:, r, i, :] = NEG
    return m.reshape(128, 8 * 512)


def t5_onehot():
    oh = np.zeros((33, 2, 128, 128), np.float32)
    q = np.arange(128)[:, None]
    k = np.arange(128)[None, :]
    bd = t5_bucket_np(k - q)
    allowed = (k // 64) <= (q // 64)
    bd = np.where(allowed, bd, 32)
    bn = t5_bucket_np(k - q - 128)
    for b in range(33):
        oh[b, 0] = (bd == b)
        oh[b, 1] = (bn == b)
    return oh.reshape(33, 2 * 128 * 128)


def prep_smalls(inp, qh):
    s = np.zeros((128, NS), np.float32)
    gs = [inp["g_mix"][0], inp["g_cross"][0], inp["g_mem"][0], inp["g_mlp"][0], inp["g_final"]]
    for gi, g in enumerate(gs):
        s[:, NS_G + gi * 32:NS_G + (gi + 1) * 32] = np.asarray(g, np.float32).reshape(32, 128).T
    s[:, NS_GSUB:NS_GSUB + 2] = np.asarray(inp["g_subln"][0], np.float32).reshape(2, 128).T
    s[:, NS_PAR + qh] = 1.0
    s[0:16, NS_BF] = np.asarray(inp["b_forget"][0], np.float32)
    s[0:32, NS_RB:NS_RB + 8] = np.asarray(inp["rel_bias"], np.float32)
    s[:, NS_RB15:NS_RB15 + 8] = np.asarray(inp["rel_bias"], np.float32)[15][None, :]
    lam = np.stack([inp["lambda_q1"][0], inp["lambda_k1"][0], inp["lambda_q2"][0], inp["lambda_k2"][0]])
    s[:, NS_LAM:NS_LAM + 512] = np.asarray(lam, np.float32).reshape(1, 512)
    s[:, NS_ID:NS_ID + 128] = np.eye(128, dtype=np.float32)
    s[127, NS_E127:NS_E127 + 128] = 1.0
    return s


def prep_core(inp, c, names):
    b, qh = c // 2, c % 2
    m = {}
    xT = None
    if "xa" in names or "xo" in names or "xot" in names:
        xT = np.ascontiguousarray(np.asarray(inp["x"][b], np.float32).T).reshape(KC, 128, S)
    if "xa" in names:
        m["xa"] = xT
    if "xo" in names:
        m["xo"] = np.ascontiguousarray(xT[:, :, own_tokens(qh)])
    if "xot" in names:
        m["xot"] = np.ascontiguousarray(m["xo"].reshape(KC, 128, NOWN // 256, 256).transpose(2, 1, 0, 3))
    if "xa" in names:
        m["xa"] = np.ascontiguousarray(xT.reshape(KC, 128, S // 256, 256).transpose(2, 1, 0, 3))
    if "memT" in names:
        m["memT"] = np.ascontiguousarray(np.asarray(inp["mem"][b], np.float32).T).reshape(KC, 128, NMEM)
    if "smalls" in names:
        m["smalls"] = prep_smalls(inp, qh)
    if "maskF" in names:
        m["maskF"] = fox_mask_table(qh)
    if "oh" in names:
        m["oh"] = t5_onehot()
    return m


def shared_inputs(inp, names):
    m = {}
    for nm in ("w_in", "w_out", "wq_mem", "wk_mem", "wv_mem", "wo_mem"):
        if nm in names:
            m[nm] = np.asarray(inp[nm][0], np.float32)
    if "w_up0" in names:
        w = np.asarray(inp["w_up"][0], np.float32)
        m["w_up0"] = np.ascontiguousarray(w[:, :DFF // 2])
        m["w_up1"] = np.ascontiguousarray(w[:, DFF // 2:])
    if "w_dn0" in names:
        w = np.asarray(inp["w_down"][0], np.float32)
        m["w_dn0"] = w[:DFF // 2]
        m["w_dn1"] = w[DFF // 2:]
    return m


def kernel(**inputs):
    nc, cx = build()
    sh = shared_inputs(inputs, cx.inputs)
    in_maps = []
    for c in range(8):
        m = prep_core(inputs, c, cx.inputs)
        m.update(sh)
        in_maps.append(m)
    res = run_bass_kernel_spmd(nc, in_maps, core_ids=list(range(8)))
    out = np.empty((4, S, D), np.float32)
    for c in range(8):
        b, qh = c // 2, c % 2
        o = np.asarray(res.results[c]["outT"], np.float32).reshape(NOWN // 256, 128, KC, 256)
        out[b, own_tokens(qh), :] = o.transpose(0, 3, 2, 1).reshape(NOWN, D)
    return out
```
